# Optimizing a Trainium2 kernel written in Bass

```python
import jax
import jax.numpy as jnp
from jax import lax
import numpy as np

D_MODEL = 1024
BATCH = 2
SEQ = 8192
DEPTH = 2
DEC_BATCH = 2
DEC_SEQ = 16384
PAST_LEN = 128

BRANCH_W = 256
N_BRANCH = 4
EPS = 1e-6
NEG_INF = -1e30

HEAD_DIM = 64
ATTN_GROUPS = ((128, 1), (512, 4), (2048, 16))
N_GROUPS = 3
HEADS_PER_GROUP = 4
N_ATTN_HEADS = N_GROUPS * HEADS_PER_GROUP
ATTN_QKV = N_ATTN_HEADS * HEAD_DIM
HALF_KEYS = 64
Q_BLOCK = 128

POOL_WINDOWS = (2, 4, 8, 16)
POOL_GROUP = BRANCH_W // 4

CHUNK = 128
SG_GROUPS = 4
SG_GROUP_W = BRANCH_W // SG_GROUPS

RWKV_HEADS = 4
RWKV_N = BRANCH_W // RWKV_HEADS
DECAY_RANK = 64
AAA_RANK = 64
LAT_W = DECAY_RANK + AAA_RANK
N_DIRS = 2
GN_EPS = 64e-5

SEG_WIDTHS = (3 * ATTN_QKV, BRANCH_W, BRANCH_W, BRANCH_W, 2 * BRANCH_W, BRANCH_W, 3 * BRANCH_W, N_DIRS * LAT_W, BRANCH_W, N_BRANCH * D_MODEL)
PROJ_WIDTH = 3 * ATTN_QKV + 10 * BRANCH_W + N_DIRS * LAT_W + N_BRANCH * D_MODEL

kernel_name = "hybrid_bidir_encoder"


def rms_norm(x, g):
    xf = x.astype(jnp.float32)
    y = xf * lax.rsqrt(jnp.mean(xf * xf, axis=-1, keepdims=True) + EPS)
    return (y * g.astype(jnp.float32)).astype(x.dtype)


def dilated_attention(q, k, v, q_norm_g, k_norm_g):
    B, S, _ = q.shape

    def split_heads(t):
        return t.reshape(B, S, N_GROUPS, HEADS_PER_GROUP, HEAD_DIM).transpose(0, 2, 3, 1, 4)

    qh = split_heads(rms_norm(q.reshape(B, S, N_ATTN_HEADS, HEAD_DIM), q_norm_g)) * (HEAD_DIM ** -0.5)
    kh = split_heads(rms_norm(k.reshape(B, S, N_ATTN_HEADS, HEAD_DIM), k_norm_g))
    vh = split_heads(v)
    dil = jnp.array([d for _, d in ATTN_GROUPS], dtype=jnp.int32)
    offsets = dil[:, None] * jnp.arange(-HALF_KEYS, HALF_KEYS + 1, dtype=jnp.int32)[None, :]
    slopes = 2.0 ** (-8.0 * jnp.arange(1, N_ATTN_HEADS + 1, dtype=jnp.float32) / N_ATTN_HEADS)
    bias = -slopes.reshape(N_GROUPS, HEADS_PER_GROUP)[:, :, None] * jnp.abs(offsets).astype(jnp.float32)[:, None, :]
    gather = jax.vmap(lambda t, ii: t[:, :, ii], in_axes=(1, 0), out_axes=1)

    def attend_block(i):
        start = i * Q_BLOCK
        pos = start + jnp.arange(Q_BLOCK, dtype=jnp.int32)
        idx = pos[None, :, None] + offsets[:, None, :]
        valid = (idx >= 0) & (idx < S)
        idx = jnp.clip(idx, 0, S - 1)
        qb = lax.dynamic_slice_in_dim(qh, start, Q_BLOCK, axis=3)
        kb = gather(kh, idx)
        vb = gather(vh, idx)
        s = jnp.einsum('bghqd,bghqwd->bghqw', qb, kb).astype(jnp.float32) + bias[None, :, :, None, :]
        s = jnp.where(valid[None, :, None], s, NEG_INF)
        lse = jax.nn.logsumexp(s, axis=-1)
        p = jnp.exp(s - lse[..., None])
        o = jnp.einsum('bghqw,bghqwd->bghqd', p.astype(vb.dtype), vb)
        alpha = jax.nn.softmax(lse, axis=1)
        return jnp.einsum('bghq,bghqd->bhqd', alpha.astype(o.dtype), o)

    out = lax.map(attend_block, jnp.arange(S // Q_BLOCK, dtype=jnp.int32))
    return out.transpose(1, 0, 3, 2, 4).reshape(B, S, HEADS_PER_GROUP * HEAD_DIM)


def pool_mixer(u, pool_w, pool_scale):
    B, S, _ = u.shape
    uf = u.astype(jnp.float32)
    t = jnp.arange(S)
    diffs = []
    for gi, w in enumerate(POOL_WINDOWS):
        h = w // 2
        ug = uf[..., gi * POOL_GROUP:(gi + 1) * POOL_GROUP]
        padded = jnp.pad(ug, ((0, 0), (h, h), (0, 0)))
        cs = jnp.concatenate([jnp.zeros_like(padded[:, :1]), jnp.cumsum(padded, axis=1)], axis=1)
        win_sum = cs[:, 2 * h:2 * h + S] - cs[:, :S]
        count = (jnp.minimum(t + h, S) - jnp.maximum(t - h, 0)).astype(jnp.float32)
        diffs.append(win_sum / count[None, :, None] - ug)
    d = jnp.stack(diffs, axis=2)
    y = jnp.einsum('bsgc,gce->bsge', d, pool_w.astype(jnp.float32)).reshape(B, S, BRANCH_W)
    return (y * pool_scale.astype(jnp.float32)).astype(u.dtype)


def spatial_gating(uv, sg_norm_g, sg_w, sg_b):
    B, S, _ = uv.shape
    u, v = jnp.split(uv, 2, axis=-1)
    v = rms_norm(v, sg_norm_g).reshape(B, S // CHUNK, CHUNK, SG_GROUPS, SG_GROUP_W)
    sv = jnp.einsum('gts,bcsgd->bctgd', sg_w, v) + sg_b.T[None, None, :, :, None]
    return u * sv.reshape(B, S, BRANCH_W)


def token_shift(x, direction):
    if direction == 0:
        return jnp.pad(x, ((0, 0), (1, 0), (0, 0)))[:, :-1]
    return jnp.pad(x, ((0, 0), (0, 1), (0, 0)))[:, 1:]


def rwkv_step(state, inp):
    r, decay, k, v, kk, a = inp[0], inp[1], inp[2], inp[3], inp[4], inp[5]
    sa = jnp.einsum('...ij,...j->...i', state, -kk)
    state = state * decay[..., None, :] + sa[..., :, None] * (kk * a)[..., None, :] + v[..., :, None] * k[..., None, :]
    return state, jnp.einsum('...ij,...j->...i', state, r)


def rwkv7_bidirectional(rkv, lat, mu_rkv, mu_lat, w0, w_up, a0, a_up, k_k, k_a, r_k, ln_g, ln_b):
    B, S, _ = rkv.shape
    dtype = rkv.dtype
    rkv = rkv.astype(jnp.float32)
    lat = lat.astype(jnp.float32)

    def heads(t):
        return t.reshape(t.shape[:-1] + (RWKV_HEADS, RWKV_N))

    lat_dirs = jnp.split(lat, N_DIRS, axis=-1)
    seqs, bonus = [], []
    for d in range(N_DIRS):
        xr = rkv + (token_shift(rkv, d) - rkv) * mu_rkv[d]
        xl = lat_dirs[d] + (token_shift(lat_dirs[d], d) - lat_dirs[d]) * mu_lat[d]
        r, k, v = (heads(t) for t in jnp.split(xr, 3, axis=-1))
        lw, la = jnp.split(xl, [DECAY_RANK], axis=-1)
        w_log = -jax.nn.softplus(-(w0[d] + jnp.tanh(lw) @ w_up[d])) - 0.5
        decay = heads(jnp.exp(-jnp.exp(w_log)))
        a = heads(jax.nn.sigmoid(a0[d] + la @ a_up[d]))
        kk = k * heads(k_k[d])
        kk = kk * lax.rsqrt(jnp.sum(kk * kk, axis=-1, keepdims=True) + 1e-12)
        k = k * (1.0 + (a - 1.0) * heads(k_a[d]))
        bonus.append((jnp.sum(r * k * heads(r_k[d]), axis=-1, keepdims=True) * v).reshape(B, S, BRANCH_W))
        step_in = jnp.stack([r, decay, k, v, kk, a], axis=0)
        if d == 1:
            step_in = step_in[:, :, ::-1]
        seqs.append(step_in)
    xs = jnp.moveaxis(jnp.stack(seqs, axis=1), 3, 0)
    state0 = jnp.zeros((N_DIRS, B, RWKV_HEADS, RWKV_N, RWKV_N), jnp.float32)
    _, o = lax.scan(rwkv_step, state0, xs)
    o = jnp.moveaxis(o, 0, 2)
    outs = (o[0], o[1][:, ::-1])
    y = 0.0
    for d in range(N_DIRS):
        od = outs[d]
        mu = jnp.mean(od, axis=-1, keepdims=True)
        var = jnp.mean(jnp.square(od - mu), axis=-1, keepdims=True)
        on = ((od - mu) * lax.rsqrt(var + GN_EPS)).reshape(B, S, BRANCH_W) * ln_g + ln_b
        y = y + on + bonus[d]
    return y.astype(dtype)


def encoder_layer(x, norm_g, w_in, q_norm_g, k_norm_g, pool_w, pool_scale, sg_norm_g, sg_w, sg_b,
                  mu_rkv, mu_lat, w0, w_up, a0, a_up, k_k, k_a, r_k, ln_g, ln_b, w_branch, w_out):
    B, S, _ = x.shape
    h = rms_norm(x, norm_g)
    proj = jnp.einsum('bsd,de->bse', h, w_in)
    split_points = [int(p) for p in np.cumsum(SEG_WIDTHS)[:-1]]
    qkv, z_a, u_b, z_b, uv_c, z_c, rkv_d, lat_d, z_d, gate_logits = jnp.split(proj, split_points, axis=-1)
    q, k, v = jnp.split(qkv, 3, axis=-1)
    y_a = dilated_attention(q, k, v, q_norm_g, k_norm_g)
    y_b = pool_mixer(u_b, pool_w, pool_scale)
    y_c = spatial_gating(uv_c, sg_norm_g, sg_w, sg_b)
    y_d = rwkv7_bidirectional(rkv_d, lat_d, mu_rkv, mu_lat, w0, w_up, a0, a_up, k_k, k_a, r_k, ln_g, ln_b)
    ys = jnp.stack([(y * jax.nn.silu(z)).astype(x.dtype)
                    for y, z in ((y_a, z_a), (y_b, z_b), (y_c, z_c), (y_d, z_d))], axis=2)
    branch = jnp.einsum('bsnc,ncd->bsnd', ys, w_branch)
    gates = jax.nn.sigmoid(gate_logits.reshape(B, S, N_BRANCH, D_MODEL))
    merged = jnp.sum(gates * branch, axis=2)
    return x + jnp.einsum('bsd,de->bse', merged, w_out)


def encoder_trunk(x, params):
    for l in range(DEPTH):
        x = encoder_layer(x, *[p[l] for p in params])
    return x


def setup_inputs(seed: int = 0) -> dict:
    key = jax.random.key(seed)
    ks = jax.random.split(key, 24)
    L = DEPTH

    def nrm(k, shape, scale):
        return jax.random.normal(k, shape, jnp.float32) * scale

    return {
        "x_prompt": nrm(ks[0], (BATCH, SEQ, D_MODEL), 1.0),
        "x_sample": nrm(ks[1], (DEC_BATCH, DEC_SEQ, D_MODEL), 1.0),
        "norm_g": 1.0 + nrm(ks[2], (L, D_MODEL), 0.02),
        "w_in": nrm(ks[3], (L, D_MODEL, PROJ_WIDTH), D_MODEL ** -0.5),
        "q_norm_g": 1.0 + nrm(ks[4], (L, HEAD_DIM), 0.02),
        "k_norm_g": 1.0 + nrm(ks[5], (L, HEAD_DIM), 0.02),
        "pool_w": nrm(ks[6], (L, len(POOL_WINDOWS), POOL_GROUP, POOL_GROUP), POOL_GROUP ** -0.5),
        "pool_scale": 1.0 + nrm(ks[7], (L, BRANCH_W), 0.02),
        "sg_norm_g": 1.0 + nrm(ks[8], (L, BRANCH_W), 0.02),
        "sg_w": nrm(ks[9], (L, SG_GROUPS, CHUNK, CHUNK), CHUNK ** -0.5),
        "sg_b": 1.0 + nrm(ks[10], (L, SG_GROUPS, CHUNK), 0.02),
        "mu_rkv": jax.random.uniform(ks[11], (L, N_DIRS, 3 * BRANCH_W), jnp.float32),
        "mu_lat": jax.random.uniform(ks[12], (L, N_DIRS, LAT_W), jnp.float32),
        "w0": jax.random.uniform(ks[13], (L, N_DIRS, BRANCH_W), jnp.float32, minval=-6.0, maxval=1.0),
        "w_up": nrm(ks[14], (L, N_DIRS, DECAY_RANK, BRANCH_W), 0.1),
        "a0": nrm(ks[15], (L, N_DIRS, BRANCH_W), 0.1),
        "a_up": nrm(ks[16], (L, N_DIRS, AAA_RANK, BRANCH_W), 0.1),
        "k_k": 0.85 + nrm(ks[17], (L, N_DIRS, BRANCH_W), 0.02),
        "k_a": 1.0 + nrm(ks[18], (L, N_DIRS, BRANCH_W), 0.02),
        "r_k": nrm(ks[19], (L, N_DIRS, BRANCH_W), 0.1),
        "ln_g": 1.0 + nrm(ks[20], (L, BRANCH_W), 0.02),
        "ln_b": nrm(ks[21], (L, BRANCH_W), 0.02),
        "w_branch": nrm(ks[22], (L, N_BRANCH, BRANCH_W, D_MODEL), BRANCH_W ** -0.5),
        "w_out": nrm(ks[23], (L, D_MODEL, D_MODEL), D_MODEL ** -0.5),
    }


def reference(x_prompt, x_sample, norm_g, w_in, q_norm_g, k_norm_g, pool_w, pool_scale, sg_norm_g, sg_w, sg_b,
              mu_rkv, mu_lat, w0, w_up, a0, a_up, k_k, k_a, r_k, ln_g, ln_b, w_branch, w_out):
    params = (norm_g, w_in, q_norm_g, k_norm_g, pool_w, pool_scale, sg_norm_g, sg_w, sg_b,
              mu_rkv, mu_lat, w0, w_up, a0, a_up, k_k, k_a, r_k, ln_g, ln_b, w_branch, w_out)
    y_prompt = encoder_trunk(x_prompt, params)
    y_sample = encoder_trunk(x_sample, params)
    return (y_prompt, y_sample)
```

```python
import numpy as np
import concourse.bass as bass
import concourse.mybir as mybir
from concourse.bass_utils import run_bass_kernel_spmd

F32 = mybir.dt.float32
BF16 = mybir.dt.bfloat16
AF = mybir.ActivationFunctionType
ALU = mybir.AluOpType
AX = mybir.AxisListType

EPOCH = 24000
NDMA_SEM = 12


class Res:
    __slots__ = ("name", "last_w", "reads")

    def __init__(self, name=""):
        self.name = name
        self.last_w = None
        self.reads = []


class Prog:
    ENG = ("pe", "act", "dve", "pool", "sp")

    def __init__(self, nc, same_engine_sync=True):
        self.nc = nc
        self.same_engine_sync = same_engine_sync
        self.ops = {e: [] for e in self.ENG}
        self.cnt = {e: 0 for e in self.ENG}
        self.sems = {e: [nc.alloc_semaphore(name=f"s_{e}_0")] for e in ("pe", "act", "dve", "pool")}
        self.seen = {e: {} for e in self.ENG}
        self.dma_sems = {q: [nc.alloc_semaphore(name=f"d_{q}_{k}") for k in range(NDMA_SEM)] for q in ("sp", "pool")}
        self.dma_n = {"sp": 0, "pool": 0}
        self.n_wait = 0

    def _deps(self, reads, writes):
        deps = []
        for r in reads:
            if r.last_w is not None:
                deps.extend(r.last_w)
        for w in writes:
            if w.last_w is not None:
                deps.extend(w.last_w)
            deps.extend(w.reads)
        return deps

    def _waits(self, e, deps, own_sem_ids):
        waits = {}
        seen = self.seen[e]
        for (sem, val) in deps:
            sid = id(sem)
            if sid in own_sem_ids and (e == "pe" or not self.same_engine_sync):
                continue
            if seen.get(sid, 0) >= val:
                continue
            if sid not in waits or waits[sid][1] < val:
                waits[sid] = (sem, val)
        for sid, (sem, val) in waits.items():
            seen[sid] = val
        self.n_wait += len(waits)
        return list(waits.values())

    def _commit(self, ev, reads, writes, is_dma=False):
        for r in reads:
            r.reads.append(ev)
        for w in writes:
            if is_dma and w.last_w is not None and not w.reads:
                w.last_w = w.last_w + [ev]
            else:
                w.last_w = [ev]
            w.reads = []

    def op(self, e, fn, reads=(), writes=()):
        deps = self._deps(reads, writes)
        own = {id(s) for s in self.sems[e]}
        waits = self._waits(e, deps, own)
        if self.cnt[e] >= EPOCH:
            self.sems[e].append(self.nc.alloc_semaphore(name=f"s_{e}_{len(self.sems[e])}"))
            self.cnt[e] = 0
        self.cnt[e] += 1
        sem = self.sems[e][-1]
        ev = (sem, self.cnt[e])
        self.ops[e].append((waits, fn, sem, 1))
        self._commit(ev, reads, writes)
        return ev

    def dma(self, q, out, in_, reads=(), writes=(), **kw):
        deps = self._deps(reads, writes)
        n = self.dma_n[q]
        self.dma_n[q] += 1
        k = n % NDMA_SEM
        use = n // NDMA_SEM
        sem = self.dma_sems[q][k]
        if use > 0:
            deps.append((sem, 16 * use))
        own = {id(s) for s in self.sems[q]} if q in self.sems else set()
        waits = self._waits(q, deps, own)
        ev = (sem, 16 * (use + 1))
        self.ops[q].append((waits, (lambda eng, o=out, i=in_, kw=kw: eng.dma_start(out=o, in_=i, **kw)), sem, 16))
        self._commit(ev, reads, writes, is_dma=True)
        return ev

    def final_wait(self, e, evs):
        waits = self._waits(e, list(evs), set())
        self.ops[e].append((waits, None, None, 0))

    def emit(self):
        nc = self.nc
        ops = self.ops

        def run(eng, lst):
            for (waits, fn, sem, inc) in lst:
                for (s, v) in waits:
                    eng.wait_ge(s, v)
                if fn is not None:
                    ins = fn(eng)
                    ins.then_inc(sem, inc)

        with nc.Block() as block:
            @block.tensor
            def _(eng):
                run(eng, ops["pe"])

            @block.scalar
            def _(eng):
                run(eng, ops["act"])

            @block.vector
            def _(eng):
                run(eng, ops["dve"])

            @block.gpsimd
            def _(eng):
                run(eng, ops["pool"])

            @block.sync
            def _(eng):
                run(eng, ops["sp"])


D = 1024
PW = 9216
P1W = 5120
EPS = 1e-6
GN_EPS = 64e-5
ATT_W = (1, 2, 8)
ATT_D = (1, 4, 16)
C_Q, C_K, C_V, C_ZA, C_UB, C_ZB, C_UVC, C_ZC, C_RKV, C_LAT, C_ZD = 0, 768, 1536, 2304, 2560, 2816, 3072, 3584, 3840, 4608, 4864


class Ctx:
    pass


_UID = [0]


def uname(n):
    _UID[0] += 1
    return f"{n}_u{_UID[0]}"


DEBUG_SCRATCH = False
DEBUG_BARRIER = False
DBG_DIR = 0
DBG_TILE = 0


def build_program(S, L=2, branches=(0, 1, 2, 3), same_engine_sync=True):
    from contextlib import ExitStack
    NT = S // 128
    nc = bass.Bass("TRN2", target_bir_lowering=False)
    P = Prog(nc, same_engine_sync=same_engine_sync)
    c = Ctx()
    c.nc, c.P, c.S, c.NT, c.L, c.branches = nc, P, S, NT, L, branches

    def din(name, shape, dt=F32):
        return nc.dram_tensor(name, list(shape), dt, kind="ExternalInput").ap()

    def dscr(name, shape, dt=F32):
        return nc.dram_tensor(name, list(shape), dt, kind=("ExternalOutput" if DEBUG_SCRATCH else "Internal")).ap()

    c.x_in = din("x", [S, D])
    c.valid = din("valid", [128, NT])
    c.invc = din("invc", [128, NT * 4])
    c.ident = din("ident", [128, 128])
    c.amask = din("amask", [25, 128, 512])
    c.bandc = din("bandc", [128, 4 * 128])
    c.bandh = din("bandh", [16, 4 * 128])
    c.tri = din("tri", [128, 642])
    c.w = {}
    for name, shape in (("norm_g", [L, D]), ("w_in", [L, D, PW]), ("q_norm_g", [L, 64]), ("k_norm_g", [L, 64]),
                        ("pool_w", [L, 4, 64, 64]), ("pool_scale", [L, 256]), ("sg_norm_g", [L, 256]),
                        ("sg_wT", [L, 4, 128, 128]), ("sg_bT", [L, 128, 4]), ("mu_rkv", [L, 2, 768]), ("mu_lat", [L, 2, 128]),
                        ("w0", [L, 2, 256]), ("w_up", [L, 2, 64, 256]), ("a0", [L, 2, 256]), ("a_up", [L, 2, 64, 256]),
                        ("k_k", [L, 2, 256]), ("k_a", [L, 2, 256]), ("r_k", [L, 2, 256]), ("ln_g", [L, 256]), ("ln_b", [L, 256]),
                        ("w_branch", [L, 4, 256, D]), ("w_out", [L, D, D])):
        c.w[name] = din(name, shape)
    c.y_out = nc.dram_tensor("y", [S, D], F32, kind="ExternalOutput").ap()

    c.x1 = dscr("x1", [S, D])
    c.qT = dscr("qT_scr", [NT, 64, 1536], BF16)
    c.kT = dscr("kT_scr", [NT, 64, 1536], BF16)
    c.va = dscr("va_scr", [NT, 128, 780], BF16)
    c.zs = dscr("zs_scr", [S, 1024])
    c.ub = dscr("ub_scr", [S + 32, 256])
    c.uvc = dscr("uvc_scr", [S, 512])
    c.rkvl = dscr("rkvl_scr", [S + 2, 1024])
    c.ys = dscr("ys_scr", [S, 1024], BF16)
    c.osc = dscr("o_scr", [2, S, 256])
    c.r = {k: [Res(f"{k}{i}") for i in range(NT)] for k in ("x1", "qT", "kT", "va", "zs", "ub", "uvc", "rkvl", "ys", "osc0", "osc1", "y")}
    c.r_pad = Res("pads")
    if DEBUG_SCRATCH:
        c.dbg_proj = dscr("dbg_proj", [S, P1W])
        c.dbg_hT = dscr("dbg_hT", [S, D], BF16)

    c.idb = nc.alloc_sbuf_tensor("idb", [128, 128], BF16)
    c.idf = nc.alloc_sbuf_tensor("idf", [128, 128], F32)
    c.zero = nc.alloc_sbuf_tensor("zero", [128, 1024], F32)
    c.validt = nc.alloc_sbuf_tensor("validt", [128, NT], F32)
    c.r_const = Res("const")
    P.dma("sp", c.idf[:], c.ident, writes=[c.r_const])
    P.dma("pool", c.idb[:], c.ident, writes=[c.r_const])
    P.dma("sp", c.validt[:], c.valid, writes=[c.r_const])
    P.op("pool", lambda e: e.memset(c.zero[:], 0.0), [], [c.r_const])
    P.dma("sp", c.ub[0:16, :], c.zero[0:16, 0:256], reads=[c.r_const], writes=[c.r_pad])
    P.dma("sp", c.ub[S + 16:S + 32, :], c.zero[0:16, 0:256], reads=[c.r_const], writes=[c.r_pad])
    P.dma("sp", c.rkvl[0:1, :], c.zero[0:1, :], reads=[c.r_const], writes=[c.r_pad])
    P.dma("sp", c.rkvl[S + 1:S + 2, :], c.zero[0:1, :], reads=[c.r_const], writes=[c.r_pad])

    c.pbf = [nc.alloc_psum_tensor(f"pbf{i}", [128, 1024], BF16) for i in range(2)]
    c.pf = [nc.alloc_psum_tensor(f"pf{i}", [128, 512], F32) for i in range(6)]
    c.r_pbf = [Res(f"pbf{i}") for i in range(2)]
    c.r_pf = [Res(f"pf{i}") for i in range(6)]

    for l in range(L):
        x_src = c.x_in if l == 0 else c.x1
        x_src_res = None if l == 0 else c.r["x1"]
        x_dst = c.x1 if l < L - 1 else c.y_out
        x_dst_res = c.r["x1"] if l < L - 1 else c.r["y"]
        phase1(c, l, x_src, x_src_res)
        barrier(c)
        if 2 in branches:
            phase2c(c, l)
            barrier(c)
        if 1 in branches:
            phase2b(c, l)
            barrier(c)
        if 0 in branches:
            phase2a(c, l)
            barrier(c)
        if 3 in branches:
            phase2d(c, l)
            barrier(c)
        phase3(c, l, x_src, x_src_res, x_dst, x_dst_res)
        barrier(c)
    P.emit()
    return nc


def barrier(c):
    P = c.P
    evs = []
    for e in ("pe", "act", "dve", "pool"):
        if P.cnt[e] > 0:
            evs.append((P.sems[e][-1], P.cnt[e]))
    for q in ("sp", "pool"):
        n = P.dma_n[q]
        for k, s in enumerate(P.dma_sems[q]):
            uses = (n - k + NDMA_SEM - 1) // NDMA_SEM if n > k else 0
            if uses > 0:
                evs.append((s, 16 * uses))
    for e in Prog.ENG:
        P.final_wait(e, evs)


class Pool2:
    def __init__(self, stack, nc, name, shape, dt, n=2):
        self.t = [stack.enter_context(nc.sbuf_tensor(uname(f"{name}{i}"), list(shape), dt)) for i in range(n)]
        self.r = [Res(f"{name}{i}") for i in range(n)]
        self.n = n
        self.i = -1

    def next(self):
        self.i = (self.i + 1) % self.n
        return self.t[self.i], self.r[self.i]


def rmsnorm_T(c, st, xt, r_xt, gbc, r_w, bufs, width=1024):
    P = c.P
    junk, r_junk = bufs["junk"].next()
    ss, r_ss = bufs["ss"].next()
    h, r_h = bufs["h"].next()
    hT, r_hT = bufs["hT"].next()
    P.op("act", lambda e: e.activation(out=junk[:], in_=xt[:], func=AF.Square, accum_out=ss[:, 0:1]), [r_xt], [r_junk, r_ss])
    P.op("dve", lambda e: e.tensor_scalar(out=ss[:, 1:2], in0=ss[:, 0:1], scalar1=1.0 / width, scalar2=EPS, op0=ALU.mult, op1=ALU.add), [r_ss], [r_ss])
    P.op("act", lambda e: e.activation(out=ss[:, 2:3], in_=ss[:, 1:2], func=AF.Sqrt), [r_ss], [r_ss])
    P.op("dve", lambda e: e.reciprocal(out=ss[:, 3:4], in_=ss[:, 2:3]), [r_ss], [r_ss])
    P.op("dve", lambda e: e.scalar_tensor_tensor(out=h[:], in0=xt[:], scalar=ss[:, 3:4], in1=gbc[:], op0=ALU.mult, op1=ALU.mult),
         [r_xt, r_ss, r_w], [r_h])
    nk = width // 128
    pb, r_pb = c.pbf[0], c.r_pbf[0]
    for kc in range(nk):
        P.op("pe", lambda e, kc=kc: e.transpose(out=pb[:, kc * 128:(kc + 1) * 128], in_=h[:, kc * 128:(kc + 1) * 128], identity=c.idb[:]),
             [r_h, c.r_const], [r_pb])
    P.op("act", lambda e: e.copy(out=hT[:, 0:width], in_=pb[:, 0:width]), [r_pb], [r_hT])
    return hT, r_hT


def phase1(c, l, x_src, x_src_res):
    from contextlib import ExitStack
    P, nc, NT = c.P, c.nc, c.NT
    with ExitStack() as st:
        sb = lambda n, s, d: st.enter_context(nc.sbuf_tensor(uname(n), list(s), d))
        W = sb("W1", [128, 8, P1W], BF16)
        r_W = Res("W1")
        for kc in range(8):
            P.dma("pool", W[:, kc, :], c.w["w_in"][l, kc * 128:(kc + 1) * 128, 0:P1W], writes=[r_W])
        gbc = sb("gbc", [128, D], F32)
        g64 = sb("g64", [128, 128], F32)
        gqk = sb("gqk", [128, 1536], F32)
        r_w = Res("p1w")
        P.dma("sp", gbc[:], c.w["norm_g"][l:l + 1, :].partition_broadcast(128), writes=[r_w])
        P.dma("sp", g64[:, 0:64], c.w["q_norm_g"][l:l + 1, :].partition_broadcast(128), writes=[r_w])
        P.dma("sp", g64[:, 64:128], c.w["k_norm_g"][l:l + 1, :].partition_broadcast(128), writes=[r_w])
        P.op("dve", lambda e: e.tensor_scalar(out=gqk[:, 0:768].rearrange("p (h d) -> p h d", d=64),
                                              in0=g64[:, 0:64].unsqueeze(1).to_broadcast([128, 12, 64]),
                                              scalar1=0.125, scalar2=None, op0=ALU.mult), [r_w], [r_w])
        P.op("dve", lambda e: e.tensor_copy(out=gqk[:, 768:1536].rearrange("p (h d) -> p h d", d=64),
                                            in_=g64[:, 64:128].unsqueeze(1).to_broadcast([128, 12, 64])), [r_w], [r_w])
        bufs = {"junk": Pool2(st, nc, "junk", [128, D], BF16, 1), "ss": Pool2(st, nc, "ss", [128, 4], F32, 2),
                "h": Pool2(st, nc, "h", [128, D], BF16, 2), "hT": Pool2(st, nc, "hT", [128, D], BF16, 2)}
        if DEBUG_BARRIER:
            barrier(c)
        xb = Pool2(st, nc, "xt", [128, D], F32, 2)
        pj = Pool2(st, nc, "proj", [128, P1W], F32, 2)
        sqb = Pool2(st, nc, "sq", [128, 1536], F32, 1)
        s24 = Pool2(st, nc, "s24", [128, 72], F32, 2)
        qkn = Pool2(st, nc, "qkn", [128, 1536], BF16, 2)
        qkT = Pool2(st, nc, "qkT", [64, 3072], BF16, 2)
        vab = Pool2(st, nc, "vaug", [128, 780], BF16, 2)
        zsb = Pool2(st, nc, "zsb", [128, 1024], F32, 2)
        def body(i):
            rows = slice(i * 128, (i + 1) * 128)
            xt, r_xt = xb.next()
            P.dma("sp", xt[:], x_src[rows, :], reads=([x_src_res[i]] if x_src_res else []), writes=[r_xt])
            hT, r_hT = rmsnorm_T(c, st, xt, r_xt, gbc, r_w, bufs)
            proj, r_pj = pj.next()
            for blk in range(P1W // 512):
                b = blk % 4
                ps, r_ps = c.pf[b], c.r_pf[b]
                for kc in range(8):
                    P.op("pe", lambda e, kc=kc, blk=blk, ps=ps: e.matmul(out=ps[:, :], lhsT=hT[:, kc * 128:(kc + 1) * 128],
                                                                    rhs=W[:, kc, blk * 512:(blk + 1) * 512], start=(kc == 0), stop=(kc == 7)),
                         [r_hT, r_W], [r_ps])
                if blk % 2 == 0:
                    P.op("act", lambda e, blk=blk, ps=ps: e.copy(out=proj[:, blk * 512:(blk + 1) * 512], in_=ps[:, :]), [r_ps], [r_pj])
                else:
                    P.op("dve", lambda e, blk=blk, ps=ps: e.tensor_copy(out=proj[:, blk * 512:(blk + 1) * 512], in_=ps[:, :]), [r_ps], [r_pj])
            if DEBUG_SCRATCH:
                P.dma("pool", c.dbg_proj[rows, :], proj[:, :], reads=[r_pj], writes=[Res()])
                P.dma("pool", c.dbg_hT[rows, :], hT[:, :], reads=[r_hT], writes=[Res()])
            if 1 in c.branches:
                P.dma("pool", c.ub[16 + i * 128:16 + (i + 1) * 128, :], proj[:, C_UB:C_UB + 256], reads=[r_pj], writes=[c.r["ub"][i]])
            if 2 in c.branches:
                P.dma("pool", c.uvc[rows, :], proj[:, C_UVC:C_UVC + 512], reads=[r_pj], writes=[c.r["uvc"][i]])
            if 3 in c.branches:
                P.dma("pool", c.rkvl[1 + i * 128:1 + (i + 1) * 128, :], proj[:, C_RKV:C_RKV + 1024], reads=[r_pj], writes=[c.r["rkvl"][i]])
            zt, r_zt = zsb.next()
            for b, cz in enumerate((C_ZA, C_ZB, C_ZC, C_ZD)):
                P.op("act", lambda e, b=b, cz=cz: e.activation(out=zt[:, b * 256:(b + 1) * 256], in_=proj[:, cz:cz + 256], func=AF.Silu),
                     [r_pj], [r_zt])
            P.dma("pool", c.zs[rows, :], zt[:], reads=[r_zt], writes=[c.r["zs"][i]])
            if 0 in c.branches:
                sq, r_sq = sqb.next()
                s, r_s = s24.next()
                qn, r_qn = qkn.next()
                P.op("pool", lambda e: e.tensor_tensor(out=sq[:], in0=proj[:, 0:1536], in1=proj[:, 0:1536], op=ALU.mult), [r_pj], [r_sq])
                P.op("dve", lambda e: e.tensor_reduce(out=s[:, 0:24], in_=sq[:].rearrange("p (h d) -> p h d", d=64), axis=AX.X, op=ALU.add), [r_sq], [r_s])
                P.op("dve", lambda e: e.tensor_scalar(out=s[:, 24:48], in0=s[:, 0:24], scalar1=1.0 / 64, scalar2=EPS, op0=ALU.mult, op1=ALU.add), [r_s], [r_s])
                P.op("act", lambda e: e.activation(out=s[:, 0:24], in_=s[:, 24:48], func=AF.Sqrt), [r_s], [r_s])
                P.op("dve", lambda e: e.reciprocal(out=s[:, 48:72], in_=s[:, 0:24]), [r_s], [r_s])
                P.op("dve", lambda e: e.tensor_tensor(out=sq[:].rearrange("p (h d) -> p h d", d=64), in0=proj[:, 0:1536].rearrange("p (h d) -> p h d", d=64),
                                                      in1=s[:, 48:72].unsqueeze(2).to_broadcast([128, 24, 64]), op=ALU.mult), [r_pj, r_s], [r_sq])
                P.op("pool", lambda e: e.tensor_tensor(out=qn[:], in0=sq[:], in1=gqk[:], op=ALU.mult), [r_sq, r_w], [r_qn])
                qT, r_qT = qkT.next()
                for grp in range(3):
                    pb, r_pb = c.pbf[1], c.r_pbf[1]
                    for j in range(8):
                        hh = grp * 8 + j
                        P.op("pe", lambda e, hh=hh, j=j: e.transpose(out=pb[0:64, j * 128:(j + 1) * 128], in_=qn[:, hh * 64:(hh + 1) * 64], identity=c.idb[:]),
                             [r_qn, c.r_const], [r_pb])
                    if grp % 2 == 0:
                        P.op("dve", lambda e, grp=grp: e.tensor_copy(out=qT[:, grp * 1024:(grp + 1) * 1024], in_=pb[0:64, :]), [r_pb], [r_qT])
                    else:
                        P.op("act", lambda e, grp=grp: e.copy(out=qT[:, grp * 1024:(grp + 1) * 1024], in_=pb[0:64, :]), [r_pb], [r_qT])
                P.dma("pool", c.qT[i], qT[:, 0:1536], reads=[r_qT], writes=[c.r["qT"][i]])
                P.dma("pool", c.kT[i], qT[:, 1536:3072], reads=[r_qT], writes=[c.r["kT"][i]])
                va, r_va = vab.next()
                P.op("act", lambda e: e.copy(out=va[:].rearrange("p (h d) -> p h d", d=65)[:, :, 0:64],
                                             in_=proj[:, C_V:C_V + 768].rearrange("p (h d) -> p h d", d=64)), [r_pj], [r_va])
                P.op("dve", lambda e, i=i: e.tensor_copy(out=va[:].rearrange("p (h d) -> p h d", d=65)[:, :, 64:65],
                                                    in_=c.validt[:, i:i + 1].unsqueeze(1).to_broadcast([128, 12, 1])), [c.r_const], [r_va])
                P.dma("pool", c.va[i], va[:], reads=[r_va], writes=[c.r["va"][i]])

        for i in range(NT):
            body(i)


def phase2c(c, l):
    from contextlib import ExitStack
    P, nc, NT = c.P, c.nc, c.NT
    with ExitStack() as st:
        sb = lambda n, s, d: st.enter_context(nc.sbuf_tensor(uname(n), list(s), d))
        sgw = sb("sgw", [128, 4, 128], BF16)
        sgn = sb("sgn", [128, 256], F32)
        sgb = sb("sgb", [128, 4], F32)
        r_w = Res("p2cw")
        P.dma("pool", sgw[:], c.w["sg_wT"][l].rearrange("g s t -> s g t"), writes=[r_w])
        P.dma("sp", sgn[:], c.w["sg_norm_g"][l:l + 1, :].partition_broadcast(128), writes=[r_w])
        P.dma("sp", sgb[:], c.w["sg_bT"][l], writes=[r_w])
        uvb = Pool2(st, nc, "uv", [128, 512], F32, 2)
        zcb = Pool2(st, nc, "zc", [128, 256], F32, 2)
        jb = Pool2(st, nc, "junkc", [128, 256], F32, 1)
        ssb = Pool2(st, nc, "ssc", [128, 4], F32, 2)
        vnb = Pool2(st, nc, "vn", [128, 256], BF16, 2)
        svb = Pool2(st, nc, "sv", [128, 256], F32, 2)
        yb = Pool2(st, nc, "ysc", [128, 256], BF16, 2)
        def body(i):
            rows = slice(i * 128, (i + 1) * 128)
            uv, r_uv = uvb.next()
            zc, r_zc = zcb.next()
            P.dma("sp", uv[:], c.uvc[rows, :], reads=[c.r["uvc"][i]], writes=[r_uv])
            P.dma("sp", zc[:], c.zs[rows, 512:768], reads=[c.r["zs"][i]], writes=[r_zc])
            junk, r_j = jb.next()
            ss, r_ss = ssb.next()
            vn, r_vn = vnb.next()
            sv, r_sv = svb.next()
            ysc, r_y = yb.next()
            P.op("act", lambda e: e.activation(out=junk[:], in_=uv[:, 256:512], func=AF.Square, accum_out=ss[:, 0:1]), [r_uv], [r_j, r_ss])
            P.op("dve", lambda e: e.tensor_scalar(out=ss[:, 1:2], in0=ss[:, 0:1], scalar1=1.0 / 256, scalar2=EPS, op0=ALU.mult, op1=ALU.add), [r_ss], [r_ss])
            P.op("act", lambda e: e.activation(out=ss[:, 2:3], in_=ss[:, 1:2], func=AF.Sqrt), [r_ss], [r_ss])
            P.op("dve", lambda e: e.reciprocal(out=ss[:, 3:4], in_=ss[:, 2:3]), [r_ss], [r_ss])
            P.op("dve", lambda e: e.scalar_tensor_tensor(out=vn[:], in0=uv[:, 256:512], scalar=ss[:, 3:4], in1=sgn[:], op0=ALU.mult, op1=ALU.mult),
                 [r_uv, r_ss, r_w], [r_vn])
            ps, r_ps = c.pf[i % 2], c.r_pf[i % 2]
            for g in range(4):
                P.op("pe", lambda e, g=g, ps=ps: e.matmul(out=ps[:, g * 64:(g + 1) * 64], lhsT=sgw[:, g, :], rhs=vn[:, g * 64:(g + 1) * 64], start=True, stop=True),
                     [r_vn, r_w], [r_ps])
            P.op("dve", lambda e, ps=ps: e.tensor_tensor(out=sv[:].rearrange("p (g d) -> p g d", d=64), in0=ps[:, 0:256].rearrange("p (g d) -> p g d", d=64),
                                                  in1=sgb[:].unsqueeze(2).to_broadcast([128, 4, 64]), op=ALU.add), [r_ps, r_w], [r_sv])
            P.op("pool", lambda e: e.tensor_tensor(out=sv[:], in0=sv[:], in1=uv[:, 0:256], op=ALU.mult), [r_sv, r_uv], [r_sv])
            P.op("dve", lambda e: e.tensor_tensor(out=ysc[:], in0=sv[:], in1=zc[:], op=ALU.mult), [r_sv, r_zc], [r_y])
            P.dma("pool", c.ys[rows, 512:768], ysc[:], reads=[r_y], writes=[c.r["ys"][i]])

        for i in range(NT):
            body(i)


def phase3(c, l, x_src, x_src_res, x_dst, x_dst_res):
    from contextlib import ExitStack
    P, nc, NT = c.P, c.nc, c.NT
    with ExitStack() as st:
        sb = lambda n, s, d: st.enter_context(nc.sbuf_tensor(uname(n), list(s), d))
        Wg = sb("Wg", [128, 8, 4096], BF16)
        Wbr = sb("Wbr", [128, 8, 1024], BF16)
        Wo = sb("Wo", [128, 8, 1024], BF16)
        gbc = sb("gbc3", [128, D], F32)
        r_W = Res("W3")
        for kc in range(8):
            P.dma("pool", Wg[:, kc, :], c.w["w_in"][l, kc * 128:(kc + 1) * 128, P1W:PW], writes=[r_W])
        wbr_flat = c.w["w_branch"][l].rearrange("b c n -> (b c) n")
        for kc in range(8):
            P.dma("pool", Wbr[:, kc, :], wbr_flat[kc * 128:(kc + 1) * 128, :], writes=[r_W])
            P.dma("pool", Wo[:, kc, :], c.w["w_out"][l, kc * 128:(kc + 1) * 128, :], writes=[r_W])
        P.dma("sp", gbc[:], c.w["norm_g"][l:l + 1, :].partition_broadcast(128), writes=[r_W])
        bufs = {"junk": Pool2(st, nc, "junk3", [128, D], BF16, 1), "ss": Pool2(st, nc, "ss3", [128, 4], F32, 2),
                "h": Pool2(st, nc, "h3", [128, D], BF16, 2), "hT": Pool2(st, nc, "hT3", [128, D], BF16, 2)}
        xb = Pool2(st, nc, "xt3", [128, D], F32, 2)
        ysb = Pool2(st, nc, "ys3", [128, D], BF16, 2)
        ysTb = Pool2(st, nc, "ysT3", [128, D], BF16, 2)
        gsb = Pool2(st, nc, "gs3", [128, 512], F32, 2)
        tmpb = Pool2(st, nc, "tmp3", [128, 512], F32, 2)
        accb = Pool2(st, nc, "acc3", [128, D], F32, 2)
        mb = Pool2(st, nc, "m3", [128, D], BF16, 2)
        mTb = Pool2(st, nc, "mT3", [128, D], BF16, 2)
        xob = Pool2(st, nc, "xo3", [128, D], F32, 2)
        nbr = len(c.branches)
        def body(i):
            rows = slice(i * 128, (i + 1) * 128)
            xt, r_xt = xb.next()
            P.dma("sp", xt[:], x_src[rows, :], reads=([x_src_res[i]] if x_src_res else []), writes=[r_xt])
            ys, r_ys = ysb.next()
            P.dma("sp", ys[:], c.ys[rows, :], reads=[c.r["ys"][i]], writes=[r_ys])
            hT, r_hT = rmsnorm_T(c, st, xt, r_xt, gbc, r_W, bufs)
            ysT, r_ysT = ysTb.next()
            pb, r_pb = c.pbf[1], c.r_pbf[1]
            for kc in range(8):
                P.op("pe", lambda e, kc=kc: e.transpose(out=pb[:, kc * 128:(kc + 1) * 128], in_=ys[:, kc * 128:(kc + 1) * 128], identity=c.idb[:]),
                     [r_ys, c.r_const], [r_pb])
            P.op("dve", lambda e: e.tensor_copy(out=ysT[:], in_=pb[:, :]), [r_pb], [r_ysT])
            acc, r_acc = accb.next()
            for bi, b in enumerate(c.branches):
                for cb in range(2):
                    pg, r_pg = c.pf[(2 * bi + cb) % 2], c.r_pf[(2 * bi + cb) % 2]
                    pbr, r_pbr = c.pf[2 + (2 * bi + cb) % 2], c.r_pf[2 + (2 * bi + cb) % 2]
                    col = b * 1024 + cb * 512
                    for kc in range(8):
                        P.op("pe", lambda e, kc=kc, col=col, pg=pg: e.matmul(out=pg[:, :], lhsT=hT[:, kc * 128:(kc + 1) * 128], rhs=Wg[:, kc, col:col + 512],
                                                                      start=(kc == 0), stop=(kc == 7)), [r_hT, r_W], [r_pg])
                    gs, r_gs = gsb.next()
                    P.op("act", lambda e, pg=pg, gs=gs: e.activation(out=gs[:], in_=pg[:, :], func=AF.Sigmoid), [r_pg], [r_gs])
                    for kk in range(2):
                        kc = 2 * b + kk
                        P.op("pe", lambda e, kc=kc, kk=kk, cb=cb, pbr=pbr: e.matmul(out=pbr[:, :], lhsT=ysT[:, kc * 128:(kc + 1) * 128],
                                                                             rhs=Wbr[:, kc, cb * 512:(cb + 1) * 512], start=(kk == 0), stop=(kk == 1)),
                             [r_ysT, r_W], [r_pbr])
                    if bi == 0:
                        P.op("dve", lambda e, cb=cb, gs=gs, pbr=pbr: e.tensor_tensor(out=acc[:, cb * 512:(cb + 1) * 512], in0=gs[:], in1=pbr[:, :], op=ALU.mult),
                             [r_gs, r_pbr], [r_acc])
                    else:
                        tmp, r_tmp = tmpb.next()
                        P.op("dve", lambda e, gs=gs, pbr=pbr, tmp=tmp: e.tensor_tensor(out=tmp[:], in0=gs[:], in1=pbr[:, :], op=ALU.mult), [r_gs, r_pbr], [r_tmp])
                        P.op("pool", lambda e, cb=cb, tmp=tmp: e.tensor_tensor(out=acc[:, cb * 512:(cb + 1) * 512], in0=acc[:, cb * 512:(cb + 1) * 512], in1=tmp[:], op=ALU.add),
                             [r_tmp, r_acc], [r_acc])
            m, r_m = mb.next()
            mT, r_mT = mTb.next()
            P.op("act", lambda e: e.copy(out=m[:], in_=acc[:]), [r_acc], [r_m])
            pb, r_pb = c.pbf[1], c.r_pbf[1]
            for kc in range(8):
                P.op("pe", lambda e, kc=kc: e.transpose(out=pb[:, kc * 128:(kc + 1) * 128], in_=m[:, kc * 128:(kc + 1) * 128], identity=c.idb[:]),
                     [r_m, c.r_const], [r_pb])
            P.op("act", lambda e: e.copy(out=mT[:], in_=pb[:, :]), [r_pb], [r_mT])
            xo, r_xo = xob.next()
            for cb in range(2):
                po, r_po = c.pf[4 + cb], c.r_pf[4 + cb]
                for kc in range(8):
                    P.op("pe", lambda e, kc=kc, cb=cb, po=po: e.matmul(out=po[:, :], lhsT=mT[:, kc * 128:(kc + 1) * 128], rhs=Wo[:, kc, cb * 512:(cb + 1) * 512],
                                                                start=(kc == 0), stop=(kc == 7)), [r_mT, r_W], [r_po])
                P.op("dve", lambda e, cb=cb, po=po: e.tensor_tensor(out=xo[:, cb * 512:(cb + 1) * 512], in0=po[:, :], in1=xt[:, cb * 512:(cb + 1) * 512], op=ALU.add),
                     [r_po, r_xt], [r_xo])
            P.dma("pool", x_dst[rows, :], xo[:], reads=[r_xo], writes=[x_dst_res[i]])

        for i in range(NT):
            body(i)


def phase2b(c, l):
    from contextlib import ExitStack
    P, nc, NT = c.P, c.nc, c.NT
    with ExitStack() as st:
        sb = lambda n, s, d: st.enter_context(nc.sbuf_tensor(uname(n), list(s), d))
        bandc = sb("bandc", [128, 512], F32)
        bandh = sb("bandh", [16, 512], F32)
        pw = sb("pw", [128, 2, 128], BF16)
        psc = sb("psc", [128, 256], F32)
        invc = sb("invc", [128, NT * 4], F32)
        r_w = Res("p2bw")
        P.dma("sp", bandc[:], c.bandc, writes=[r_w])
        P.dma("sp", bandh[:], c.bandh, writes=[r_w])
        P.dma("sp", psc[:], c.w["pool_scale"][l:l + 1, :].partition_broadcast(128), writes=[r_w])
        P.dma("sp", invc[:], c.invc, writes=[r_w])
        P.op("pool", lambda e: e.memset(pw[:], 0.0), [], [r_w])
        for g in range(4):
            j, gl = g // 2, g % 2
            P.dma("pool", pw[gl * 64:(gl + 1) * 64, j, gl * 64:(gl + 1) * 64], c.w["pool_w"][l, g], writes=[r_w])
        ub_ = Pool2(st, nc, "ub", [128, 256], F32, 2)
        uh_ = Pool2(st, nc, "uh", [16, 256], F32, 2)
        zb_ = Pool2(st, nc, "zb", [128, 256], F32, 2)
        d_ = Pool2(st, nc, "dpool", [128, 256], BF16, 2)
        dT_ = Pool2(st, nc, "dT", [128, 256], BF16, 2)
        y_ = Pool2(st, nc, "ybf", [128, 256], F32, 2)
        yo_ = Pool2(st, nc, "ybo", [128, 256], BF16, 2)

        def body(i):
            rows = slice(i * 128, (i + 1) * 128)
            u, r_u = ub_.next()
            uh, r_uh = uh_.next()
            zb, r_zb = zb_.next()
            d, r_d = d_.next()
            dT, r_dT = dT_.next()
            y, r_y = y_.next()
            yo, r_yo = yo_.next()
            nb = [c.r["ub"][j] for j in (i - 1, i, i + 1) if 0 <= j < NT] + [c.r_pad]
            P.dma("sp", u[:], c.ub[16 + i * 128:16 + (i + 1) * 128, :], reads=[c.r["ub"][i]], writes=[r_u])
            P.dma("sp", uh[0:8, :], c.ub[16 + i * 128 - 8:16 + i * 128, :], reads=nb, writes=[r_uh])
            P.dma("sp", uh[8:16, :], c.ub[16 + (i + 1) * 128:16 + (i + 1) * 128 + 8, :], reads=nb, writes=[r_uh])
            P.dma("sp", zb[:], c.zs[rows, 256:512], reads=[c.r["zs"][i]], writes=[r_zb])
            ps, r_ps = c.pf[i % 2], c.r_pf[i % 2]
            for g in range(4):
                P.op("pe", lambda e, g=g: e.matmul(out=ps[:, g * 64:(g + 1) * 64], lhsT=bandc[:, g * 128:(g + 1) * 128], rhs=u[:, g * 64:(g + 1) * 64],
                                                   start=True, stop=False), [r_u, r_w], [r_ps])
                P.op("pe", lambda e, g=g: e.matmul(out=ps[:, g * 64:(g + 1) * 64], lhsT=bandh[:, g * 128:(g + 1) * 128], rhs=uh[:, g * 64:(g + 1) * 64],
                                                   start=False, stop=True), [r_uh, r_w], [r_ps])
            for g in range(4):
                P.op("dve", lambda e, g=g: e.scalar_tensor_tensor(out=d[:, g * 64:(g + 1) * 64], in0=ps[:, g * 64:(g + 1) * 64],
                                                                  scalar=invc[:, i * 4 + g:i * 4 + g + 1], in1=u[:, g * 64:(g + 1) * 64],
                                                                  op0=ALU.mult, op1=ALU.subtract), [r_ps, r_u, r_w], [r_d])
            pb, r_pb = c.pbf[1], c.r_pbf[1]
            for j in range(2):
                P.op("pe", lambda e, j=j: e.transpose(out=pb[:, j * 128:(j + 1) * 128], in_=d[:, j * 128:(j + 1) * 128], identity=c.idb[:]),
                     [r_d, c.r_const], [r_pb])
            P.op("act", lambda e: e.copy(out=dT[:], in_=pb[:, 0:256]), [r_pb], [r_dT])
            ps2, r_ps2 = c.pf[2 + i % 2], c.r_pf[2 + i % 2]
            for j in range(2):
                P.op("pe", lambda e, j=j: e.matmul(out=ps2[:, j * 128:(j + 1) * 128], lhsT=dT[:, j * 128:(j + 1) * 128], rhs=pw[:, j, :], start=True, stop=True),
                     [r_dT, r_w], [r_ps2])
            P.op("dve", lambda e: e.tensor_tensor(out=y[:], in0=ps2[:, 0:256], in1=psc[:], op=ALU.mult), [r_ps2, r_w], [r_y])
            P.op("pool", lambda e: e.tensor_tensor(out=yo[:], in0=y[:], in1=zb[:], op=ALU.mult), [r_y, r_zb], [r_yo])
            P.dma("pool", c.ys[rows, 256:512], yo[:], reads=[r_yo], writes=[c.r["ys"][i]])

        for i in range(NT):
            body(i)


def phase2a(c, l):
    from contextlib import ExitStack
    P, nc, NT = c.P, c.nc, c.NT
    with ExitStack() as st:
        sb = lambda n, s, d: st.enter_context(nc.sbuf_tensor(uname(n), list(s), d))
        masks = sb("amask", [128, 25, 512], BF16)
        r_w = Res("p2aw")
        for ci in range(25):
            P.dma("pool", masks[:, ci, :], c.amask[ci], writes=[r_w])
        depth = [2 * w + 2 for w in ATT_W]
        kring = [Pool2(st, nc, f"kr{g}_", [64, 512], BF16, depth[g]) for g in range(3)]
        vring = [Pool2(st, nc, f"vr{g}_", [128, 260], BF16, depth[g]) for g in range(3)]
        loaded = [-1, -1, -1]
        q_ = Pool2(st, nc, "qTa", [64, 1536], BF16, 2)
        za_ = Pool2(st, nc, "za", [128, 256], F32, 2)
        e_ = Pool2(st, nc, "eexp", [128, 512], F32, 2)
        p_ = Pool2(st, nc, "pexp", [128, 512], BF16, 3)
        dn_ = Pool2(st, nc, "den", [128, 8], F32, 2)
        ya_ = Pool2(st, nc, "yaf", [128, 256], F32, 2)
        yo_ = Pool2(st, nc, "yao", [128, 256], BF16, 2)
        mcol = []
        ci = 0
        for g in range(3):
            mcol.append({coff: ci + k for k, coff in enumerate(range(-ATT_W[g], ATT_W[g] + 1))})
            ci += 2 * ATT_W[g] + 1

        def ensure(g, upto):
            while loaded[g] < min(upto, NT - 1):
                j = loaded[g] + 1
                kt, r_kt = kring[g].t[j % depth[g]], kring[g].r[j % depth[g]]
                vt, r_vt = vring[g].t[j % depth[g]], vring[g].r[j % depth[g]]
                P.dma("sp", kt[:], c.kT[j][:, g * 512:(g + 1) * 512], reads=[c.r["kT"][j]], writes=[r_kt])
                P.dma("sp", vt[:], c.va[j][:, g * 260:(g + 1) * 260], reads=[c.r["va"][j]], writes=[r_vt])
                loaded[g] = j

        def body(i):
            rows = slice(i * 128, (i + 1) * 128)
            for g in range(3):
                ensure(g, i + ATT_W[g])
            qT, r_q = q_.next()
            za, r_za = za_.next()
            P.dma("sp", qT[:], c.qT[i], reads=[c.r["qT"][i]], writes=[r_q])
            P.dma("sp", za[:], c.zs[rows, 0:256], reads=[c.r["zs"][i]], writes=[r_za])
            chunks = [(g, coff) for g in range(3) for coff in range(-ATT_W[g], ATT_W[g] + 1) if 0 <= i + coff < NT]
            pso, r_pso = c.pf[4 + i % 2], c.r_pf[4 + i % 2]
            for idx, (g, coff) in enumerate(chunks):
                j = i + coff
                kt, r_kt = kring[g].t[j % depth[g]], kring[g].r[j % depth[g]]
                vt, r_vt = vring[g].t[j % depth[g]], vring[g].r[j % depth[g]]
                pss, r_pss = c.pf[idx % 3], c.r_pf[idx % 3]
                for h in range(4):
                    P.op("pe", lambda e, h=h, g=g, kt=kt, pss=pss: e.matmul(out=pss[:, h * 128:(h + 1) * 128], lhsT=kt[:, h * 128:(h + 1) * 128],
                                                                     rhs=qT[:, (g * 4 + h) * 128:(g * 4 + h + 1) * 128], start=True, stop=True),
                         [r_kt, r_q], [r_pss])
                ex, r_ex = e_.next()
                pp, r_pp = p_.next()
                P.op("act", lambda e, pss=pss, ex=ex: e.activation(out=ex[:], in_=pss[:, :], func=AF.Exp), [r_pss], [r_ex])
                mc = mcol[g][coff]
                eng = "dve" if idx % 2 == 0 else "pool"
                P.op(eng, lambda e, ex=ex, pp=pp, mc=mc: e.tensor_tensor(out=pp[:], in0=ex[:], in1=masks[:, mc, :], op=ALU.mult), [r_ex, r_w], [r_pp])
                for h in range(4):
                    P.op("pe", lambda e, h=h, vt=vt, pp=pp, idx=idx: e.matmul(out=pso[:, h * 65:(h + 1) * 65], lhsT=pp[:, h * 128:(h + 1) * 128],
                                                                       rhs=vt[:, h * 65:(h + 1) * 65], start=(idx == 0 and h == 0), stop=(idx == len(chunks) - 1), skip_group_check=True),
                         [r_pp, r_vt], [r_pso])
            dn, r_dn = dn_.next()
            ya, r_ya = ya_.next()
            yo, r_yo = yo_.next()
            pv = pso[:, 0:260].rearrange("p (h d) -> p h d", d=65)
            P.op("dve", lambda e: e.tensor_scalar(out=dn[:, 0:4].unsqueeze(2), in0=pv[:, :, 64:65], scalar1=1e-30, scalar2=None, op0=ALU.max), [r_pso], [r_dn])
            P.op("dve", lambda e: e.reciprocal(out=dn[:, 4:8], in_=dn[:, 0:4]), [r_dn], [r_dn])
            P.op("dve", lambda e: e.tensor_tensor(out=ya[:].rearrange("p (h d) -> p h d", d=64), in0=pv[:, :, 0:64],
                                                  in1=dn[:, 4:8].unsqueeze(2).to_broadcast([128, 4, 64]), op=ALU.mult), [r_pso, r_dn], [r_ya])
            P.op("pool", lambda e: e.tensor_tensor(out=yo[:], in0=ya[:], in1=za[:], op=ALU.mult), [r_ya, r_za], [r_yo])
            P.dma("pool", c.ys[rows, 0:256], yo[:], reads=[r_yo], writes=[c.r["ys"][i]])

        for i in range(NT):
            body(i)


def phase2d(c, l):
    from contextlib import ExitStack
    P, nc, NT = c.P, c.nc, c.NT
    NEG = -float(np.exp(-0.5))
    def run_dir(d):
        with ExitStack() as st:
            sb = lambda n, s, dt: st.enter_context(nc.sbuf_tensor(uname(n), list(s), dt))
            r_w = Res("p2dw")
            tri = sb("tri", [128, 642], F32)
            P.dma("sp", tri[:], c.tri, writes=[r_w])
            incl = tri[:, 128 * d:128 * d + 128]
            ones_m = tri[:, 512:640]
            ones_c = tri[:, 640:641]
            mask4 = sb("mask4", [128, 512], F32)
            maskT4 = sb("maskT4", [128, 512], F32)
            for q in range(4):
                src = tri[:, 256 + 128 * d:384 + 128 * d] if q % 2 == 0 else incl
                P.op("dve", lambda e, q=q, src=src: e.tensor_copy(out=mask4[:, q * 128:(q + 1) * 128], in_=src), [r_w], [r_w])
                P.op("dve", lambda e, q=q: e.tensor_copy(out=maskT4[:, q * 128:(q + 1) * 128], in_=tri[:, 256 + 128 * (1 - d):384 + 128 * (1 - d)]), [r_w], [r_w])
            identB = sb("identB", [64, 4, 64], F32)
            P.op("dve", lambda e: e.tensor_copy(out=identB[:], in_=c.idf[0:64, 0:64].unsqueeze(1).to_broadcast([64, 4, 64])), [c.r_const], [r_w])
            mu_r = sb("mu_r", [128, 768], F32)
            mu_l = sb("mu_l", [128, 128], F32)
            bias_wa = sb("bias_wa", [128, 512], F32)
            kkv = sb("kkv", [128, 256], F32)
            kav = sb("kav", [128, 256], F32)
            rkp = sb("rkp", [128, 256], F32)
            lng = sb("lng", [128, 256], F32)
            lnb = sb("lnb", [128, 256], F32)
            Wud = sb("Wud", [128, 512], BF16)
            bc = lambda ap: ap.partition_broadcast(128)
            P.dma("sp", mu_r[:], bc(c.w["mu_rkv"][l, d:d + 1, :]), writes=[r_w])
            P.dma("sp", mu_l[:], bc(c.w["mu_lat"][l, d:d + 1, :]), writes=[r_w])
            P.dma("sp", bias_wa[:, 0:256], bc(c.w["w0"][l, d:d + 1, :]), writes=[r_w])
            P.dma("sp", bias_wa[:, 256:512], bc(c.w["a0"][l, d:d + 1, :]), writes=[r_w])
            P.dma("sp", kkv[:], bc(c.w["k_k"][l, d:d + 1, :]), writes=[r_w])
            P.dma("sp", kav[:], bc(c.w["k_a"][l, d:d + 1, :]), writes=[r_w])
            P.dma("sp", rkp[:], bc(c.w["r_k"][l, d:d + 1, :]), writes=[r_w])
            P.dma("sp", lng[:], bc(c.w["ln_g"][l:l + 1, :]), writes=[r_w])
            P.dma("sp", lnb[:], bc(c.w["ln_b"][l:l + 1, :]), writes=[r_w])
            P.op("pool", lambda e: e.memset(Wud[:], 0.0), [], [r_w])
            P.dma("pool", Wud[0:64, 0:256], c.w["w_up"][l, d], writes=[r_w])
            P.dma("pool", Wud[64:128, 256:512], c.w["a_up"][l, d], writes=[r_w])
            ST = Pool2(st, nc, "ST", [64, 256], F32, 2)
            st0, r_st0 = ST.next()
            P.op("pool", lambda e: e.memset(st0[:], 0.0), [], [r_st0])
            state = [st0, r_st0]

            def mk(name, shape, dt, n=2):
                return Pool2(st, nc, name, shape, dt, n)
            cur_, sh_, zd_, o0_ = mk("cur", [128, 1024], F32), mk("sh", [128, 1024], F32), mk("zd", [128, 256], F32), mk("o0", [128, 256], F32)
            xr_, xl_, tl_, tlT_ = mk("xr", [128, 768], F32), mk("xl", [128, 128], F32), mk("tl", [128, 128], BF16), mk("tlT", [128, 128], BF16)
            sg_, lw_ = mk("sg", [128, 512], F32), mk("logw", [128, 256], F32)
            kk_, sq_, s4_, k2_, bon_, beta_ = mk("kk", [128, 256], F32), mk("sqd", [128, 256], F32, 1), mk("s4", [128, 32], F32), mk("k2", [128, 256], F32), mk("bon", [128, 256], F32), mk("beta", [128, 256], F32)
            cum_, ex_, tmp_ = mk("cum", [128, 256], F32), mk("exps", [128, 1024], F32), mk("tmpd", [128, 256], F32, 2)
            opb_ = mk("opb", [128, 7, 256], BF16)
            fTA_, fTB_ = mk("fTA", [64, 1024], BF16), mk("fTB", [64, 1024], BF16)
            AB_, AK_, NTa_ = mk("AB", [128, 1024], BF16), mk("AK", [128, 1024], BF16), mk("NTa", [128, 512], BF16, 3)
            Nn_ = mk("Nn", [128, 512], BF16, 3)
            Z_ = mk("Z", [128, 512], BF16, 3)
            gcol_, Q_, Pm_, Dm_ = mk("gcol", [64, 4], F32), mk("Q", [64, 512], F32), mk("Pm", [64, 256], F32), mk("Dm", [64, 256], F32)
            osb_, on_, yo_ = mk("osb", [128, 256], F32), mk("on", [128, 256], F32), mk("yod", [128, 256], BF16)
            bank_i = [0]

            def nb():
                bank_i[0] = (bank_i[0] + 1) % 4
                return c.pf[bank_i[0]], c.r_pf[bank_i[0]]

            def body(i):
                rows = slice(i * 128, (i + 1) * 128)
                v3 = lambda ap, dd=64: ap.rearrange("p (h d) -> p h d", d=dd)
                dbg_on = DEBUG_SCRATCH and d == DBG_DIR and i == DBG_TILE and l == 0

                def dbg(name, t, shape, dt, rs):
                    if dbg_on:
                        o = nc.dram_tensor("dbg_" + name, list(shape), dt, kind="ExternalOutput").ap()
                        P.dma("pool", o, t, reads=rs, writes=[Res()])
                cur, r_cur = cur_.next()
                sh, r_sh = sh_.next()
                nbr = [c.r["rkvl"][j] for j in (i - 1, i, i + 1) if 0 <= j < NT] + [c.r_pad]
                P.dma("sp", cur[:], c.rkvl[1 + i * 128:1 + (i + 1) * 128, :], reads=[c.r["rkvl"][i]], writes=[r_cur])
                so = 0 if d == 0 else 2
                P.dma("sp", sh[:], c.rkvl[so + i * 128:so + (i + 1) * 128, :], reads=nbr, writes=[r_sh])
                if d == 1:
                    zd, r_zd = zd_.next()
                    o0, r_o0 = o0_.next()
                    P.dma("sp", zd[:], c.zs[rows, 768:1024], reads=[c.r["zs"][i]], writes=[r_zd])
                    P.dma("sp", o0[:], c.osc[0, rows, :], reads=[c.r["osc0"][i]], writes=[r_o0])
                xr, r_xr = xr_.next()
                xl, r_xl = xl_.next()
                P.op("pool", lambda e: e.tensor_tensor(out=xr[:], in0=sh[:, 0:768], in1=cur[:, 0:768], op=ALU.subtract), [r_sh, r_cur], [r_xr])
                P.op("dve", lambda e: e.tensor_tensor(out=xr[:], in0=xr[:], in1=mu_r[:], op=ALU.mult), [r_xr, r_w], [r_xr])
                P.op("pool", lambda e: e.tensor_tensor(out=xr[:], in0=xr[:], in1=cur[:, 0:768], op=ALU.add), [r_xr, r_cur], [r_xr])
                lo = 768 + 128 * d
                P.op("dve", lambda e: e.tensor_tensor(out=xl[:], in0=sh[:, lo:lo + 128], in1=cur[:, lo:lo + 128], op=ALU.subtract), [r_sh, r_cur], [r_xl])
                P.op("dve", lambda e: e.tensor_tensor(out=xl[:], in0=xl[:], in1=mu_l[:], op=ALU.mult), [r_xl, r_w], [r_xl])
                P.op("dve", lambda e: e.tensor_tensor(out=xl[:], in0=xl[:], in1=cur[:, lo:lo + 128], op=ALU.add), [r_xl, r_cur], [r_xl])
                r_, k_, v_ = xr[:, 0:256], xr[:, 256:512], xr[:, 512:768]
                tl, r_tl = tl_.next()
                tlT, r_tlT = tlT_.next()
                P.op("act", lambda e: e.activation(out=tl[:, 0:64], in_=xl[:, 0:64], func=AF.Tanh), [r_xl], [r_tl])
                P.op("act", lambda e: e.copy(out=tl[:, 64:128], in_=xl[:, 64:128]), [r_xl], [r_tl])
                pb, r_pb = c.pbf[1], c.r_pbf[1]
                P.op("pe", lambda e: e.transpose(out=pb[:, 0:128], in_=tl[:], identity=c.idb[:]), [r_tl, c.r_const], [r_pb])
                P.op("act", lambda e: e.copy(out=tlT[:], in_=pb[:, 0:128]), [r_pb], [r_tlT])
                ps, r_ps = nb()
                P.op("pe", lambda e: e.matmul(out=ps[:, :], lhsT=tlT[:], rhs=Wud[:], start=True, stop=True), [r_tlT, r_w], [r_ps])
                sg, r_sg = sg_.next()
                lw, r_lw = lw_.next()
                P.op("dve", lambda e: e.tensor_tensor(out=sg[:], in0=ps[:, :], in1=bias_wa[:], op=ALU.add), [r_ps, r_w], [r_sg])
                P.op("act", lambda e: e.activation(out=sg[:], in_=sg[:], func=AF.Sigmoid), [r_sg], [r_sg])
                P.op("act", lambda e: e.mul(out=lw[:], in_=sg[:, 0:256], mul=NEG), [r_sg], [r_lw])
                a_ = sg[:, 256:512]
                kk, r_kk = kk_.next()
                sq, r_sq = sq_.next()
                s4, r_s4 = s4_.next()
                P.op("dve", lambda e: e.tensor_tensor(out=kk[:], in0=k_, in1=kkv[:], op=ALU.mult), [r_xr, r_w], [r_kk])
                P.op("pool", lambda e: e.tensor_tensor(out=sq[:], in0=kk[:], in1=kk[:], op=ALU.mult), [r_kk], [r_sq])
                P.op("dve", lambda e: e.tensor_reduce(out=s4[:, 0:4], in_=v3(sq[:]), axis=AX.X, op=ALU.add), [r_sq], [r_s4])
                P.op("dve", lambda e: e.tensor_scalar(out=s4[:, 4:8], in0=s4[:, 0:4], scalar1=1e-12, scalar2=None, op0=ALU.add), [r_s4], [r_s4])
                P.op("act", lambda e: e.activation(out=s4[:, 0:4], in_=s4[:, 4:8], func=AF.Sqrt), [r_s4], [r_s4])
                P.op("dve", lambda e: e.reciprocal(out=s4[:, 8:12], in_=s4[:, 0:4]), [r_s4], [r_s4])
                P.op("dve", lambda e: e.tensor_tensor(out=v3(kk[:]), in0=v3(kk[:]), in1=s4[:, 8:12].unsqueeze(2).to_broadcast([128, 4, 64]), op=ALU.mult), [r_kk, r_s4], [r_kk])
                k2, r_k2 = k2_.next()
                P.op("dve", lambda e: e.scalar_tensor_tensor(out=k2[:], in0=a_, scalar=-1.0, in1=kav[:], op0=ALU.add, op1=ALU.mult), [r_sg, r_w], [r_k2])
                P.op("dve", lambda e: e.scalar_tensor_tensor(out=k2[:], in0=k2[:], scalar=1.0, in1=k_, op0=ALU.add, op1=ALU.mult), [r_k2, r_xr], [r_k2])
                bon, r_bon = bon_.next()
                P.op("pool", lambda e: e.tensor_tensor(out=bon[:], in0=r_, in1=k2[:], op=ALU.mult), [r_xr, r_k2], [r_bon])
                P.op("pool", lambda e: e.tensor_tensor(out=bon[:], in0=bon[:], in1=rkp[:], op=ALU.mult), [r_bon, r_w], [r_bon])
                P.op("dve", lambda e: e.tensor_reduce(out=s4[:, 12:16], in_=v3(bon[:]), axis=AX.X, op=ALU.add), [r_bon], [r_s4])
                P.op("dve", lambda e: e.tensor_tensor(out=v3(bon[:]), in0=v3(v_), in1=s4[:, 12:16].unsqueeze(2).to_broadcast([128, 4, 64]), op=ALU.mult), [r_xr, r_s4], [r_bon])
                beta, r_beta = beta_.next()
                P.op("pool", lambda e: e.tensor_tensor(out=beta[:], in0=kk[:], in1=a_, op=ALU.mult), [r_kk, r_sg], [r_beta])
                pc, r_pc = nb()
                P.op("pe", lambda e: e.matmul(out=pc[:, 0:256], lhsT=incl, rhs=lw[:], start=True, stop=True), [r_lw, r_w], [r_pc])
                P.op("pe", lambda e: e.matmul(out=pc[:, 256:512], lhsT=ones_m, rhs=lw[:], start=True, stop=True), [r_lw, r_w], [r_pc])
                pg, r_pg = nb()
                for h in range(4):
                    P.op("pe", lambda e, h=h: e.matmul(out=pg[0:64, h:h + 1], lhsT=lw[:, h * 64:(h + 1) * 64], rhs=ones_c, start=True, stop=True), [r_lw, r_w], [r_pg])
                gcol, r_gcol = gcol_.next()
                P.op("act", lambda e: e.activation(out=gcol[:], in_=pg[0:64, 0:4], func=AF.Exp), [r_pg], [r_gcol])
                cum, r_cum = cum_.next()
                ex, r_ex = ex_.next()
                t3, r_t3 = tmp_.next()
                t4, r_t4 = tmp_.next()
                ecum, encum, eexc, edec = ex[:, 0:256], ex[:, 256:512], ex[:, 512:768], ex[:, 768:1024]
                P.op("act", lambda e: e.copy(out=cum[:], in_=pc[:, 0:256]), [r_pc], [r_cum])
                P.op("act", lambda e: e.activation(out=ecum, in_=pc[:, 0:256], func=AF.Exp), [r_pc], [r_ex])
                P.op("act", lambda e: e.activation(out=encum, in_=pc[:, 0:256], func=AF.Exp, scale=-1.0), [r_pc], [r_ex])
                P.op("dve", lambda e: e.tensor_tensor(out=t3[:], in0=cum[:], in1=lw[:], op=ALU.subtract), [r_cum, r_lw], [r_t3])
                P.op("act", lambda e: e.activation(out=eexc, in_=t3[:], func=AF.Exp), [r_t3], [r_ex])
                P.op("dve", lambda e: e.tensor_tensor(out=t4[:], in0=pc[:, 256:512], in1=cum[:], op=ALU.subtract), [r_pc, r_cum], [r_t4])
                P.op("act", lambda e: e.activation(out=edec, in_=t4[:], func=AF.Exp), [r_t4], [r_ex])
                opb, r_opb = opb_.next()
                P.op("dve", lambda e: e.tensor_tensor(out=opb[:, 0, :], in0=r_, in1=ecum, op=ALU.mult), [r_xr, r_ex], [r_opb])
                P.op("pool", lambda e: e.tensor_tensor(out=opb[:, 1, :], in0=k2[:], in1=encum, op=ALU.mult), [r_k2, r_ex], [r_opb])
                P.op("dve", lambda e: e.tensor_tensor(out=opb[:, 2, :], in0=beta[:], in1=encum, op=ALU.mult), [r_beta, r_ex], [r_opb])
                P.op("dve", lambda e: e.scalar_tensor_tensor(out=opb[:, 3, :], in0=kk[:], scalar=-1.0, in1=eexc, op0=ALU.mult, op1=ALU.mult), [r_kk, r_ex], [r_opb])
                P.op("pool", lambda e: e.tensor_tensor(out=opb[:, 4, :], in0=k2[:], in1=edec, op=ALU.mult), [r_k2, r_ex], [r_opb])
                P.op("pool", lambda e: e.tensor_tensor(out=opb[:, 5, :], in0=beta[:], in1=edec, op=ALU.mult), [r_beta, r_ex], [r_opb])
                P.op("act", lambda e: e.copy(out=opb[:, 6, :], in_=v_), [r_xr], [r_opb])
                rb, kt, bt, ab, Kh, Bh, vb = (opb[:, q, :] for q in range(7))
                pa, r_pa = c.pbf[0], c.r_pbf[0]
                pb, r_pb = c.pbf[1], c.r_pbf[1]
                for h in range(4):
                    hs = slice(h * 64, (h + 1) * 64)
                    P.op("pe", lambda e, h=h, hs=hs: e.transpose(out=pa[0:64, h * 128:(h + 1) * 128], in_=bt[:, hs], identity=c.idb[:]), [r_opb, c.r_const], [r_pa])
                    P.op("pe", lambda e, h=h, hs=hs: e.transpose(out=pa[0:64, (4 + h) * 128:(5 + h) * 128], in_=kt[:, hs], identity=c.idb[:]), [r_opb, c.r_const], [r_pa])
                    P.op("pe", lambda e, h=h, hs=hs: e.transpose(out=pb[0:64, (2 * h) * 128:(2 * h + 1) * 128], in_=ab[:, hs], identity=c.idb[:]), [r_opb, c.r_const], [r_pb])
                    P.op("pe", lambda e, h=h, hs=hs: e.transpose(out=pb[0:64, (2 * h + 1) * 128:(2 * h + 2) * 128], in_=rb[:, hs], identity=c.idb[:]), [r_opb, c.r_const], [r_pb])
                fTA, r_fTA = fTA_.next()
                fTB, r_fTB = fTB_.next()
                P.op("act", lambda e: e.copy(out=fTA[:], in_=pa[0:64, :]), [r_pa], [r_fTA])
                P.op("dve", lambda e: e.tensor_copy(out=fTB[:], in_=pb[0:64, :]), [r_pb], [r_fTB])
                AB, r_AB = AB_.next()
                AK, r_AK = AK_.next()
                for (dst, r_dst, off) in ((AB, r_AB, 0), (AK, r_AK, 4)):
                    for pr in range(2):
                        px, r_px = nb()
                        for hh in range(2):
                            h = 2 * pr + hh
                            P.op("pe", lambda e, h=h, hh=hh, px=px, off=off: e.matmul(out=px[:, hh * 256:(hh + 1) * 256], lhsT=fTA[:, (off + h) * 128:(off + h + 1) * 128],
                                                                               rhs=fTB[:, 2 * h * 128:(2 * h + 2) * 128], start=True, stop=True), [r_fTA, r_fTB], [r_px])
                        P.op("dve", lambda e, pr=pr, px=px, dst=dst: e.tensor_tensor(out=dst[:, pr * 512:(pr + 1) * 512], in0=px[:, :], in1=mask4[:], op=ALU.mult), [r_px, r_w], [r_dst])
                py, r_py = nb()
                for h in range(4):
                    P.op("pe", lambda e, h=h: e.matmul(out=py[:, h * 128:(h + 1) * 128], lhsT=fTB[:, 2 * h * 128:(2 * h + 1) * 128], rhs=fTA[:, h * 128:(h + 1) * 128],
                                                       start=True, stop=True), [r_fTA, r_fTB], [r_py])
                NTc, r_NTc = NTa_.next()
                P.op("dve", lambda e: e.tensor_tensor(out=NTc[:], in0=py[:, :], in1=maskT4[:], op=ALU.mult), [r_py, r_w], [r_NTc])
                pz, r_pz = nb()
                for h in range(4):
                    P.op("pe", lambda e, h=h: e.matmul(out=pz[:, h * 64:(h + 1) * 64], lhsT=AK[:, h * 256:h * 256 + 128], rhs=vb[:, h * 64:(h + 1) * 64], start=True, stop=True),
                         [r_AK, r_opb], [r_pz])
                Z, r_Z = Z_.next()
                P.op("dve", lambda e, Z=Z: e.tensor_copy(out=v3(Z[:], 128)[:, :, 0:64], in_=v3(ab)), [r_opb], [r_Z])
                P.op("act", lambda e, Z=Z: e.copy(out=v3(Z[:], 128)[:, :, 64:128], in_=v3(pz[:, 0:256])), [r_pz], [r_Z])
                dbg("Z0", Z[:], [128, 512], BF16, [r_Z]); dbg("NT0", NTc[:], [128, 512], BF16, [r_NTc]); dbg("AK", AK[:], [128, 1024], BF16, [r_AK])
                Ncur = [AB[:, h * 256:h * 256 + 128] for h in range(4)]
                r_N = r_AB
                for kq in range(7):
                    pq, r_pq = nb()
                    for h in range(4):
                        P.op("pe", lambda e, h=h, N=Ncur[h], Z=Z, pq=pq: e.matmul(out=pq[:, h * 128:(h + 1) * 128], lhsT=N, rhs=Z[:, h * 128:(h + 1) * 128], start=True, stop=True),
                             [r_N, r_Z], [r_pq])
                    Zn, r_Zn = Z_.next()
                    P.op("dve", lambda e, Z=Z, Zn=Zn, pq=pq: e.tensor_tensor(out=Zn[:], in0=pq[:, :], in1=Z[:], op=ALU.add), [r_pq, r_Z], [r_Zn])
                    if kq < 6:
                        p1, r_p1 = nb()
                        for h in range(4):
                            P.op("pe", lambda e, h=h, N=Ncur[h], NTc=NTc, p1=p1: e.matmul(out=p1[:, h * 128:(h + 1) * 128], lhsT=NTc[:, h * 128:(h + 1) * 128], rhs=N, start=True, stop=True),
                                 [r_N, r_NTc], [r_p1])
                        p2, r_p2 = nb()
                        for h in range(4):
                            P.op("pe", lambda e, h=h, N=Ncur[h], NTc=NTc, p2=p2: e.matmul(out=p2[:, h * 128:(h + 1) * 128], lhsT=N, rhs=NTc[:, h * 128:(h + 1) * 128], start=True, stop=True),
                                 [r_N, r_NTc], [r_p2])
                        Nn, r_Nn = Nn_.next()
                        NTn, r_NTn = NTa_.next()
                        P.op("act", lambda e, Nn=Nn, p1=p1: e.copy(out=Nn[:], in_=p1[:, :]), [r_p1], [r_Nn])
                        P.op("dve", lambda e, NTn=NTn, p2=p2: e.tensor_copy(out=NTn[:], in_=p2[:, :]), [r_p2], [r_NTn])
                        dbg(f"Nn{kq}", Nn[:], [128, 512], BF16, [r_Nn]); dbg(f"NTn{kq}", NTn[:], [128, 512], BF16, [r_NTn]); dbg(f"Zn{kq}", Zn[:], [128, 512], BF16, [r_Zn])
                        Ncur = [Nn[:, h * 128:(h + 1) * 128] for h in range(4)]
                        r_N, NTc, r_NTc = r_Nn, NTn, r_NTn
                    Z, r_Z = Zn, r_Zn
                WT = [Z[:, h * 128:h * 128 + 64] for h in range(4)]
                XT = [Z[:, h * 128 + 64:(h + 1) * 128] for h in range(4)]
                pQ, r_pQ = nb()
                for h in range(4):
                    P.op("pe", lambda e, h=h: e.matmul(out=pQ[0:64, h * 128:(h + 1) * 128], lhsT=WT[h], rhs=AB[:, h * 256 + 128:(h + 1) * 256], start=True, stop=True),
                         [r_Z, r_AB], [r_pQ])
                Q, r_Q = Q_.next()
                P.op("dve", lambda e: e.tensor_tensor(out=v3(Q[:], 128), in0=v3(pQ[0:64, :], 128), in1=fTB[:].rearrange("p (h two t) -> p h two t", two=2, t=128)[:, :, 1, :], op=ALU.add),
                     [r_pQ, r_fTB], [r_Q])
                pP, r_pP = nb()
                for h in range(4):
                    P.op("pe", lambda e, h=h: e.matmul(out=pP[0:64, h * 64:(h + 1) * 64], lhsT=WT[h], rhs=Bh[:, h * 64:(h + 1) * 64], start=True, stop=True), [r_Z, r_opb], [r_pP])
                Pm, r_Pm = Pm_.next()
                P.op("dve", lambda e: e.tensor_tensor(out=v3(Pm[:]), in0=identB[:], in1=gcol[:].unsqueeze(2).to_broadcast([64, 4, 64]), op=ALU.mult), [r_w, r_gcol], [r_Pm])
                P.op("dve", lambda e: e.tensor_tensor(out=Pm[:], in0=Pm[:], in1=pP[0:64, 0:256], op=ALU.add), [r_Pm, r_pP], [r_Pm])
                pD, r_pD = nb()
                for h in range(4):
                    P.op("pe", lambda e, h=h: e.matmul(out=pD[0:64, h * 64:(h + 1) * 64], lhsT=Bh[:, h * 64:(h + 1) * 64], rhs=XT[h], start=(h == 0), stop=False, skip_group_check=True),
                         [r_Z, r_opb], [r_pD])
                    P.op("pe", lambda e, h=h: e.matmul(out=pD[0:64, h * 64:(h + 1) * 64], lhsT=Kh[:, h * 64:(h + 1) * 64], rhs=vb[:, h * 64:(h + 1) * 64], start=False, stop=True, skip_group_check=True),
                         [r_opb], [r_pD])
                Dm, r_Dm = Dm_.next()
                P.op("act", lambda e: e.copy(out=Dm[:], in_=pD[0:64, 0:256]), [r_pD], [r_Dm])
                S0, r_S0 = state
                pO, r_pO = c.pf[5], c.r_pf[5]
                for h in range(4):
                    P.op("pe", lambda e, h=h: e.matmul(out=pO[:, h * 64:(h + 1) * 64], lhsT=AB[:, h * 256 + 128:(h + 1) * 256], rhs=XT[h], start=(h == 0), stop=False, skip_group_check=True),
                         [r_AB, r_Z], [r_pO])
                    P.op("pe", lambda e, h=h: e.matmul(out=pO[:, h * 64:(h + 1) * 64], lhsT=AK[:, h * 256 + 128:(h + 1) * 256], rhs=vb[:, h * 64:(h + 1) * 64], start=False, stop=False, skip_group_check=True),
                         [r_AK, r_opb], [r_pO])
                    P.op("pe", lambda e, h=h, S0=S0: e.matmul(out=pO[:, h * 64:(h + 1) * 64], lhsT=Q[:, h * 128:(h + 1) * 128], rhs=S0[:, h * 64:(h + 1) * 64], start=False, stop=True, skip_group_check=True),
                         [r_Q, r_S0], [r_pO])
                pS, r_pS = c.pf[4], c.r_pf[4]
                for h in range(4):
                    P.op("pe", lambda e, h=h, S0=S0: e.matmul(out=pS[0:64, h * 64:(h + 1) * 64], lhsT=Pm[:, h * 64:(h + 1) * 64], rhs=S0[:, h * 64:(h + 1) * 64], start=True, stop=True),
                         [r_Pm, r_S0], [r_pS])
                S1, r_S1 = ST.next()
                P.op("dve", lambda e, S1=S1: e.tensor_tensor(out=S1[:], in0=pS[0:64, 0:256], in1=Dm[:], op=ALU.add), [r_pS, r_Dm], [r_S1])
                state[0], state[1] = S1, r_S1
                osb, r_osb = osb_.next()
                on, r_on = on_.next()
                P.op("act", lambda e: e.copy(out=osb[:], in_=pO[:, 0:256]), [r_pO], [r_osb])
                P.op("dve", lambda e: e.tensor_reduce(out=s4[:, 16:20], in_=v3(osb[:]), axis=AX.X, op=ALU.add), [r_osb], [r_s4])
                P.op("pool", lambda e: e.tensor_tensor(out=on[:], in0=osb[:], in1=osb[:], op=ALU.mult), [r_osb], [r_on])
                P.op("dve", lambda e: e.tensor_reduce(out=s4[:, 20:24], in_=v3(on[:]), axis=AX.X, op=ALU.add), [r_on], [r_s4])
                P.op("dve", lambda e: e.tensor_scalar(out=s4[:, 16:20], in0=s4[:, 16:20], scalar1=1.0 / 64, scalar2=None, op0=ALU.mult), [r_s4], [r_s4])
                P.op("dve", lambda e: e.tensor_tensor(out=s4[:, 24:28], in0=s4[:, 16:20], in1=s4[:, 16:20], op=ALU.mult), [r_s4], [r_s4])
                P.op("dve", lambda e: e.scalar_tensor_tensor(out=s4[:, 20:24], in0=s4[:, 20:24], scalar=1.0 / 64, in1=s4[:, 24:28], op0=ALU.mult, op1=ALU.subtract), [r_s4], [r_s4])
                P.op("dve", lambda e: e.tensor_scalar(out=s4[:, 20:24], in0=s4[:, 20:24], scalar1=GN_EPS, scalar2=None, op0=ALU.add), [r_s4], [r_s4])
                P.op("act", lambda e: e.activation(out=s4[:, 24:28], in_=s4[:, 20:24], func=AF.Sqrt), [r_s4], [r_s4])
                P.op("dve", lambda e: e.reciprocal(out=s4[:, 28:32], in_=s4[:, 24:28]), [r_s4], [r_s4])
                P.op("dve", lambda e: e.tensor_tensor(out=v3(on[:]), in0=v3(osb[:]), in1=s4[:, 16:20].unsqueeze(2).to_broadcast([128, 4, 64]), op=ALU.subtract), [r_osb, r_s4], [r_on])
                P.op("dve", lambda e: e.tensor_tensor(out=v3(on[:]), in0=v3(on[:]), in1=s4[:, 28:32].unsqueeze(2).to_broadcast([128, 4, 64]), op=ALU.mult), [r_on, r_s4], [r_on])
                P.op("pool", lambda e: e.tensor_tensor(out=on[:], in0=on[:], in1=lng[:], op=ALU.mult), [r_on, r_w], [r_on])
                P.op("pool", lambda e: e.tensor_tensor(out=on[:], in0=on[:], in1=lnb[:], op=ALU.add), [r_on, r_w], [r_on])
                P.op("pool", lambda e: e.tensor_tensor(out=on[:], in0=on[:], in1=bon[:], op=ALU.add), [r_on, r_bon], [r_on])
                if dbg_on:
                    dbg("xr", xr[:], [128, 768], F32, [r_xr]); dbg("sg", sg[:], [128, 512], F32, [r_sg]); dbg("lw", lw[:], [128, 256], F32, [r_lw])
                    dbg("kk", kk[:], [128, 256], F32, [r_kk]); dbg("ex", ex[:], [128, 1024], F32, [r_ex]); dbg("cum", cum[:], [128, 256], F32, [r_cum])
                    dbg("opb", opb[:].rearrange("p a b -> p (a b)"), [128, 7 * 256], BF16, [r_opb]); dbg("AB", AB[:], [128, 1024], BF16, [r_AB])
                    dbg("Z", Z[:], [128, 512], BF16, [r_Z]); dbg("Q", Q[:], [64, 512], F32, [r_Q]); dbg("osb", osb[:], [128, 256], F32, [r_osb])
                    dbg("s4", s4[:], [128, 32], F32, [r_s4]); dbg("bon", bon[:], [128, 256], F32, [r_bon]); dbg("k2", k2[:], [128, 256], F32, [r_k2])
                    dbg("Pm", Pm[:], [64, 256], F32, [r_Pm]); dbg("Dm", Dm[:], [64, 256], F32, [r_Dm]); dbg("gcol", gcol[:], [64, 4], F32, [r_gcol]); dbg("S1", S1[:], [64, 256], F32, [r_S1])
                if d == 0:
                    P.dma("pool", c.osc[0, rows, :], on[:], reads=[r_on], writes=[c.r["osc0"][i]])
                else:
                    yo, r_yo = yo_.next()
                    P.op("dve", lambda e: e.tensor_tensor(out=on[:], in0=on[:], in1=o0[:], op=ALU.add), [r_on, r_o0], [r_on])
                    P.op("dve", lambda e: e.tensor_tensor(out=yo[:], in0=on[:], in1=zd[:], op=ALU.mult), [r_on, r_zd], [r_yo])
                    P.dma("pool", c.ys[rows, 768:1024], yo[:], reads=[r_yo], writes=[c.r["ys"][i]])

            for i in (range(NT) if d == 0 else range(NT - 1, -1, -1)):
                body(i)
        barrier(c)

    for d in range(2):
        run_dir(d)


def host_consts(S, seq_len):
    NT = S // 128
    t = np.arange(S)
    valid = (t < seq_len).astype(np.float32).reshape(NT, 128).T.copy()
    invc = np.ones((S, 4), np.float32)
    for g, w in enumerate((2, 4, 8, 16)):
        h = w // 2
        cnt = np.minimum(t + h, seq_len) - np.maximum(t - h, 0)
        invc[:, g] = np.where(t < seq_len, 1.0 / np.maximum(cnt, 1), 1.0)
    invc = invc.reshape(NT, 128, 4).transpose(1, 0, 2).reshape(128, NT * 4).copy()
    ident = np.eye(128, dtype=np.float32)
    s = np.arange(128)[:, None]
    tt = np.arange(128)[None, :]
    bandc = np.zeros((128, 4, 128), np.float32)
    bandh = np.zeros((16, 4, 128), np.float32)
    for g, w in enumerate((2, 4, 8, 16)):
        h = w // 2
        bandc[:, g, :] = ((s >= tt - h) & (s <= tt + h - 1))
        r = np.arange(16)[:, None]
        srel = np.where(r < 8, r - 8, 128 + (r - 8))
        bandh[:, g, :] = ((srel >= tt - h) & (srel <= tt + h - 1))
    slopes = 2.0 ** (-8.0 * np.arange(1, 13) / 12.0)
    amask = np.zeros((25, 128, 4, 128), np.float32)
    ci = 0
    key = np.arange(128)[:, None]
    q = np.arange(128)[None, :]
    for g in range(3):
        d = ATT_D[g]
        for coff in range(-ATT_W[g], ATT_W[g] + 1):
            delta = coff * 128 + key - q
            ok = (delta % d == 0) & (np.abs(delta) <= 64 * d)
            for h in range(4):
                amask[ci, :, h, :] = np.where(ok, np.exp(-slopes[g * 4 + h] * np.abs(delta)), 0.0)
            ci += 1
    tri = np.zeros((128, 642), np.float32)
    tri[:, 0:128] = (s <= tt)
    tri[:, 128:256] = (s >= tt)
    tri[:, 256:384] = (s < tt)
    tri[:, 384:512] = (s > tt)
    tri[:, 512:642] = 1.0
    return dict(valid=valid, invc=invc, ident=ident, amask=amask.reshape(25, 128, 512), bandc=bandc.reshape(128, 512),
                bandh=bandh.reshape(16, 512), tri=tri)


def host_weights(inp):
    w = {}
    for k in ("norm_g", "w_in", "q_norm_g", "k_norm_g", "pool_w", "pool_scale", "sg_norm_g", "mu_rkv", "mu_lat", "w0", "w_up",
              "a0", "a_up", "k_k", "k_a", "r_k", "ln_g", "ln_b", "w_branch", "w_out"):
        w[k] = np.ascontiguousarray(np.asarray(inp[k], dtype=np.float32))
    w["sg_wT"] = np.ascontiguousarray(np.transpose(np.asarray(inp["sg_w"], np.float32), (0, 1, 3, 2)))
    w["sg_bT"] = np.ascontiguousarray(np.transpose(np.asarray(inp["sg_b"], np.float32), (0, 2, 1)))
    return w


_NC_CACHE = {}


def run_sequences(seqs, S, inp, branches=(0, 1, 2, 3), L=2, n_cores=8):
    key = (S, L, tuple(branches))
    if key not in _NC_CACHE:
        _NC_CACHE[key] = build_program(S, L=L, branches=branches)
    nc = _NC_CACHE[key]
    w = host_weights(inp)
    in_maps = []
    for ci in range(n_cores):
        if ci < len(seqs):
            x = np.zeros((S, D), np.float32)
            x[:seqs[ci].shape[0]] = seqs[ci]
            m = dict(x=x, **host_consts(S, seqs[ci].shape[0]))
        else:
            m = dict(x=np.zeros((S, D), np.float32), **host_consts(S, 0))
        m.update(w)
        in_maps.append(m)
    res = run_bass_kernel_spmd(nc, in_maps, core_ids=list(range(n_cores)))
    if DEBUG_SCRATCH:
        global LAST_RESULTS
        LAST_RESULTS = res.results
    return [res.results[ci]["y"][:seqs[ci].shape[0]] for ci in range(len(seqs))]


def kernel(**inputs):
    xp = np.asarray(inputs["x_prompt"], np.float32)
    xs = np.asarray(inputs["x_sample"], np.float32)
    S = xs.shape[1]
    seqs = [xp[b] for b in range(xp.shape[0])] + [xs[b] for b in range(xs.shape[0])]
    outs = run_sequences(seqs, S, inputs)
    nb = xp.shape[0]
    y_prompt = np.stack(outs[:nb], 0).astype(np.float32)
    y_sample = np.stack(outs[nb:], 0).astype(np.float32)
    return (y_prompt, y_sample)
```

```python
import numpy as np
import concourse.bass as bass
import concourse.mybir as mybir
from concourse.bass_utils import run_bass_kernel_spmd

F32 = mybir.dt.float32
BF16 = mybir.dt.bfloat16
AF = mybir.ActivationFunctionType
ALU = mybir.AluOpType
AX = mybir.AxisListType

EPOCH = 24000
NDMA_SEM = 12


class Res:
    __slots__ = ("name", "last_w", "reads")

    def __init__(self, name=""):
        self.name = name
        self.last_w = None
        self.reads = []


class Prog:
    ENG = ("pe", "act", "dve", "pool", "sp")

    def __init__(self, nc, same_engine_sync=True):
        self.nc = nc
        self.same_engine_sync = same_engine_sync
        self.ops = {e: [] for e in self.ENG}
        self.cnt = {e: 0 for e in self.ENG}
        self.sems = {e: [nc.alloc_semaphore(name=f"s_{e}_0")] for e in ("pe", "act", "dve", "pool")}
        self.seen = {e: {} for e in self.ENG}
        self.dma_sems = {q: [nc.alloc_semaphore(name=f"d_{q}_{k}") for k in range(NDMA_SEM)] for q in ("sp", "pool")}
        self.dma_n = {"sp": 0, "pool": 0}
        self.n_wait = 0

    def _deps(self, reads, writes):
        deps = []
        for r in reads:
            if r.last_w is not None:
                deps.extend(r.last_w)
        for w in writes:
            if w.last_w is not None:
                deps.extend(w.last_w)
            deps.extend(w.reads)
        return deps

    def _waits(self, e, deps, own_sem_ids):
        waits = {}
        seen = self.seen[e]
        for (sem, val) in deps:
            sid = id(sem)
            if sid in own_sem_ids and (e == "pe" or not self.same_engine_sync):
                continue
            if seen.get(sid, 0) >= val:
                continue
            if sid not in waits or waits[sid][1] < val:
                waits[sid] = (sem, val)
        for sid, (sem, val) in waits.items():
            seen[sid] = val
        self.n_wait += len(waits)
        return list(waits.values())

    def _commit(self, ev, reads, writes, is_dma=False):
        for r in reads:
            r.reads.append(ev)
        for w in writes:
            if is_dma and w.last_w is not None and not w.reads:
                w.last_w = w.last_w + [ev]
            else:
                w.last_w = [ev]
            w.reads = []

    def op(self, e, fn, reads=(), writes=()):
        deps = self._deps(reads, writes)
        own = {id(s) for s in self.sems[e]}
        waits = self._waits(e, deps, own)
        if self.cnt[e] >= EPOCH:
            self.sems[e].append(self.nc.alloc_semaphore(name=f"s_{e}_{len(self.sems[e])}"))
            self.cnt[e] = 0
        self.cnt[e] += 1
        sem = self.sems[e][-1]
        ev = (sem, self.cnt[e])
        self.ops[e].append((waits, fn, sem, 1))
        self._commit(ev, reads, writes)
        return ev

    def dma(self, q, out, in_, reads=(), writes=(), **kw):
        deps = self._deps(reads, writes)
        n = self.dma_n[q]
        self.dma_n[q] += 1
        k = n % NDMA_SEM
        use = n // NDMA_SEM
        sem = self.dma_sems[q][k]
        if use > 0:
            deps.append((sem, 16 * use))
        own = {id(s) for s in self.sems[q]} if q in self.sems else set()
        waits = self._waits(q, deps, own)
        ev = (sem, 16 * (use + 1))
        self.ops[q].append((waits, (lambda eng, o=out, i=in_, kw=kw: eng.dma_start(out=o, in_=i, **kw)), sem, 16))
        self._commit(ev, reads, writes, is_dma=True)
        return ev

    def final_wait(self, e, evs):
        waits = self._waits(e, list(evs), set())
        self.ops[e].append((waits, None, None, 0))

    def emit(self):
        nc = self.nc
        ops = self.ops

        def run(eng, lst):
            for (waits, fn, sem, inc) in lst:
                for (s, v) in waits:
                    eng.wait_ge(s, v)
                if fn is not None:
                    ins = fn(eng)
                    ins.then_inc(sem, inc)

        with nc.Block() as block:
            @block.tensor
            def _(eng):
                run(eng, ops["pe"])

            @block.scalar
            def _(eng):
                run(eng, ops["act"])

            @block.vector
            def _(eng):
                run(eng, ops["dve"])

            @block.gpsimd
            def _(eng):
                run(eng, ops["pool"])

            @block.sync
            def _(eng):
                run(eng, ops["sp"])


D = 1024
PW = 9216
P1W = 5120
EPS = 1e-6
GN_EPS = 64e-5
ATT_W = (1, 2, 8)
ATT_D = (1, 4, 16)
C_Q, C_K, C_V, C_ZA, C_UB, C_ZB, C_UVC, C_ZC, C_RKV, C_LAT, C_ZD = 0, 768, 1536, 2304, 2560, 2816, 3072, 3584, 3840, 4608, 4864


class Ctx:
    pass


_UID = [0]


def uname(n):
    _UID[0] += 1
    return f"{n}_u{_UID[0]}"


DEBUG_SCRATCH = False
DEBUG_BARRIER = False
DBG_DIR = 0
DBG_TILE = 0


def build_program(S, L=2, branches=(0, 1, 2, 3), same_engine_sync=True):
    from contextlib import ExitStack
    NT = S // 128
    nc = bass.Bass("TRN2", target_bir_lowering=False)
    P = Prog(nc, same_engine_sync=same_engine_sync)
    c = Ctx()
    c.nc, c.P, c.S, c.NT, c.L, c.branches = nc, P, S, NT, L, branches

    def din(name, shape, dt=F32):
        return nc.dram_tensor(name, list(shape), dt, kind="ExternalInput").ap()

    def dscr(name, shape, dt=F32):
        return nc.dram_tensor(name, list(shape), dt, kind=("ExternalOutput" if DEBUG_SCRATCH else "Internal")).ap()

    c.x_in = din("x", [S, D])
    c.valid = din("valid", [128, NT])
    c.invc = din("invc", [128, NT * 4])
    c.ident = din("ident", [128, 128])
    c.amask = din("amask", [25, 128, 512])
    c.bandc = din("bandc", [128, 4 * 128])
    c.bandh = din("bandh", [16, 4 * 128])
    c.tri = din("tri", [128, 642])
    c.w = {}
    for name, shape in (("norm_g", [L, D]), ("w_in", [L, D, PW]), ("q_norm_g", [L, 64]), ("k_norm_g", [L, 64]),
                        ("pool_w", [L, 4, 64, 64]), ("pool_scale", [L, 256]), ("sg_norm_g", [L, 256]),
                        ("sg_wT", [L, 4, 128, 128]), ("sg_bT", [L, 128, 4]), ("mu_rkv", [L, 2, 768]), ("mu_lat", [L, 2, 128]),
                        ("w0", [L, 2, 256]), ("w_up", [L, 2, 64, 256]), ("a0", [L, 2, 256]), ("a_up", [L, 2, 64, 256]),
                        ("k_k", [L, 2, 256]), ("k_a", [L, 2, 256]), ("r_k", [L, 2, 256]), ("ln_g", [L, 256]), ("ln_b", [L, 256]),
                        ("w_branch", [L, 4, 256, D]), ("w_out", [L, D, D])):
        c.w[name] = din(name, shape)
    c.y_out = nc.dram_tensor("y", [S, D], F32, kind="ExternalOutput").ap()

    c.x1 = dscr("x1", [S, D])
    c.qT = dscr("qT_scr", [NT, 64, 1536], BF16)
    c.kT = dscr("kT_scr", [NT, 64, 1536], BF16)
    c.va = dscr("va_scr", [NT, 128, 780], BF16)
    c.zs = dscr("zs_scr", [S, 1024])
    c.ub = dscr("ub_scr", [S + 32, 256])
    c.uvc = dscr("uvc_scr", [S, 512])
    c.rkvl = dscr("rkvl_scr", [S + 2, 1024])
    c.ys = dscr("ys_scr", [S, 1024], BF16)
    c.osc = dscr("o_scr", [2, S, 256])
    c.r = {k: [Res(f"{k}{i}") for i in range(NT)] for k in ("x1", "qT", "kT", "va", "zs", "ub", "uvc", "rkvl", "ys", "osc0", "osc1", "y")}
    c.r_pad = Res("pads")
    if DEBUG_SCRATCH:
        c.dbg_proj = dscr("dbg_proj", [S, P1W])
        c.dbg_hT = dscr("dbg_hT", [S, D], BF16)

    c.idb = nc.alloc_sbuf_tensor("idb", [128, 128], BF16)
    c.idf = nc.alloc_sbuf_tensor("idf", [128, 128], F32)
    c.zero = nc.alloc_sbuf_tensor("zero", [128, 1024], F32)
    c.validt = nc.alloc_sbuf_tensor("validt", [128, NT], F32)
    c.r_const = Res("const")
    P.dma("sp", c.idf[:], c.ident, writes=[c.r_const])
    P.dma("pool", c.idb[:], c.ident, writes=[c.r_const])
    P.dma("sp", c.validt[:], c.valid, writes=[c.r_const])
    P.op("pool", lambda e: e.memset(c.zero[:], 0.0), [], [c.r_const])
    P.dma("sp", c.ub[0:16, :], c.zero[0:16, 0:256], reads=[c.r_const], writes=[c.r_pad])
    P.dma("sp", c.ub[S + 16:S + 32, :], c.zero[0:16, 0:256], reads=[c.r_const], writes=[c.r_pad])
    P.dma("sp", c.rkvl[0:1, :], c.zero[0:1, :], reads=[c.r_const], writes=[c.r_pad])
    P.dma("sp", c.rkvl[S + 1:S + 2, :], c.zero[0:1, :], reads=[c.r_const], writes=[c.r_pad])

    c.pbf = [nc.alloc_psum_tensor(f"pbf{i}", [128, 1024], BF16) for i in range(2)]
    c.pf = [nc.alloc_psum_tensor(f"pf{i}", [128, 512], F32) for i in range(6)]
    c.r_pbf = [Res(f"pbf{i}") for i in range(2)]
    c.r_pf = [Res(f"pf{i}") for i in range(6)]

    for l in range(L):
        x_src = c.x_in if l == 0 else c.x1
        x_src_res = None if l == 0 else c.r["x1"]
        x_dst = c.x1 if l < L - 1 else c.y_out
        x_dst_res = c.r["x1"] if l < L - 1 else c.r["y"]
        phase1(c, l, x_src, x_src_res)
        barrier(c)
        if 2 in branches:
            phase2c(c, l)
            barrier(c)
        if 1 in branches:
            phase2b(c, l)
            barrier(c)
        if 0 in branches:
            phase2a(c, l)
            barrier(c)
        if 3 in branches:
            phase2d(c, l)
            barrier(c)
        phase3(c, l, x_src, x_src_res, x_dst, x_dst_res)
        barrier(c)
    P.emit()
    return nc


def barrier(c):
    P = c.P
    evs = []
    for e in ("pe", "act", "dve", "pool"):
        if P.cnt[e] > 0:
            evs.append((P.sems[e][-1], P.cnt[e]))
    for q in ("sp", "pool"):
        n = P.dma_n[q]
        for k, s in enumerate(P.dma_sems[q]):
            uses = (n - k + NDMA_SEM - 1) // NDMA_SEM if n > k else 0
            if uses > 0:
                evs.append((s, 16 * uses))
    for e in Prog.ENG:
        P.final_wait(e, evs)


class Pool2:
    def __init__(self, stack, nc, name, shape, dt, n=2):
        self.t = [stack.enter_context(nc.sbuf_tensor(uname(f"{name}{i}"), list(shape), dt)) for i in range(n)]
        self.r = [Res(f"{name}{i}") for i in range(n)]
        self.n = n
        self.i = -1

    def next(self):
        self.i = (self.i + 1) % self.n
        return self.t[self.i], self.r[self.i]


NL = 2


def run_lanes(lane_tiles, body, skew=2):
    K = len(lane_tiles)
    its = [iter(t) for t in lane_tiles]
    gens = [None] * K
    done = [False] * K
    rnd = 0
    while not all(done):
        for k in range(K):
            if done[k]:
                continue
            if gens[k] is None:
                if rnd < k * skew:
                    continue
                i = next(its[k], None)
                if i is None:
                    done[k] = True
                    continue
                gens[k] = body(i, k)
            try:
                next(gens[k])
            except StopIteration:
                gens[k] = None
        rnd += 1


def lane_banks(c, k):
    return [(c.pf[3 * k + j], c.r_pf[3 * k + j]) for j in range(3)], (c.pbf[k], c.r_pbf[k])


def rms_h(c, xt, r_xt, gbc, r_w, pl, width=1024):
    P = c.P
    junk, r_junk = pl["junk"].next()
    ss, r_ss = pl["ss"].next()
    h, r_h = pl["h"].next()
    P.op("act", lambda e: e.activation(out=junk[:], in_=xt[:], func=AF.Square, accum_out=ss[:, 0:1]), [r_xt], [r_junk, r_ss])
    P.op("dve", lambda e: e.tensor_scalar(out=ss[:, 1:2], in0=ss[:, 0:1], scalar1=1.0 / width, scalar2=EPS, op0=ALU.mult, op1=ALU.add), [r_ss], [r_ss])
    P.op("act", lambda e: e.activation(out=ss[:, 2:3], in_=ss[:, 1:2], func=AF.Sqrt), [r_ss], [r_ss])
    P.op("dve", lambda e: e.reciprocal(out=ss[:, 3:4], in_=ss[:, 2:3]), [r_ss], [r_ss])
    P.op("dve", lambda e: e.scalar_tensor_tensor(out=h[:], in0=xt[:], scalar=ss[:, 3:4], in1=gbc[:], op0=ALU.mult, op1=ALU.mult),
         [r_xt, r_ss, r_w], [r_h])
    return h, r_h


def transpose8(c, src, r_src, dst, r_dst, pbk, eng="act", n=8):
    P = c.P
    pb, r_pb = pbk
    for kc in range(n):
        P.op("pe", lambda e, kc=kc: e.transpose(out=pb[:, kc * 128:(kc + 1) * 128], in_=src[:, kc * 128:(kc + 1) * 128], identity=c.idb[:]),
             [r_src, c.r_const], [r_pb])
    if eng == "act":
        P.op("act", lambda e: e.copy(out=dst[:, 0:n * 128], in_=pb[:, 0:n * 128]), [r_pb], [r_dst])
    else:
        P.op("dve", lambda e: e.tensor_copy(out=dst[:, 0:n * 128], in_=pb[:, 0:n * 128]), [r_pb], [r_dst])


def phase1(c, l, x_src, x_src_res):
    from contextlib import ExitStack
    P, nc, NT = c.P, c.nc, c.NT
    with ExitStack() as st:
        sb = lambda n, s, d: st.enter_context(nc.sbuf_tensor(uname(n), list(s), d))
        W = sb("W1", [128, 8, P1W], BF16)
        r_W = Res("W1")
        for kc in range(8):
            P.dma("pool", W[:, kc, :], c.w["w_in"][l, kc * 128:(kc + 1) * 128, 0:P1W], writes=[r_W])
        gbc = sb("gbc", [128, D], F32)
        g64 = sb("g64", [128, 128], F32)
        gqk = sb("gqk", [128, 1536], F32)
        r_w = Res("p1w")
        P.dma("sp", gbc[:], c.w["norm_g"][l:l + 1, :].partition_broadcast(128), writes=[r_w])
        P.dma("sp", g64[:, 0:64], c.w["q_norm_g"][l:l + 1, :].partition_broadcast(128), writes=[r_w])
        P.dma("sp", g64[:, 64:128], c.w["k_norm_g"][l:l + 1, :].partition_broadcast(128), writes=[r_w])
        P.op("dve", lambda e: e.tensor_scalar(out=gqk[:, 0:768].rearrange("p (h d) -> p h d", d=64),
                                              in0=g64[:, 0:64].unsqueeze(1).to_broadcast([128, 12, 64]),
                                              scalar1=0.125, scalar2=None, op0=ALU.mult), [r_w], [r_w])
        P.op("dve", lambda e: e.tensor_copy(out=gqk[:, 768:1536].rearrange("p (h d) -> p h d", d=64),
                                            in_=g64[:, 64:128].unsqueeze(1).to_broadcast([128, 12, 64])), [r_w], [r_w])

        def mkpools(k):
            mk = lambda name, shape, dt, n=1: Pool2(st, nc, f"{name}_l{k}_", shape, dt, n)
            d = dict(junk=mk("junk", [128, D], BF16), ss=mk("ss", [128, 4], F32), h=mk("h", [128, D], BF16), hT=mk("hT", [128, D], BF16),
                     xt=mk("xt", [128, D], F32), proj=mk("proj", [128, P1W], F32), zsb=mk("zsb", [128, 1024], F32))
            if 0 in c.branches:
                d.update(sq=mk("sq", [128, 1536], F32), s24=mk("s24", [128, 72], F32), qkn=mk("qkn", [128, 1536], BF16),
                         qkT=mk("qkT", [64, 3072], BF16), vaug=mk("vaug", [128, 780], BF16))
            return d
        pools = [mkpools(k) for k in range(NL)]

        def body(i, k):
            pl = pools[k]
            banks, pbk = lane_banks(c, k)
            rows = slice(i * 128, (i + 1) * 128)
            xt, r_xt = pl["xt"].next()
            P.dma("sp", xt[:], x_src[rows, :], reads=([x_src_res[i]] if x_src_res else []), writes=[r_xt])
            h, r_h = rms_h(c, xt, r_xt, gbc, r_w, pl)
            yield
            hT, r_hT = pl["hT"].next()
            transpose8(c, h, r_h, hT, r_hT, pbk)
            yield
            proj, r_pj = pl["proj"].next()
            for blk in range(P1W // 512):
                ps, r_ps = banks[blk % 3]
                for kc in range(8):
                    P.op("pe", lambda e, kc=kc, blk=blk, ps=ps: e.matmul(out=ps[:, :], lhsT=hT[:, kc * 128:(kc + 1) * 128],
                                                                    rhs=W[:, kc, blk * 512:(blk + 1) * 512], start=(kc == 0), stop=(kc == 7)),
                         [r_hT, r_W], [r_ps])
                if blk % 2 == 0:
                    P.op("act", lambda e, blk=blk, ps=ps: e.copy(out=proj[:, blk * 512:(blk + 1) * 512], in_=ps[:, :]), [r_ps], [r_pj])
                else:
                    P.op("dve", lambda e, blk=blk, ps=ps: e.tensor_copy(out=proj[:, blk * 512:(blk + 1) * 512], in_=ps[:, :]), [r_ps], [r_pj])
                if blk % 2 == 1:
                    yield
            if 1 in c.branches:
                P.dma("pool", c.ub[16 + i * 128:16 + (i + 1) * 128, :], proj[:, C_UB:C_UB + 256], reads=[r_pj], writes=[c.r["ub"][i]])
            if 2 in c.branches:
                P.dma("pool", c.uvc[rows, :], proj[:, C_UVC:C_UVC + 512], reads=[r_pj], writes=[c.r["uvc"][i]])
            if 3 in c.branches:
                P.dma("pool", c.rkvl[1 + i * 128:1 + (i + 1) * 128, :], proj[:, C_RKV:C_RKV + 1024], reads=[r_pj], writes=[c.r["rkvl"][i]])
            zt, r_zt = pl["zsb"].next()
            for b, cz in enumerate((C_ZA, C_ZB, C_ZC, C_ZD)):
                P.op("act", lambda e, b=b, cz=cz: e.activation(out=zt[:, b * 256:(b + 1) * 256], in_=proj[:, cz:cz + 256], func=AF.Silu),
                     [r_pj], [r_zt])
            P.dma("pool", c.zs[rows, :], zt[:], reads=[r_zt], writes=[c.r["zs"][i]])
            yield
            if 0 in c.branches:
                sq, r_sq = pl["sq"].next()
                s, r_s = pl["s24"].next()
                qn, r_qn = pl["qkn"].next()
                v24 = lambda ap: ap.rearrange("p (h d) -> p h d", d=64)
                P.op("pool", lambda e: e.tensor_tensor(out=sq[:], in0=proj[:, 0:1536], in1=proj[:, 0:1536], op=ALU.mult), [r_pj], [r_sq])
                P.op("dve", lambda e: e.tensor_reduce(out=s[:, 0:24], in_=v24(sq[:]), axis=AX.X, op=ALU.add), [r_sq], [r_s])
                P.op("dve", lambda e: e.tensor_scalar(out=s[:, 24:48], in0=s[:, 0:24], scalar1=1.0 / 64, scalar2=EPS, op0=ALU.mult, op1=ALU.add), [r_s], [r_s])
                P.op("act", lambda e: e.activation(out=s[:, 0:24], in_=s[:, 24:48], func=AF.Sqrt), [r_s], [r_s])
                P.op("dve", lambda e: e.reciprocal(out=s[:, 48:72], in_=s[:, 0:24]), [r_s], [r_s])
                yield
                P.op("dve", lambda e: e.tensor_tensor(out=v24(sq[:]), in0=v24(proj[:, 0:1536]),
                                                      in1=s[:, 48:72].unsqueeze(2).to_broadcast([128, 24, 64]), op=ALU.mult), [r_pj, r_s], [r_sq])
                P.op("pool", lambda e: e.tensor_tensor(out=qn[:], in0=sq[:], in1=gqk[:], op=ALU.mult), [r_sq, r_w], [r_qn])
                va, r_va = pl["vaug"].next()
                P.op("act", lambda e: e.copy(out=va[:].rearrange("p (h d) -> p h d", d=65)[:, :, 0:64],
                                             in_=proj[:, C_V:C_V + 768].rearrange("p (h d) -> p h d", d=64)), [r_pj], [r_va])
                P.op("dve", lambda e: e.tensor_copy(out=va[:].rearrange("p (h d) -> p h d", d=65)[:, :, 64:65],
                                                    in_=c.validt[:, i:i + 1].unsqueeze(1).to_broadcast([128, 12, 1])), [c.r_const], [r_va])
                P.dma("pool", c.va[i], va[:], reads=[r_va], writes=[c.r["va"][i]])
                yield
                qT, r_qT = pl["qkT"].next()
                pb, r_pb = pbk
                for grp in range(3):
                    for j in range(8):
                        hh = grp * 8 + j
                        P.op("pe", lambda e, hh=hh, j=j: e.transpose(out=pb[0:64, j * 128:(j + 1) * 128], in_=qn[:, hh * 64:(hh + 1) * 64], identity=c.idb[:]),
                             [r_qn, c.r_const], [r_pb])
                    if grp % 2 == 0:
                        P.op("dve", lambda e, grp=grp: e.tensor_copy(out=qT[:, grp * 1024:(grp + 1) * 1024], in_=pb[0:64, :]), [r_pb], [r_qT])
                    else:
                        P.op("act", lambda e, grp=grp: e.copy(out=qT[:, grp * 1024:(grp + 1) * 1024], in_=pb[0:64, :]), [r_pb], [r_qT])
                    yield
                P.dma("pool", c.qT[i], qT[:, 0:1536], reads=[r_qT], writes=[c.r["qT"][i]])
                P.dma("pool", c.kT[i], qT[:, 1536:3072], reads=[r_qT], writes=[c.r["kT"][i]])

        run_lanes([list(range(k, NT, NL)) for k in range(NL)], body, skew=4)


def phase2c(c, l):
    from contextlib import ExitStack
    P, nc, NT = c.P, c.nc, c.NT
    with ExitStack() as st:
        sb = lambda n, s, d: st.enter_context(nc.sbuf_tensor(uname(n), list(s), d))
        sgw = sb("sgw", [128, 4, 128], BF16)
        sgn = sb("sgn", [128, 256], F32)
        sgb = sb("sgb", [128, 4], F32)
        r_w = Res("p2cw")
        P.dma("pool", sgw[:], c.w["sg_wT"][l].rearrange("g s t -> s g t"), writes=[r_w])
        P.dma("sp", sgn[:], c.w["sg_norm_g"][l:l + 1, :].partition_broadcast(128), writes=[r_w])
        P.dma("sp", sgb[:], c.w["sg_bT"][l], writes=[r_w])

        def mkpools(k):
            mk = lambda name, shape, dt, n=2: Pool2(st, nc, f"{name}_l{k}_", shape, dt, n)
            return dict(uv=mk("uv", [128, 512], F32), zc=mk("zc", [128, 256], F32), junk=mk("junkc", [128, 256], F32, 1), ss=mk("ssc", [128, 4], F32),
                        vn=mk("vn", [128, 256], BF16), sv=mk("sv", [128, 256], F32), ys=mk("ysc", [128, 256], BF16))
        pools = [mkpools(k) for k in range(6)]

        def body(i, k):
            pl = pools[k]
            banks = [(c.pf[k], c.r_pf[k])]
            rows = slice(i * 128, (i + 1) * 128)
            uv, r_uv = pl["uv"].next()
            zc, r_zc = pl["zc"].next()
            P.dma("sp", uv[:], c.uvc[rows, :], reads=[c.r["uvc"][i]], writes=[r_uv])
            P.dma("sp", zc[:], c.zs[rows, 512:768], reads=[c.r["zs"][i]], writes=[r_zc])
            junk, r_j = pl["junk"].next()
            ss, r_ss = pl["ss"].next()
            vn, r_vn = pl["vn"].next()
            sv, r_sv = pl["sv"].next()
            ysc, r_y = pl["ys"].next()
            P.op("act", lambda e: e.activation(out=junk[:], in_=uv[:, 256:512], func=AF.Square, accum_out=ss[:, 0:1]), [r_uv], [r_j, r_ss])
            P.op("dve", lambda e: e.tensor_scalar(out=ss[:, 1:2], in0=ss[:, 0:1], scalar1=1.0 / 256, scalar2=EPS, op0=ALU.mult, op1=ALU.add), [r_ss], [r_ss])
            yield
            P.op("act", lambda e: e.activation(out=ss[:, 2:3], in_=ss[:, 1:2], func=AF.Sqrt), [r_ss], [r_ss])
            P.op("dve", lambda e: e.reciprocal(out=ss[:, 3:4], in_=ss[:, 2:3]), [r_ss], [r_ss])
            P.op("dve", lambda e: e.scalar_tensor_tensor(out=vn[:], in0=uv[:, 256:512], scalar=ss[:, 3:4], in1=sgn[:], op0=ALU.mult, op1=ALU.mult),
                 [r_uv, r_ss, r_w], [r_vn])
            yield
            ps, r_ps = banks[0]
            for g in range(4):
                P.op("pe", lambda e, g=g: e.matmul(out=ps[:, g * 64:(g + 1) * 64], lhsT=sgw[:, g, :], rhs=vn[:, g * 64:(g + 1) * 64], start=True, stop=True),
                     [r_vn, r_w], [r_ps])
            P.op("dve", lambda e: e.tensor_tensor(out=sv[:].rearrange("p (g d) -> p g d", d=64), in0=ps[:, 0:256].rearrange("p (g d) -> p g d", d=64),
                                                  in1=sgb[:].unsqueeze(2).to_broadcast([128, 4, 64]), op=ALU.add), [r_ps, r_w], [r_sv])
            yield
            P.op("pool", lambda e: e.tensor_tensor(out=sv[:], in0=sv[:], in1=uv[:, 0:256], op=ALU.mult), [r_sv, r_uv], [r_sv])
            P.op("dve", lambda e: e.tensor_tensor(out=ysc[:], in0=sv[:], in1=zc[:], op=ALU.mult), [r_sv, r_zc], [r_y])
            P.dma("pool", c.ys[rows, 512:768], ysc[:], reads=[r_y], writes=[c.r["ys"][i]])

        run_lanes([list(range(k, NT, 6)) for k in range(6)], body, skew=1)


def phase3(c, l, x_src, x_src_res, x_dst, x_dst_res):
    from contextlib import ExitStack
    P, nc, NT = c.P, c.nc, c.NT
    with ExitStack() as st:
        sb = lambda n, s, d: st.enter_context(nc.sbuf_tensor(uname(n), list(s), d))
        Wg = sb("Wg", [128, 8, 4096], BF16)
        Wbr = sb("Wbr", [128, 8, 1024], BF16)
        Wo = sb("Wo", [128, 8, 1024], BF16)
        gbc = sb("gbc3", [128, D], F32)
        r_W = Res("W3")
        for kc in range(8):
            P.dma("pool", Wg[:, kc, :], c.w["w_in"][l, kc * 128:(kc + 1) * 128, P1W:PW], writes=[r_W])
        wbr_flat = c.w["w_branch"][l].rearrange("b c n -> (b c) n")
        for kc in range(8):
            P.dma("pool", Wbr[:, kc, :], wbr_flat[kc * 128:(kc + 1) * 128, :], writes=[r_W])
            P.dma("pool", Wo[:, kc, :], c.w["w_out"][l, kc * 128:(kc + 1) * 128, :], writes=[r_W])
        P.dma("sp", gbc[:], c.w["norm_g"][l:l + 1, :].partition_broadcast(128), writes=[r_W])

        def mkpools(k):
            mk = lambda name, shape, dt, n=1: Pool2(st, nc, f"{name}_l{k}_", shape, dt, n)
            return dict(junk=mk("junk3", [128, D], BF16), ss=mk("ss3", [128, 4], F32), h=mk("h3", [128, D], BF16), hT=mk("hT3", [128, D], BF16),
                        xt=mk("xt3", [128, D], F32), ys=mk("ys3", [128, D], BF16), ysT=mk("ysT3", [128, D], BF16), gs=mk("gs3", [128, 512], F32, 2),
                        tmp=mk("tmp3", [128, 512], F32, 2), acc=mk("acc3", [128, D], F32), m=mk("m3", [128, D], BF16), mT=mk("mT3", [128, D], BF16),
                        xo=mk("xo3", [128, D], F32))
        pools = [mkpools(k) for k in range(NL)]

        def body(i, k):
            pl = pools[k]
            banks, pbk = lane_banks(c, k)
            rows = slice(i * 128, (i + 1) * 128)
            xt, r_xt = pl["xt"].next()
            P.dma("sp", xt[:], x_src[rows, :], reads=([x_src_res[i]] if x_src_res else []), writes=[r_xt])
            ys, r_ys = pl["ys"].next()
            P.dma("sp", ys[:], c.ys[rows, :], reads=[c.r["ys"][i]], writes=[r_ys])
            h, r_h = rms_h(c, xt, r_xt, gbc, r_W, pl)
            yield
            hT, r_hT = pl["hT"].next()
            transpose8(c, h, r_h, hT, r_hT, pbk)
            yield
            ysT, r_ysT = pl["ysT"].next()
            transpose8(c, ys, r_ys, ysT, r_ysT, pbk, eng="dve")
            yield
            acc, r_acc = pl["acc"].next()
            for bi, b in enumerate(c.branches):
                for cb in range(2):
                    pg, r_pg = banks[0]
                    pbr, r_pbr = banks[1]
                    col = b * 1024 + cb * 512
                    for kc in range(8):
                        P.op("pe", lambda e, kc=kc, col=col: e.matmul(out=pg[:, :], lhsT=hT[:, kc * 128:(kc + 1) * 128], rhs=Wg[:, kc, col:col + 512],
                                                                      start=(kc == 0), stop=(kc == 7)), [r_hT, r_W], [r_pg])
                    gs, r_gs = pl["gs"].next()
                    P.op("act", lambda e, gs=gs: e.activation(out=gs[:], in_=pg[:, :], func=AF.Sigmoid), [r_pg], [r_gs])
                    for kk in range(2):
                        kc = 2 * b + kk
                        P.op("pe", lambda e, kc=kc, kk=kk, cb=cb: e.matmul(out=pbr[:, :], lhsT=ysT[:, kc * 128:(kc + 1) * 128],
                                                                             rhs=Wbr[:, kc, cb * 512:(cb + 1) * 512], start=(kk == 0), stop=(kk == 1)),
                             [r_ysT, r_W], [r_pbr])
                    if bi == 0:
                        P.op("dve", lambda e, cb=cb, gs=gs: e.tensor_tensor(out=acc[:, cb * 512:(cb + 1) * 512], in0=gs[:], in1=pbr[:, :], op=ALU.mult),
                             [r_gs, r_pbr], [r_acc])
                    else:
                        tmp, r_tmp = pl["tmp"].next()
                        P.op("dve", lambda e, gs=gs, tmp=tmp: e.tensor_tensor(out=tmp[:], in0=gs[:], in1=pbr[:, :], op=ALU.mult), [r_gs, r_pbr], [r_tmp])
                        P.op("pool", lambda e, cb=cb, tmp=tmp: e.tensor_tensor(out=acc[:, cb * 512:(cb + 1) * 512], in0=acc[:, cb * 512:(cb + 1) * 512], in1=tmp[:], op=ALU.add),
                             [r_tmp, r_acc], [r_acc])
                    yield
            m, r_m = pl["m"].next()
            mT, r_mT = pl["mT"].next()
            P.op("act", lambda e: e.copy(out=m[:], in_=acc[:]), [r_acc], [r_m])
            yield
            transpose8(c, m, r_m, mT, r_mT, pbk)
            yield
            xo, r_xo = pl["xo"].next()
            po, r_po = banks[2]
            for cb in range(2):
                for kc in range(8):
                    P.op("pe", lambda e, kc=kc, cb=cb: e.matmul(out=po[:, :], lhsT=mT[:, kc * 128:(kc + 1) * 128], rhs=Wo[:, kc, cb * 512:(cb + 1) * 512],
                                                                start=(kc == 0), stop=(kc == 7)), [r_mT, r_W], [r_po])
                P.op("dve", lambda e, cb=cb: e.tensor_tensor(out=xo[:, cb * 512:(cb + 1) * 512], in0=po[:, :], in1=xt[:, cb * 512:(cb + 1) * 512], op=ALU.add),
                     [r_po, r_xt], [r_xo])
                yield
            P.dma("pool", x_dst[rows, :], xo[:], reads=[r_xo], writes=[x_dst_res[i]])

        run_lanes([list(range(k, NT, NL)) for k in range(NL)], body, skew=5)


def phase2b(c, l):
    from contextlib import ExitStack
    P, nc, NT = c.P, c.nc, c.NT
    with ExitStack() as st:
        sb = lambda n, s, d: st.enter_context(nc.sbuf_tensor(uname(n), list(s), d))
        bandc = sb("bandc", [128, 512], F32)
        bandh = sb("bandh", [16, 512], F32)
        pw = sb("pw", [128, 2, 128], BF16)
        psc = sb("psc", [128, 256], F32)
        invc = sb("invc", [128, NT * 4], F32)
        r_w = Res("p2bw")
        P.dma("sp", bandc[:], c.bandc, writes=[r_w])
        P.dma("sp", bandh[:], c.bandh, writes=[r_w])
        P.dma("sp", psc[:], c.w["pool_scale"][l:l + 1, :].partition_broadcast(128), writes=[r_w])
        P.dma("sp", invc[:], c.invc, writes=[r_w])
        P.op("pool", lambda e: e.memset(pw[:], 0.0), [], [r_w])
        for g in range(4):
            j, gl = g // 2, g % 2
            P.dma("pool", pw[gl * 64:(gl + 1) * 64, j, gl * 64:(gl + 1) * 64], c.w["pool_w"][l, g], writes=[r_w])

        def mkpools(k):
            mk = lambda name, shape, dt, n=2: Pool2(st, nc, f"{name}_l{k}_", shape, dt, n)
            return dict(u=mk("ub", [128, 256], F32), uh=mk("uh", [16, 256], F32), zb=mk("zb", [128, 256], F32), d=mk("dpool", [128, 256], BF16),
                        dT=mk("dT", [128, 256], BF16), y=mk("ybf", [128, 256], F32), yo=mk("ybo", [128, 256], BF16))
        pools = [mkpools(k) for k in range(3)]

        def body(i, k):
            pl = pools[k]
            banks = [(c.pf[2 * k], c.r_pf[2 * k]), (c.pf[2 * k + 1], c.r_pf[2 * k + 1])]
            pbk = (c.pbf[k % 2], c.r_pbf[k % 2])
            rows = slice(i * 128, (i + 1) * 128)
            u, r_u = pl["u"].next()
            uh, r_uh = pl["uh"].next()
            zb, r_zb = pl["zb"].next()
            d, r_d = pl["d"].next()
            dT, r_dT = pl["dT"].next()
            y, r_y = pl["y"].next()
            yo, r_yo = pl["yo"].next()
            nb = [c.r["ub"][j] for j in (i - 1, i, i + 1) if 0 <= j < NT] + [c.r_pad]
            P.dma("sp", u[:], c.ub[16 + i * 128:16 + (i + 1) * 128, :], reads=[c.r["ub"][i]], writes=[r_u])
            P.dma("sp", uh[0:8, :], c.ub[16 + i * 128 - 8:16 + i * 128, :], reads=nb, writes=[r_uh])
            P.dma("sp", uh[8:16, :], c.ub[16 + (i + 1) * 128:16 + (i + 1) * 128 + 8, :], reads=nb, writes=[r_uh])
            P.dma("sp", zb[:], c.zs[rows, 256:512], reads=[c.r["zs"][i]], writes=[r_zb])
            yield
            ps, r_ps = banks[0]
            for g in range(4):
                P.op("pe", lambda e, g=g: e.matmul(out=ps[:, g * 64:(g + 1) * 64], lhsT=bandc[:, g * 128:(g + 1) * 128], rhs=u[:, g * 64:(g + 1) * 64],
                                                   start=True, stop=False), [r_u, r_w], [r_ps])
                P.op("pe", lambda e, g=g: e.matmul(out=ps[:, g * 64:(g + 1) * 64], lhsT=bandh[:, g * 128:(g + 1) * 128], rhs=uh[:, g * 64:(g + 1) * 64],
                                                   start=False, stop=True), [r_uh, r_w], [r_ps])
            for g in range(4):
                P.op("dve", lambda e, g=g: e.scalar_tensor_tensor(out=d[:, g * 64:(g + 1) * 64], in0=ps[:, g * 64:(g + 1) * 64],
                                                                  scalar=invc[:, i * 4 + g:i * 4 + g + 1], in1=u[:, g * 64:(g + 1) * 64],
                                                                  op0=ALU.mult, op1=ALU.subtract), [r_ps, r_u, r_w], [r_d])
            yield
            transpose8(c, d, r_d, dT, r_dT, pbk, n=2)
            yield
            ps2, r_ps2 = banks[1]
            for j in range(2):
                P.op("pe", lambda e, j=j: e.matmul(out=ps2[:, j * 128:(j + 1) * 128], lhsT=dT[:, j * 128:(j + 1) * 128], rhs=pw[:, j, :], start=True, stop=True),
                     [r_dT, r_w], [r_ps2])
            P.op("dve", lambda e: e.tensor_tensor(out=y[:], in0=ps2[:, 0:256], in1=psc[:], op=ALU.mult), [r_ps2, r_w], [r_y])
            yield
            P.op("pool", lambda e: e.tensor_tensor(out=yo[:], in0=y[:], in1=zb[:], op=ALU.mult), [r_y, r_zb], [r_yo])
            P.dma("pool", c.ys[rows, 256:512], yo[:], reads=[r_yo], writes=[c.r["ys"][i]])

        run_lanes([list(range(k, NT, 3)) for k in range(3)], body, skew=2)


def phase2a(c, l):
    from contextlib import ExitStack
    P, nc, NT = c.P, c.nc, c.NT
    with ExitStack() as st:
        sb = lambda n, s, d: st.enter_context(nc.sbuf_tensor(uname(n), list(s), d))
        masks = sb("amask", [128, 25, 512], BF16)
        r_w = Res("p2aw")
        for ci in range(25):
            P.dma("pool", masks[:, ci, :], c.amask[ci], writes=[r_w])
        depth = [2 * w + 2 for w in ATT_W]
        kring = [Pool2(st, nc, f"kr{g}_", [64, 512], BF16, depth[g]) for g in range(3)]
        vring = [Pool2(st, nc, f"vr{g}_", [128, 260], BF16, depth[g]) for g in range(3)]
        loaded = [-1, -1, -1]
        q_ = Pool2(st, nc, "qTa", [64, 1536], BF16, 2)
        za_ = Pool2(st, nc, "za", [128, 256], F32, 2)
        e_ = Pool2(st, nc, "eexp", [128, 512], BF16, 3)
        p_ = Pool2(st, nc, "pexp", [128, 512], BF16, 6)
        dn_ = Pool2(st, nc, "den", [128, 8], F32, 2)
        ya_ = Pool2(st, nc, "yaf", [128, 256], F32, 2)
        yo_ = Pool2(st, nc, "yao", [128, 256], BF16, 2)
        mcol = []
        ci = 0
        for g in range(3):
            mcol.append({coff: ci + k for k, coff in enumerate(range(-ATT_W[g], ATT_W[g] + 1))})
            ci += 2 * ATT_W[g] + 1

        def ensure(g, upto):
            while loaded[g] < min(upto, NT - 1):
                j = loaded[g] + 1
                kt, r_kt = kring[g].t[j % depth[g]], kring[g].r[j % depth[g]]
                vt, r_vt = vring[g].t[j % depth[g]], vring[g].r[j % depth[g]]
                P.dma("sp", kt[:], c.kT[j][:, g * 512:(g + 1) * 512], reads=[c.r["kT"][j]], writes=[r_kt])
                P.dma("sp", vt[:], c.va[j][:, g * 260:(g + 1) * 260], reads=[c.r["va"][j]], writes=[r_vt])
                loaded[g] = j

        LOOK = 3
        tiles = {}

        def tile_begin(i):
            rows = slice(i * 128, (i + 1) * 128)
            for g in range(3):
                ensure(g, i + ATT_W[g])
            qT, r_q = q_.next()
            za, r_za = za_.next()
            P.dma("sp", qT[:], c.qT[i], reads=[c.r["qT"][i]], writes=[r_q])
            P.dma("sp", za[:], c.zs[rows, 0:256], reads=[c.r["zs"][i]], writes=[r_za])
            chunks = [(g, coff) for g in range(3) for coff in range(-ATT_W[g], ATT_W[g] + 1) if 0 <= i + coff < NT]
            tiles[i] = dict(qT=qT, r_q=r_q, za=za, r_za=r_za, n=len(chunks), pso=c.pf[4 + i % 2], r_pso=c.r_pf[4 + i % 2])
            return chunks

        def stage_qk(i, idx, g, coff, seq):
            t = tiles[i]
            qT, r_q = t["qT"], t["r_q"]
            j = i + coff
            kt, r_kt = kring[g].t[j % depth[g]], kring[g].r[j % depth[g]]
            pss, r_pss = c.pf[seq % 4], c.r_pf[seq % 4]
            for h in range(4):
                P.op("pe", lambda e, h=h: e.matmul(out=pss[:, h * 128:(h + 1) * 128], lhsT=kt[:, h * 128:(h + 1) * 128],
                                                   rhs=qT[:, (g * 4 + h) * 128:(g * 4 + h + 1) * 128], start=True, stop=True),
                     [r_kt, r_q], [r_pss])
            ex, r_ex = e_.next()
            pp, r_pp = p_.next()
            P.op("act", lambda e: e.activation(out=ex[:], in_=pss[:, :], func=AF.Exp), [r_pss], [r_ex])
            mc = mcol[g][coff]
            P.op("dve", lambda e: e.tensor_tensor(out=pp[:], in0=ex[:], in1=masks[:, mc, :], op=ALU.mult), [r_ex, r_w], [r_pp])
            return pp, r_pp

        def stage_pv(i, idx, g, coff, pp, r_pp):
            t = tiles[i]
            pso, r_pso = t["pso"], t["r_pso"]
            j = i + coff
            vt, r_vt = vring[g].t[j % depth[g]], vring[g].r[j % depth[g]]
            n = t["n"]
            for h in range(4):
                P.op("pe", lambda e, h=h: e.matmul(out=pso[:, h * 65:(h + 1) * 65], lhsT=pp[:, h * 128:(h + 1) * 128],
                                                   rhs=vt[:, h * 65:(h + 1) * 65], start=(idx == 0 and h == 0), stop=(idx == n - 1), skip_group_check=True),
                     [r_pp, r_vt], [r_pso])
            if idx == n - 1:
                tile_end(i)

        def tile_end(i):
            rows = slice(i * 128, (i + 1) * 128)
            t = tiles.pop(i)
            pso, r_pso, za, r_za = t["pso"], t["r_pso"], t["za"], t["r_za"]
            dn, r_dn = dn_.next()
            ya, r_ya = ya_.next()
            yo, r_yo = yo_.next()
            pv = pso[:, 0:260].rearrange("p (h d) -> p h d", d=65)
            P.op("dve", lambda e: e.tensor_scalar(out=dn[:, 0:4].unsqueeze(2), in0=pv[:, :, 64:65], scalar1=1e-30, scalar2=None, op0=ALU.max), [r_pso], [r_dn])
            P.op("dve", lambda e: e.reciprocal(out=dn[:, 4:8], in_=dn[:, 0:4]), [r_dn], [r_dn])
            P.op("dve", lambda e: e.tensor_tensor(out=ya[:].rearrange("p (h d) -> p h d", d=64), in0=pv[:, :, 0:64],
                                                  in1=dn[:, 4:8].unsqueeze(2).to_broadcast([128, 4, 64]), op=ALU.mult), [r_pso, r_dn], [r_ya])
            P.op("pool", lambda e: e.tensor_tensor(out=yo[:], in0=ya[:], in1=za[:], op=ALU.mult), [r_ya, r_za], [r_yo])
            P.dma("pool", c.ys[rows, 0:256], yo[:], reads=[r_yo], writes=[c.r["ys"][i]])

        def stream():
            for i in range(NT):
                first = True
                chunks = None
                for idx in range(10 ** 9):
                    if first:
                        chunks = tile_begin(i)
                        first = False
                    if idx >= len(chunks):
                        break
                    g, coff = chunks[idx]
                    yield (i, idx, g, coff)

        pend = []
        for seq, (i, idx, g, coff) in enumerate(stream()):
            pp, r_pp = stage_qk(i, idx, g, coff, seq)
            pend.append((i, idx, g, coff, pp, r_pp))
            if len(pend) > LOOK:
                stage_pv(*pend.pop(0))
        while pend:
            stage_pv(*pend.pop(0))


def phase2d(c, l):
    from contextlib import ExitStack
    P, nc, NT = c.P, c.nc, c.NT
    NEG = -float(np.exp(-0.5))
    def run_dir(d):
        with ExitStack() as st:
            sb = lambda n, s, dt: st.enter_context(nc.sbuf_tensor(uname(n), list(s), dt))
            r_w = Res("p2dw")
            tri = sb("tri", [128, 642], F32)
            P.dma("sp", tri[:], c.tri, writes=[r_w])
            incl = tri[:, 128 * d:128 * d + 128]
            ones_m = tri[:, 512:640]
            ones_c = tri[:, 640:641]
            mask4 = sb("mask4", [128, 512], F32)
            maskT4 = sb("maskT4", [128, 512], F32)
            for q in range(4):
                src = tri[:, 256 + 128 * d:384 + 128 * d] if q % 2 == 0 else incl
                P.op("dve", lambda e, q=q, src=src: e.tensor_copy(out=mask4[:, q * 128:(q + 1) * 128], in_=src), [r_w], [r_w])
                P.op("dve", lambda e, q=q: e.tensor_copy(out=maskT4[:, q * 128:(q + 1) * 128], in_=tri[:, 256 + 128 * (1 - d):384 + 128 * (1 - d)]), [r_w], [r_w])
            identB = sb("identB", [64, 4, 64], F32)
            P.op("dve", lambda e: e.tensor_copy(out=identB[:], in_=c.idf[0:64, 0:64].unsqueeze(1).to_broadcast([64, 4, 64])), [c.r_const], [r_w])
            mu_r = sb("mu_r", [128, 768], F32)
            mu_l = sb("mu_l", [128, 128], F32)
            bias_wa = sb("bias_wa", [128, 512], F32)
            kkv = sb("kkv", [128, 256], F32)
            kav = sb("kav", [128, 256], F32)
            rkp = sb("rkp", [128, 256], F32)
            lng = sb("lng", [128, 256], F32)
            lnb = sb("lnb", [128, 256], F32)
            Wud = sb("Wud", [128, 512], BF16)
            bc = lambda ap: ap.partition_broadcast(128)
            P.dma("sp", mu_r[:], bc(c.w["mu_rkv"][l, d:d + 1, :]), writes=[r_w])
            P.dma("sp", mu_l[:], bc(c.w["mu_lat"][l, d:d + 1, :]), writes=[r_w])
            P.dma("sp", bias_wa[:, 0:256], bc(c.w["w0"][l, d:d + 1, :]), writes=[r_w])
            P.dma("sp", bias_wa[:, 256:512], bc(c.w["a0"][l, d:d + 1, :]), writes=[r_w])
            P.dma("sp", kkv[:], bc(c.w["k_k"][l, d:d + 1, :]), writes=[r_w])
            P.dma("sp", kav[:], bc(c.w["k_a"][l, d:d + 1, :]), writes=[r_w])
            P.dma("sp", rkp[:], bc(c.w["r_k"][l, d:d + 1, :]), writes=[r_w])
            P.dma("sp", lng[:], bc(c.w["ln_g"][l:l + 1, :]), writes=[r_w])
            P.dma("sp", lnb[:], bc(c.w["ln_b"][l:l + 1, :]), writes=[r_w])
            P.op("pool", lambda e: e.memset(Wud[:], 0.0), [], [r_w])
            P.dma("pool", Wud[0:64, 0:256], c.w["w_up"][l, d], writes=[r_w])
            P.dma("pool", Wud[64:128, 256:512], c.w["a_up"][l, d], writes=[r_w])
            ST = Pool2(st, nc, "ST", [64, 256], F32, 2)
            st0, r_st0 = ST.next()
            P.op("pool", lambda e: e.memset(st0[:], 0.0), [], [r_st0])
            state = [st0, r_st0]

            def mk(name, shape, dt, n=2):
                return Pool2(st, nc, name, shape, dt, n)
            cur_, sh_, zd_, o0_ = mk("cur", [128, 1024], F32), mk("sh", [128, 1024], F32), mk("zd", [128, 256], F32), mk("o0", [128, 256], F32)
            xr_, xl_, tl_, tlT_ = mk("xr", [128, 768], F32), mk("xl", [128, 128], F32), mk("tl", [128, 128], BF16), mk("tlT", [128, 128], BF16)
            sg_, lw_ = mk("sg", [128, 512], F32), mk("logw", [128, 256], F32)
            kk_, sq_, s4_, k2_, bon_, beta_ = mk("kk", [128, 256], F32), mk("sqd", [128, 256], F32, 1), mk("s4", [128, 32], F32), mk("k2", [128, 256], F32), mk("bon", [128, 256], F32), mk("beta", [128, 256], F32)
            cum_, ex_, tmp_ = mk("cum", [128, 256], F32), mk("exps", [128, 1024], F32), mk("tmpd", [128, 256], F32, 2)
            opb_ = mk("opb", [128, 7, 256], BF16)
            fTA_, fTB_ = mk("fTA", [64, 1024], BF16), mk("fTB", [64, 1024], BF16)
            AB_, AK_, NTa_ = mk("AB", [128, 1024], BF16), mk("AK", [128, 1024], BF16), mk("NTa", [128, 512], BF16, 3)
            Nn_ = mk("Nn", [128, 512], BF16, 3)
            Z_ = mk("Z", [128, 512], BF16, 3)
            gcol_, Q_, Pm_, Dm_ = mk("gcol", [64, 4], F32), mk("Q", [64, 512], F32), mk("Pm", [64, 256], F32), mk("Dm", [64, 256], F32)
            osb_, on_, yo_ = mk("osb", [128, 256], F32), mk("on", [128, 256], F32), mk("yod", [128, 256], BF16)
            bank_i = [0]

            def nb():
                bank_i[0] = (bank_i[0] + 1) % 4
                return c.pf[bank_i[0]], c.r_pf[bank_i[0]]

            def body(i):
                rows = slice(i * 128, (i + 1) * 128)
                v3 = lambda ap, dd=64: ap.rearrange("p (h d) -> p h d", d=dd)
                dbg_on = DEBUG_SCRATCH and d == DBG_DIR and i == DBG_TILE and l == 0

                def dbg(name, t, shape, dt, rs):
                    if dbg_on:
                        o = nc.dram_tensor("dbg_" + name, list(shape), dt, kind="ExternalOutput").ap()
                        P.dma("pool", o, t, reads=rs, writes=[Res()])
                cur, r_cur = cur_.next()
                sh, r_sh = sh_.next()
                nbr = [c.r["rkvl"][j] for j in (i - 1, i, i + 1) if 0 <= j < NT] + [c.r_pad]
                P.dma("sp", cur[:], c.rkvl[1 + i * 128:1 + (i + 1) * 128, :], reads=[c.r["rkvl"][i]], writes=[r_cur])
                so = 0 if d == 0 else 2
                P.dma("sp", sh[:], c.rkvl[so + i * 128:so + (i + 1) * 128, :], reads=nbr, writes=[r_sh])
                if d == 1:
                    zd, r_zd = zd_.next()
                    o0, r_o0 = o0_.next()
                    P.dma("sp", zd[:], c.zs[rows, 768:1024], reads=[c.r["zs"][i]], writes=[r_zd])
                    P.dma("sp", o0[:], c.osc[0, rows, :], reads=[c.r["osc0"][i]], writes=[r_o0])
                xr, r_xr = xr_.next()
                xl, r_xl = xl_.next()
                P.op("pool", lambda e: e.tensor_tensor(out=xr[:], in0=sh[:, 0:768], in1=cur[:, 0:768], op=ALU.subtract), [r_sh, r_cur], [r_xr])
                P.op("dve", lambda e: e.tensor_tensor(out=xr[:], in0=xr[:], in1=mu_r[:], op=ALU.mult), [r_xr, r_w], [r_xr])
                P.op("pool", lambda e: e.tensor_tensor(out=xr[:], in0=xr[:], in1=cur[:, 0:768], op=ALU.add), [r_xr, r_cur], [r_xr])
                lo = 768 + 128 * d
                P.op("dve", lambda e: e.tensor_tensor(out=xl[:], in0=sh[:, lo:lo + 128], in1=cur[:, lo:lo + 128], op=ALU.subtract), [r_sh, r_cur], [r_xl])
                P.op("dve", lambda e: e.tensor_tensor(out=xl[:], in0=xl[:], in1=mu_l[:], op=ALU.mult), [r_xl, r_w], [r_xl])
                P.op("dve", lambda e: e.tensor_tensor(out=xl[:], in0=xl[:], in1=cur[:, lo:lo + 128], op=ALU.add), [r_xl, r_cur], [r_xl])
                r_, k_, v_ = xr[:, 0:256], xr[:, 256:512], xr[:, 512:768]
                tl, r_tl = tl_.next()
                tlT, r_tlT = tlT_.next()
                P.op("act", lambda e: e.activation(out=tl[:, 0:64], in_=xl[:, 0:64], func=AF.Tanh), [r_xl], [r_tl])
                P.op("act", lambda e: e.copy(out=tl[:, 64:128], in_=xl[:, 64:128]), [r_xl], [r_tl])
                pb, r_pb = c.pbf[1], c.r_pbf[1]
                P.op("pe", lambda e: e.transpose(out=pb[:, 0:128], in_=tl[:], identity=c.idb[:]), [r_tl, c.r_const], [r_pb])
                P.op("act", lambda e: e.copy(out=tlT[:], in_=pb[:, 0:128]), [r_pb], [r_tlT])
                ps, r_ps = nb()
                P.op("pe", lambda e: e.matmul(out=ps[:, :], lhsT=tlT[:], rhs=Wud[:], start=True, stop=True), [r_tlT, r_w], [r_ps])
                sg, r_sg = sg_.next()
                lw, r_lw = lw_.next()
                P.op("dve", lambda e: e.tensor_tensor(out=sg[:], in0=ps[:, :], in1=bias_wa[:], op=ALU.add), [r_ps, r_w], [r_sg])
                P.op("act", lambda e: e.activation(out=sg[:], in_=sg[:], func=AF.Sigmoid), [r_sg], [r_sg])
                P.op("act", lambda e: e.mul(out=lw[:], in_=sg[:, 0:256], mul=NEG), [r_sg], [r_lw])
                a_ = sg[:, 256:512]
                kk, r_kk = kk_.next()
                sq, r_sq = sq_.next()
                s4, r_s4 = s4_.next()
                P.op("dve", lambda e: e.tensor_tensor(out=kk[:], in0=k_, in1=kkv[:], op=ALU.mult), [r_xr, r_w], [r_kk])
                P.op("pool", lambda e: e.tensor_tensor(out=sq[:], in0=kk[:], in1=kk[:], op=ALU.mult), [r_kk], [r_sq])
                P.op("dve", lambda e: e.tensor_reduce(out=s4[:, 0:4], in_=v3(sq[:]), axis=AX.X, op=ALU.add), [r_sq], [r_s4])
                P.op("dve", lambda e: e.tensor_scalar(out=s4[:, 4:8], in0=s4[:, 0:4], scalar1=1e-12, scalar2=None, op0=ALU.add), [r_s4], [r_s4])
                P.op("act", lambda e: e.activation(out=s4[:, 0:4], in_=s4[:, 4:8], func=AF.Sqrt), [r_s4], [r_s4])
                P.op("dve", lambda e: e.reciprocal(out=s4[:, 8:12], in_=s4[:, 0:4]), [r_s4], [r_s4])
                P.op("dve", lambda e: e.tensor_tensor(out=v3(kk[:]), in0=v3(kk[:]), in1=s4[:, 8:12].unsqueeze(2).to_broadcast([128, 4, 64]), op=ALU.mult), [r_kk, r_s4], [r_kk])
                k2, r_k2 = k2_.next()
                P.op("dve", lambda e: e.scalar_tensor_tensor(out=k2[:], in0=a_, scalar=-1.0, in1=kav[:], op0=ALU.add, op1=ALU.mult), [r_sg, r_w], [r_k2])
                P.op("dve", lambda e: e.scalar_tensor_tensor(out=k2[:], in0=k2[:], scalar=1.0, in1=k_, op0=ALU.add, op1=ALU.mult), [r_k2, r_xr], [r_k2])
                bon, r_bon = bon_.next()
                P.op("pool", lambda e: e.tensor_tensor(out=bon[:], in0=r_, in1=k2[:], op=ALU.mult), [r_xr, r_k2], [r_bon])
                P.op("pool", lambda e: e.tensor_tensor(out=bon[:], in0=bon[:], in1=rkp[:], op=ALU.mult), [r_bon, r_w], [r_bon])
                P.op("dve", lambda e: e.tensor_reduce(out=s4[:, 12:16], in_=v3(bon[:]), axis=AX.X, op=ALU.add), [r_bon], [r_s4])
                P.op("dve", lambda e: e.tensor_tensor(out=v3(bon[:]), in0=v3(v_), in1=s4[:, 12:16].unsqueeze(2).to_broadcast([128, 4, 64]), op=ALU.mult), [r_xr, r_s4], [r_bon])
                beta, r_beta = beta_.next()
                P.op("pool", lambda e: e.tensor_tensor(out=beta[:], in0=kk[:], in1=a_, op=ALU.mult), [r_kk, r_sg], [r_beta])
                pc, r_pc = nb()
                P.op("pe", lambda e: e.matmul(out=pc[:, 0:256], lhsT=incl, rhs=lw[:], start=True, stop=True), [r_lw, r_w], [r_pc])
                P.op("pe", lambda e: e.matmul(out=pc[:, 256:512], lhsT=ones_m, rhs=lw[:], start=True, stop=True), [r_lw, r_w], [r_pc])
                pg, r_pg = nb()
                for h in range(4):
                    P.op("pe", lambda e, h=h: e.matmul(out=pg[0:64, h:h + 1], lhsT=lw[:, h * 64:(h + 1) * 64], rhs=ones_c, start=True, stop=True), [r_lw, r_w], [r_pg])
                gcol, r_gcol = gcol_.next()
                P.op("act", lambda e: e.activation(out=gcol[:], in_=pg[0:64, 0:4], func=AF.Exp), [r_pg], [r_gcol])
                cum, r_cum = cum_.next()
                ex, r_ex = ex_.next()
                t3, r_t3 = tmp_.next()
                t4, r_t4 = tmp_.next()
                ecum, encum, eexc, edec = ex[:, 0:256], ex[:, 256:512], ex[:, 512:768], ex[:, 768:1024]
                P.op("act", lambda e: e.copy(out=cum[:], in_=pc[:, 0:256]), [r_pc], [r_cum])
                P.op("act", lambda e: e.activation(out=ecum, in_=pc[:, 0:256], func=AF.Exp), [r_pc], [r_ex])
                P.op("act", lambda e: e.activation(out=encum, in_=pc[:, 0:256], func=AF.Exp, scale=-1.0), [r_pc], [r_ex])
                P.op("dve", lambda e: e.tensor_tensor(out=t3[:], in0=cum[:], in1=lw[:], op=ALU.subtract), [r_cum, r_lw], [r_t3])
                P.op("act", lambda e: e.activation(out=eexc, in_=t3[:], func=AF.Exp), [r_t3], [r_ex])
                P.op("dve", lambda e: e.tensor_tensor(out=t4[:], in0=pc[:, 256:512], in1=cum[:], op=ALU.subtract), [r_pc, r_cum], [r_t4])
                P.op("act", lambda e: e.activation(out=edec, in_=t4[:], func=AF.Exp), [r_t4], [r_ex])
                opb, r_opb = opb_.next()
                P.op("dve", lambda e: e.tensor_tensor(out=opb[:, 0, :], in0=r_, in1=ecum, op=ALU.mult), [r_xr, r_ex], [r_opb])
                P.op("pool", lambda e: e.tensor_tensor(out=opb[:, 1, :], in0=k2[:], in1=encum, op=ALU.mult), [r_k2, r_ex], [r_opb])
                P.op("dve", lambda e: e.tensor_tensor(out=opb[:, 2, :], in0=beta[:], in1=encum, op=ALU.mult), [r_beta, r_ex], [r_opb])
                P.op("dve", lambda e: e.scalar_tensor_tensor(out=opb[:, 3, :], in0=kk[:], scalar=-1.0, in1=eexc, op0=ALU.mult, op1=ALU.mult), [r_kk, r_ex], [r_opb])
                P.op("pool", lambda e: e.tensor_tensor(out=opb[:, 4, :], in0=k2[:], in1=edec, op=ALU.mult), [r_k2, r_ex], [r_opb])
                P.op("pool", lambda e: e.tensor_tensor(out=opb[:, 5, :], in0=beta[:], in1=edec, op=ALU.mult), [r_beta, r_ex], [r_opb])
                P.op("act", lambda e: e.copy(out=opb[:, 6, :], in_=v_), [r_xr], [r_opb])
                rb, kt, bt, ab, Kh, Bh, vb = (opb[:, q, :] for q in range(7))
                pa, r_pa = c.pbf[0], c.r_pbf[0]
                pb, r_pb = c.pbf[1], c.r_pbf[1]
                for h in range(4):
                    hs = slice(h * 64, (h + 1) * 64)
                    P.op("pe", lambda e, h=h, hs=hs: e.transpose(out=pa[0:64, h * 128:(h + 1) * 128], in_=bt[:, hs], identity=c.idb[:]), [r_opb, c.r_const], [r_pa])
                    P.op("pe", lambda e, h=h, hs=hs: e.transpose(out=pa[0:64, (4 + h) * 128:(5 + h) * 128], in_=kt[:, hs], identity=c.idb[:]), [r_opb, c.r_const], [r_pa])
                    P.op("pe", lambda e, h=h, hs=hs: e.transpose(out=pb[0:64, (2 * h) * 128:(2 * h + 1) * 128], in_=ab[:, hs], identity=c.idb[:]), [r_opb, c.r_const], [r_pb])
                    P.op("pe", lambda e, h=h, hs=hs: e.transpose(out=pb[0:64, (2 * h + 1) * 128:(2 * h + 2) * 128], in_=rb[:, hs], identity=c.idb[:]), [r_opb, c.r_const], [r_pb])
                fTA, r_fTA = fTA_.next()
                fTB, r_fTB = fTB_.next()
                P.op("act", lambda e: e.copy(out=fTA[:], in_=pa[0:64, :]), [r_pa], [r_fTA])
                P.op("dve", lambda e: e.tensor_copy(out=fTB[:], in_=pb[0:64, :]), [r_pb], [r_fTB])
                AB, r_AB = AB_.next()
                AK, r_AK = AK_.next()
                for (dst, r_dst, off) in ((AB, r_AB, 0), (AK, r_AK, 4)):
                    for pr in range(2):
                        px, r_px = nb()
                        for hh in range(2):
                            h = 2 * pr + hh
                            P.op("pe", lambda e, h=h, hh=hh, px=px, off=off: e.matmul(out=px[:, hh * 256:(hh + 1) * 256], lhsT=fTA[:, (off + h) * 128:(off + h + 1) * 128],
                                                                               rhs=fTB[:, 2 * h * 128:(2 * h + 2) * 128], start=True, stop=True), [r_fTA, r_fTB], [r_px])
                        P.op("dve", lambda e, pr=pr, px=px, dst=dst: e.tensor_tensor(out=dst[:, pr * 512:(pr + 1) * 512], in0=px[:, :], in1=mask4[:], op=ALU.mult), [r_px, r_w], [r_dst])
                py, r_py = nb()
                for h in range(4):
                    P.op("pe", lambda e, h=h: e.matmul(out=py[:, h * 128:(h + 1) * 128], lhsT=fTB[:, 2 * h * 128:(2 * h + 1) * 128], rhs=fTA[:, h * 128:(h + 1) * 128],
                                                       start=True, stop=True), [r_fTA, r_fTB], [r_py])
                NTc, r_NTc = NTa_.next()
                P.op("dve", lambda e: e.tensor_tensor(out=NTc[:], in0=py[:, :], in1=maskT4[:], op=ALU.mult), [r_py, r_w], [r_NTc])
                pz, r_pz = nb()
                for h in range(4):
                    P.op("pe", lambda e, h=h: e.matmul(out=pz[:, h * 64:(h + 1) * 64], lhsT=AK[:, h * 256:h * 256 + 128], rhs=vb[:, h * 64:(h + 1) * 64], start=True, stop=True),
                         [r_AK, r_opb], [r_pz])
                Z, r_Z = Z_.next()
                P.op("dve", lambda e, Z=Z: e.tensor_copy(out=v3(Z[:], 128)[:, :, 0:64], in_=v3(ab)), [r_opb], [r_Z])
                P.op("act", lambda e, Z=Z: e.copy(out=v3(Z[:], 128)[:, :, 64:128], in_=v3(pz[:, 0:256])), [r_pz], [r_Z])
                dbg("Z0", Z[:], [128, 512], BF16, [r_Z]); dbg("NT0", NTc[:], [128, 512], BF16, [r_NTc]); dbg("AK", AK[:], [128, 1024], BF16, [r_AK])
                Ncur = [AB[:, h * 256:h * 256 + 128] for h in range(4)]
                r_N = r_AB
                for kq in range(7):
                    pq, r_pq = nb()
                    for h in range(4):
                        P.op("pe", lambda e, h=h, N=Ncur[h], Z=Z, pq=pq: e.matmul(out=pq[:, h * 128:(h + 1) * 128], lhsT=N, rhs=Z[:, h * 128:(h + 1) * 128], start=True, stop=True),
                             [r_N, r_Z], [r_pq])
                    Zn, r_Zn = Z_.next()
                    P.op("dve", lambda e, Z=Z, Zn=Zn, pq=pq: e.tensor_tensor(out=Zn[:], in0=pq[:, :], in1=Z[:], op=ALU.add), [r_pq, r_Z], [r_Zn])
                    if kq < 6:
                        p1, r_p1 = nb()
                        for h in range(4):
                            P.op("pe", lambda e, h=h, N=Ncur[h], NTc=NTc, p1=p1: e.matmul(out=p1[:, h * 128:(h + 1) * 128], lhsT=NTc[:, h * 128:(h + 1) * 128], rhs=N, start=True, stop=True),
                                 [r_N, r_NTc], [r_p1])
                        p2, r_p2 = nb()
                        for h in range(4):
                            P.op("pe", lambda e, h=h, N=Ncur[h], NTc=NTc, p2=p2: e.matmul(out=p2[:, h * 128:(h + 1) * 128], lhsT=N, rhs=NTc[:, h * 128:(h + 1) * 128], start=True, stop=True),
                                 [r_N, r_NTc], [r_p2])
                        Nn, r_Nn = Nn_.next()
                        NTn, r_NTn = NTa_.next()
                        P.op("act", lambda e, Nn=Nn, p1=p1: e.copy(out=Nn[:], in_=p1[:, :]), [r_p1], [r_Nn])
                        P.op("dve", lambda e, NTn=NTn, p2=p2: e.tensor_copy(out=NTn[:], in_=p2[:, :]), [r_p2], [r_NTn])
                        dbg(f"Nn{kq}", Nn[:], [128, 512], BF16, [r_Nn]); dbg(f"NTn{kq}", NTn[:], [128, 512], BF16, [r_NTn]); dbg(f"Zn{kq}", Zn[:], [128, 512], BF16, [r_Zn])
                        Ncur = [Nn[:, h * 128:(h + 1) * 128] for h in range(4)]
                        r_N, NTc, r_NTc = r_Nn, NTn, r_NTn
                    Z, r_Z = Zn, r_Zn
                WT = [Z[:, h * 128:h * 128 + 64] for h in range(4)]
                XT = [Z[:, h * 128 + 64:(h + 1) * 128] for h in range(4)]
                pQ, r_pQ = nb()
                for h in range(4):
                    P.op("pe", lambda e, h=h: e.matmul(out=pQ[0:64, h * 128:(h + 1) * 128], lhsT=WT[h], rhs=AB[:, h * 256 + 128:(h + 1) * 256], start=True, stop=True),
                         [r_Z, r_AB], [r_pQ])
                Q, r_Q = Q_.next()
                P.op("dve", lambda e: e.tensor_tensor(out=v3(Q[:], 128), in0=v3(pQ[0:64, :], 128), in1=fTB[:].rearrange("p (h two t) -> p h two t", two=2, t=128)[:, :, 1, :], op=ALU.add),
                     [r_pQ, r_fTB], [r_Q])
                pP, r_pP = nb()
                for h in range(4):
                    P.op("pe", lambda e, h=h: e.matmul(out=pP[0:64, h * 64:(h + 1) * 64], lhsT=WT[h], rhs=Bh[:, h * 64:(h + 1) * 64], start=True, stop=True), [r_Z, r_opb], [r_pP])
                Pm, r_Pm = Pm_.next()
                P.op("dve", lambda e: e.tensor_tensor(out=v3(Pm[:]), in0=identB[:], in1=gcol[:].unsqueeze(2).to_broadcast([64, 4, 64]), op=ALU.mult), [r_w, r_gcol], [r_Pm])
                P.op("dve", lambda e: e.tensor_tensor(out=Pm[:], in0=Pm[:], in1=pP[0:64, 0:256], op=ALU.add), [r_Pm, r_pP], [r_Pm])
                pD, r_pD = nb()
                for h in range(4):
                    P.op("pe", lambda e, h=h: e.matmul(out=pD[0:64, h * 64:(h + 1) * 64], lhsT=Bh[:, h * 64:(h + 1) * 64], rhs=XT[h], start=(h == 0), stop=False, skip_group_check=True),
                         [r_Z, r_opb], [r_pD])
                    P.op("pe", lambda e, h=h: e.matmul(out=pD[0:64, h * 64:(h + 1) * 64], lhsT=Kh[:, h * 64:(h + 1) * 64], rhs=vb[:, h * 64:(h + 1) * 64], start=False, stop=True, skip_group_check=True),
                         [r_opb], [r_pD])
                Dm, r_Dm = Dm_.next()
                P.op("act", lambda e: e.copy(out=Dm[:], in_=pD[0:64, 0:256]), [r_pD], [r_Dm])
                S0, r_S0 = state
                pO, r_pO = c.pf[5], c.r_pf[5]
                for h in range(4):
                    P.op("pe", lambda e, h=h: e.matmul(out=pO[:, h * 64:(h + 1) * 64], lhsT=AB[:, h * 256 + 128:(h + 1) * 256], rhs=XT[h], start=(h == 0), stop=False, skip_group_check=True),
                         [r_AB, r_Z], [r_pO])
                    P.op("pe", lambda e, h=h: e.matmul(out=pO[:, h * 64:(h + 1) * 64], lhsT=AK[:, h * 256 + 128:(h + 1) * 256], rhs=vb[:, h * 64:(h + 1) * 64], start=False, stop=False, skip_group_check=True),
                         [r_AK, r_opb], [r_pO])
                    P.op("pe", lambda e, h=h, S0=S0: e.matmul(out=pO[:, h * 64:(h + 1) * 64], lhsT=Q[:, h * 128:(h + 1) * 128], rhs=S0[:, h * 64:(h + 1) * 64], start=False, stop=True, skip_group_check=True),
                         [r_Q, r_S0], [r_pO])
                pS, r_pS = c.pf[4], c.r_pf[4]
                for h in range(4):
                    P.op("pe", lambda e, h=h, S0=S0: e.matmul(out=pS[0:64, h * 64:(h + 1) * 64], lhsT=Pm[:, h * 64:(h + 1) * 64], rhs=S0[:, h * 64:(h + 1) * 64], start=True, stop=True),
                         [r_Pm, r_S0], [r_pS])
                S1, r_S1 = ST.next()
                P.op("dve", lambda e, S1=S1: e.tensor_tensor(out=S1[:], in0=pS[0:64, 0:256], in1=Dm[:], op=ALU.add), [r_pS, r_Dm], [r_S1])
                state[0], state[1] = S1, r_S1
                osb, r_osb = osb_.next()
                on, r_on = on_.next()
                P.op("act", lambda e: e.copy(out=osb[:], in_=pO[:, 0:256]), [r_pO], [r_osb])
                P.op("dve", lambda e: e.tensor_reduce(out=s4[:, 16:20], in_=v3(osb[:]), axis=AX.X, op=ALU.add), [r_osb], [r_s4])
                P.op("pool", lambda e: e.tensor_tensor(out=on[:], in0=osb[:], in1=osb[:], op=ALU.mult), [r_osb], [r_on])
                P.op("dve", lambda e: e.tensor_reduce(out=s4[:, 20:24], in_=v3(on[:]), axis=AX.X, op=ALU.add), [r_on], [r_s4])
                P.op("dve", lambda e: e.tensor_scalar(out=s4[:, 16:20], in0=s4[:, 16:20], scalar1=1.0 / 64, scalar2=None, op0=ALU.mult), [r_s4], [r_s4])
                P.op("dve", lambda e: e.tensor_tensor(out=s4[:, 24:28], in0=s4[:, 16:20], in1=s4[:, 16:20], op=ALU.mult), [r_s4], [r_s4])
                P.op("dve", lambda e: e.scalar_tensor_tensor(out=s4[:, 20:24], in0=s4[:, 20:24], scalar=1.0 / 64, in1=s4[:, 24:28], op0=ALU.mult, op1=ALU.subtract), [r_s4], [r_s4])
                P.op("dve", lambda e: e.tensor_scalar(out=s4[:, 20:24], in0=s4[:, 20:24], scalar1=GN_EPS, scalar2=None, op0=ALU.add), [r_s4], [r_s4])
                P.op("act", lambda e: e.activation(out=s4[:, 24:28], in_=s4[:, 20:24], func=AF.Sqrt), [r_s4], [r_s4])
                P.op("dve", lambda e: e.reciprocal(out=s4[:, 28:32], in_=s4[:, 24:28]), [r_s4], [r_s4])
                P.op("dve", lambda e: e.tensor_tensor(out=v3(on[:]), in0=v3(osb[:]), in1=s4[:, 16:20].unsqueeze(2).to_broadcast([128, 4, 64]), op=ALU.subtract), [r_osb, r_s4], [r_on])
                P.op("dve", lambda e: e.tensor_tensor(out=v3(on[:]), in0=v3(on[:]), in1=s4[:, 28:32].unsqueeze(2).to_broadcast([128, 4, 64]), op=ALU.mult), [r_on, r_s4], [r_on])
                P.op("pool", lambda e: e.tensor_tensor(out=on[:], in0=on[:], in1=lng[:], op=ALU.mult), [r_on, r_w], [r_on])
                P.op("pool", lambda e: e.tensor_tensor(out=on[:], in0=on[:], in1=lnb[:], op=ALU.add), [r_on, r_w], [r_on])
                P.op("pool", lambda e: e.tensor_tensor(out=on[:], in0=on[:], in1=bon[:], op=ALU.add), [r_on, r_bon], [r_on])
                if dbg_on:
                    dbg("xr", xr[:], [128, 768], F32, [r_xr]); dbg("sg", sg[:], [128, 512], F32, [r_sg]); dbg("lw", lw[:], [128, 256], F32, [r_lw])
                    dbg("kk", kk[:], [128, 256], F32, [r_kk]); dbg("ex", ex[:], [128, 1024], F32, [r_ex]); dbg("cum", cum[:], [128, 256], F32, [r_cum])
                    dbg("opb", opb[:].rearrange("p a b -> p (a b)"), [128, 7 * 256], BF16, [r_opb]); dbg("AB", AB[:], [128, 1024], BF16, [r_AB])
                    dbg("Z", Z[:], [128, 512], BF16, [r_Z]); dbg("Q", Q[:], [64, 512], F32, [r_Q]); dbg("osb", osb[:], [128, 256], F32, [r_osb])
                    dbg("s4", s4[:], [128, 32], F32, [r_s4]); dbg("bon", bon[:], [128, 256], F32, [r_bon]); dbg("k2", k2[:], [128, 256], F32, [r_k2])
                    dbg("Pm", Pm[:], [64, 256], F32, [r_Pm]); dbg("Dm", Dm[:], [64, 256], F32, [r_Dm]); dbg("gcol", gcol[:], [64, 4], F32, [r_gcol]); dbg("S1", S1[:], [64, 256], F32, [r_S1])
                if d == 0:
                    P.dma("pool", c.osc[0, rows, :], on[:], reads=[r_on], writes=[c.r["osc0"][i]])
                else:
                    yo, r_yo = yo_.next()
                    P.op("dve", lambda e: e.tensor_tensor(out=on[:], in0=on[:], in1=o0[:], op=ALU.add), [r_on, r_o0], [r_on])
                    P.op("dve", lambda e: e.tensor_tensor(out=yo[:], in0=on[:], in1=zd[:], op=ALU.mult), [r_on, r_zd], [r_yo])
                    P.dma("pool", c.ys[rows, 768:1024], yo[:], reads=[r_yo], writes=[c.r["ys"][i]])

            for i in (range(NT) if d == 0 else range(NT - 1, -1, -1)):
                body(i)
        barrier(c)

    for d in range(2):
        run_dir(d)


def host_consts(S, seq_len):
    NT = S // 128
    t = np.arange(S)
    valid = (t < seq_len).astype(np.float32).reshape(NT, 128).T.copy()
    invc = np.ones((S, 4), np.float32)
    for g, w in enumerate((2, 4, 8, 16)):
        h = w // 2
        cnt = np.minimum(t + h, seq_len) - np.maximum(t - h, 0)
        invc[:, g] = np.where(t < seq_len, 1.0 / np.maximum(cnt, 1), 1.0)
    invc = invc.reshape(NT, 128, 4).transpose(1, 0, 2).reshape(128, NT * 4).copy()
    ident = np.eye(128, dtype=np.float32)
    s = np.arange(128)[:, None]
    tt = np.arange(128)[None, :]
    bandc = np.zeros((128, 4, 128), np.float32)
    bandh = np.zeros((16, 4, 128), np.float32)
    for g, w in enumerate((2, 4, 8, 16)):
        h = w // 2
        bandc[:, g, :] = ((s >= tt - h) & (s <= tt + h - 1))
        r = np.arange(16)[:, None]
        srel = np.where(r < 8, r - 8, 128 + (r - 8))
        bandh[:, g, :] = ((srel >= tt - h) & (srel <= tt + h - 1))
    slopes = 2.0 ** (-8.0 * np.arange(1, 13) / 12.0)
    amask = np.zeros((25, 128, 4, 128), np.float32)
    ci = 0
    key = np.arange(128)[:, None]
    q = np.arange(128)[None, :]
    for g in range(3):
        d = ATT_D[g]
        for coff in range(-ATT_W[g], ATT_W[g] + 1):
            delta = coff * 128 + key - q
            ok = (delta % d == 0) & (np.abs(delta) <= 64 * d)
            for h in range(4):
                amask[ci, :, h, :] = np.where(ok, np.exp(-slopes[g * 4 + h] * np.abs(delta)), 0.0)
            ci += 1
    tri = np.zeros((128, 642), np.float32)
    tri[:, 0:128] = (s <= tt)
    tri[:, 128:256] = (s >= tt)
    tri[:, 256:384] = (s < tt)
    tri[:, 384:512] = (s > tt)
    tri[:, 512:642] = 1.0
    return dict(valid=valid, invc=invc, ident=ident, amask=amask.reshape(25, 128, 512), bandc=bandc.reshape(128, 512),
                bandh=bandh.reshape(16, 512), tri=tri)


def host_weights(inp):
    w = {}
    for k in ("norm_g", "w_in", "q_norm_g", "k_norm_g", "pool_w", "pool_scale", "sg_norm_g", "mu_rkv", "mu_lat", "w0", "w_up",
              "a0", "a_up", "k_k", "k_a", "r_k", "ln_g", "ln_b", "w_branch", "w_out"):
        w[k] = np.ascontiguousarray(np.asarray(inp[k], dtype=np.float32))
    w["sg_wT"] = np.ascontiguousarray(np.transpose(np.asarray(inp["sg_w"], np.float32), (0, 1, 3, 2)))
    w["sg_bT"] = np.ascontiguousarray(np.transpose(np.asarray(inp["sg_b"], np.float32), (0, 2, 1)))
    return w


_NC_CACHE = {}


def run_sequences(seqs, S, inp, branches=(0, 1, 2, 3), L=2, n_cores=8):
    key = (S, L, tuple(branches))
    if key not in _NC_CACHE:
        _NC_CACHE[key] = build_program(S, L=L, branches=branches)
    nc = _NC_CACHE[key]
    w = host_weights(inp)
    in_maps = []
    for ci in range(n_cores):
        if ci < len(seqs):
            x = np.zeros((S, D), np.float32)
            x[:seqs[ci].shape[0]] = seqs[ci]
            m = dict(x=x, **host_consts(S, seqs[ci].shape[0]))
        else:
            m = dict(x=np.zeros((S, D), np.float32), **host_consts(S, 0))
        m.update(w)
        in_maps.append(m)
    res = run_bass_kernel_spmd(nc, in_maps, core_ids=list(range(n_cores)))
    if DEBUG_SCRATCH:
        global LAST_RESULTS
        LAST_RESULTS = res.results
    return [res.results[ci]["y"][:seqs[ci].shape[0]] for ci in range(len(seqs))]


def kernel(**inputs):
    xp = np.asarray(inputs["x_prompt"], np.float32)
    xs = np.asarray(inputs["x_sample"], np.float32)
    S = xs.shape[1]
    seqs = [xp[b] for b in range(xp.shape[0])] + [xs[b] for b in range(xs.shape[0])]
    outs = run_sequences(seqs, S, inputs)
    nb = xp.shape[0]
    y_prompt = np.stack(outs[:nb], 0).astype(np.float32)
    y_sample = np.stack(outs[nb:], 0).astype(np.float32)
    return (y_prompt, y_sample)
```

```python
import numpy as np
import concourse.bass as bass
import concourse.mybir as mybir
from concourse.bass_utils import run_bass_kernel_spmd

F32 = mybir.dt.float32
BF16 = mybir.dt.bfloat16
AF = mybir.ActivationFunctionType
ALU = mybir.AluOpType
AX = mybir.AxisListType

EPOCH = 24000
NDMA_SEM = 12


class Res:
    __slots__ = ("name", "last_w", "reads")

    def __init__(self, name=""):
        self.name = name
        self.last_w = None
        self.reads = []


class Prog:
    ENG = ("pe", "act", "dve", "pool", "sp")

    def __init__(self, nc, same_engine_sync=True):
        self.nc = nc
        self.same_engine_sync = same_engine_sync
        self.ops = {e: [] for e in self.ENG}
        self.cnt = {e: 0 for e in self.ENG}
        self.sems = {e: [nc.alloc_semaphore(name=f"s_{e}_0")] for e in ("pe", "act", "dve", "pool")}
        self.seen = {e: {} for e in self.ENG}
        self.dma_sems = {q: [nc.alloc_semaphore(name=f"d_{q}_{k}") for k in range(NDMA_SEM)] for q in ("sp", "pool")}
        self.dma_n = {"sp": 0, "pool": 0}
        self.n_wait = 0

    def _deps(self, reads, writes):
        deps = []
        for r in reads:
            if r.last_w is not None:
                deps.extend(r.last_w)
        for w in writes:
            if w.last_w is not None:
                deps.extend(w.last_w)
            deps.extend(w.reads)
        return deps

    def _waits(self, e, deps, own_sem_ids):
        waits = {}
        seen = self.seen[e]
        for (sem, val) in deps:
            sid = id(sem)
            if sid in own_sem_ids and (e == "pe" or not self.same_engine_sync):
                continue
            if seen.get(sid, 0) >= val:
                continue
            if sid not in waits or waits[sid][1] < val:
                waits[sid] = (sem, val)
        for sid, (sem, val) in waits.items():
            seen[sid] = val
        self.n_wait += len(waits)
        return list(waits.values())

    def _commit(self, ev, reads, writes, is_dma=False):
        for r in reads:
            r.reads.append(ev)
        for w in writes:
            if is_dma and w.last_w is not None and not w.reads:
                w.last_w = w.last_w + [ev]
            else:
                w.last_w = [ev]
            w.reads = []

    def op(self, e, fn, reads=(), writes=()):
        deps = self._deps(reads, writes)
        own = {id(s) for s in self.sems[e]}
        waits = self._waits(e, deps, own)
        if self.cnt[e] >= EPOCH:
            self.sems[e].append(self.nc.alloc_semaphore(name=f"s_{e}_{len(self.sems[e])}"))
            self.cnt[e] = 0
        self.cnt[e] += 1
        sem = self.sems[e][-1]
        ev = (sem, self.cnt[e])
        self.ops[e].append((waits, fn, sem, 1))
        self._commit(ev, reads, writes)
        return ev

    def dma(self, q, out, in_, reads=(), writes=(), **kw):
        deps = self._deps(reads, writes)
        n = self.dma_n[q]
        self.dma_n[q] += 1
        k = n % NDMA_SEM
        use = n // NDMA_SEM
        sem = self.dma_sems[q][k]
        if use > 0:
            deps.append((sem, 16 * use))
        own = {id(s) for s in self.sems[q]} if q in self.sems else set()
        waits = self._waits(q, deps, own)
        ev = (sem, 16 * (use + 1))
        self.ops[q].append((waits, (lambda eng, o=out, i=in_, kw=kw: eng.dma_start(out=o, in_=i, **kw)), sem, 16))
        self._commit(ev, reads, writes, is_dma=True)
        return ev

    def final_wait(self, e, evs):
        waits = self._waits(e, list(evs), set())
        self.ops[e].append((waits, None, None, 0))

    def emit(self):
        nc = self.nc
        ops = self.ops

        def run(eng, lst):
            for (waits, fn, sem, inc) in lst:
                for (s, v) in waits:
                    eng.wait_ge(s, v)
                if fn is not None:
                    ins = fn(eng)
                    ins.then_inc(sem, inc)

        with nc.Block() as block:
            @block.tensor
            def _(eng):
                run(eng, ops["pe"])

            @block.scalar
            def _(eng):
                run(eng, ops["act"])

            @block.vector
            def _(eng):
                run(eng, ops["dve"])

            @block.gpsimd
            def _(eng):
                run(eng, ops["pool"])

            @block.sync
            def _(eng):
                run(eng, ops["sp"])


D = 1024
PW = 9216
P1W = 5120
EPS = 1e-6
GN_EPS = 64e-5
ATT_W = (1, 2, 8)
ATT_D = (1, 4, 16)
C_Q, C_K, C_V, C_ZA, C_UB, C_ZB, C_UVC, C_ZC, C_RKV, C_LAT, C_ZD = 0, 768, 1536, 2304, 2560, 2816, 3072, 3584, 3840, 4608, 4864


class Ctx:
    pass


_UID = [0]


def uname(n):
    _UID[0] += 1
    return f"{n}_u{_UID[0]}"


DEBUG_SCRATCH = False
DEBUG_BARRIER = False
DBG_DIR = 0
DBG_TILE = 0


def build_program(S, L=2, branches=(0, 1, 2, 3), same_engine_sync=True):
    from contextlib import ExitStack
    NT = S // 128
    nc = bass.Bass("TRN2", target_bir_lowering=False)
    P = Prog(nc, same_engine_sync=same_engine_sync)
    c = Ctx()
    c.nc, c.P, c.S, c.NT, c.L, c.branches = nc, P, S, NT, L, branches

    def din(name, shape, dt=F32):
        return nc.dram_tensor(name, list(shape), dt, kind="ExternalInput").ap()

    def dscr(name, shape, dt=F32):
        return nc.dram_tensor(name, list(shape), dt, kind=("ExternalOutput" if DEBUG_SCRATCH else "Internal")).ap()

    c.x_in = din("x", [S, D])
    c.valid = din("valid", [128, NT])
    c.invc = din("invc", [128, NT * 4])
    c.ident = din("ident", [128, 128])
    c.amask = din("amask", [25, 128, 512])
    c.bandc = din("bandc", [128, 4 * 128])
    c.bandh = din("bandh", [16, 4 * 128])
    c.tri = din("tri", [128, 642])
    c.w = {}
    for name, shape in (("norm_g", [L, D]), ("w_in", [L, D, PW]), ("q_norm_g", [L, 64]), ("k_norm_g", [L, 64]),
                        ("pool_w", [L, 4, 64, 64]), ("pool_scale", [L, 256]), ("sg_norm_g", [L, 256]),
                        ("sg_wT", [L, 4, 128, 128]), ("sg_bT", [L, 128, 4]), ("mu_rkv", [L, 2, 768]), ("mu_lat", [L, 2, 128]),
                        ("w0", [L, 2, 256]), ("w_up", [L, 2, 64, 256]), ("a0", [L, 2, 256]), ("a_up", [L, 2, 64, 256]),
                        ("k_k", [L, 2, 256]), ("k_a", [L, 2, 256]), ("r_k", [L, 2, 256]), ("ln_g", [L, 256]), ("ln_b", [L, 256]),
                        ("w_branch", [L, 4, 256, D]), ("w_out", [L, D, D])):
        c.w[name] = din(name, shape)
    c.y_out = nc.dram_tensor("y", [S, D], F32, kind="ExternalOutput").ap()

    c.x1 = dscr("x1", [S, D])
    c.qT = dscr("qT_scr", [NT, 64, 1536], BF16)
    c.kT = dscr("kT_scr", [NT, 64, 1536], BF16)
    c.va = dscr("va_scr", [NT, 128, 780], BF16)
    c.zs = dscr("zs_scr", [S, 1024])
    c.ub = dscr("ub_scr", [S + 32, 256])
    c.uvc = dscr("uvc_scr", [S, 512])
    c.rkvl = dscr("rkvl_scr", [S + 2, 1024])
    c.ys = dscr("ys_scr", [S, 1024], BF16)
    c.osc = dscr("o_scr", [2, S, 256])
    c.r = {k: [Res(f"{k}{i}") for i in range(NT)] for k in ("x1", "qT", "kT", "va", "zs", "ub", "uvc", "rkvl", "ys", "osc0", "osc1", "y")}
    c.r_pad = Res("pads")
    if DEBUG_SCRATCH:
        c.dbg_proj = dscr("dbg_proj", [S, P1W])
        c.dbg_hT = dscr("dbg_hT", [S, D], BF16)

    c.idb = nc.alloc_sbuf_tensor("idb", [128, 128], BF16)
    c.idf = nc.alloc_sbuf_tensor("idf", [128, 128], F32)
    c.zero = nc.alloc_sbuf_tensor("zero", [128, 1024], F32)
    c.validt = nc.alloc_sbuf_tensor("validt", [128, NT], F32)
    c.r_const = Res("const")
    P.dma("sp", c.idf[:], c.ident, writes=[c.r_const])
    P.dma("pool", c.idb[:], c.ident, writes=[c.r_const])
    P.dma("sp", c.validt[:], c.valid, writes=[c.r_const])
    P.op("pool", lambda e: e.memset(c.zero[:], 0.0), [], [c.r_const])
    P.dma("sp", c.ub[0:16, :], c.zero[0:16, 0:256], reads=[c.r_const], writes=[c.r_pad])
    P.dma("sp", c.ub[S + 16:S + 32, :], c.zero[0:16, 0:256], reads=[c.r_const], writes=[c.r_pad])
    P.dma("sp", c.rkvl[0:1, :], c.zero[0:1, :], reads=[c.r_const], writes=[c.r_pad])
    P.dma("sp", c.rkvl[S + 1:S + 2, :], c.zero[0:1, :], reads=[c.r_const], writes=[c.r_pad])

    c.pbf = [nc.alloc_psum_tensor(f"pbf{i}", [128, 1024], BF16) for i in range(2)]
    c.pf = [nc.alloc_psum_tensor(f"pf{i}", [128, 512], F32) for i in range(6)]
    c.r_pbf = [Res(f"pbf{i}") for i in range(2)]
    c.r_pf = [Res(f"pf{i}") for i in range(6)]

    for l in range(L):
        x_src = c.x_in if l == 0 else c.x1
        x_src_res = None if l == 0 else c.r["x1"]
        x_dst = c.x1 if l < L - 1 else c.y_out
        x_dst_res = c.r["x1"] if l < L - 1 else c.r["y"]
        phase1(c, l, x_src, x_src_res)
        barrier(c)
        if 2 in branches:
            phase2c(c, l)
            barrier(c)
        if 1 in branches:
            phase2b(c, l)
            barrier(c)
        if 0 in branches:
            phase2a(c, l)
            barrier(c)
        if 3 in branches:
            phase2d(c, l)
            barrier(c)
        phase3(c, l, x_src, x_src_res, x_dst, x_dst_res)
        barrier(c)
    P.emit()
    return nc


def barrier(c):
    P = c.P
    evs = []
    for e in ("pe", "act", "dve", "pool"):
        if P.cnt[e] > 0:
            evs.append((P.sems[e][-1], P.cnt[e]))
    for q in ("sp", "pool"):
        n = P.dma_n[q]
        for k, s in enumerate(P.dma_sems[q]):
            uses = (n - k + NDMA_SEM - 1) // NDMA_SEM if n > k else 0
            if uses > 0:
                evs.append((s, 16 * uses))
    for e in Prog.ENG:
        P.final_wait(e, evs)


class Pool2:
    def __init__(self, stack, nc, name, shape, dt, n=2):
        self.t = [stack.enter_context(nc.sbuf_tensor(uname(f"{name}{i}"), list(shape), dt)) for i in range(n)]
        self.r = [Res(f"{name}{i}") for i in range(n)]
        self.n = n
        self.i = -1

    def next(self):
        self.i = (self.i + 1) % self.n
        return self.t[self.i], self.r[self.i]


NL = 2


def run_lanes(lane_tiles, body, skew=2):
    K = len(lane_tiles)
    its = [iter(t) for t in lane_tiles]
    gens = [None] * K
    done = [False] * K
    rnd = 0
    while not all(done):
        for k in range(K):
            if done[k]:
                continue
            if gens[k] is None:
                if rnd < k * skew:
                    continue
                i = next(its[k], None)
                if i is None:
                    done[k] = True
                    continue
                gens[k] = body(i, k)
            try:
                next(gens[k])
            except StopIteration:
                gens[k] = None
        rnd += 1


def lane_banks(c, k):
    return [(c.pf[3 * k + j], c.r_pf[3 * k + j]) for j in range(3)], (c.pbf[k], c.r_pbf[k])


def rms_h(c, xt, r_xt, gbc, r_w, pl, width=1024):
    P = c.P
    junk, r_junk = pl["junk"].next()
    ss, r_ss = pl["ss"].next()
    h, r_h = pl["h"].next()
    P.op("act", lambda e: e.activation(out=junk[:], in_=xt[:], func=AF.Square, accum_out=ss[:, 0:1]), [r_xt], [r_junk, r_ss])
    P.op("dve", lambda e: e.tensor_scalar(out=ss[:, 1:2], in0=ss[:, 0:1], scalar1=1.0 / width, scalar2=EPS, op0=ALU.mult, op1=ALU.add), [r_ss], [r_ss])
    P.op("act", lambda e: e.activation(out=ss[:, 2:3], in_=ss[:, 1:2], func=AF.Sqrt), [r_ss], [r_ss])
    P.op("dve", lambda e: e.reciprocal(out=ss[:, 3:4], in_=ss[:, 2:3]), [r_ss], [r_ss])
    P.op("dve", lambda e: e.scalar_tensor_tensor(out=h[:], in0=xt[:], scalar=ss[:, 3:4], in1=gbc[:], op0=ALU.mult, op1=ALU.mult),
         [r_xt, r_ss, r_w], [r_h])
    return h, r_h


def transpose8(c, src, r_src, dst, r_dst, pbk, eng="act", n=8):
    P = c.P
    pb, r_pb = pbk
    for kc in range(n):
        P.op("pe", lambda e, kc=kc: e.transpose(out=pb[:, kc * 128:(kc + 1) * 128], in_=src[:, kc * 128:(kc + 1) * 128], identity=c.idb[:]),
             [r_src, c.r_const], [r_pb])
    if eng == "act":
        P.op("act", lambda e: e.copy(out=dst[:, 0:n * 128], in_=pb[:, 0:n * 128]), [r_pb], [r_dst])
    else:
        P.op("dve", lambda e: e.tensor_copy(out=dst[:, 0:n * 128], in_=pb[:, 0:n * 128]), [r_pb], [r_dst])


def phase1(c, l, x_src, x_src_res):
    from contextlib import ExitStack
    P, nc, NT = c.P, c.nc, c.NT
    with ExitStack() as st:
        sb = lambda n, s, d: st.enter_context(nc.sbuf_tensor(uname(n), list(s), d))
        W = sb("W1", [128, 8, P1W], BF16)
        r_W = Res("W1")
        for kc in range(8):
            P.dma("pool", W[:, kc, :], c.w["w_in"][l, kc * 128:(kc + 1) * 128, 0:P1W], writes=[r_W])
        gbc = sb("gbc", [128, D], F32)
        g64 = sb("g64", [128, 128], F32)
        gqk = sb("gqk", [128, 1536], F32)
        r_w = Res("p1w")
        P.dma("sp", gbc[:], c.w["norm_g"][l:l + 1, :].partition_broadcast(128), writes=[r_w])
        P.dma("sp", g64[:, 0:64], c.w["q_norm_g"][l:l + 1, :].partition_broadcast(128), writes=[r_w])
        P.dma("sp", g64[:, 64:128], c.w["k_norm_g"][l:l + 1, :].partition_broadcast(128), writes=[r_w])
        P.op("dve", lambda e: e.tensor_scalar(out=gqk[:, 0:768].rearrange("p (h d) -> p h d", d=64),
                                              in0=g64[:, 0:64].unsqueeze(1).to_broadcast([128, 12, 64]),
                                              scalar1=0.125, scalar2=None, op0=ALU.mult), [r_w], [r_w])
        P.op("dve", lambda e: e.tensor_copy(out=gqk[:, 768:1536].rearrange("p (h d) -> p h d", d=64),
                                            in_=g64[:, 64:128].unsqueeze(1).to_broadcast([128, 12, 64])), [r_w], [r_w])

        def mkpools(k):
            mk = lambda name, shape, dt, n=1: Pool2(st, nc, f"{name}_l{k}_", shape, dt, n)
            d = dict(junk=mk("junk", [128, D], BF16), ss=mk("ss", [128, 4], F32), h=mk("h", [128, D], BF16), hT=mk("hT", [128, D], BF16),
                     xt=mk("xt", [128, D], F32), proj=mk("proj", [128, P1W], F32), zsb=mk("zsb", [128, 1024], F32))
            if 0 in c.branches:
                d.update(sq=mk("sq", [128, 1536], F32), s24=mk("s24", [128, 72], F32), qkn=mk("qkn", [128, 1536], BF16),
                         qkT=mk("qkT", [64, 3072], BF16), vaug=mk("vaug", [128, 780], BF16))
            return d
        pools = [mkpools(k) for k in range(NL)]

        def body(i, k):
            pl = pools[k]
            banks, pbk = lane_banks(c, k)
            rows = slice(i * 128, (i + 1) * 128)
            xt, r_xt = pl["xt"].next()
            P.dma("sp", xt[:], x_src[rows, :], reads=([x_src_res[i]] if x_src_res else []), writes=[r_xt])
            h, r_h = rms_h(c, xt, r_xt, gbc, r_w, pl)
            yield
            hT, r_hT = pl["hT"].next()
            transpose8(c, h, r_h, hT, r_hT, pbk)
            yield
            proj, r_pj = pl["proj"].next()
            for blk in range(P1W // 512):
                ps, r_ps = banks[blk % 3]
                for kc in range(8):
                    P.op("pe", lambda e, kc=kc, blk=blk, ps=ps: e.matmul(out=ps[:, :], lhsT=hT[:, kc * 128:(kc + 1) * 128],
                                                                    rhs=W[:, kc, blk * 512:(blk + 1) * 512], start=(kc == 0), stop=(kc == 7)),
                         [r_hT, r_W], [r_ps])
                if blk % 2 == 0:
                    P.op("act", lambda e, blk=blk, ps=ps: e.copy(out=proj[:, blk * 512:(blk + 1) * 512], in_=ps[:, :]), [r_ps], [r_pj])
                else:
                    P.op("dve", lambda e, blk=blk, ps=ps: e.tensor_copy(out=proj[:, blk * 512:(blk + 1) * 512], in_=ps[:, :]), [r_ps], [r_pj])
                if blk % 2 == 1:
                    yield
            if 1 in c.branches:
                P.dma("pool", c.ub[16 + i * 128:16 + (i + 1) * 128, :], proj[:, C_UB:C_UB + 256], reads=[r_pj], writes=[c.r["ub"][i]])
            if 2 in c.branches:
                P.dma("pool", c.uvc[rows, :], proj[:, C_UVC:C_UVC + 512], reads=[r_pj], writes=[c.r["uvc"][i]])
            if 3 in c.branches:
                P.dma("pool", c.rkvl[1 + i * 128:1 + (i + 1) * 128, :], proj[:, C_RKV:C_RKV + 1024], reads=[r_pj], writes=[c.r["rkvl"][i]])
            zt, r_zt = pl["zsb"].next()
            for b, cz in enumerate((C_ZA, C_ZB, C_ZC, C_ZD)):
                P.op("act", lambda e, b=b, cz=cz: e.activation(out=zt[:, b * 256:(b + 1) * 256], in_=proj[:, cz:cz + 256], func=AF.Silu),
                     [r_pj], [r_zt])
            P.dma("pool", c.zs[rows, :], zt[:], reads=[r_zt], writes=[c.r["zs"][i]])
            yield
            if 0 in c.branches:
                sq, r_sq = pl["sq"].next()
                s, r_s = pl["s24"].next()
                qn, r_qn = pl["qkn"].next()
                v24 = lambda ap: ap.rearrange("p (h d) -> p h d", d=64)
                P.op("pool", lambda e: e.tensor_tensor(out=sq[:], in0=proj[:, 0:1536], in1=proj[:, 0:1536], op=ALU.mult), [r_pj], [r_sq])
                P.op("dve", lambda e: e.tensor_reduce(out=s[:, 0:24], in_=v24(sq[:]), axis=AX.X, op=ALU.add), [r_sq], [r_s])
                P.op("dve", lambda e: e.tensor_scalar(out=s[:, 24:48], in0=s[:, 0:24], scalar1=1.0 / 64, scalar2=EPS, op0=ALU.mult, op1=ALU.add), [r_s], [r_s])
                P.op("act", lambda e: e.activation(out=s[:, 0:24], in_=s[:, 24:48], func=AF.Sqrt), [r_s], [r_s])
                P.op("dve", lambda e: e.reciprocal(out=s[:, 48:72], in_=s[:, 0:24]), [r_s], [r_s])
                yield
                P.op("dve", lambda e: e.tensor_tensor(out=v24(sq[:]), in0=v24(proj[:, 0:1536]),
                                                      in1=s[:, 48:72].unsqueeze(2).to_broadcast([128, 24, 64]), op=ALU.mult), [r_pj, r_s], [r_sq])
                P.op("pool", lambda e: e.tensor_tensor(out=qn[:], in0=sq[:], in1=gqk[:], op=ALU.mult), [r_sq, r_w], [r_qn])
                va, r_va = pl["vaug"].next()
                P.op("act", lambda e: e.copy(out=va[:].rearrange("p (h d) -> p h d", d=65)[:, :, 0:64],
                                             in_=proj[:, C_V:C_V + 768].rearrange("p (h d) -> p h d", d=64)), [r_pj], [r_va])
                P.op("dve", lambda e: e.tensor_copy(out=va[:].rearrange("p (h d) -> p h d", d=65)[:, :, 64:65],
                                                    in_=c.validt[:, i:i + 1].unsqueeze(1).to_broadcast([128, 12, 1])), [c.r_const], [r_va])
                P.dma("pool", c.va[i], va[:], reads=[r_va], writes=[c.r["va"][i]])
                yield
                qT, r_qT = pl["qkT"].next()
                pb, r_pb = pbk
                for grp in range(3):
                    for j in range(8):
                        hh = grp * 8 + j
                        P.op("pe", lambda e, hh=hh, j=j: e.transpose(out=pb[0:64, j * 128:(j + 1) * 128], in_=qn[:, hh * 64:(hh + 1) * 64], identity=c.idb[:]),
                             [r_qn, c.r_const], [r_pb])
                    if grp % 2 == 0:
                        P.op("dve", lambda e, grp=grp: e.tensor_copy(out=qT[:, grp * 1024:(grp + 1) * 1024], in_=pb[0:64, :]), [r_pb], [r_qT])
                    else:
                        P.op("act", lambda e, grp=grp: e.copy(out=qT[:, grp * 1024:(grp + 1) * 1024], in_=pb[0:64, :]), [r_pb], [r_qT])
                    yield
                P.dma("pool", c.qT[i], qT[:, 0:1536], reads=[r_qT], writes=[c.r["qT"][i]])
                P.dma("pool", c.kT[i], qT[:, 1536:3072], reads=[r_qT], writes=[c.r["kT"][i]])

        run_lanes([list(range(k, NT, NL)) for k in range(NL)], body, skew=4)


def phase2c(c, l):
    from contextlib import ExitStack
    P, nc, NT = c.P, c.nc, c.NT
    with ExitStack() as st:
        sb = lambda n, s, d: st.enter_context(nc.sbuf_tensor(uname(n), list(s), d))
        sgw = sb("sgw", [128, 4, 128], BF16)
        sgn = sb("sgn", [128, 256], F32)
        sgb = sb("sgb", [128, 4], F32)
        r_w = Res("p2cw")
        P.dma("pool", sgw[:], c.w["sg_wT"][l].rearrange("g s t -> s g t"), writes=[r_w])
        P.dma("sp", sgn[:], c.w["sg_norm_g"][l:l + 1, :].partition_broadcast(128), writes=[r_w])
        P.dma("sp", sgb[:], c.w["sg_bT"][l], writes=[r_w])

        def mkpools(k):
            mk = lambda name, shape, dt, n=2: Pool2(st, nc, f"{name}_l{k}_", shape, dt, n)
            return dict(uv=mk("uv", [128, 512], F32), zc=mk("zc", [128, 256], F32), junk=mk("junkc", [128, 256], F32, 1), ss=mk("ssc", [128, 4], F32),
                        vn=mk("vn", [128, 256], BF16), sv=mk("sv", [128, 256], F32), ys=mk("ysc", [128, 256], BF16))
        pools = [mkpools(k) for k in range(6)]

        def body(i, k):
            pl = pools[k]
            banks = [(c.pf[k], c.r_pf[k])]
            rows = slice(i * 128, (i + 1) * 128)
            uv, r_uv = pl["uv"].next()
            zc, r_zc = pl["zc"].next()
            P.dma("sp", uv[:], c.uvc[rows, :], reads=[c.r["uvc"][i]], writes=[r_uv])
            P.dma("sp", zc[:], c.zs[rows, 512:768], reads=[c.r["zs"][i]], writes=[r_zc])
            junk, r_j = pl["junk"].next()
            ss, r_ss = pl["ss"].next()
            vn, r_vn = pl["vn"].next()
            sv, r_sv = pl["sv"].next()
            ysc, r_y = pl["ys"].next()
            P.op("act", lambda e: e.activation(out=junk[:], in_=uv[:, 256:512], func=AF.Square, accum_out=ss[:, 0:1]), [r_uv], [r_j, r_ss])
            P.op("dve", lambda e: e.tensor_scalar(out=ss[:, 1:2], in0=ss[:, 0:1], scalar1=1.0 / 256, scalar2=EPS, op0=ALU.mult, op1=ALU.add), [r_ss], [r_ss])
            yield
            P.op("act", lambda e: e.activation(out=ss[:, 2:3], in_=ss[:, 1:2], func=AF.Sqrt), [r_ss], [r_ss])
            P.op("dve", lambda e: e.reciprocal(out=ss[:, 3:4], in_=ss[:, 2:3]), [r_ss], [r_ss])
            P.op("dve", lambda e: e.scalar_tensor_tensor(out=vn[:], in0=uv[:, 256:512], scalar=ss[:, 3:4], in1=sgn[:], op0=ALU.mult, op1=ALU.mult),
                 [r_uv, r_ss, r_w], [r_vn])
            yield
            ps, r_ps = banks[0]
            for g in range(4):
                P.op("pe", lambda e, g=g: e.matmul(out=ps[:, g * 64:(g + 1) * 64], lhsT=sgw[:, g, :], rhs=vn[:, g * 64:(g + 1) * 64], start=True, stop=True),
                     [r_vn, r_w], [r_ps])
            P.op("dve", lambda e: e.tensor_tensor(out=sv[:].rearrange("p (g d) -> p g d", d=64), in0=ps[:, 0:256].rearrange("p (g d) -> p g d", d=64),
                                                  in1=sgb[:].unsqueeze(2).to_broadcast([128, 4, 64]), op=ALU.add), [r_ps, r_w], [r_sv])
            yield
            P.op("pool", lambda e: e.tensor_tensor(out=sv[:], in0=sv[:], in1=uv[:, 0:256], op=ALU.mult), [r_sv, r_uv], [r_sv])
            P.op("dve", lambda e: e.tensor_tensor(out=ysc[:], in0=sv[:], in1=zc[:], op=ALU.mult), [r_sv, r_zc], [r_y])
            P.dma("pool", c.ys[rows, 512:768], ysc[:], reads=[r_y], writes=[c.r["ys"][i]])

        run_lanes([list(range(k, NT, 6)) for k in range(6)], body, skew=1)


def phase3(c, l, x_src, x_src_res, x_dst, x_dst_res):
    from contextlib import ExitStack
    P, nc, NT = c.P, c.nc, c.NT
    with ExitStack() as st:
        sb = lambda n, s, d: st.enter_context(nc.sbuf_tensor(uname(n), list(s), d))
        Wg = sb("Wg", [128, 8, 4096], BF16)
        Wbr = sb("Wbr", [128, 8, 1024], BF16)
        Wo = sb("Wo", [128, 8, 1024], BF16)
        gbc = sb("gbc3", [128, D], F32)
        r_W = Res("W3")
        for kc in range(8):
            P.dma("pool", Wg[:, kc, :], c.w["w_in"][l, kc * 128:(kc + 1) * 128, P1W:PW], writes=[r_W])
        wbr_flat = c.w["w_branch"][l].rearrange("b c n -> (b c) n")
        for kc in range(8):
            P.dma("pool", Wbr[:, kc, :], wbr_flat[kc * 128:(kc + 1) * 128, :], writes=[r_W])
            P.dma("pool", Wo[:, kc, :], c.w["w_out"][l, kc * 128:(kc + 1) * 128, :], writes=[r_W])
        P.dma("sp", gbc[:], c.w["norm_g"][l:l + 1, :].partition_broadcast(128), writes=[r_W])

        def mkpools(k):
            mk = lambda name, shape, dt, n=1: Pool2(st, nc, f"{name}_l{k}_", shape, dt, n)
            return dict(junk=mk("junk3", [128, D], BF16), ss=mk("ss3", [128, 4], F32), h=mk("h3", [128, D], BF16), hT=mk("hT3", [128, D], BF16),
                        xt=mk("xt3", [128, D], F32), ys=mk("ys3", [128, D], BF16), ysT=mk("ysT3", [128, D], BF16), gs=mk("gs3", [128, 512], F32, 2),
                        tmp=mk("tmp3", [128, 512], F32, 2), acc=mk("acc3", [128, D], F32), m=mk("m3", [128, D], BF16), mT=mk("mT3", [128, D], BF16),
                        xo=mk("xo3", [128, D], F32))
        pools = [mkpools(k) for k in range(NL)]

        def body(i, k):
            pl = pools[k]
            banks, pbk = lane_banks(c, k)
            rows = slice(i * 128, (i + 1) * 128)
            xt, r_xt = pl["xt"].next()
            P.dma("sp", xt[:], x_src[rows, :], reads=([x_src_res[i]] if x_src_res else []), writes=[r_xt])
            ys, r_ys = pl["ys"].next()
            P.dma("sp", ys[:], c.ys[rows, :], reads=[c.r["ys"][i]], writes=[r_ys])
            h, r_h = rms_h(c, xt, r_xt, gbc, r_W, pl)
            yield
            hT, r_hT = pl["hT"].next()
            transpose8(c, h, r_h, hT, r_hT, pbk)
            yield
            ysT, r_ysT = pl["ysT"].next()
            transpose8(c, ys, r_ys, ysT, r_ysT, pbk, eng="dve")
            yield
            acc, r_acc = pl["acc"].next()
            for bi, b in enumerate(c.branches):
                for cb in range(2):
                    pg, r_pg = banks[0]
                    pbr, r_pbr = banks[1]
                    col = b * 1024 + cb * 512
                    for kc in range(8):
                        P.op("pe", lambda e, kc=kc, col=col: e.matmul(out=pg[:, :], lhsT=hT[:, kc * 128:(kc + 1) * 128], rhs=Wg[:, kc, col:col + 512],
                                                                      start=(kc == 0), stop=(kc == 7)), [r_hT, r_W], [r_pg])
                    gs, r_gs = pl["gs"].next()
                    P.op("act", lambda e, gs=gs: e.activation(out=gs[:], in_=pg[:, :], func=AF.Sigmoid), [r_pg], [r_gs])
                    for kk in range(2):
                        kc = 2 * b + kk
                        P.op("pe", lambda e, kc=kc, kk=kk, cb=cb: e.matmul(out=pbr[:, :], lhsT=ysT[:, kc * 128:(kc + 1) * 128],
                                                                             rhs=Wbr[:, kc, cb * 512:(cb + 1) * 512], start=(kk == 0), stop=(kk == 1)),
                             [r_ysT, r_W], [r_pbr])
                    if bi == 0:
                        P.op("dve", lambda e, cb=cb, gs=gs: e.tensor_tensor(out=acc[:, cb * 512:(cb + 1) * 512], in0=gs[:], in1=pbr[:, :], op=ALU.mult),
                             [r_gs, r_pbr], [r_acc])
                    else:
                        tmp, r_tmp = pl["tmp"].next()
                        P.op("dve", lambda e, gs=gs, tmp=tmp: e.tensor_tensor(out=tmp[:], in0=gs[:], in1=pbr[:, :], op=ALU.mult), [r_gs, r_pbr], [r_tmp])
                        P.op("pool", lambda e, cb=cb, tmp=tmp: e.tensor_tensor(out=acc[:, cb * 512:(cb + 1) * 512], in0=acc[:, cb * 512:(cb + 1) * 512], in1=tmp[:], op=ALU.add),
                             [r_tmp, r_acc], [r_acc])
                    yield
            m, r_m = pl["m"].next()
            mT, r_mT = pl["mT"].next()
            P.op("act", lambda e: e.copy(out=m[:], in_=acc[:]), [r_acc], [r_m])
            yield
            transpose8(c, m, r_m, mT, r_mT, pbk)
            yield
            xo, r_xo = pl["xo"].next()
            po, r_po = banks[2]
            for cb in range(2):
                for kc in range(8):
                    P.op("pe", lambda e, kc=kc, cb=cb: e.matmul(out=po[:, :], lhsT=mT[:, kc * 128:(kc + 1) * 128], rhs=Wo[:, kc, cb * 512:(cb + 1) * 512],
                                                                start=(kc == 0), stop=(kc == 7)), [r_mT, r_W], [r_po])
                P.op("dve", lambda e, cb=cb: e.tensor_tensor(out=xo[:, cb * 512:(cb + 1) * 512], in0=po[:, :], in1=xt[:, cb * 512:(cb + 1) * 512], op=ALU.add),
                     [r_po, r_xt], [r_xo])
                yield
            P.dma("pool", x_dst[rows, :], xo[:], reads=[r_xo], writes=[x_dst_res[i]])

        run_lanes([list(range(k, NT, NL)) for k in range(NL)], body, skew=5)


def phase2b(c, l):
    from contextlib import ExitStack
    P, nc, NT = c.P, c.nc, c.NT
    with ExitStack() as st:
        sb = lambda n, s, d: st.enter_context(nc.sbuf_tensor(uname(n), list(s), d))
        bandc = sb("bandc", [128, 512], F32)
        bandh = sb("bandh", [16, 512], F32)
        pw = sb("pw", [128, 2, 128], BF16)
        psc = sb("psc", [128, 256], F32)
        invc = sb("invc", [128, NT * 4], F32)
        r_w = Res("p2bw")
        P.dma("sp", bandc[:], c.bandc, writes=[r_w])
        P.dma("sp", bandh[:], c.bandh, writes=[r_w])
        P.dma("sp", psc[:], c.w["pool_scale"][l:l + 1, :].partition_broadcast(128), writes=[r_w])
        P.dma("sp", invc[:], c.invc, writes=[r_w])
        P.op("pool", lambda e: e.memset(pw[:], 0.0), [], [r_w])
        for g in range(4):
            j, gl = g // 2, g % 2
            P.dma("pool", pw[gl * 64:(gl + 1) * 64, j, gl * 64:(gl + 1) * 64], c.w["pool_w"][l, g], writes=[r_w])

        def mkpools(k):
            mk = lambda name, shape, dt, n=2: Pool2(st, nc, f"{name}_l{k}_", shape, dt, n)
            return dict(u=mk("ub", [128, 256], F32), uh=mk("uh", [16, 256], F32), zb=mk("zb", [128, 256], F32), d=mk("dpool", [128, 256], BF16),
                        dT=mk("dT", [128, 256], BF16), y=mk("ybf", [128, 256], F32), yo=mk("ybo", [128, 256], BF16))
        pools = [mkpools(k) for k in range(3)]

        def body(i, k):
            pl = pools[k]
            banks = [(c.pf[2 * k], c.r_pf[2 * k]), (c.pf[2 * k + 1], c.r_pf[2 * k + 1])]
            pbk = (c.pbf[k % 2], c.r_pbf[k % 2])
            rows = slice(i * 128, (i + 1) * 128)
            u, r_u = pl["u"].next()
            uh, r_uh = pl["uh"].next()
            zb, r_zb = pl["zb"].next()
            d, r_d = pl["d"].next()
            dT, r_dT = pl["dT"].next()
            y, r_y = pl["y"].next()
            yo, r_yo = pl["yo"].next()
            nb = [c.r["ub"][j] for j in (i - 1, i, i + 1) if 0 <= j < NT] + [c.r_pad]
            P.dma("sp", u[:], c.ub[16 + i * 128:16 + (i + 1) * 128, :], reads=[c.r["ub"][i]], writes=[r_u])
            P.dma("sp", uh[0:8, :], c.ub[16 + i * 128 - 8:16 + i * 128, :], reads=nb, writes=[r_uh])
            P.dma("sp", uh[8:16, :], c.ub[16 + (i + 1) * 128:16 + (i + 1) * 128 + 8, :], reads=nb, writes=[r_uh])
            P.dma("sp", zb[:], c.zs[rows, 256:512], reads=[c.r["zs"][i]], writes=[r_zb])
            yield
            ps, r_ps = banks[0]
            for g in range(4):
                P.op("pe", lambda e, g=g: e.matmul(out=ps[:, g * 64:(g + 1) * 64], lhsT=bandc[:, g * 128:(g + 1) * 128], rhs=u[:, g * 64:(g + 1) * 64],
                                                   start=True, stop=False), [r_u, r_w], [r_ps])
                P.op("pe", lambda e, g=g: e.matmul(out=ps[:, g * 64:(g + 1) * 64], lhsT=bandh[:, g * 128:(g + 1) * 128], rhs=uh[:, g * 64:(g + 1) * 64],
                                                   start=False, stop=True), [r_uh, r_w], [r_ps])
            for g in range(4):
                P.op("dve", lambda e, g=g: e.scalar_tensor_tensor(out=d[:, g * 64:(g + 1) * 64], in0=ps[:, g * 64:(g + 1) * 64],
                                                                  scalar=invc[:, i * 4 + g:i * 4 + g + 1], in1=u[:, g * 64:(g + 1) * 64],
                                                                  op0=ALU.mult, op1=ALU.subtract), [r_ps, r_u, r_w], [r_d])
            yield
            transpose8(c, d, r_d, dT, r_dT, pbk, n=2)
            yield
            ps2, r_ps2 = banks[1]
            for j in range(2):
                P.op("pe", lambda e, j=j: e.matmul(out=ps2[:, j * 128:(j + 1) * 128], lhsT=dT[:, j * 128:(j + 1) * 128], rhs=pw[:, j, :], start=True, stop=True),
                     [r_dT, r_w], [r_ps2])
            P.op("dve", lambda e: e.tensor_tensor(out=y[:], in0=ps2[:, 0:256], in1=psc[:], op=ALU.mult), [r_ps2, r_w], [r_y])
            yield
            P.op("pool", lambda e: e.tensor_tensor(out=yo[:], in0=y[:], in1=zb[:], op=ALU.mult), [r_y, r_zb], [r_yo])
            P.dma("pool", c.ys[rows, 256:512], yo[:], reads=[r_yo], writes=[c.r["ys"][i]])

        run_lanes([list(range(k, NT, 3)) for k in range(3)], body, skew=2)


def phase2a(c, l):
    from contextlib import ExitStack
    P, nc, NT = c.P, c.nc, c.NT
    with ExitStack() as st:
        sb = lambda n, s, d: st.enter_context(nc.sbuf_tensor(uname(n), list(s), d))
        masks = sb("amask", [128, 25, 512], BF16)
        r_w = Res("p2aw")
        for ci in range(25):
            P.dma("pool", masks[:, ci, :], c.amask[ci], writes=[r_w])
        depth = [2 * w + 2 for w in ATT_W]
        kring = [Pool2(st, nc, f"kr{g}_", [64, 512], BF16, depth[g]) for g in range(3)]
        vring = [Pool2(st, nc, f"vr{g}_", [128, 260], BF16, depth[g]) for g in range(3)]
        loaded = [-1, -1, -1]
        q_ = Pool2(st, nc, "qTa", [64, 1536], BF16, 2)
        za_ = Pool2(st, nc, "za", [128, 256], F32, 2)
        e_ = Pool2(st, nc, "eexp", [128, 512], BF16, 3)
        p_ = Pool2(st, nc, "pexp", [128, 512], BF16, 6)
        dn_ = Pool2(st, nc, "den", [128, 8], F32, 2)
        ya_ = Pool2(st, nc, "yaf", [128, 256], F32, 2)
        yo_ = Pool2(st, nc, "yao", [128, 256], BF16, 2)
        mcol = []
        ci = 0
        for g in range(3):
            mcol.append({coff: ci + k for k, coff in enumerate(range(-ATT_W[g], ATT_W[g] + 1))})
            ci += 2 * ATT_W[g] + 1

        def ensure(g, upto):
            while loaded[g] < min(upto, NT - 1):
                j = loaded[g] + 1
                kt, r_kt = kring[g].t[j % depth[g]], kring[g].r[j % depth[g]]
                vt, r_vt = vring[g].t[j % depth[g]], vring[g].r[j % depth[g]]
                P.dma("sp", kt[:], c.kT[j][:, g * 512:(g + 1) * 512], reads=[c.r["kT"][j]], writes=[r_kt])
                P.dma("sp", vt[:], c.va[j][:, g * 260:(g + 1) * 260], reads=[c.r["va"][j]], writes=[r_vt])
                loaded[g] = j

        LOOK = 3
        tiles = {}

        def tile_begin(i):
            rows = slice(i * 128, (i + 1) * 128)
            for g in range(3):
                ensure(g, i + ATT_W[g])
            qT, r_q = q_.next()
            za, r_za = za_.next()
            P.dma("sp", qT[:], c.qT[i], reads=[c.r["qT"][i]], writes=[r_q])
            P.dma("sp", za[:], c.zs[rows, 0:256], reads=[c.r["zs"][i]], writes=[r_za])
            chunks = [(g, coff) for g in range(3) for coff in range(-ATT_W[g], ATT_W[g] + 1) if 0 <= i + coff < NT]
            tiles[i] = dict(qT=qT, r_q=r_q, za=za, r_za=r_za, n=len(chunks), pso=c.pf[4 + i % 2], r_pso=c.r_pf[4 + i % 2])
            return chunks

        def stage_qk(i, idx, g, coff, seq):
            t = tiles[i]
            qT, r_q = t["qT"], t["r_q"]
            j = i + coff
            kt, r_kt = kring[g].t[j % depth[g]], kring[g].r[j % depth[g]]
            pss, r_pss = c.pf[seq % 4], c.r_pf[seq % 4]
            for h in range(4):
                P.op("pe", lambda e, h=h: e.matmul(out=pss[:, h * 128:(h + 1) * 128], lhsT=kt[:, h * 128:(h + 1) * 128],
                                                   rhs=qT[:, (g * 4 + h) * 128:(g * 4 + h + 1) * 128], start=True, stop=True),
                     [r_kt, r_q], [r_pss])
            ex, r_ex = e_.next()
            pp, r_pp = p_.next()
            P.op("act", lambda e: e.activation(out=ex[:], in_=pss[:, :], func=AF.Exp), [r_pss], [r_ex])
            mc = mcol[g][coff]
            P.op("dve", lambda e: e.tensor_tensor(out=pp[:], in0=ex[:], in1=masks[:, mc, :], op=ALU.mult), [r_ex, r_w], [r_pp])
            return pp, r_pp

        def stage_pv(i, idx, g, coff, pp, r_pp):
            t = tiles[i]
            pso, r_pso = t["pso"], t["r_pso"]
            j = i + coff
            vt, r_vt = vring[g].t[j % depth[g]], vring[g].r[j % depth[g]]
            n = t["n"]
            for h in range(4):
                P.op("pe", lambda e, h=h: e.matmul(out=pso[:, h * 65:(h + 1) * 65], lhsT=pp[:, h * 128:(h + 1) * 128],
                                                   rhs=vt[:, h * 65:(h + 1) * 65], start=(idx == 0 and h == 0), stop=(idx == n - 1), skip_group_check=True),
                     [r_pp, r_vt], [r_pso])
            if idx == n - 1:
                tile_end(i)

        def tile_end(i):
            rows = slice(i * 128, (i + 1) * 128)
            t = tiles.pop(i)
            pso, r_pso, za, r_za = t["pso"], t["r_pso"], t["za"], t["r_za"]
            dn, r_dn = dn_.next()
            ya, r_ya = ya_.next()
            yo, r_yo = yo_.next()
            pv = pso[:, 0:260].rearrange("p (h d) -> p h d", d=65)
            P.op("dve", lambda e: e.tensor_scalar(out=dn[:, 0:4].unsqueeze(2), in0=pv[:, :, 64:65], scalar1=1e-30, scalar2=None, op0=ALU.max), [r_pso], [r_dn])
            P.op("dve", lambda e: e.reciprocal(out=dn[:, 4:8], in_=dn[:, 0:4]), [r_dn], [r_dn])
            P.op("dve", lambda e: e.tensor_tensor(out=ya[:].rearrange("p (h d) -> p h d", d=64), in0=pv[:, :, 0:64],
                                                  in1=dn[:, 4:8].unsqueeze(2).to_broadcast([128, 4, 64]), op=ALU.mult), [r_pso, r_dn], [r_ya])
            P.op("pool", lambda e: e.tensor_tensor(out=yo[:], in0=ya[:], in1=za[:], op=ALU.mult), [r_ya, r_za], [r_yo])
            P.dma("pool", c.ys[rows, 0:256], yo[:], reads=[r_yo], writes=[c.r["ys"][i]])

        def stream():
            for i in range(NT):
                first = True
                chunks = None
                for idx in range(10 ** 9):
                    if first:
                        chunks = tile_begin(i)
                        first = False
                    if idx >= len(chunks):
                        break
                    g, coff = chunks[idx]
                    yield (i, idx, g, coff)

        pend = []
        for seq, (i, idx, g, coff) in enumerate(stream()):
            pp, r_pp = stage_qk(i, idx, g, coff, seq)
            pend.append((i, idx, g, coff, pp, r_pp))
            if len(pend) > LOOK:
                stage_pv(*pend.pop(0))
        while pend:
            stage_pv(*pend.pop(0))


def phase2d(c, l):
    from contextlib import ExitStack
    P, nc, NT = c.P, c.nc, c.NT
    NEG = -float(np.exp(-0.5))
    def setup_dir(d, st):
        if True:
            sb = lambda n, s, dt: st.enter_context(nc.sbuf_tensor(uname(n), list(s), dt))
            r_w = Res("p2dw")
            tri = sb("tri", [128, 642], F32)
            P.dma("sp", tri[:], c.tri, writes=[r_w])
            incl = tri[:, 128 * d:128 * d + 128]
            ones_m = tri[:, 512:640]
            ones_c = tri[:, 640:641]
            mask4 = sb("mask4", [128, 512], F32)
            maskT4 = sb("maskT4", [128, 512], F32)
            for q in range(4):
                src = tri[:, 256 + 128 * d:384 + 128 * d] if q % 2 == 0 else incl
                P.op("dve", lambda e, q=q, src=src: e.tensor_copy(out=mask4[:, q * 128:(q + 1) * 128], in_=src), [r_w], [r_w])
                P.op("dve", lambda e, q=q: e.tensor_copy(out=maskT4[:, q * 128:(q + 1) * 128], in_=tri[:, 256 + 128 * (1 - d):384 + 128 * (1 - d)]), [r_w], [r_w])
            identB = sb("identB", [64, 4, 64], F32)
            P.op("dve", lambda e: e.tensor_copy(out=identB[:], in_=c.idf[0:64, 0:64].unsqueeze(1).to_broadcast([64, 4, 64])), [c.r_const], [r_w])
            mu_r = sb("mu_r", [128, 768], F32)
            mu_l = sb("mu_l", [128, 128], F32)
            bias_wa = sb("bias_wa", [128, 512], F32)
            kkv = sb("kkv", [128, 256], F32)
            kav = sb("kav", [128, 256], F32)
            rkp = sb("rkp", [128, 256], F32)
            lng = sb("lng", [128, 256], F32)
            lnb = sb("lnb", [128, 256], F32)
            Wud = sb("Wud", [128, 512], BF16)
            bc = lambda ap: ap.partition_broadcast(128)
            P.dma("sp", mu_r[:], bc(c.w["mu_rkv"][l, d:d + 1, :]), writes=[r_w])
            P.dma("sp", mu_l[:], bc(c.w["mu_lat"][l, d:d + 1, :]), writes=[r_w])
            P.dma("sp", bias_wa[:, 0:256], bc(c.w["w0"][l, d:d + 1, :]), writes=[r_w])
            P.dma("sp", bias_wa[:, 256:512], bc(c.w["a0"][l, d:d + 1, :]), writes=[r_w])
            P.dma("sp", kkv[:], bc(c.w["k_k"][l, d:d + 1, :]), writes=[r_w])
            P.dma("sp", kav[:], bc(c.w["k_a"][l, d:d + 1, :]), writes=[r_w])
            P.dma("sp", rkp[:], bc(c.w["r_k"][l, d:d + 1, :]), writes=[r_w])
            P.dma("sp", lng[:], bc(c.w["ln_g"][l:l + 1, :]), writes=[r_w])
            P.dma("sp", lnb[:], bc(c.w["ln_b"][l:l + 1, :]), writes=[r_w])
            P.op("pool", lambda e: e.memset(Wud[:], 0.0), [], [r_w])
            P.dma("pool", Wud[0:64, 0:256], c.w["w_up"][l, d], writes=[r_w])
            P.dma("pool", Wud[64:128, 256:512], c.w["a_up"][l, d], writes=[r_w])
            ST = Pool2(st, nc, f"ST_d{d}_", [64, 256], F32, 2)
            st0, r_st0 = ST.next()
            P.op("pool", lambda e: e.memset(st0[:], 0.0), [], [r_st0])
            state = [st0, r_st0]

            def mk(name, shape, dt, n=1):
                return Pool2(st, nc, f"{name}_d{d}_", shape, dt, n)
            cur_, sh_ = mk("cur", [128, 1024], F32, 2), mk("sh", [128, 1024], F32, 2)
            xr_, xl_, tl_, tlT_ = mk("xr", [128, 768], F32), mk("xl", [128, 128], F32), mk("tl", [128, 128], BF16), mk("tlT", [128, 128], BF16)
            sg_, lw_ = mk("sg", [128, 512], F32), mk("logw", [128, 256], F32)
            kk_, sq_, s4_, k2_, bon_, beta_ = mk("kk", [128, 256], F32), mk("sqd", [128, 256], F32), mk("s4", [128, 32], F32), mk("k2", [128, 256], F32), mk("bon", [128, 256], F32), mk("beta", [128, 256], F32)
            cum_, ex_, tmp_ = mk("cum", [128, 256], F32), mk("exps", [128, 1024], F32), mk("tmpd", [128, 256], F32, 2)
            opb_ = mk("opb", [128, 7, 256], BF16)
            fTA_, fTB_ = mk("fTA", [64, 1024], BF16), mk("fTB", [64, 1024], BF16)
            AB_, AK_, NTa_ = mk("AB", [128, 1024], BF16), mk("AK", [128, 1024], BF16), mk("NTa", [128, 512], BF16, 3)
            Nn_ = mk("Nn", [128, 512], BF16, 3)
            Z_ = mk("Z", [128, 512], BF16, 3)
            gcol_, Q_, Pm_, Dm_ = mk("gcol", [64, 4], F32), mk("Q", [64, 512], F32), mk("Pm", [64, 256], F32), mk("Dm", [64, 256], F32)
            osb_, on_, yo_ = mk("osb", [128, 256], F32), mk("on", [128, 256], F32), mk("yod", [128, 256], BF16)
            banks, pbk = lane_banks(c, d)
            bank_i = [0]

            def nb():
                bank_i[0] = (bank_i[0] + 1) % 2
                return banks[bank_i[0]]

            def body(i):
                rows = slice(i * 128, (i + 1) * 128)
                v3 = lambda ap, dd=64: ap.rearrange("p (h d) -> p h d", d=dd)
                dbg_on = False

                def dbg(name, t, shape, dt, rs):
                    if dbg_on:
                        o = nc.dram_tensor("dbg_" + name, list(shape), dt, kind="ExternalOutput").ap()
                        P.dma("pool", o, t, reads=rs, writes=[Res()])
                cur, r_cur = cur_.next()
                sh, r_sh = sh_.next()
                nbr = [c.r["rkvl"][j] for j in (i - 1, i, i + 1) if 0 <= j < NT] + [c.r_pad]
                P.dma("sp", cur[:], c.rkvl[1 + i * 128:1 + (i + 1) * 128, :], reads=[c.r["rkvl"][i]], writes=[r_cur])
                so = 0 if d == 0 else 2
                P.dma("sp", sh[:], c.rkvl[so + i * 128:so + (i + 1) * 128, :], reads=nbr, writes=[r_sh])
                xr, r_xr = xr_.next()
                xl, r_xl = xl_.next()
                P.op("pool", lambda e: e.tensor_tensor(out=xr[:], in0=sh[:, 0:768], in1=cur[:, 0:768], op=ALU.subtract), [r_sh, r_cur], [r_xr])
                P.op("dve", lambda e: e.tensor_tensor(out=xr[:], in0=xr[:], in1=mu_r[:], op=ALU.mult), [r_xr, r_w], [r_xr])
                P.op("pool", lambda e: e.tensor_tensor(out=xr[:], in0=xr[:], in1=cur[:, 0:768], op=ALU.add), [r_xr, r_cur], [r_xr])
                lo = 768 + 128 * d
                P.op("dve", lambda e: e.tensor_tensor(out=xl[:], in0=sh[:, lo:lo + 128], in1=cur[:, lo:lo + 128], op=ALU.subtract), [r_sh, r_cur], [r_xl])
                P.op("dve", lambda e: e.tensor_tensor(out=xl[:], in0=xl[:], in1=mu_l[:], op=ALU.mult), [r_xl, r_w], [r_xl])
                P.op("dve", lambda e: e.tensor_tensor(out=xl[:], in0=xl[:], in1=cur[:, lo:lo + 128], op=ALU.add), [r_xl, r_cur], [r_xl])
                yield
                r_, k_, v_ = xr[:, 0:256], xr[:, 256:512], xr[:, 512:768]
                tl, r_tl = tl_.next()
                tlT, r_tlT = tlT_.next()
                P.op("act", lambda e: e.activation(out=tl[:, 0:64], in_=xl[:, 0:64], func=AF.Tanh), [r_xl], [r_tl])
                P.op("act", lambda e: e.copy(out=tl[:, 64:128], in_=xl[:, 64:128]), [r_xl], [r_tl])
                pb, r_pb = pbk
                P.op("pe", lambda e: e.transpose(out=pb[:, 0:128], in_=tl[:], identity=c.idb[:]), [r_tl, c.r_const], [r_pb])
                P.op("act", lambda e: e.copy(out=tlT[:], in_=pb[:, 0:128]), [r_pb], [r_tlT])
                yield
                ps, r_ps = nb()
                P.op("pe", lambda e: e.matmul(out=ps[:, :], lhsT=tlT[:], rhs=Wud[:], start=True, stop=True), [r_tlT, r_w], [r_ps])
                sg, r_sg = sg_.next()
                lw, r_lw = lw_.next()
                P.op("dve", lambda e: e.tensor_tensor(out=sg[:], in0=ps[:, :], in1=bias_wa[:], op=ALU.add), [r_ps, r_w], [r_sg])
                P.op("act", lambda e: e.activation(out=sg[:], in_=sg[:], func=AF.Sigmoid), [r_sg], [r_sg])
                P.op("act", lambda e: e.mul(out=lw[:], in_=sg[:, 0:256], mul=NEG), [r_sg], [r_lw])
                yield
                a_ = sg[:, 256:512]
                kk, r_kk = kk_.next()
                sq, r_sq = sq_.next()
                s4, r_s4 = s4_.next()
                P.op("dve", lambda e: e.tensor_tensor(out=kk[:], in0=k_, in1=kkv[:], op=ALU.mult), [r_xr, r_w], [r_kk])
                P.op("pool", lambda e: e.tensor_tensor(out=sq[:], in0=kk[:], in1=kk[:], op=ALU.mult), [r_kk], [r_sq])
                P.op("dve", lambda e: e.tensor_reduce(out=s4[:, 0:4], in_=v3(sq[:]), axis=AX.X, op=ALU.add), [r_sq], [r_s4])
                P.op("dve", lambda e: e.tensor_scalar(out=s4[:, 4:8], in0=s4[:, 0:4], scalar1=1e-12, scalar2=None, op0=ALU.add), [r_s4], [r_s4])
                P.op("act", lambda e: e.activation(out=s4[:, 0:4], in_=s4[:, 4:8], func=AF.Sqrt), [r_s4], [r_s4])
                P.op("dve", lambda e: e.reciprocal(out=s4[:, 8:12], in_=s4[:, 0:4]), [r_s4], [r_s4])
                P.op("dve", lambda e: e.tensor_tensor(out=v3(kk[:]), in0=v3(kk[:]), in1=s4[:, 8:12].unsqueeze(2).to_broadcast([128, 4, 64]), op=ALU.mult), [r_kk, r_s4], [r_kk])
                k2, r_k2 = k2_.next()
                P.op("dve", lambda e: e.scalar_tensor_tensor(out=k2[:], in0=a_, scalar=-1.0, in1=kav[:], op0=ALU.add, op1=ALU.mult), [r_sg, r_w], [r_k2])
                P.op("dve", lambda e: e.scalar_tensor_tensor(out=k2[:], in0=k2[:], scalar=1.0, in1=k_, op0=ALU.add, op1=ALU.mult), [r_k2, r_xr], [r_k2])
                bon, r_bon = bon_.next()
                P.op("pool", lambda e: e.tensor_tensor(out=bon[:], in0=r_, in1=k2[:], op=ALU.mult), [r_xr, r_k2], [r_bon])
                P.op("pool", lambda e: e.tensor_tensor(out=bon[:], in0=bon[:], in1=rkp[:], op=ALU.mult), [r_bon, r_w], [r_bon])
                P.op("dve", lambda e: e.tensor_reduce(out=s4[:, 12:16], in_=v3(bon[:]), axis=AX.X, op=ALU.add), [r_bon], [r_s4])
                P.op("dve", lambda e: e.tensor_tensor(out=v3(bon[:]), in0=v3(v_), in1=s4[:, 12:16].unsqueeze(2).to_broadcast([128, 4, 64]), op=ALU.mult), [r_xr, r_s4], [r_bon])
                beta, r_beta = beta_.next()
                P.op("pool", lambda e: e.tensor_tensor(out=beta[:], in0=kk[:], in1=a_, op=ALU.mult), [r_kk, r_sg], [r_beta])
                yield
                pc, r_pc = nb()
                P.op("pe", lambda e: e.matmul(out=pc[:, 0:256], lhsT=incl, rhs=lw[:], start=True, stop=True), [r_lw, r_w], [r_pc])
                P.op("pe", lambda e: e.matmul(out=pc[:, 256:512], lhsT=ones_m, rhs=lw[:], start=True, stop=True), [r_lw, r_w], [r_pc])
                pg, r_pg = nb()
                for h in range(4):
                    P.op("pe", lambda e, h=h: e.matmul(out=pg[0:64, h:h + 1], lhsT=lw[:, h * 64:(h + 1) * 64], rhs=ones_c, start=True, stop=True), [r_lw, r_w], [r_pg])
                gcol, r_gcol = gcol_.next()
                P.op("act", lambda e: e.activation(out=gcol[:], in_=pg[0:64, 0:4], func=AF.Exp), [r_pg], [r_gcol])
                yield
                cum, r_cum = cum_.next()
                ex, r_ex = ex_.next()
                t3, r_t3 = tmp_.next()
                t4, r_t4 = tmp_.next()
                ecum, encum, eexc, edec = ex[:, 0:256], ex[:, 256:512], ex[:, 512:768], ex[:, 768:1024]
                P.op("act", lambda e: e.copy(out=cum[:], in_=pc[:, 0:256]), [r_pc], [r_cum])
                P.op("act", lambda e: e.activation(out=ecum, in_=pc[:, 0:256], func=AF.Exp), [r_pc], [r_ex])
                P.op("act", lambda e: e.activation(out=encum, in_=pc[:, 0:256], func=AF.Exp, scale=-1.0), [r_pc], [r_ex])
                P.op("dve", lambda e: e.tensor_tensor(out=t3[:], in0=cum[:], in1=lw[:], op=ALU.subtract), [r_cum, r_lw], [r_t3])
                P.op("act", lambda e: e.activation(out=eexc, in_=t3[:], func=AF.Exp), [r_t3], [r_ex])
                P.op("dve", lambda e: e.tensor_tensor(out=t4[:], in0=pc[:, 256:512], in1=cum[:], op=ALU.subtract), [r_pc, r_cum], [r_t4])
                P.op("act", lambda e: e.activation(out=edec, in_=t4[:], func=AF.Exp), [r_t4], [r_ex])
                yield
                opb, r_opb = opb_.next()
                P.op("dve", lambda e: e.tensor_tensor(out=opb[:, 0, :], in0=r_, in1=ecum, op=ALU.mult), [r_xr, r_ex], [r_opb])
                P.op("pool", lambda e: e.tensor_tensor(out=opb[:, 1, :], in0=k2[:], in1=encum, op=ALU.mult), [r_k2, r_ex], [r_opb])
                P.op("dve", lambda e: e.tensor_tensor(out=opb[:, 2, :], in0=beta[:], in1=encum, op=ALU.mult), [r_beta, r_ex], [r_opb])
                P.op("dve", lambda e: e.scalar_tensor_tensor(out=opb[:, 3, :], in0=kk[:], scalar=-1.0, in1=eexc, op0=ALU.mult, op1=ALU.mult), [r_kk, r_ex], [r_opb])
                P.op("pool", lambda e: e.tensor_tensor(out=opb[:, 4, :], in0=k2[:], in1=edec, op=ALU.mult), [r_k2, r_ex], [r_opb])
                P.op("pool", lambda e: e.tensor_tensor(out=opb[:, 5, :], in0=beta[:], in1=edec, op=ALU.mult), [r_beta, r_ex], [r_opb])
                P.op("act", lambda e: e.copy(out=opb[:, 6, :], in_=v_), [r_xr], [r_opb])
                yield
                rb, kt, bt, ab, Kh, Bh, vb = (opb[:, q, :] for q in range(7))
                pa, r_pa = pbk
                fTA, r_fTA = fTA_.next()
                fTB, r_fTB = fTB_.next()
                for h in range(4):
                    hs = slice(h * 64, (h + 1) * 64)
                    P.op("pe", lambda e, h=h, hs=hs: e.transpose(out=pa[0:64, h * 128:(h + 1) * 128], in_=bt[:, hs], identity=c.idb[:]), [r_opb, c.r_const], [r_pa])
                    P.op("pe", lambda e, h=h, hs=hs: e.transpose(out=pa[0:64, (4 + h) * 128:(5 + h) * 128], in_=kt[:, hs], identity=c.idb[:]), [r_opb, c.r_const], [r_pa])
                P.op("act", lambda e: e.copy(out=fTA[:], in_=pa[0:64, :]), [r_pa], [r_fTA])
                yield
                for h in range(4):
                    hs = slice(h * 64, (h + 1) * 64)
                    P.op("pe", lambda e, h=h, hs=hs: e.transpose(out=pa[0:64, (2 * h) * 128:(2 * h + 1) * 128], in_=ab[:, hs], identity=c.idb[:]), [r_opb, c.r_const], [r_pa])
                    P.op("pe", lambda e, h=h, hs=hs: e.transpose(out=pa[0:64, (2 * h + 1) * 128:(2 * h + 2) * 128], in_=rb[:, hs], identity=c.idb[:]), [r_opb, c.r_const], [r_pa])
                P.op("dve", lambda e: e.tensor_copy(out=fTB[:], in_=pa[0:64, :]), [r_pa], [r_fTB])
                yield
                AB, r_AB = AB_.next()
                AK, r_AK = AK_.next()
                for (dst, r_dst, off) in ((AB, r_AB, 0), (AK, r_AK, 4)):
                    for pr in range(2):
                        px, r_px = nb()
                        for hh in range(2):
                            h = 2 * pr + hh
                            P.op("pe", lambda e, h=h, hh=hh, px=px, off=off: e.matmul(out=px[:, hh * 256:(hh + 1) * 256], lhsT=fTA[:, (off + h) * 128:(off + h + 1) * 128],
                                                                               rhs=fTB[:, 2 * h * 128:(2 * h + 2) * 128], start=True, stop=True), [r_fTA, r_fTB], [r_px])
                        P.op("dve", lambda e, pr=pr, px=px, dst=dst: e.tensor_tensor(out=dst[:, pr * 512:(pr + 1) * 512], in0=px[:, :], in1=mask4[:], op=ALU.mult), [r_px, r_w], [r_dst])
                        yield
                py, r_py = nb()
                for h in range(4):
                    P.op("pe", lambda e, h=h: e.matmul(out=py[:, h * 128:(h + 1) * 128], lhsT=fTB[:, 2 * h * 128:(2 * h + 1) * 128], rhs=fTA[:, h * 128:(h + 1) * 128],
                                                       start=True, stop=True), [r_fTA, r_fTB], [r_py])
                NTc, r_NTc = NTa_.next()
                P.op("dve", lambda e: e.tensor_tensor(out=NTc[:], in0=py[:, :], in1=maskT4[:], op=ALU.mult), [r_py, r_w], [r_NTc])
                yield
                pz, r_pz = nb()
                for h in range(4):
                    P.op("pe", lambda e, h=h: e.matmul(out=pz[:, h * 64:(h + 1) * 64], lhsT=AK[:, h * 256:h * 256 + 128], rhs=vb[:, h * 64:(h + 1) * 64], start=True, stop=True),
                         [r_AK, r_opb], [r_pz])
                Z, r_Z = Z_.next()
                P.op("dve", lambda e, Z=Z: e.tensor_copy(out=v3(Z[:], 128)[:, :, 0:64], in_=v3(ab)), [r_opb], [r_Z])
                P.op("act", lambda e, Z=Z: e.copy(out=v3(Z[:], 128)[:, :, 64:128], in_=v3(pz[:, 0:256])), [r_pz], [r_Z])
                yield
                dbg("Z0", Z[:], [128, 512], BF16, [r_Z]); dbg("NT0", NTc[:], [128, 512], BF16, [r_NTc]); dbg("AK", AK[:], [128, 1024], BF16, [r_AK])
                Ncur = [AB[:, h * 256:h * 256 + 128] for h in range(4)]
                r_N = r_AB
                for kq in range(7):
                    pq, r_pq = nb()
                    for h in range(4):
                        P.op("pe", lambda e, h=h, N=Ncur[h], Z=Z, pq=pq: e.matmul(out=pq[:, h * 128:(h + 1) * 128], lhsT=N, rhs=Z[:, h * 128:(h + 1) * 128], start=True, stop=True),
                             [r_N, r_Z], [r_pq])
                    Zn, r_Zn = Z_.next()
                    P.op("dve", lambda e, Z=Z, Zn=Zn, pq=pq: e.tensor_tensor(out=Zn[:], in0=pq[:, :], in1=Z[:], op=ALU.add), [r_pq, r_Z], [r_Zn])
                    yield
                    if kq < 6:
                        p1, r_p1 = nb()
                        for h in range(4):
                            P.op("pe", lambda e, h=h, N=Ncur[h], NTc=NTc, p1=p1: e.matmul(out=p1[:, h * 128:(h + 1) * 128], lhsT=NTc[:, h * 128:(h + 1) * 128], rhs=N, start=True, stop=True),
                                 [r_N, r_NTc], [r_p1])
                        p2, r_p2 = nb()
                        for h in range(4):
                            P.op("pe", lambda e, h=h, N=Ncur[h], NTc=NTc, p2=p2: e.matmul(out=p2[:, h * 128:(h + 1) * 128], lhsT=N, rhs=NTc[:, h * 128:(h + 1) * 128], start=True, stop=True),
                                 [r_N, r_NTc], [r_p2])
                        Nn, r_Nn = Nn_.next()
                        NTn, r_NTn = NTa_.next()
                        P.op("act", lambda e, Nn=Nn, p1=p1: e.copy(out=Nn[:], in_=p1[:, :]), [r_p1], [r_Nn])
                        P.op("dve", lambda e, NTn=NTn, p2=p2: e.tensor_copy(out=NTn[:], in_=p2[:, :]), [r_p2], [r_NTn])
                        yield
                        dbg(f"Nn{kq}", Nn[:], [128, 512], BF16, [r_Nn]); dbg(f"NTn{kq}", NTn[:], [128, 512], BF16, [r_NTn]); dbg(f"Zn{kq}", Zn[:], [128, 512], BF16, [r_Zn])
                        Ncur = [Nn[:, h * 128:(h + 1) * 128] for h in range(4)]
                        r_N, NTc, r_NTc = r_Nn, NTn, r_NTn
                    Z, r_Z = Zn, r_Zn
                WT = [Z[:, h * 128:h * 128 + 64] for h in range(4)]
                XT = [Z[:, h * 128 + 64:(h + 1) * 128] for h in range(4)]
                pQ, r_pQ = nb()
                for h in range(4):
                    P.op("pe", lambda e, h=h: e.matmul(out=pQ[0:64, h * 128:(h + 1) * 128], lhsT=WT[h], rhs=AB[:, h * 256 + 128:(h + 1) * 256], start=True, stop=True),
                         [r_Z, r_AB], [r_pQ])
                Q, r_Q = Q_.next()
                P.op("dve", lambda e: e.tensor_tensor(out=v3(Q[:], 128), in0=v3(pQ[0:64, :], 128), in1=fTB[:].rearrange("p (h two t) -> p h two t", two=2, t=128)[:, :, 1, :], op=ALU.add),
                     [r_pQ, r_fTB], [r_Q])
                yield
                pP, r_pP = nb()
                for h in range(4):
                    P.op("pe", lambda e, h=h: e.matmul(out=pP[0:64, h * 64:(h + 1) * 64], lhsT=WT[h], rhs=Bh[:, h * 64:(h + 1) * 64], start=True, stop=True), [r_Z, r_opb], [r_pP])
                Pm, r_Pm = Pm_.next()
                P.op("dve", lambda e: e.tensor_tensor(out=v3(Pm[:]), in0=identB[:], in1=gcol[:].unsqueeze(2).to_broadcast([64, 4, 64]), op=ALU.mult), [r_w, r_gcol], [r_Pm])
                P.op("dve", lambda e: e.tensor_tensor(out=Pm[:], in0=Pm[:], in1=pP[0:64, 0:256], op=ALU.add), [r_Pm, r_pP], [r_Pm])
                yield
                pD, r_pD = nb()
                for h in range(4):
                    P.op("pe", lambda e, h=h: e.matmul(out=pD[0:64, h * 64:(h + 1) * 64], lhsT=Bh[:, h * 64:(h + 1) * 64], rhs=XT[h], start=(h == 0), stop=False, skip_group_check=True),
                         [r_Z, r_opb], [r_pD])
                    P.op("pe", lambda e, h=h: e.matmul(out=pD[0:64, h * 64:(h + 1) * 64], lhsT=Kh[:, h * 64:(h + 1) * 64], rhs=vb[:, h * 64:(h + 1) * 64], start=False, stop=True, skip_group_check=True),
                         [r_opb], [r_pD])
                Dm, r_Dm = Dm_.next()
                P.op("act", lambda e: e.copy(out=Dm[:], in_=pD[0:64, 0:256]), [r_pD], [r_Dm])
                yield
                S0, r_S0 = state
                pO, r_pO = banks[2]
                for h in range(4):
                    P.op("pe", lambda e, h=h: e.matmul(out=pO[:, h * 64:(h + 1) * 64], lhsT=AB[:, h * 256 + 128:(h + 1) * 256], rhs=XT[h], start=(h == 0), stop=False, skip_group_check=True),
                         [r_AB, r_Z], [r_pO])
                    P.op("pe", lambda e, h=h: e.matmul(out=pO[:, h * 64:(h + 1) * 64], lhsT=AK[:, h * 256 + 128:(h + 1) * 256], rhs=vb[:, h * 64:(h + 1) * 64], start=False, stop=False, skip_group_check=True),
                         [r_AK, r_opb], [r_pO])
                    P.op("pe", lambda e, h=h, S0=S0: e.matmul(out=pO[:, h * 64:(h + 1) * 64], lhsT=Q[:, h * 128:(h + 1) * 128], rhs=S0[:, h * 64:(h + 1) * 64], start=False, stop=True, skip_group_check=True),
                         [r_Q, r_S0], [r_pO])
                osb, r_osb = osb_.next()
                P.op("act", lambda e: e.copy(out=osb[:], in_=pO[:, 0:256]), [r_pO], [r_osb])
                yield
                pS, r_pS = banks[2]
                for h in range(4):
                    P.op("pe", lambda e, h=h, S0=S0: e.matmul(out=pS[0:64, h * 64:(h + 1) * 64], lhsT=Pm[:, h * 64:(h + 1) * 64], rhs=S0[:, h * 64:(h + 1) * 64], start=True, stop=True),
                         [r_Pm, r_S0], [r_pS])
                S1, r_S1 = ST.next()
                P.op("dve", lambda e, S1=S1: e.tensor_tensor(out=S1[:], in0=pS[0:64, 0:256], in1=Dm[:], op=ALU.add), [r_pS, r_Dm], [r_S1])
                state[0], state[1] = S1, r_S1
                yield
                on, r_on = on_.next()
                P.op("dve", lambda e: e.tensor_reduce(out=s4[:, 16:20], in_=v3(osb[:]), axis=AX.X, op=ALU.add), [r_osb], [r_s4])
                P.op("pool", lambda e: e.tensor_tensor(out=on[:], in0=osb[:], in1=osb[:], op=ALU.mult), [r_osb], [r_on])
                P.op("dve", lambda e: e.tensor_reduce(out=s4[:, 20:24], in_=v3(on[:]), axis=AX.X, op=ALU.add), [r_on], [r_s4])
                P.op("dve", lambda e: e.tensor_scalar(out=s4[:, 16:20], in0=s4[:, 16:20], scalar1=1.0 / 64, scalar2=None, op0=ALU.mult), [r_s4], [r_s4])
                P.op("dve", lambda e: e.tensor_tensor(out=s4[:, 24:28], in0=s4[:, 16:20], in1=s4[:, 16:20], op=ALU.mult), [r_s4], [r_s4])
                P.op("dve", lambda e: e.scalar_tensor_tensor(out=s4[:, 20:24], in0=s4[:, 20:24], scalar=1.0 / 64, in1=s4[:, 24:28], op0=ALU.mult, op1=ALU.subtract), [r_s4], [r_s4])
                P.op("dve", lambda e: e.tensor_scalar(out=s4[:, 20:24], in0=s4[:, 20:24], scalar1=GN_EPS, scalar2=None, op0=ALU.add), [r_s4], [r_s4])
                P.op("act", lambda e: e.activation(out=s4[:, 24:28], in_=s4[:, 20:24], func=AF.Sqrt), [r_s4], [r_s4])
                P.op("dve", lambda e: e.reciprocal(out=s4[:, 28:32], in_=s4[:, 24:28]), [r_s4], [r_s4])
                P.op("dve", lambda e: e.tensor_tensor(out=v3(on[:]), in0=v3(osb[:]), in1=s4[:, 16:20].unsqueeze(2).to_broadcast([128, 4, 64]), op=ALU.subtract), [r_osb, r_s4], [r_on])
                P.op("dve", lambda e: e.tensor_tensor(out=v3(on[:]), in0=v3(on[:]), in1=s4[:, 28:32].unsqueeze(2).to_broadcast([128, 4, 64]), op=ALU.mult), [r_on, r_s4], [r_on])
                P.op("pool", lambda e: e.tensor_tensor(out=on[:], in0=on[:], in1=lng[:], op=ALU.mult), [r_on, r_w], [r_on])
                P.op("pool", lambda e: e.tensor_tensor(out=on[:], in0=on[:], in1=lnb[:], op=ALU.add), [r_on, r_w], [r_on])
                P.op("pool", lambda e: e.tensor_tensor(out=on[:], in0=on[:], in1=bon[:], op=ALU.add), [r_on, r_bon], [r_on])
                P.dma("pool", c.osc[d, rows, :], on[:], reads=[r_on], writes=[c.r["osc%d" % d][i]])

            return body

    with ExitStack() as st:
        bodies = [setup_dir(d, st) for d in range(2)]
        run_lanes([list(range(NT)), list(range(NT - 1, -1, -1))], lambda i, k: bodies[k](i), skew=6)
    barrier(c)
    with ExitStack() as st:
        def mkpools(k):
            mk = lambda name, shape, dt, n=2: Pool2(st, nc, f"{name}_l{k}_", shape, dt, n)
            return dict(o0=mk("e_o0", [128, 256], F32), o1=mk("e_o1", [128, 256], F32), zd=mk("e_zd", [128, 256], F32), yo=mk("e_yo", [128, 256], BF16))
        pools = [mkpools(k) for k in range(2)]

        def body_e(i, k):
            pl = pools[k]
            rows = slice(i * 128, (i + 1) * 128)
            o0, r_o0 = pl["o0"].next()
            o1, r_o1 = pl["o1"].next()
            zd, r_zd = pl["zd"].next()
            yo, r_yo = pl["yo"].next()
            P.dma("sp", o0[:], c.osc[0, rows, :], reads=[c.r["osc0"][i]], writes=[r_o0])
            P.dma("sp", o1[:], c.osc[1, rows, :], reads=[c.r["osc1"][i]], writes=[r_o1])
            P.dma("sp", zd[:], c.zs[rows, 768:1024], reads=[c.r["zs"][i]], writes=[r_zd])
            yield
            P.op("dve", lambda e: e.tensor_tensor(out=o0[:], in0=o0[:], in1=o1[:], op=ALU.add), [r_o0, r_o1], [r_o0])
            P.op("dve", lambda e: e.tensor_tensor(out=yo[:], in0=o0[:], in1=zd[:], op=ALU.mult), [r_o0, r_zd], [r_yo])
            P.dma("pool", c.ys[rows, 768:1024], yo[:], reads=[r_yo], writes=[c.r["ys"][i]])

        run_lanes([list(range(k, NT, 2)) for k in range(2)], body_e, skew=1)


def host_consts(S, seq_len):
    NT = S // 128
    t = np.arange(S)
    valid = (t < seq_len).astype(np.float32).reshape(NT, 128).T.copy()
    invc = np.ones((S, 4), np.float32)
    for g, w in enumerate((2, 4, 8, 16)):
        h = w // 2
        cnt = np.minimum(t + h, seq_len) - np.maximum(t - h, 0)
        invc[:, g] = np.where(t < seq_len, 1.0 / np.maximum(cnt, 1), 1.0)
    invc = invc.reshape(NT, 128, 4).transpose(1, 0, 2).reshape(128, NT * 4).copy()
    ident = np.eye(128, dtype=np.float32)
    s = np.arange(128)[:, None]
    tt = np.arange(128)[None, :]
    bandc = np.zeros((128, 4, 128), np.float32)
    bandh = np.zeros((16, 4, 128), np.float32)
    for g, w in enumerate((2, 4, 8, 16)):
        h = w // 2
        bandc[:, g, :] = ((s >= tt - h) & (s <= tt + h - 1))
        r = np.arange(16)[:, None]
        srel = np.where(r < 8, r - 8, 128 + (r - 8))
        bandh[:, g, :] = ((srel >= tt - h) & (srel <= tt + h - 1))
    slopes = 2.0 ** (-8.0 * np.arange(1, 13) / 12.0)
    amask = np.zeros((25, 128, 4, 128), np.float32)
    ci = 0
    key = np.arange(128)[:, None]
    q = np.arange(128)[None, :]
    for g in range(3):
        d = ATT_D[g]
        for coff in range(-ATT_W[g], ATT_W[g] + 1):
            delta = coff * 128 + key - q
            ok = (delta % d == 0) & (np.abs(delta) <= 64 * d)
            for h in range(4):
                amask[ci, :, h, :] = np.where(ok, np.exp(-slopes[g * 4 + h] * np.abs(delta)), 0.0)
            ci += 1
    tri = np.zeros((128, 642), np.float32)
    tri[:, 0:128] = (s <= tt)
    tri[:, 128:256] = (s >= tt)
    tri[:, 256:384] = (s < tt)
    tri[:, 384:512] = (s > tt)
    tri[:, 512:642] = 1.0
    return dict(valid=valid, invc=invc, ident=ident, amask=amask.reshape(25, 128, 512), bandc=bandc.reshape(128, 512),
                bandh=bandh.reshape(16, 512), tri=tri)


def host_weights(inp):
    w = {}
    for k in ("norm_g", "w_in", "q_norm_g", "k_norm_g", "pool_w", "pool_scale", "sg_norm_g", "mu_rkv", "mu_lat", "w0", "w_up",
              "a0", "a_up", "k_k", "k_a", "r_k", "ln_g", "ln_b", "w_branch", "w_out"):
        w[k] = np.ascontiguousarray(np.asarray(inp[k], dtype=np.float32))
    w["sg_wT"] = np.ascontiguousarray(np.transpose(np.asarray(inp["sg_w"], np.float32), (0, 1, 3, 2)))
    w["sg_bT"] = np.ascontiguousarray(np.transpose(np.asarray(inp["sg_b"], np.float32), (0, 2, 1)))
    return w


_NC_CACHE = {}


def run_sequences(seqs, S, inp, branches=(0, 1, 2, 3), L=2, n_cores=8):
    key = (S, L, tuple(branches))
    if key not in _NC_CACHE:
        _NC_CACHE[key] = build_program(S, L=L, branches=branches)
    nc = _NC_CACHE[key]
    w = host_weights(inp)
    in_maps = []
    for ci in range(n_cores):
        if ci < len(seqs):
            x = np.zeros((S, D), np.float32)
            x[:seqs[ci].shape[0]] = seqs[ci]
            m = dict(x=x, **host_consts(S, seqs[ci].shape[0]))
        else:
            m = dict(x=np.zeros((S, D), np.float32), **host_consts(S, 0))
        m.update(w)
        in_maps.append(m)
    res = run_bass_kernel_spmd(nc, in_maps, core_ids=list(range(n_cores)))
    if DEBUG_SCRATCH:
        global LAST_RESULTS
        LAST_RESULTS = res.results
    return [res.results[ci]["y"][:seqs[ci].shape[0]] for ci in range(len(seqs))]


def kernel(**inputs):
    xp = np.asarray(inputs["x_prompt"], np.float32)
    xs = np.asarray(inputs["x_sample"], np.float32)
    S = xs.shape[1]
    seqs = [xp[b] for b in range(xp.shape[0])] + [xs[b] for b in range(xs.shape[0])]
    outs = run_sequences(seqs, S, inputs)
    nb = xp.shape[0]
    y_prompt = np.stack(outs[:nb], 0).astype(np.float32)
    y_sample = np.stack(outs[nb:], 0).astype(np.float32)
    return (y_prompt, y_sample)
```

```python
import numpy as np
import concourse.bass as bass
import concourse.mybir as mybir
from concourse.bass_utils import run_bass_kernel_spmd

F32 = mybir.dt.float32
BF16 = mybir.dt.bfloat16
AF = mybir.ActivationFunctionType
ALU = mybir.AluOpType
AX = mybir.AxisListType

EPOCH = 24000
NDMA_SEM = 12


class Res:
    __slots__ = ("name", "last_w", "reads")

    def __init__(self, name=""):
        self.name = name
        self.last_w = None
        self.reads = []


class Prog:
    ENG = ("pe", "act", "dve", "pool", "sp")

    def __init__(self, nc, same_engine_sync=True):
        self.nc = nc
        self.same_engine_sync = same_engine_sync
        self.ops = {e: [] for e in self.ENG}
        self.cnt = {e: 0 for e in self.ENG}
        self.sems = {e: [nc.alloc_semaphore(name=f"s_{e}_0")] for e in ("pe", "act", "dve", "pool")}
        self.seen = {e: {} for e in self.ENG}
        self.dma_sems = {q: [nc.alloc_semaphore(name=f"d_{q}_{k}") for k in range(NDMA_SEM)] for q in ("sp", "pool")}
        self.dma_n = {"sp": 0, "pool": 0}
        self.n_wait = 0

    def _deps(self, reads, writes):
        deps = []
        for r in reads:
            if r.last_w is not None:
                deps.extend(r.last_w)
        for w in writes:
            if w.last_w is not None:
                deps.extend(w.last_w)
            deps.extend(w.reads)
        return deps

    def _waits(self, e, deps, own_sem_ids):
        waits = {}
        seen = self.seen[e]
        for (sem, val) in deps:
            sid = id(sem)
            if sid in own_sem_ids and (e == "pe" or not self.same_engine_sync):
                continue
            if seen.get(sid, 0) >= val:
                continue
            if sid not in waits or waits[sid][1] < val:
                waits[sid] = (sem, val)
        for sid, (sem, val) in waits.items():
            seen[sid] = val
        self.n_wait += len(waits)
        return list(waits.values())

    def _commit(self, ev, reads, writes, is_dma=False):
        for r in reads:
            r.reads.append(ev)
        for w in writes:
            if is_dma and w.last_w is not None and not w.reads:
                w.last_w = w.last_w + [ev]
            else:
                w.last_w = [ev]
            w.reads = []

    def op(self, e, fn, reads=(), writes=()):
        deps = self._deps(reads, writes)
        own = {id(s) for s in self.sems[e]}
        waits = self._waits(e, deps, own)
        if self.cnt[e] >= EPOCH:
            self.sems[e].append(self.nc.alloc_semaphore(name=f"s_{e}_{len(self.sems[e])}"))
            self.cnt[e] = 0
        self.cnt[e] += 1
        sem = self.sems[e][-1]
        ev = (sem, self.cnt[e])
        self.ops[e].append((waits, fn, sem, 1))
        self._commit(ev, reads, writes)
        return ev

    def dma(self, q, out, in_, reads=(), writes=(), **kw):
        deps = self._deps(reads, writes)
        n = self.dma_n[q]
        self.dma_n[q] += 1
        k = n % NDMA_SEM
        use = n // NDMA_SEM
        sem = self.dma_sems[q][k]
        if use > 0:
            deps.append((sem, 16 * use))
        own = {id(s) for s in self.sems[q]} if q in self.sems else set()
        waits = self._waits(q, deps, own)
        ev = (sem, 16 * (use + 1))
        self.ops[q].append((waits, (lambda eng, o=out, i=in_, kw=kw: eng.dma_start(out=o, in_=i, **kw)), sem, 16))
        self._commit(ev, reads, writes, is_dma=True)
        return ev

    def final_wait(self, e, evs):
        waits = self._waits(e, list(evs), set())
        self.ops[e].append((waits, None, None, 0))

    def emit(self):
        nc = self.nc
        ops = self.ops

        def run(eng, lst):
            for (waits, fn, sem, inc) in lst:
                for (s, v) in waits:
                    eng.wait_ge(s, v)
                if fn is not None:
                    ins = fn(eng)
                    ins.then_inc(sem, inc)

        with nc.Block() as block:
            @block.tensor
            def _(eng):
                run(eng, ops["pe"])

            @block.scalar
            def _(eng):
                run(eng, ops["act"])

            @block.vector
            def _(eng):
                run(eng, ops["dve"])

            @block.gpsimd
            def _(eng):
                run(eng, ops["pool"])

            @block.sync
            def _(eng):
                run(eng, ops["sp"])


D = 1024
PW = 9216
P1W = 5120
EPS = 1e-6
GN_EPS = 64e-5
ATT_W = (1, 2, 8)
ATT_D = (1, 4, 16)
C_Q, C_K, C_V, C_ZA, C_UB, C_ZB, C_UVC, C_ZC, C_RKV, C_LAT, C_ZD = 0, 768, 1536, 2304, 2560, 2816, 3072, 3584, 3840, 4608, 4864


class Ctx:
    pass


_UID = [0]


def uname(n):
    _UID[0] += 1
    return f"{n}_u{_UID[0]}"


DEBUG_SCRATCH = False
DEBUG_BARRIER = False
DBG_DIR = 0
DBG_TILE = 0


def build_program(S, L=2, branches=(0, 1, 2, 3), same_engine_sync=True):
    from contextlib import ExitStack
    NT = S // 128
    nc = bass.Bass("TRN2", target_bir_lowering=False)
    P = Prog(nc, same_engine_sync=same_engine_sync)
    c = Ctx()
    c.nc, c.P, c.S, c.NT, c.L, c.branches = nc, P, S, NT, L, branches

    def din(name, shape, dt=F32):
        return nc.dram_tensor(name, list(shape), dt, kind="ExternalInput").ap()

    def dscr(name, shape, dt=F32):
        return nc.dram_tensor(name, list(shape), dt, kind=("ExternalOutput" if DEBUG_SCRATCH else "Internal")).ap()

    c.x_in = din("x", [S, D])
    c.valid = din("valid", [128, NT])
    c.invc = din("invc", [128, NT * 4])
    c.ident = din("ident", [128, 128])
    c.amask = din("amask", [25, 128, 512])
    c.amask3 = din("amask3", [3, 128, 512])
    c.bandc = din("bandc", [128, 4 * 128])
    c.bandh = din("bandh", [16, 4 * 128])
    c.tri = din("tri", [128, 642])
    c.w = {}
    for name, shape in (("norm_g", [L, D]), ("w_in", [L, D, PW]), ("q_norm_g", [L, 64]), ("k_norm_g", [L, 64]),
                        ("pool_w", [L, 4, 64, 64]), ("pool_scale", [L, 256]), ("sg_norm_g", [L, 256]),
                        ("sg_wT", [L, 4, 128, 128]), ("sg_bT", [L, 128, 4]), ("mu_rkv", [L, 2, 768]), ("mu_lat", [L, 2, 128]),
                        ("w0", [L, 2, 256]), ("w_up", [L, 2, 64, 256]), ("a0", [L, 2, 256]), ("a_up", [L, 2, 64, 256]),
                        ("k_k", [L, 2, 256]), ("k_a", [L, 2, 256]), ("r_k", [L, 2, 256]), ("ln_g", [L, 256]), ("ln_b", [L, 256]),
                        ("w_branch", [L, 4, 256, D]), ("w_out", [L, D, D])):
        c.w[name] = din(name, shape)
    c.y_out = nc.dram_tensor("y", [S, D], F32, kind="ExternalOutput").ap()

    c.x1 = dscr("x1", [S, D])
    c.qT = dscr("qT_scr", [NT, 64, 1536], BF16)
    c.kT = dscr("kT_scr", [NT, 64, 1536], BF16)
    c.va = dscr("va_scr", [NT, 128, 780], BF16)
    c.zs = dscr("zs_scr", [S, 1024])
    c.ub = dscr("ub_scr", [S + 32, 256])
    c.uvc = dscr("uvc_scr", [S, 512])
    c.rkvl = dscr("rkvl_scr", [S + 2, 1024])
    c.ys = dscr("ys_scr", [S, 1024], BF16)
    c.nd3 = dscr("nd3_scr", [S, 260])
    c.osc = dscr("o_scr", [2, S, 256])
    c.r = {k: [Res(f"{k}{i}") for i in range(NT)] for k in ("x1", "qT", "kT", "va", "zs", "ub", "uvc", "rkvl", "ys", "osc0", "osc1", "y", "nd3")}
    c.r_pad = Res("pads")
    if DEBUG_SCRATCH:
        c.dbg_proj = dscr("dbg_proj", [S, P1W])
        c.dbg_hT = dscr("dbg_hT", [S, D], BF16)

    c.idb = nc.alloc_sbuf_tensor("idb", [128, 128], BF16)
    c.idf = nc.alloc_sbuf_tensor("idf", [128, 128], F32)
    c.zero = nc.alloc_sbuf_tensor("zero", [128, 1024], F32)
    c.validt = nc.alloc_sbuf_tensor("validt", [128, NT], F32)
    c.r_const = Res("const")
    P.dma("sp", c.idf[:], c.ident, writes=[c.r_const])
    P.dma("pool", c.idb[:], c.ident, writes=[c.r_const])
    P.dma("sp", c.validt[:], c.valid, writes=[c.r_const])
    P.op("pool", lambda e: e.memset(c.zero[:], 0.0), [], [c.r_const])
    P.dma("sp", c.ub[0:16, :], c.zero[0:16, 0:256], reads=[c.r_const], writes=[c.r_pad])
    P.dma("sp", c.ub[S + 16:S + 32, :], c.zero[0:16, 0:256], reads=[c.r_const], writes=[c.r_pad])
    P.dma("sp", c.rkvl[0:1, :], c.zero[0:1, :], reads=[c.r_const], writes=[c.r_pad])
    P.dma("sp", c.rkvl[S + 1:S + 2, :], c.zero[0:1, :], reads=[c.r_const], writes=[c.r_pad])

    c.pbf = [nc.alloc_psum_tensor(f"pbf{i}", [128, 1024], BF16) for i in range(2)]
    c.pf = [nc.alloc_psum_tensor(f"pf{i}", [128, 512], F32) for i in range(6)]
    c.r_pbf = [Res(f"pbf{i}") for i in range(2)]
    c.r_pf = [Res(f"pf{i}") for i in range(6)]

    for l in range(L):
        x_src = c.x_in if l == 0 else c.x1
        x_src_res = None if l == 0 else c.r["x1"]
        x_dst = c.x1 if l < L - 1 else c.y_out
        x_dst_res = c.r["x1"] if l < L - 1 else c.r["y"]
        phase1(c, l, x_src, x_src_res)
        barrier(c)
        if 2 in branches:
            phase2c(c, l)
            barrier(c)
        if 1 in branches:
            phase2b(c, l)
            barrier(c)
        if 0 in branches:
            phase2a(c, l)
            barrier(c)
        if 3 in branches:
            phase2d(c, l)
            barrier(c)
        phase3(c, l, x_src, x_src_res, x_dst, x_dst_res)
        barrier(c)
    P.emit()
    return nc


def barrier(c):
    P = c.P
    evs = []
    for e in ("pe", "act", "dve", "pool"):
        if P.cnt[e] > 0:
            evs.append((P.sems[e][-1], P.cnt[e]))
    for q in ("sp", "pool"):
        n = P.dma_n[q]
        for k, s in enumerate(P.dma_sems[q]):
            uses = (n - k + NDMA_SEM - 1) // NDMA_SEM if n > k else 0
            if uses > 0:
                evs.append((s, 16 * uses))
    for e in Prog.ENG:
        P.final_wait(e, evs)


class Pool2:
    def __init__(self, stack, nc, name, shape, dt, n=2):
        self.t = [stack.enter_context(nc.sbuf_tensor(uname(f"{name}{i}"), list(shape), dt)) for i in range(n)]
        self.r = [Res(f"{name}{i}") for i in range(n)]
        self.n = n
        self.i = -1

    def next(self):
        self.i = (self.i + 1) % self.n
        return self.t[self.i], self.r[self.i]


NL = 2


def run_lanes(lane_tiles, body, skew=2):
    K = len(lane_tiles)
    its = [iter(t) for t in lane_tiles]
    gens = [None] * K
    done = [False] * K
    rnd = 0
    while not all(done):
        for k in range(K):
            if done[k]:
                continue
            if gens[k] is None:
                if rnd < k * skew:
                    continue
                i = next(its[k], None)
                if i is None:
                    done[k] = True
                    continue
                gens[k] = body(i, k)
            try:
                next(gens[k])
            except StopIteration:
                gens[k] = None
        rnd += 1


def lane_banks(c, k):
    return [(c.pf[3 * k + j], c.r_pf[3 * k + j]) for j in range(3)], (c.pbf[k], c.r_pbf[k])


def rms_h(c, xt, r_xt, gbc, r_w, pl, width=1024):
    P = c.P
    junk, r_junk = pl["junk"].next()
    ss, r_ss = pl["ss"].next()
    h, r_h = pl["h"].next()
    P.op("act", lambda e: e.activation(out=junk[:], in_=xt[:], func=AF.Square, accum_out=ss[:, 0:1]), [r_xt], [r_junk, r_ss])
    P.op("dve", lambda e: e.tensor_scalar(out=ss[:, 1:2], in0=ss[:, 0:1], scalar1=1.0 / width, scalar2=EPS, op0=ALU.mult, op1=ALU.add), [r_ss], [r_ss])
    P.op("act", lambda e: e.activation(out=ss[:, 2:3], in_=ss[:, 1:2], func=AF.Sqrt), [r_ss], [r_ss])
    P.op("dve", lambda e: e.reciprocal(out=ss[:, 3:4], in_=ss[:, 2:3]), [r_ss], [r_ss])
    P.op("dve", lambda e: e.scalar_tensor_tensor(out=h[:], in0=xt[:], scalar=ss[:, 3:4], in1=gbc[:], op0=ALU.mult, op1=ALU.mult),
         [r_xt, r_ss, r_w], [r_h])
    return h, r_h


def transpose8(c, src, r_src, dst, r_dst, pbk, eng="act", n=8):
    P = c.P
    pb, r_pb = pbk
    for kc in range(n):
        P.op("pe", lambda e, kc=kc: e.transpose(out=pb[:, kc * 128:(kc + 1) * 128], in_=src[:, kc * 128:(kc + 1) * 128], identity=c.idb[:]),
             [r_src, c.r_const], [r_pb])
    if eng == "act":
        P.op("act", lambda e: e.copy(out=dst[:, 0:n * 128], in_=pb[:, 0:n * 128]), [r_pb], [r_dst])
    else:
        P.op("dve", lambda e: e.tensor_copy(out=dst[:, 0:n * 128], in_=pb[:, 0:n * 128]), [r_pb], [r_dst])


def phase1(c, l, x_src, x_src_res):
    from contextlib import ExitStack
    P, nc, NT = c.P, c.nc, c.NT
    with ExitStack() as st:
        sb = lambda n, s, d: st.enter_context(nc.sbuf_tensor(uname(n), list(s), d))
        W = sb("W1", [128, 8, P1W], BF16)
        r_W = Res("W1")
        for kc in range(8):
            P.dma("pool", W[:, kc, :], c.w["w_in"][l, kc * 128:(kc + 1) * 128, 0:P1W], writes=[r_W])
        gbc = sb("gbc", [128, D], F32)
        g64 = sb("g64", [128, 128], F32)
        gqk = sb("gqk", [128, 1536], F32)
        r_w = Res("p1w")
        P.dma("sp", gbc[:], c.w["norm_g"][l:l + 1, :].partition_broadcast(128), writes=[r_w])
        P.dma("sp", g64[:, 0:64], c.w["q_norm_g"][l:l + 1, :].partition_broadcast(128), writes=[r_w])
        P.dma("sp", g64[:, 64:128], c.w["k_norm_g"][l:l + 1, :].partition_broadcast(128), writes=[r_w])
        P.op("dve", lambda e: e.tensor_scalar(out=gqk[:, 0:768].rearrange("p (h d) -> p h d", d=64),
                                              in0=g64[:, 0:64].unsqueeze(1).to_broadcast([128, 12, 64]),
                                              scalar1=0.125, scalar2=None, op0=ALU.mult), [r_w], [r_w])
        P.op("dve", lambda e: e.tensor_copy(out=gqk[:, 768:1536].rearrange("p (h d) -> p h d", d=64),
                                            in_=g64[:, 64:128].unsqueeze(1).to_broadcast([128, 12, 64])), [r_w], [r_w])

        def mkpools(k):
            mk = lambda name, shape, dt, n=1: Pool2(st, nc, f"{name}_l{k}_", shape, dt, n)
            d = dict(junk=mk("junk", [128, D], BF16), ss=mk("ss", [128, 4], F32), h=mk("h", [128, D], BF16), hT=mk("hT", [128, D], BF16),
                     xt=mk("xt", [128, D], F32), proj=mk("proj", [128, P1W], F32), zsb=mk("zsb", [128, 1024], F32))
            if 0 in c.branches:
                d.update(sq=mk("sq", [128, 1536], F32), s24=mk("s24", [128, 72], F32), qkn=mk("qkn", [128, 1536], BF16),
                         qkT=mk("qkT", [64, 3072], BF16), vaug=mk("vaug", [128, 780], BF16))
            return d
        pools = [mkpools(k) for k in range(NL)]

        def body(i, k):
            pl = pools[k]
            banks, pbk = lane_banks(c, k)
            rows = slice(i * 128, (i + 1) * 128)
            xt, r_xt = pl["xt"].next()
            P.dma("sp", xt[:], x_src[rows, :], reads=([x_src_res[i]] if x_src_res else []), writes=[r_xt])
            h, r_h = rms_h(c, xt, r_xt, gbc, r_w, pl)
            yield
            hT, r_hT = pl["hT"].next()
            transpose8(c, h, r_h, hT, r_hT, pbk)
            yield
            proj, r_pj = pl["proj"].next()
            for blk in range(P1W // 512):
                ps, r_ps = banks[blk % 3]
                for kc in range(8):
                    P.op("pe", lambda e, kc=kc, blk=blk, ps=ps: e.matmul(out=ps[:, :], lhsT=hT[:, kc * 128:(kc + 1) * 128],
                                                                    rhs=W[:, kc, blk * 512:(blk + 1) * 512], start=(kc == 0), stop=(kc == 7)),
                         [r_hT, r_W], [r_ps])
                if blk % 2 == 0:
                    P.op("act", lambda e, blk=blk, ps=ps: e.copy(out=proj[:, blk * 512:(blk + 1) * 512], in_=ps[:, :]), [r_ps], [r_pj])
                else:
                    P.op("dve", lambda e, blk=blk, ps=ps: e.tensor_copy(out=proj[:, blk * 512:(blk + 1) * 512], in_=ps[:, :]), [r_ps], [r_pj])
                if blk % 2 == 1:
                    yield
            if 1 in c.branches:
                P.dma("pool", c.ub[16 + i * 128:16 + (i + 1) * 128, :], proj[:, C_UB:C_UB + 256], reads=[r_pj], writes=[c.r["ub"][i]])
            if 2 in c.branches:
                P.dma("pool", c.uvc[rows, :], proj[:, C_UVC:C_UVC + 512], reads=[r_pj], writes=[c.r["uvc"][i]])
            if 3 in c.branches:
                P.dma("pool", c.rkvl[1 + i * 128:1 + (i + 1) * 128, :], proj[:, C_RKV:C_RKV + 1024], reads=[r_pj], writes=[c.r["rkvl"][i]])
            zt, r_zt = pl["zsb"].next()
            for b, cz in enumerate((C_ZA, C_ZB, C_ZC, C_ZD)):
                P.op("act", lambda e, b=b, cz=cz: e.activation(out=zt[:, b * 256:(b + 1) * 256], in_=proj[:, cz:cz + 256], func=AF.Silu),
                     [r_pj], [r_zt])
            P.dma("pool", c.zs[rows, :], zt[:], reads=[r_zt], writes=[c.r["zs"][i]])
            yield
            if 0 in c.branches:
                sq, r_sq = pl["sq"].next()
                s, r_s = pl["s24"].next()
                qn, r_qn = pl["qkn"].next()
                v24 = lambda ap: ap.rearrange("p (h d) -> p h d", d=64)
                P.op("pool", lambda e: e.tensor_tensor(out=sq[:], in0=proj[:, 0:1536], in1=proj[:, 0:1536], op=ALU.mult), [r_pj], [r_sq])
                P.op("dve", lambda e: e.tensor_reduce(out=s[:, 0:24], in_=v24(sq[:]), axis=AX.X, op=ALU.add), [r_sq], [r_s])
                P.op("dve", lambda e: e.tensor_scalar(out=s[:, 24:48], in0=s[:, 0:24], scalar1=1.0 / 64, scalar2=EPS, op0=ALU.mult, op1=ALU.add), [r_s], [r_s])
                P.op("act", lambda e: e.activation(out=s[:, 0:24], in_=s[:, 24:48], func=AF.Sqrt), [r_s], [r_s])
                P.op("dve", lambda e: e.reciprocal(out=s[:, 48:72], in_=s[:, 0:24]), [r_s], [r_s])
                yield
                P.op("dve", lambda e: e.tensor_tensor(out=v24(sq[:]), in0=v24(proj[:, 0:1536]),
                                                      in1=s[:, 48:72].unsqueeze(2).to_broadcast([128, 24, 64]), op=ALU.mult), [r_pj, r_s], [r_sq])
                P.op("pool", lambda e: e.tensor_tensor(out=qn[:], in0=sq[:], in1=gqk[:], op=ALU.mult), [r_sq, r_w], [r_qn])
                va, r_va = pl["vaug"].next()
                P.op("act", lambda e: e.copy(out=va[:].rearrange("p (h d) -> p h d", d=65)[:, :, 0:64],
                                             in_=proj[:, C_V:C_V + 768].rearrange("p (h d) -> p h d", d=64)), [r_pj], [r_va])
                P.op("dve", lambda e: e.tensor_copy(out=va[:].rearrange("p (h d) -> p h d", d=65)[:, :, 64:65],
                                                    in_=c.validt[:, i:i + 1].unsqueeze(1).to_broadcast([128, 12, 1])), [c.r_const], [r_va])
                P.dma("pool", c.va[i], va[:], reads=[r_va], writes=[c.r["va"][i]])
                yield
                qT, r_qT = pl["qkT"].next()
                pb, r_pb = pbk
                for grp in range(3):
                    for j in range(8):
                        hh = grp * 8 + j
                        P.op("pe", lambda e, hh=hh, j=j: e.transpose(out=pb[0:64, j * 128:(j + 1) * 128], in_=qn[:, hh * 64:(hh + 1) * 64], identity=c.idb[:]),
                             [r_qn, c.r_const], [r_pb])
                    if grp % 2 == 0:
                        P.op("dve", lambda e, grp=grp: e.tensor_copy(out=qT[:, grp * 1024:(grp + 1) * 1024], in_=pb[0:64, :]), [r_pb], [r_qT])
                    else:
                        P.op("act", lambda e, grp=grp: e.copy(out=qT[:, grp * 1024:(grp + 1) * 1024], in_=pb[0:64, :]), [r_pb], [r_qT])
                    yield
                P.dma("pool", c.qT[i], qT[:, 0:1536], reads=[r_qT], writes=[c.r["qT"][i]])
                P.dma("pool", c.kT[i], qT[:, 1536:3072], reads=[r_qT], writes=[c.r["kT"][i]])

        run_lanes([list(range(k, NT, NL)) for k in range(NL)], body, skew=4)


def phase2c(c, l):
    from contextlib import ExitStack
    P, nc, NT = c.P, c.nc, c.NT
    with ExitStack() as st:
        sb = lambda n, s, d: st.enter_context(nc.sbuf_tensor(uname(n), list(s), d))
        sgw = sb("sgw", [128, 4, 128], BF16)
        sgn = sb("sgn", [128, 256], F32)
        sgb = sb("sgb", [128, 4], F32)
        r_w = Res("p2cw")
        P.dma("pool", sgw[:], c.w["sg_wT"][l].rearrange("g s t -> s g t"), writes=[r_w])
        P.dma("sp", sgn[:], c.w["sg_norm_g"][l:l + 1, :].partition_broadcast(128), writes=[r_w])
        P.dma("sp", sgb[:], c.w["sg_bT"][l], writes=[r_w])

        def mkpools(k):
            mk = lambda name, shape, dt, n=2: Pool2(st, nc, f"{name}_l{k}_", shape, dt, n)
            return dict(uv=mk("uv", [128, 512], F32), zc=mk("zc", [128, 256], F32), junk=mk("junkc", [128, 256], F32, 1), ss=mk("ssc", [128, 4], F32),
                        vn=mk("vn", [128, 256], BF16), sv=mk("sv", [128, 256], F32), ys=mk("ysc", [128, 256], BF16))
        pools = [mkpools(k) for k in range(6)]

        def body(i, k):
            pl = pools[k]
            banks = [(c.pf[k], c.r_pf[k])]
            rows = slice(i * 128, (i + 1) * 128)
            uv, r_uv = pl["uv"].next()
            zc, r_zc = pl["zc"].next()
            P.dma("sp", uv[:], c.uvc[rows, :], reads=[c.r["uvc"][i]], writes=[r_uv])
            P.dma("sp", zc[:], c.zs[rows, 512:768], reads=[c.r["zs"][i]], writes=[r_zc])
            junk, r_j = pl["junk"].next()
            ss, r_ss = pl["ss"].next()
            vn, r_vn = pl["vn"].next()
            sv, r_sv = pl["sv"].next()
            ysc, r_y = pl["ys"].next()
            P.op("act", lambda e: e.activation(out=junk[:], in_=uv[:, 256:512], func=AF.Square, accum_out=ss[:, 0:1]), [r_uv], [r_j, r_ss])
            P.op("dve", lambda e: e.tensor_scalar(out=ss[:, 1:2], in0=ss[:, 0:1], scalar1=1.0 / 256, scalar2=EPS, op0=ALU.mult, op1=ALU.add), [r_ss], [r_ss])
            yield
            P.op("act", lambda e: e.activation(out=ss[:, 2:3], in_=ss[:, 1:2], func=AF.Sqrt), [r_ss], [r_ss])
            P.op("dve", lambda e: e.reciprocal(out=ss[:, 3:4], in_=ss[:, 2:3]), [r_ss], [r_ss])
            P.op("dve", lambda e: e.scalar_tensor_tensor(out=vn[:], in0=uv[:, 256:512], scalar=ss[:, 3:4], in1=sgn[:], op0=ALU.mult, op1=ALU.mult),
                 [r_uv, r_ss, r_w], [r_vn])
            yield
            ps, r_ps = banks[0]
            for g in range(4):
                P.op("pe", lambda e, g=g: e.matmul(out=ps[:, g * 64:(g + 1) * 64], lhsT=sgw[:, g, :], rhs=vn[:, g * 64:(g + 1) * 64], start=True, stop=True),
                     [r_vn, r_w], [r_ps])
            P.op("dve", lambda e: e.tensor_tensor(out=sv[:].rearrange("p (g d) -> p g d", d=64), in0=ps[:, 0:256].rearrange("p (g d) -> p g d", d=64),
                                                  in1=sgb[:].unsqueeze(2).to_broadcast([128, 4, 64]), op=ALU.add), [r_ps, r_w], [r_sv])
            yield
            P.op("pool", lambda e: e.tensor_tensor(out=sv[:], in0=sv[:], in1=uv[:, 0:256], op=ALU.mult), [r_sv, r_uv], [r_sv])
            P.op("dve", lambda e: e.tensor_tensor(out=ysc[:], in0=sv[:], in1=zc[:], op=ALU.mult), [r_sv, r_zc], [r_y])
            P.dma("pool", c.ys[rows, 512:768], ysc[:], reads=[r_y], writes=[c.r["ys"][i]])

        run_lanes([list(range(k, NT, 6)) for k in range(6)], body, skew=1)


def phase3(c, l, x_src, x_src_res, x_dst, x_dst_res):
    from contextlib import ExitStack
    P, nc, NT = c.P, c.nc, c.NT
    with ExitStack() as st:
        sb = lambda n, s, d: st.enter_context(nc.sbuf_tensor(uname(n), list(s), d))
        Wg = sb("Wg", [128, 8, 4096], BF16)
        Wbr = sb("Wbr", [128, 8, 1024], BF16)
        Wo = sb("Wo", [128, 8, 1024], BF16)
        gbc = sb("gbc3", [128, D], F32)
        r_W = Res("W3")
        for kc in range(8):
            P.dma("pool", Wg[:, kc, :], c.w["w_in"][l, kc * 128:(kc + 1) * 128, P1W:PW], writes=[r_W])
        wbr_flat = c.w["w_branch"][l].rearrange("b c n -> (b c) n")
        for kc in range(8):
            P.dma("pool", Wbr[:, kc, :], wbr_flat[kc * 128:(kc + 1) * 128, :], writes=[r_W])
            P.dma("pool", Wo[:, kc, :], c.w["w_out"][l, kc * 128:(kc + 1) * 128, :], writes=[r_W])
        P.dma("sp", gbc[:], c.w["norm_g"][l:l + 1, :].partition_broadcast(128), writes=[r_W])

        def mkpools(k):
            mk = lambda name, shape, dt, n=1: Pool2(st, nc, f"{name}_l{k}_", shape, dt, n)
            return dict(junk=mk("junk3", [128, D], BF16), ss=mk("ss3", [128, 4], F32), h=mk("h3", [128, D], BF16), hT=mk("hT3", [128, D], BF16),
                        xt=mk("xt3", [128, D], F32), ys=mk("ys3", [128, D], BF16), ysT=mk("ysT3", [128, D], BF16), gs=mk("gs3", [128, 512], F32, 2),
                        tmp=mk("tmp3", [128, 512], F32, 2), acc=mk("acc3", [128, D], F32), m=mk("m3", [128, D], BF16), mT=mk("mT3", [128, D], BF16),
                        xo=mk("xo3", [128, D], F32))
        pools = [mkpools(k) for k in range(NL)]

        def body(i, k):
            pl = pools[k]
            banks, pbk = lane_banks(c, k)
            rows = slice(i * 128, (i + 1) * 128)
            xt, r_xt = pl["xt"].next()
            P.dma("sp", xt[:], x_src[rows, :], reads=([x_src_res[i]] if x_src_res else []), writes=[r_xt])
            ys, r_ys = pl["ys"].next()
            P.dma("sp", ys[:], c.ys[rows, :], reads=[c.r["ys"][i]], writes=[r_ys])
            h, r_h = rms_h(c, xt, r_xt, gbc, r_W, pl)
            yield
            hT, r_hT = pl["hT"].next()
            transpose8(c, h, r_h, hT, r_hT, pbk)
            yield
            ysT, r_ysT = pl["ysT"].next()
            transpose8(c, ys, r_ys, ysT, r_ysT, pbk, eng="dve")
            yield
            acc, r_acc = pl["acc"].next()
            for bi, b in enumerate(c.branches):
                for cb in range(2):
                    pg, r_pg = banks[0]
                    pbr, r_pbr = banks[1]
                    col = b * 1024 + cb * 512
                    for kc in range(8):
                        P.op("pe", lambda e, kc=kc, col=col: e.matmul(out=pg[:, :], lhsT=hT[:, kc * 128:(kc + 1) * 128], rhs=Wg[:, kc, col:col + 512],
                                                                      start=(kc == 0), stop=(kc == 7)), [r_hT, r_W], [r_pg])
                    gs, r_gs = pl["gs"].next()
                    P.op("act", lambda e, gs=gs: e.activation(out=gs[:], in_=pg[:, :], func=AF.Sigmoid), [r_pg], [r_gs])
                    for kk in range(2):
                        kc = 2 * b + kk
                        P.op("pe", lambda e, kc=kc, kk=kk, cb=cb: e.matmul(out=pbr[:, :], lhsT=ysT[:, kc * 128:(kc + 1) * 128],
                                                                             rhs=Wbr[:, kc, cb * 512:(cb + 1) * 512], start=(kk == 0), stop=(kk == 1)),
                             [r_ysT, r_W], [r_pbr])
                    if bi == 0:
                        P.op("dve", lambda e, cb=cb, gs=gs: e.tensor_tensor(out=acc[:, cb * 512:(cb + 1) * 512], in0=gs[:], in1=pbr[:, :], op=ALU.mult),
                             [r_gs, r_pbr], [r_acc])
                    else:
                        tmp, r_tmp = pl["tmp"].next()
                        P.op("dve", lambda e, gs=gs, tmp=tmp: e.tensor_tensor(out=tmp[:], in0=gs[:], in1=pbr[:, :], op=ALU.mult), [r_gs, r_pbr], [r_tmp])
                        P.op("pool", lambda e, cb=cb, tmp=tmp: e.tensor_tensor(out=acc[:, cb * 512:(cb + 1) * 512], in0=acc[:, cb * 512:(cb + 1) * 512], in1=tmp[:], op=ALU.add),
                             [r_tmp, r_acc], [r_acc])
                    yield
            m, r_m = pl["m"].next()
            mT, r_mT = pl["mT"].next()
            P.op("act", lambda e: e.copy(out=m[:], in_=acc[:]), [r_acc], [r_m])
            yield
            transpose8(c, m, r_m, mT, r_mT, pbk)
            yield
            xo, r_xo = pl["xo"].next()
            po, r_po = banks[2]
            for cb in range(2):
                for kc in range(8):
                    P.op("pe", lambda e, kc=kc, cb=cb: e.matmul(out=po[:, :], lhsT=mT[:, kc * 128:(kc + 1) * 128], rhs=Wo[:, kc, cb * 512:(cb + 1) * 512],
                                                                start=(kc == 0), stop=(kc == 7)), [r_mT, r_W], [r_po])
                P.op("dve", lambda e, cb=cb: e.tensor_tensor(out=xo[:, cb * 512:(cb + 1) * 512], in0=po[:, :], in1=xt[:, cb * 512:(cb + 1) * 512], op=ALU.add),
                     [r_po, r_xt], [r_xo])
                yield
            P.dma("pool", x_dst[rows, :], xo[:], reads=[r_xo], writes=[x_dst_res[i]])

        run_lanes([list(range(k, NT, NL)) for k in range(NL)], body, skew=5)


def phase2b(c, l):
    from contextlib import ExitStack
    P, nc, NT = c.P, c.nc, c.NT
    with ExitStack() as st:
        sb = lambda n, s, d: st.enter_context(nc.sbuf_tensor(uname(n), list(s), d))
        bandc = sb("bandc", [128, 512], F32)
        bandh = sb("bandh", [16, 512], F32)
        pw = sb("pw", [128, 2, 128], BF16)
        psc = sb("psc", [128, 256], F32)
        invc = sb("invc", [128, NT * 4], F32)
        r_w = Res("p2bw")
        P.dma("sp", bandc[:], c.bandc, writes=[r_w])
        P.dma("sp", bandh[:], c.bandh, writes=[r_w])
        P.dma("sp", psc[:], c.w["pool_scale"][l:l + 1, :].partition_broadcast(128), writes=[r_w])
        P.dma("sp", invc[:], c.invc, writes=[r_w])
        P.op("pool", lambda e: e.memset(pw[:], 0.0), [], [r_w])
        for g in range(4):
            j, gl = g // 2, g % 2
            P.dma("pool", pw[gl * 64:(gl + 1) * 64, j, gl * 64:(gl + 1) * 64], c.w["pool_w"][l, g], writes=[r_w])

        def mkpools(k):
            mk = lambda name, shape, dt, n=2: Pool2(st, nc, f"{name}_l{k}_", shape, dt, n)
            return dict(u=mk("ub", [128, 256], F32), uh=mk("uh", [16, 256], F32), zb=mk("zb", [128, 256], F32), d=mk("dpool", [128, 256], BF16),
                        dT=mk("dT", [128, 256], BF16), y=mk("ybf", [128, 256], F32), yo=mk("ybo", [128, 256], BF16))
        pools = [mkpools(k) for k in range(3)]

        def body(i, k):
            pl = pools[k]
            banks = [(c.pf[2 * k], c.r_pf[2 * k]), (c.pf[2 * k + 1], c.r_pf[2 * k + 1])]
            pbk = (c.pbf[k % 2], c.r_pbf[k % 2])
            rows = slice(i * 128, (i + 1) * 128)
            u, r_u = pl["u"].next()
            uh, r_uh = pl["uh"].next()
            zb, r_zb = pl["zb"].next()
            d, r_d = pl["d"].next()
            dT, r_dT = pl["dT"].next()
            y, r_y = pl["y"].next()
            yo, r_yo = pl["yo"].next()
            nb = [c.r["ub"][j] for j in (i - 1, i, i + 1) if 0 <= j < NT] + [c.r_pad]
            P.dma("sp", u[:], c.ub[16 + i * 128:16 + (i + 1) * 128, :], reads=[c.r["ub"][i]], writes=[r_u])
            P.dma("sp", uh[0:8, :], c.ub[16 + i * 128 - 8:16 + i * 128, :], reads=nb, writes=[r_uh])
            P.dma("sp", uh[8:16, :], c.ub[16 + (i + 1) * 128:16 + (i + 1) * 128 + 8, :], reads=nb, writes=[r_uh])
            P.dma("sp", zb[:], c.zs[rows, 256:512], reads=[c.r["zs"][i]], writes=[r_zb])
            yield
            ps, r_ps = banks[0]
            for g in range(4):
                P.op("pe", lambda e, g=g: e.matmul(out=ps[:, g * 64:(g + 1) * 64], lhsT=bandc[:, g * 128:(g + 1) * 128], rhs=u[:, g * 64:(g + 1) * 64],
                                                   start=True, stop=False), [r_u, r_w], [r_ps])
                P.op("pe", lambda e, g=g: e.matmul(out=ps[:, g * 64:(g + 1) * 64], lhsT=bandh[:, g * 128:(g + 1) * 128], rhs=uh[:, g * 64:(g + 1) * 64],
                                                   start=False, stop=True), [r_uh, r_w], [r_ps])
            for g in range(4):
                P.op("dve", lambda e, g=g: e.scalar_tensor_tensor(out=d[:, g * 64:(g + 1) * 64], in0=ps[:, g * 64:(g + 1) * 64],
                                                                  scalar=invc[:, i * 4 + g:i * 4 + g + 1], in1=u[:, g * 64:(g + 1) * 64],
                                                                  op0=ALU.mult, op1=ALU.subtract), [r_ps, r_u, r_w], [r_d])
            yield
            transpose8(c, d, r_d, dT, r_dT, pbk, n=2)
            yield
            ps2, r_ps2 = banks[1]
            for j in range(2):
                P.op("pe", lambda e, j=j: e.matmul(out=ps2[:, j * 128:(j + 1) * 128], lhsT=dT[:, j * 128:(j + 1) * 128], rhs=pw[:, j, :], start=True, stop=True),
                     [r_dT, r_w], [r_ps2])
            P.op("dve", lambda e: e.tensor_tensor(out=y[:], in0=ps2[:, 0:256], in1=psc[:], op=ALU.mult), [r_ps2, r_w], [r_y])
            yield
            P.op("pool", lambda e: e.tensor_tensor(out=yo[:], in0=y[:], in1=zb[:], op=ALU.mult), [r_y, r_zb], [r_yo])
            P.dma("pool", c.ys[rows, 256:512], yo[:], reads=[r_yo], writes=[c.r["ys"][i]])

        run_lanes([list(range(k, NT, 3)) for k in range(3)], body, skew=2)


def phase2a(c, l):
    from contextlib import ExitStack
    P, nc, NT = c.P, c.nc, c.NT
    use_sb = (NT % 16 == 0)
    NG = 2 if use_sb else 3
    if use_sb:
        phase2a_g3(c, l)
        barrier(c)
    with ExitStack() as st:
        sb = lambda n, s, d: st.enter_context(nc.sbuf_tensor(uname(n), list(s), d))
        masks = sb("amask", [128, 25, 512], BF16)
        r_w = Res("p2aw")
        for ci in range(25):
            P.dma("pool", masks[:, ci, :], c.amask[ci], writes=[r_w])
        depth = [2 * w + 2 for w in ATT_W]
        kring = [Pool2(st, nc, f"kr{g}_", [64, 512], BF16, depth[g]) for g in range(NG)]
        vring = [Pool2(st, nc, f"vr{g}_", [128, 260], BF16, depth[g]) for g in range(NG)]
        nd_ = Pool2(st, nc, "nd3t", [128, 260], F32, 2)
        loaded = [-1, -1, -1]
        q_ = Pool2(st, nc, "qTa", [64, 1536], BF16, 2)
        za_ = Pool2(st, nc, "za", [128, 256], F32, 2)
        e_ = Pool2(st, nc, "eexp", [128, 512], BF16, 3)
        p_ = Pool2(st, nc, "pexp", [128, 512], BF16, 6)
        dn_ = Pool2(st, nc, "den", [128, 8], F32, 2)
        ya_ = Pool2(st, nc, "yaf", [128, 256], F32, 2)
        yo_ = Pool2(st, nc, "yao", [128, 256], BF16, 2)
        mcol = []
        ci = 0
        for g in range(3):
            mcol.append({coff: ci + k for k, coff in enumerate(range(-ATT_W[g], ATT_W[g] + 1))})
            ci += 2 * ATT_W[g] + 1

        def ensure(g, upto):
            while loaded[g] < min(upto, NT - 1):
                j = loaded[g] + 1
                kt, r_kt = kring[g].t[j % depth[g]], kring[g].r[j % depth[g]]
                vt, r_vt = vring[g].t[j % depth[g]], vring[g].r[j % depth[g]]
                P.dma("sp", kt[:], c.kT[j][:, g * 512:(g + 1) * 512], reads=[c.r["kT"][j]], writes=[r_kt])
                P.dma("sp", vt[:], c.va[j][:, g * 260:(g + 1) * 260], reads=[c.r["va"][j]], writes=[r_vt])
                loaded[g] = j

        LOOK = 3
        tiles = {}

        def tile_begin(i):
            rows = slice(i * 128, (i + 1) * 128)
            for g in range(NG):
                ensure(g, i + ATT_W[g])
            qT, r_q = q_.next()
            za, r_za = za_.next()
            P.dma("sp", qT[:], c.qT[i], reads=[c.r["qT"][i]], writes=[r_q])
            P.dma("sp", za[:], c.zs[rows, 0:256], reads=[c.r["zs"][i]], writes=[r_za])
            ndt, r_ndt = (None, None)
            if use_sb:
                ndt, r_ndt = nd_.next()
                P.dma("sp", ndt[:], c.nd3[rows, :], reads=[c.r["nd3"][i // 16]], writes=[r_ndt])
            chunks = [(g, coff) for g in range(NG) for coff in range(-ATT_W[g], ATT_W[g] + 1) if 0 <= i + coff < NT]
            tiles[i] = dict(qT=qT, r_q=r_q, za=za, r_za=r_za, n=len(chunks), pso=c.pf[4 + i % 2], r_pso=c.r_pf[4 + i % 2], ndt=ndt, r_ndt=r_ndt)
            return chunks

        def stage_qk(i, idx, g, coff, seq):
            t = tiles[i]
            qT, r_q = t["qT"], t["r_q"]
            j = i + coff
            kt, r_kt = kring[g].t[j % depth[g]], kring[g].r[j % depth[g]]
            pss, r_pss = c.pf[seq % 4], c.r_pf[seq % 4]
            for h in range(4):
                P.op("pe", lambda e, h=h: e.matmul(out=pss[:, h * 128:(h + 1) * 128], lhsT=kt[:, h * 128:(h + 1) * 128],
                                                   rhs=qT[:, (g * 4 + h) * 128:(g * 4 + h + 1) * 128], start=True, stop=True),
                     [r_kt, r_q], [r_pss])
            ex, r_ex = e_.next()
            pp, r_pp = p_.next()
            P.op("act", lambda e: e.activation(out=ex[:], in_=pss[:, :], func=AF.Exp), [r_pss], [r_ex])
            mc = mcol[g][coff]
            P.op("dve", lambda e: e.tensor_tensor(out=pp[:], in0=ex[:], in1=masks[:, mc, :], op=ALU.mult), [r_ex, r_w], [r_pp])
            return pp, r_pp

        def stage_pv(i, idx, g, coff, pp, r_pp):
            t = tiles[i]
            pso, r_pso = t["pso"], t["r_pso"]
            j = i + coff
            vt, r_vt = vring[g].t[j % depth[g]], vring[g].r[j % depth[g]]
            n = t["n"]
            for h in range(4):
                P.op("pe", lambda e, h=h: e.matmul(out=pso[:, h * 65:(h + 1) * 65], lhsT=pp[:, h * 128:(h + 1) * 128],
                                                   rhs=vt[:, h * 65:(h + 1) * 65], start=(idx == 0 and h == 0), stop=(idx == n - 1), skip_group_check=True),
                     [r_pp, r_vt], [r_pso])
            if idx == n - 1:
                tile_end(i)

        def tile_end(i):
            rows = slice(i * 128, (i + 1) * 128)
            t = tiles.pop(i)
            pso, r_pso, za, r_za = t["pso"], t["r_pso"], t["za"], t["r_za"]
            dn, r_dn = dn_.next()
            ya, r_ya = ya_.next()
            yo, r_yo = yo_.next()
            if use_sb:
                ndt, r_ndt = t["ndt"], t["r_ndt"]
                P.op("dve", lambda e: e.tensor_tensor(out=ndt[:], in0=pso[:, 0:260], in1=ndt[:], op=ALU.add), [r_pso, r_ndt], [r_ndt])
                pv = ndt[:].rearrange("p (h d) -> p h d", d=65)
                r_pso = r_ndt
            else:
                pv = pso[:, 0:260].rearrange("p (h d) -> p h d", d=65)
            P.op("dve", lambda e: e.tensor_scalar(out=dn[:, 0:4].unsqueeze(2), in0=pv[:, :, 64:65], scalar1=1e-30, scalar2=None, op0=ALU.max), [r_pso], [r_dn])
            P.op("dve", lambda e: e.reciprocal(out=dn[:, 4:8], in_=dn[:, 0:4]), [r_dn], [r_dn])
            P.op("dve", lambda e: e.tensor_tensor(out=ya[:].rearrange("p (h d) -> p h d", d=64), in0=pv[:, :, 0:64],
                                                  in1=dn[:, 4:8].unsqueeze(2).to_broadcast([128, 4, 64]), op=ALU.mult), [r_pso, r_dn], [r_ya])
            P.op("pool", lambda e: e.tensor_tensor(out=yo[:], in0=ya[:], in1=za[:], op=ALU.mult), [r_ya, r_za], [r_yo])
            P.dma("pool", c.ys[rows, 0:256], yo[:], reads=[r_yo], writes=[c.r["ys"][i]])

        def stream():
            for i in range(NT):
                first = True
                chunks = None
                for idx in range(10 ** 9):
                    if first:
                        chunks = tile_begin(i)
                        first = False
                    if idx >= len(chunks):
                        break
                    g, coff = chunks[idx]
                    yield (i, idx, g, coff)

        pend = []
        for seq, (i, idx, g, coff) in enumerate(stream()):
            pp, r_pp = stage_qk(i, idx, g, coff, seq)
            pend.append((i, idx, g, coff, pp, r_pp))
            if len(pend) > LOOK:
                stage_pv(*pend.pop(0))
        while pend:
            stage_pv(*pend.pop(0))


def phase2a_g3(c, l):
    from contextlib import ExitStack
    P, nc, NT, S = c.P, c.nc, c.NT, c.S
    NSB = NT // 16
    vflat = c.va.rearrange("t p c -> (t p) c").rearrange("(m r) c -> r m c", r=16)
    ndv = c.nd3.rearrange("(m r) c -> r m c", r=16)
    with ExitStack() as st:
        sb_ = lambda n, s_, d: st.enter_context(nc.sbuf_tensor(uname(n), list(s_), d))
        masks = sb_("amask3", [128, 3, 512], BF16)
        r_w = Res("p2a3w")
        for ci in range(3):
            P.dma("pool", masks[:, ci, :], c.amask3[ci], writes=[r_w])
        stage_ = Pool2(st, nc, "g3stage", [64, 16 * 512], BF16, 2)
        kp_ = Pool2(st, nc, "g3kp", [64, 4 * 16 * 128], BF16, 3)
        qp_ = Pool2(st, nc, "g3qp", [64, 4 * 16 * 128], BF16, 2)
        vr_ = Pool2(st, nc, "g3v", [128, 260], BF16, 48)
        e_ = Pool2(st, nc, "g3e", [128, 512], BF16, 3)
        p_ = Pool2(st, nc, "g3p", [128, 512], BF16, 6)
        o_ = Pool2(st, nc, "g3o", [128, 260], F32, 3)
        kload = {}

        def permute(src_scr, sbi, dst, r_dst, col0):
            stg, r_stg = stage_.next()
            P.dma("sp", stg[:].rearrange("p (t c) -> p t c", c=512), src_scr[sbi * 16:(sbi + 1) * 16, :, col0:col0 + 512].rearrange("t p c -> p t c"),
                  reads=[c.r["qT" if src_scr is c.qT else "kT"][j] for j in range(sbi * 16, (sbi + 1) * 16)], writes=[r_stg])
            sv = stg[:].rearrange("p (t h pp r) -> p t h pp r", t=16, h=4, pp=8, r=16)
            dv = dst[:].rearrange("p (h r t pp) -> p h r t pp", h=4, r=16, t=16, pp=8)
            for h in range(4):
                eng = "act" if h % 2 == 0 else "pool"
                if eng == "act":
                    P.op("act", lambda e, h=h: e.copy(out=dv[:, h], in_=sv[:, :, h].rearrange("p t pp r -> p r t pp")), [r_stg], [r_dst])
                else:
                    P.op("pool", lambda e, h=h: e.tensor_copy(out=dv[:, h], in_=sv[:, :, h].rearrange("p t pp r -> p r t pp")), [r_stg], [r_dst])

        def ensure_k(sbi):
            if sbi in kload or not (0 <= sbi < NSB):
                return
            kp, r_kp = kp_.t[sbi % 3], kp_.r[sbi % 3]
            permute(c.kT, sbi, kp, r_kp, 1024)
            for r in range(16):
                vt, r_vt = vr_.t[(sbi % 3) * 16 + r], vr_.r[(sbi % 3) * 16 + r]
                P.dma("sp", vt[:], vflat[r, sbi * 128:(sbi + 1) * 128, 520:780], reads=[c.r["va"][j] for j in range(sbi * 16, (sbi + 1) * 16)], writes=[r_vt])
            kload[sbi] = True

        LOOK = 3
        units = []
        for sbi in range(NSB):
            for r in range(16):
                cl = [cf for cf in (-1, 0, 1) if 0 <= sbi + cf < NSB]
                for idx, cf in enumerate(cl):
                    units.append((sbi, r, cf, idx, len(cl)))
        qcur = {}
        pend = []

        def stage_pv(sbi, r, cf, idx, n, pp, r_pp, seq):
            pso, r_pso = c.pf[4 + (sbi * 16 + r) % 2], c.r_pf[4 + (sbi * 16 + r) % 2]
            sk = sbi + cf
            vt, r_vt = vr_.t[(sk % 3) * 16 + r], vr_.r[(sk % 3) * 16 + r]
            for h in range(4):
                P.op("pe", lambda e, h=h: e.matmul(out=pso[:, h * 65:(h + 1) * 65], lhsT=pp[:, h * 128:(h + 1) * 128], rhs=vt[:, h * 65:(h + 1) * 65],
                                                   start=(idx == 0 and h == 0), stop=(idx == n - 1), skip_group_check=True), [r_pp, r_vt], [r_pso])
            if idx == n - 1:
                ot, r_ot = o_.next()
                P.op("act", lambda e: e.copy(out=ot[:], in_=pso[:, 0:260]), [r_pso], [r_ot])
                P.dma("pool", ndv[r, sbi * 128:(sbi + 1) * 128, :], ot[:], reads=[r_ot], writes=[c.r["nd3"][sbi]])

        for seq, (sbi, r, cf, idx, n) in enumerate(units):
            if sbi not in qcur:
                while pend:
                    stage_pv(*pend.pop(0))
                for s2 in (sbi - 1, sbi, sbi + 1):
                    ensure_k(s2)
                qp, r_qp = qp_.next()
                permute(c.qT, sbi, qp, r_qp, 1024)
                qcur.clear()
                qcur[sbi] = (qp, r_qp)
            qp, r_qp = qcur[sbi]
            sk = sbi + cf
            kp, r_kp = kp_.t[sk % 3], kp_.r[sk % 3]
            pss, r_pss = c.pf[seq % 4], c.r_pf[seq % 4]
            for h in range(4):
                o0 = (h * 16 + r) * 128
                P.op("pe", lambda e, h=h, o0=o0, kp=kp, qp=qp, pss=pss: e.matmul(out=pss[:, h * 128:(h + 1) * 128], lhsT=kp[:, o0:o0 + 128], rhs=qp[:, o0:o0 + 128], start=True, stop=True),
                     [r_kp, r_qp], [r_pss])
            ex, r_ex = e_.next()
            pp, r_pp = p_.next()
            P.op("act", lambda e, ex=ex, pss=pss: e.activation(out=ex[:], in_=pss[:, :], func=AF.Exp), [r_pss], [r_ex])
            P.op("dve", lambda e, ex=ex, pp=pp, cf=cf: e.tensor_tensor(out=pp[:], in0=ex[:], in1=masks[:, cf + 1, :], op=ALU.mult), [r_ex, r_w], [r_pp])
            pend.append((sbi, r, cf, idx, n, pp, r_pp, seq))
            if len(pend) > LOOK:
                stage_pv(*pend.pop(0))
        while pend:
            stage_pv(*pend.pop(0))


def phase2d(c, l):
    from contextlib import ExitStack
    P, nc, NT = c.P, c.nc, c.NT
    NEG = -float(np.exp(-0.5))
    def setup_dir(d, st):
        if True:
            sb = lambda n, s, dt: st.enter_context(nc.sbuf_tensor(uname(n), list(s), dt))
            r_w = Res("p2dw")
            tri = sb("tri", [128, 642], F32)
            P.dma("sp", tri[:], c.tri, writes=[r_w])
            incl = tri[:, 128 * d:128 * d + 128]
            ones_m = tri[:, 512:640]
            ones_c = tri[:, 640:641]
            mask4 = sb("mask4", [128, 512], F32)
            maskT4 = sb("maskT4", [128, 512], F32)
            for q in range(4):
                src = tri[:, 256 + 128 * d:384 + 128 * d] if q % 2 == 0 else incl
                P.op("dve", lambda e, q=q, src=src: e.tensor_copy(out=mask4[:, q * 128:(q + 1) * 128], in_=src), [r_w], [r_w])
                P.op("dve", lambda e, q=q: e.tensor_copy(out=maskT4[:, q * 128:(q + 1) * 128], in_=tri[:, 256 + 128 * (1 - d):384 + 128 * (1 - d)]), [r_w], [r_w])
            identB = sb("identB", [64, 4, 64], F32)
            P.op("dve", lambda e: e.tensor_copy(out=identB[:], in_=c.idf[0:64, 0:64].unsqueeze(1).to_broadcast([64, 4, 64])), [c.r_const], [r_w])
            mu_r = sb("mu_r", [128, 768], F32)
            mu_l = sb("mu_l", [128, 128], F32)
            bias_wa = sb("bias_wa", [128, 512], F32)
            kkv = sb("kkv", [128, 256], F32)
            kav = sb("kav", [128, 256], F32)
            rkp = sb("rkp", [128, 256], F32)
            lng = sb("lng", [128, 256], F32)
            lnb = sb("lnb", [128, 256], F32)
            Wud = sb("Wud", [128, 512], BF16)
            bc = lambda ap: ap.partition_broadcast(128)
            P.dma("sp", mu_r[:], bc(c.w["mu_rkv"][l, d:d + 1, :]), writes=[r_w])
            P.dma("sp", mu_l[:], bc(c.w["mu_lat"][l, d:d + 1, :]), writes=[r_w])
            P.dma("sp", bias_wa[:, 0:256], bc(c.w["w0"][l, d:d + 1, :]), writes=[r_w])
            P.dma("sp", bias_wa[:, 256:512], bc(c.w["a0"][l, d:d + 1, :]), writes=[r_w])
            P.dma("sp", kkv[:], bc(c.w["k_k"][l, d:d + 1, :]), writes=[r_w])
            P.dma("sp", kav[:], bc(c.w["k_a"][l, d:d + 1, :]), writes=[r_w])
            P.dma("sp", rkp[:], bc(c.w["r_k"][l, d:d + 1, :]), writes=[r_w])
            P.dma("sp", lng[:], bc(c.w["ln_g"][l:l + 1, :]), writes=[r_w])
            P.dma("sp", lnb[:], bc(c.w["ln_b"][l:l + 1, :]), writes=[r_w])
            P.op("pool", lambda e: e.memset(Wud[:], 0.0), [], [r_w])
            P.dma("pool", Wud[0:64, 0:256], c.w["w_up"][l, d], writes=[r_w])
            P.dma("pool", Wud[64:128, 256:512], c.w["a_up"][l, d], writes=[r_w])
            ST = Pool2(st, nc, f"ST_d{d}_", [64, 256], F32, 2)
            st0, r_st0 = ST.next()
            P.op("pool", lambda e: e.memset(st0[:], 0.0), [], [r_st0])
            state = [st0, r_st0]

            def mk(name, shape, dt, n=1):
                return Pool2(st, nc, f"{name}_d{d}_", shape, dt, n)
            cur_, sh_ = mk("cur", [128, 1024], F32, 2), mk("sh", [128, 1024], F32, 2)
            xr_, xl_, tl_, tlT_ = mk("xr", [128, 768], F32), mk("xl", [128, 128], F32), mk("tl", [128, 128], BF16), mk("tlT", [128, 128], BF16)
            sg_, lw_ = mk("sg", [128, 512], F32), mk("logw", [128, 256], F32)
            kk_, sq_, s4_, k2_, bon_, beta_ = mk("kk", [128, 256], F32), mk("sqd", [128, 256], F32), mk("s4", [128, 32], F32), mk("k2", [128, 256], F32), mk("bon", [128, 256], F32), mk("beta", [128, 256], F32)
            cum_, ex_, tmp_ = mk("cum", [128, 256], F32), mk("exps", [128, 1024], F32), mk("tmpd", [128, 256], F32, 2)
            opb_ = mk("opb", [128, 7, 256], BF16)
            fTA_, fTB_ = mk("fTA", [64, 1024], BF16), mk("fTB", [64, 1024], BF16)
            AB_, AK_, NTa_ = mk("AB", [128, 1024], BF16), mk("AK", [128, 1024], BF16), mk("NTa", [128, 512], BF16, 3)
            Nn_ = mk("Nn", [128, 512], BF16, 3)
            Z_ = mk("Z", [128, 512], BF16, 3)
            gcol_, Q_, Pm_, Dm_ = mk("gcol", [64, 4], F32), mk("Q", [64, 512], F32), mk("Pm", [64, 256], F32), mk("Dm", [64, 256], F32)
            osb_, on_, yo_ = mk("osb", [128, 256], F32), mk("on", [128, 256], F32), mk("yod", [128, 256], BF16)
            banks, pbk = lane_banks(c, d)
            bank_i = [0]

            def nb():
                bank_i[0] = (bank_i[0] + 1) % 2
                return banks[bank_i[0]]

            def body(i):
                rows = slice(i * 128, (i + 1) * 128)
                v3 = lambda ap, dd=64: ap.rearrange("p (h d) -> p h d", d=dd)
                dbg_on = False

                def dbg(name, t, shape, dt, rs):
                    if dbg_on:
                        o = nc.dram_tensor("dbg_" + name, list(shape), dt, kind="ExternalOutput").ap()
                        P.dma("pool", o, t, reads=rs, writes=[Res()])
                cur, r_cur = cur_.next()
                sh, r_sh = sh_.next()
                nbr = [c.r["rkvl"][j] for j in (i - 1, i, i + 1) if 0 <= j < NT] + [c.r_pad]
                P.dma("sp", cur[:], c.rkvl[1 + i * 128:1 + (i + 1) * 128, :], reads=[c.r["rkvl"][i]], writes=[r_cur])
                so = 0 if d == 0 else 2
                P.dma("sp", sh[:], c.rkvl[so + i * 128:so + (i + 1) * 128, :], reads=nbr, writes=[r_sh])
                xr, r_xr = xr_.next()
                xl, r_xl = xl_.next()
                P.op("pool", lambda e: e.tensor_tensor(out=xr[:], in0=sh[:, 0:768], in1=cur[:, 0:768], op=ALU.subtract), [r_sh, r_cur], [r_xr])
                P.op("dve", lambda e: e.tensor_tensor(out=xr[:], in0=xr[:], in1=mu_r[:], op=ALU.mult), [r_xr, r_w], [r_xr])
                P.op("pool", lambda e: e.tensor_tensor(out=xr[:], in0=xr[:], in1=cur[:, 0:768], op=ALU.add), [r_xr, r_cur], [r_xr])
                lo = 768 + 128 * d
                P.op("dve", lambda e: e.tensor_tensor(out=xl[:], in0=sh[:, lo:lo + 128], in1=cur[:, lo:lo + 128], op=ALU.subtract), [r_sh, r_cur], [r_xl])
                P.op("dve", lambda e: e.tensor_tensor(out=xl[:], in0=xl[:], in1=mu_l[:], op=ALU.mult), [r_xl, r_w], [r_xl])
                P.op("dve", lambda e: e.tensor_tensor(out=xl[:], in0=xl[:], in1=cur[:, lo:lo + 128], op=ALU.add), [r_xl, r_cur], [r_xl])
                yield
                r_, k_, v_ = xr[:, 0:256], xr[:, 256:512], xr[:, 512:768]
                tl, r_tl = tl_.next()
                tlT, r_tlT = tlT_.next()
                P.op("act", lambda e: e.activation(out=tl[:, 0:64], in_=xl[:, 0:64], func=AF.Tanh), [r_xl], [r_tl])
                P.op("act", lambda e: e.copy(out=tl[:, 64:128], in_=xl[:, 64:128]), [r_xl], [r_tl])
                pb, r_pb = pbk
                P.op("pe", lambda e: e.transpose(out=pb[:, 0:128], in_=tl[:], identity=c.idb[:]), [r_tl, c.r_const], [r_pb])
                P.op("act", lambda e: e.copy(out=tlT[:], in_=pb[:, 0:128]), [r_pb], [r_tlT])
                yield
                ps, r_ps = nb()
                P.op("pe", lambda e: e.matmul(out=ps[:, :], lhsT=tlT[:], rhs=Wud[:], start=True, stop=True), [r_tlT, r_w], [r_ps])
                sg, r_sg = sg_.next()
                lw, r_lw = lw_.next()
                P.op("dve", lambda e: e.tensor_tensor(out=sg[:], in0=ps[:, :], in1=bias_wa[:], op=ALU.add), [r_ps, r_w], [r_sg])
                P.op("act", lambda e: e.activation(out=sg[:], in_=sg[:], func=AF.Sigmoid), [r_sg], [r_sg])
                P.op("act", lambda e: e.mul(out=lw[:], in_=sg[:, 0:256], mul=NEG), [r_sg], [r_lw])
                yield
                a_ = sg[:, 256:512]
                kk, r_kk = kk_.next()
                sq, r_sq = sq_.next()
                s4, r_s4 = s4_.next()
                P.op("dve", lambda e: e.tensor_tensor(out=kk[:], in0=k_, in1=kkv[:], op=ALU.mult), [r_xr, r_w], [r_kk])
                P.op("pool", lambda e: e.tensor_tensor(out=sq[:], in0=kk[:], in1=kk[:], op=ALU.mult), [r_kk], [r_sq])
                P.op("dve", lambda e: e.tensor_reduce(out=s4[:, 0:4], in_=v3(sq[:]), axis=AX.X, op=ALU.add), [r_sq], [r_s4])
                P.op("dve", lambda e: e.tensor_scalar(out=s4[:, 4:8], in0=s4[:, 0:4], scalar1=1e-12, scalar2=None, op0=ALU.add), [r_s4], [r_s4])
                P.op("act", lambda e: e.activation(out=s4[:, 0:4], in_=s4[:, 4:8], func=AF.Sqrt), [r_s4], [r_s4])
                P.op("dve", lambda e: e.reciprocal(out=s4[:, 8:12], in_=s4[:, 0:4]), [r_s4], [r_s4])
                P.op("dve", lambda e: e.tensor_tensor(out=v3(kk[:]), in0=v3(kk[:]), in1=s4[:, 8:12].unsqueeze(2).to_broadcast([128, 4, 64]), op=ALU.mult), [r_kk, r_s4], [r_kk])
                k2, r_k2 = k2_.next()
                P.op("dve", lambda e: e.scalar_tensor_tensor(out=k2[:], in0=a_, scalar=-1.0, in1=kav[:], op0=ALU.add, op1=ALU.mult), [r_sg, r_w], [r_k2])
                P.op("dve", lambda e: e.scalar_tensor_tensor(out=k2[:], in0=k2[:], scalar=1.0, in1=k_, op0=ALU.add, op1=ALU.mult), [r_k2, r_xr], [r_k2])
                bon, r_bon = bon_.next()
                P.op("pool", lambda e: e.tensor_tensor(out=bon[:], in0=r_, in1=k2[:], op=ALU.mult), [r_xr, r_k2], [r_bon])
                P.op("pool", lambda e: e.tensor_tensor(out=bon[:], in0=bon[:], in1=rkp[:], op=ALU.mult), [r_bon, r_w], [r_bon])
                P.op("dve", lambda e: e.tensor_reduce(out=s4[:, 12:16], in_=v3(bon[:]), axis=AX.X, op=ALU.add), [r_bon], [r_s4])
                P.op("dve", lambda e: e.tensor_tensor(out=v3(bon[:]), in0=v3(v_), in1=s4[:, 12:16].unsqueeze(2).to_broadcast([128, 4, 64]), op=ALU.mult), [r_xr, r_s4], [r_bon])
                beta, r_beta = beta_.next()
                P.op("pool", lambda e: e.tensor_tensor(out=beta[:], in0=kk[:], in1=a_, op=ALU.mult), [r_kk, r_sg], [r_beta])
                yield
                pc, r_pc = nb()
                P.op("pe", lambda e: e.matmul(out=pc[:, 0:256], lhsT=incl, rhs=lw[:], start=True, stop=True), [r_lw, r_w], [r_pc])
                P.op("pe", lambda e: e.matmul(out=pc[:, 256:512], lhsT=ones_m, rhs=lw[:], start=True, stop=True), [r_lw, r_w], [r_pc])
                pg, r_pg = nb()
                for h in range(4):
                    P.op("pe", lambda e, h=h: e.matmul(out=pg[0:64, h:h + 1], lhsT=lw[:, h * 64:(h + 1) * 64], rhs=ones_c, start=True, stop=True), [r_lw, r_w], [r_pg])
                gcol, r_gcol = gcol_.next()
                P.op("act", lambda e: e.activation(out=gcol[:], in_=pg[0:64, 0:4], func=AF.Exp), [r_pg], [r_gcol])
                yield
                cum, r_cum = cum_.next()
                ex, r_ex = ex_.next()
                t3, r_t3 = tmp_.next()
                t4, r_t4 = tmp_.next()
                ecum, encum, eexc, edec = ex[:, 0:256], ex[:, 256:512], ex[:, 512:768], ex[:, 768:1024]
                P.op("act", lambda e: e.copy(out=cum[:], in_=pc[:, 0:256]), [r_pc], [r_cum])
                P.op("act", lambda e: e.activation(out=ecum, in_=pc[:, 0:256], func=AF.Exp), [r_pc], [r_ex])
                P.op("act", lambda e: e.activation(out=encum, in_=pc[:, 0:256], func=AF.Exp, scale=-1.0), [r_pc], [r_ex])
                P.op("dve", lambda e: e.tensor_tensor(out=t3[:], in0=cum[:], in1=lw[:], op=ALU.subtract), [r_cum, r_lw], [r_t3])
                P.op("act", lambda e: e.activation(out=eexc, in_=t3[:], func=AF.Exp), [r_t3], [r_ex])
                P.op("dve", lambda e: e.tensor_tensor(out=t4[:], in0=pc[:, 256:512], in1=cum[:], op=ALU.subtract), [r_pc, r_cum], [r_t4])
                P.op("act", lambda e: e.activation(out=edec, in_=t4[:], func=AF.Exp), [r_t4], [r_ex])
                yield
                opb, r_opb = opb_.next()
                P.op("dve", lambda e: e.tensor_tensor(out=opb[:, 0, :], in0=r_, in1=ecum, op=ALU.mult), [r_xr, r_ex], [r_opb])
                P.op("pool", lambda e: e.tensor_tensor(out=opb[:, 1, :], in0=k2[:], in1=encum, op=ALU.mult), [r_k2, r_ex], [r_opb])
                P.op("dve", lambda e: e.tensor_tensor(out=opb[:, 2, :], in0=beta[:], in1=encum, op=ALU.mult), [r_beta, r_ex], [r_opb])
                P.op("dve", lambda e: e.scalar_tensor_tensor(out=opb[:, 3, :], in0=kk[:], scalar=-1.0, in1=eexc, op0=ALU.mult, op1=ALU.mult), [r_kk, r_ex], [r_opb])
                P.op("pool", lambda e: e.tensor_tensor(out=opb[:, 4, :], in0=k2[:], in1=edec, op=ALU.mult), [r_k2, r_ex], [r_opb])
                P.op("pool", lambda e: e.tensor_tensor(out=opb[:, 5, :], in0=beta[:], in1=edec, op=ALU.mult), [r_beta, r_ex], [r_opb])
                P.op("act", lambda e: e.copy(out=opb[:, 6, :], in_=v_), [r_xr], [r_opb])
                yield
                rb, kt, bt, ab, Kh, Bh, vb = (opb[:, q, :] for q in range(7))
                pa, r_pa = pbk
                fTA, r_fTA = fTA_.next()
                fTB, r_fTB = fTB_.next()
                for h in range(4):
                    hs = slice(h * 64, (h + 1) * 64)
                    P.op("pe", lambda e, h=h, hs=hs: e.transpose(out=pa[0:64, h * 128:(h + 1) * 128], in_=bt[:, hs], identity=c.idb[:]), [r_opb, c.r_const], [r_pa])
                    P.op("pe", lambda e, h=h, hs=hs: e.transpose(out=pa[0:64, (4 + h) * 128:(5 + h) * 128], in_=kt[:, hs], identity=c.idb[:]), [r_opb, c.r_const], [r_pa])
                P.op("act", lambda e: e.copy(out=fTA[:], in_=pa[0:64, :]), [r_pa], [r_fTA])
                yield
                for h in range(4):
                    hs = slice(h * 64, (h + 1) * 64)
                    P.op("pe", lambda e, h=h, hs=hs: e.transpose(out=pa[0:64, (2 * h) * 128:(2 * h + 1) * 128], in_=ab[:, hs], identity=c.idb[:]), [r_opb, c.r_const], [r_pa])
                    P.op("pe", lambda e, h=h, hs=hs: e.transpose(out=pa[0:64, (2 * h + 1) * 128:(2 * h + 2) * 128], in_=rb[:, hs], identity=c.idb[:]), [r_opb, c.r_const], [r_pa])
                P.op("dve", lambda e: e.tensor_copy(out=fTB[:], in_=pa[0:64, :]), [r_pa], [r_fTB])
                yield
                AB, r_AB = AB_.next()
                AK, r_AK = AK_.next()
                for (dst, r_dst, off) in ((AB, r_AB, 0), (AK, r_AK, 4)):
                    for pr in range(2):
                        px, r_px = nb()
                        for hh in range(2):
                            h = 2 * pr + hh
                            P.op("pe", lambda e, h=h, hh=hh, px=px, off=off: e.matmul(out=px[:, hh * 256:(hh + 1) * 256], lhsT=fTA[:, (off + h) * 128:(off + h + 1) * 128],
                                                                               rhs=fTB[:, 2 * h * 128:(2 * h + 2) * 128], start=True, stop=True), [r_fTA, r_fTB], [r_px])
                        P.op("dve", lambda e, pr=pr, px=px, dst=dst: e.tensor_tensor(out=dst[:, pr * 512:(pr + 1) * 512], in0=px[:, :], in1=mask4[:], op=ALU.mult), [r_px, r_w], [r_dst])
                        yield
                py, r_py = nb()
                for h in range(4):
                    P.op("pe", lambda e, h=h: e.matmul(out=py[:, h * 128:(h + 1) * 128], lhsT=fTB[:, 2 * h * 128:(2 * h + 1) * 128], rhs=fTA[:, h * 128:(h + 1) * 128],
                                                       start=True, stop=True), [r_fTA, r_fTB], [r_py])
                NTc, r_NTc = NTa_.next()
                P.op("dve", lambda e: e.tensor_tensor(out=NTc[:], in0=py[:, :], in1=maskT4[:], op=ALU.mult), [r_py, r_w], [r_NTc])
                yield
                pz, r_pz = nb()
                for h in range(4):
                    P.op("pe", lambda e, h=h: e.matmul(out=pz[:, h * 64:(h + 1) * 64], lhsT=AK[:, h * 256:h * 256 + 128], rhs=vb[:, h * 64:(h + 1) * 64], start=True, stop=True),
                         [r_AK, r_opb], [r_pz])
                Z, r_Z = Z_.next()
                P.op("dve", lambda e, Z=Z: e.tensor_copy(out=v3(Z[:], 128)[:, :, 0:64], in_=v3(ab)), [r_opb], [r_Z])
                P.op("act", lambda e, Z=Z: e.copy(out=v3(Z[:], 128)[:, :, 64:128], in_=v3(pz[:, 0:256])), [r_pz], [r_Z])
                yield
                dbg("Z0", Z[:], [128, 512], BF16, [r_Z]); dbg("NT0", NTc[:], [128, 512], BF16, [r_NTc]); dbg("AK", AK[:], [128, 1024], BF16, [r_AK])
                Ncur = [AB[:, h * 256:h * 256 + 128] for h in range(4)]
                r_N = r_AB
                for kq in range(7):
                    pq, r_pq = nb()
                    for h in range(4):
                        P.op("pe", lambda e, h=h, N=Ncur[h], Z=Z, pq=pq: e.matmul(out=pq[:, h * 128:(h + 1) * 128], lhsT=N, rhs=Z[:, h * 128:(h + 1) * 128], start=True, stop=True),
                             [r_N, r_Z], [r_pq])
                    Zn, r_Zn = Z_.next()
                    P.op("dve", lambda e, Z=Z, Zn=Zn, pq=pq: e.tensor_tensor(out=Zn[:], in0=pq[:, :], in1=Z[:], op=ALU.add), [r_pq, r_Z], [r_Zn])
                    yield
                    if kq < 6:
                        p1, r_p1 = nb()
                        for h in range(4):
                            P.op("pe", lambda e, h=h, N=Ncur[h], NTc=NTc, p1=p1: e.matmul(out=p1[:, h * 128:(h + 1) * 128], lhsT=NTc[:, h * 128:(h + 1) * 128], rhs=N, start=True, stop=True),
                                 [r_N, r_NTc], [r_p1])
                        p2, r_p2 = nb()
                        for h in range(4):
                            P.op("pe", lambda e, h=h, N=Ncur[h], NTc=NTc, p2=p2: e.matmul(out=p2[:, h * 128:(h + 1) * 128], lhsT=N, rhs=NTc[:, h * 128:(h + 1) * 128], start=True, stop=True),
                                 [r_N, r_NTc], [r_p2])
                        Nn, r_Nn = Nn_.next()
                        NTn, r_NTn = NTa_.next()
                        P.op("act", lambda e, Nn=Nn, p1=p1: e.copy(out=Nn[:], in_=p1[:, :]), [r_p1], [r_Nn])
                        P.op("dve", lambda e, NTn=NTn, p2=p2: e.tensor_copy(out=NTn[:], in_=p2[:, :]), [r_p2], [r_NTn])
                        yield
                        dbg(f"Nn{kq}", Nn[:], [128, 512], BF16, [r_Nn]); dbg(f"NTn{kq}", NTn[:], [128, 512], BF16, [r_NTn]); dbg(f"Zn{kq}", Zn[:], [128, 512], BF16, [r_Zn])
                        Ncur = [Nn[:, h * 128:(h + 1) * 128] for h in range(4)]
                        r_N, NTc, r_NTc = r_Nn, NTn, r_NTn
                    Z, r_Z = Zn, r_Zn
                WT = [Z[:, h * 128:h * 128 + 64] for h in range(4)]
                XT = [Z[:, h * 128 + 64:(h + 1) * 128] for h in range(4)]
                pQ, r_pQ = nb()
                for h in range(4):
                    P.op("pe", lambda e, h=h: e.matmul(out=pQ[0:64, h * 128:(h + 1) * 128], lhsT=WT[h], rhs=AB[:, h * 256 + 128:(h + 1) * 256], start=True, stop=True),
                         [r_Z, r_AB], [r_pQ])
                Q, r_Q = Q_.next()
                P.op("dve", lambda e: e.tensor_tensor(out=v3(Q[:], 128), in0=v3(pQ[0:64, :], 128), in1=fTB[:].rearrange("p (h two t) -> p h two t", two=2, t=128)[:, :, 1, :], op=ALU.add),
                     [r_pQ, r_fTB], [r_Q])
                yield
                pP, r_pP = nb()
                for h in range(4):
                    P.op("pe", lambda e, h=h: e.matmul(out=pP[0:64, h * 64:(h + 1) * 64], lhsT=WT[h], rhs=Bh[:, h * 64:(h + 1) * 64], start=True, stop=True), [r_Z, r_opb], [r_pP])
                Pm, r_Pm = Pm_.next()
                P.op("dve", lambda e: e.tensor_tensor(out=v3(Pm[:]), in0=identB[:], in1=gcol[:].unsqueeze(2).to_broadcast([64, 4, 64]), op=ALU.mult), [r_w, r_gcol], [r_Pm])
                P.op("dve", lambda e: e.tensor_tensor(out=Pm[:], in0=Pm[:], in1=pP[0:64, 0:256], op=ALU.add), [r_Pm, r_pP], [r_Pm])
                yield
                pD, r_pD = nb()
                for h in range(4):
                    P.op("pe", lambda e, h=h: e.matmul(out=pD[0:64, h * 64:(h + 1) * 64], lhsT=Bh[:, h * 64:(h + 1) * 64], rhs=XT[h], start=(h == 0), stop=False, skip_group_check=True),
                         [r_Z, r_opb], [r_pD])
                    P.op("pe", lambda e, h=h: e.matmul(out=pD[0:64, h * 64:(h + 1) * 64], lhsT=Kh[:, h * 64:(h + 1) * 64], rhs=vb[:, h * 64:(h + 1) * 64], start=False, stop=True, skip_group_check=True),
                         [r_opb], [r_pD])
                Dm, r_Dm = Dm_.next()
                P.op("act", lambda e: e.copy(out=Dm[:], in_=pD[0:64, 0:256]), [r_pD], [r_Dm])
                yield
                S0, r_S0 = state
                pO, r_pO = banks[2]
                for h in range(4):
                    P.op("pe", lambda e, h=h: e.matmul(out=pO[:, h * 64:(h + 1) * 64], lhsT=AB[:, h * 256 + 128:(h + 1) * 256], rhs=XT[h], start=(h == 0), stop=False, skip_group_check=True),
                         [r_AB, r_Z], [r_pO])
                    P.op("pe", lambda e, h=h: e.matmul(out=pO[:, h * 64:(h + 1) * 64], lhsT=AK[:, h * 256 + 128:(h + 1) * 256], rhs=vb[:, h * 64:(h + 1) * 64], start=False, stop=False, skip_group_check=True),
                         [r_AK, r_opb], [r_pO])
                    P.op("pe", lambda e, h=h, S0=S0: e.matmul(out=pO[:, h * 64:(h + 1) * 64], lhsT=Q[:, h * 128:(h + 1) * 128], rhs=S0[:, h * 64:(h + 1) * 64], start=False, stop=True, skip_group_check=True),
                         [r_Q, r_S0], [r_pO])
                osb, r_osb = osb_.next()
                P.op("act", lambda e: e.copy(out=osb[:], in_=pO[:, 0:256]), [r_pO], [r_osb])
                yield
                pS, r_pS = banks[2]
                for h in range(4):
                    P.op("pe", lambda e, h=h, S0=S0: e.matmul(out=pS[0:64, h * 64:(h + 1) * 64], lhsT=Pm[:, h * 64:(h + 1) * 64], rhs=S0[:, h * 64:(h + 1) * 64], start=True, stop=True),
                         [r_Pm, r_S0], [r_pS])
                S1, r_S1 = ST.next()
                P.op("dve", lambda e, S1=S1: e.tensor_tensor(out=S1[:], in0=pS[0:64, 0:256], in1=Dm[:], op=ALU.add), [r_pS, r_Dm], [r_S1])
                state[0], state[1] = S1, r_S1
                yield
                on, r_on = on_.next()
                P.op("dve", lambda e: e.tensor_reduce(out=s4[:, 16:20], in_=v3(osb[:]), axis=AX.X, op=ALU.add), [r_osb], [r_s4])
                P.op("pool", lambda e: e.tensor_tensor(out=on[:], in0=osb[:], in1=osb[:], op=ALU.mult), [r_osb], [r_on])
                P.op("dve", lambda e: e.tensor_reduce(out=s4[:, 20:24], in_=v3(on[:]), axis=AX.X, op=ALU.add), [r_on], [r_s4])
                P.op("dve", lambda e: e.tensor_scalar(out=s4[:, 16:20], in0=s4[:, 16:20], scalar1=1.0 / 64, scalar2=None, op0=ALU.mult), [r_s4], [r_s4])
                P.op("dve", lambda e: e.tensor_tensor(out=s4[:, 24:28], in0=s4[:, 16:20], in1=s4[:, 16:20], op=ALU.mult), [r_s4], [r_s4])
                P.op("dve", lambda e: e.scalar_tensor_tensor(out=s4[:, 20:24], in0=s4[:, 20:24], scalar=1.0 / 64, in1=s4[:, 24:28], op0=ALU.mult, op1=ALU.subtract), [r_s4], [r_s4])
                P.op("dve", lambda e: e.tensor_scalar(out=s4[:, 20:24], in0=s4[:, 20:24], scalar1=GN_EPS, scalar2=None, op0=ALU.add), [r_s4], [r_s4])
                P.op("act", lambda e: e.activation(out=s4[:, 24:28], in_=s4[:, 20:24], func=AF.Sqrt), [r_s4], [r_s4])
                P.op("dve", lambda e: e.reciprocal(out=s4[:, 28:32], in_=s4[:, 24:28]), [r_s4], [r_s4])
                P.op("dve", lambda e: e.tensor_tensor(out=v3(on[:]), in0=v3(osb[:]), in1=s4[:, 16:20].unsqueeze(2).to_broadcast([128, 4, 64]), op=ALU.subtract), [r_osb, r_s4], [r_on])
                P.op("dve", lambda e: e.tensor_tensor(out=v3(on[:]), in0=v3(on[:]), in1=s4[:, 28:32].unsqueeze(2).to_broadcast([128, 4, 64]), op=ALU.mult), [r_on, r_s4], [r_on])
                P.op("pool", lambda e: e.tensor_tensor(out=on[:], in0=on[:], in1=lng[:], op=ALU.mult), [r_on, r_w], [r_on])
                P.op("pool", lambda e: e.tensor_tensor(out=on[:], in0=on[:], in1=lnb[:], op=ALU.add), [r_on, r_w], [r_on])
                P.op("pool", lambda e: e.tensor_tensor(out=on[:], in0=on[:], in1=bon[:], op=ALU.add), [r_on, r_bon], [r_on])
                P.dma("pool", c.osc[d, rows, :], on[:], reads=[r_on], writes=[c.r["osc%d" % d][i]])

            return body

    with ExitStack() as st:
        bodies = [setup_dir(d, st) for d in range(2)]
        run_lanes([list(range(NT)), list(range(NT - 1, -1, -1))], lambda i, k: bodies[k](i), skew=6)
    barrier(c)
    with ExitStack() as st:
        def mkpools(k):
            mk = lambda name, shape, dt, n=2: Pool2(st, nc, f"{name}_l{k}_", shape, dt, n)
            return dict(o0=mk("e_o0", [128, 256], F32), o1=mk("e_o1", [128, 256], F32), zd=mk("e_zd", [128, 256], F32), yo=mk("e_yo", [128, 256], BF16))
        pools = [mkpools(k) for k in range(2)]

        def body_e(i, k):
            pl = pools[k]
            rows = slice(i * 128, (i + 1) * 128)
            o0, r_o0 = pl["o0"].next()
            o1, r_o1 = pl["o1"].next()
            zd, r_zd = pl["zd"].next()
            yo, r_yo = pl["yo"].next()
            P.dma("sp", o0[:], c.osc[0, rows, :], reads=[c.r["osc0"][i]], writes=[r_o0])
            P.dma("sp", o1[:], c.osc[1, rows, :], reads=[c.r["osc1"][i]], writes=[r_o1])
            P.dma("sp", zd[:], c.zs[rows, 768:1024], reads=[c.r["zs"][i]], writes=[r_zd])
            yield
            P.op("dve", lambda e: e.tensor_tensor(out=o0[:], in0=o0[:], in1=o1[:], op=ALU.add), [r_o0, r_o1], [r_o0])
            P.op("dve", lambda e: e.tensor_tensor(out=yo[:], in0=o0[:], in1=zd[:], op=ALU.mult), [r_o0, r_zd], [r_yo])
            P.dma("pool", c.ys[rows, 768:1024], yo[:], reads=[r_yo], writes=[c.r["ys"][i]])

        run_lanes([list(range(k, NT, 2)) for k in range(2)], body_e, skew=1)


def host_consts(S, seq_len):
    NT = S // 128
    t = np.arange(S)
    valid = (t < seq_len).astype(np.float32).reshape(NT, 128).T.copy()
    invc = np.ones((S, 4), np.float32)
    for g, w in enumerate((2, 4, 8, 16)):
        h = w // 2
        cnt = np.minimum(t + h, seq_len) - np.maximum(t - h, 0)
        invc[:, g] = np.where(t < seq_len, 1.0 / np.maximum(cnt, 1), 1.0)
    invc = invc.reshape(NT, 128, 4).transpose(1, 0, 2).reshape(128, NT * 4).copy()
    ident = np.eye(128, dtype=np.float32)
    s = np.arange(128)[:, None]
    tt = np.arange(128)[None, :]
    bandc = np.zeros((128, 4, 128), np.float32)
    bandh = np.zeros((16, 4, 128), np.float32)
    for g, w in enumerate((2, 4, 8, 16)):
        h = w // 2
        bandc[:, g, :] = ((s >= tt - h) & (s <= tt + h - 1))
        r = np.arange(16)[:, None]
        srel = np.where(r < 8, r - 8, 128 + (r - 8))
        bandh[:, g, :] = ((srel >= tt - h) & (srel <= tt + h - 1))
    slopes = 2.0 ** (-8.0 * np.arange(1, 13) / 12.0)
    amask = np.zeros((25, 128, 4, 128), np.float32)
    ci = 0
    key = np.arange(128)[:, None]
    q = np.arange(128)[None, :]
    for g in range(3):
        d = ATT_D[g]
        for coff in range(-ATT_W[g], ATT_W[g] + 1):
            delta = coff * 128 + key - q
            ok = (delta % d == 0) & (np.abs(delta) <= 64 * d)
            for h in range(4):
                amask[ci, :, h, :] = np.where(ok, np.exp(-slopes[g * 4 + h] * np.abs(delta)), 0.0)
            ci += 1
    amask3 = np.zeros((3, 128, 4, 128), np.float32)
    for ci3, coff in enumerate((-1, 0, 1)):
        dm = coff * 128 + key - q
        ok = np.abs(dm) <= 64
        for h in range(4):
            amask3[ci3, :, h, :] = np.where(ok, np.exp(-slopes[8 + h] * 16.0 * np.abs(dm)), 0.0)
    tri = np.zeros((128, 642), np.float32)
    tri[:, 0:128] = (s <= tt)
    tri[:, 128:256] = (s >= tt)
    tri[:, 256:384] = (s < tt)
    tri[:, 384:512] = (s > tt)
    tri[:, 512:642] = 1.0
    return dict(valid=valid, invc=invc, ident=ident, amask=amask.reshape(25, 128, 512), amask3=amask3.reshape(3, 128, 512), bandc=bandc.reshape(128, 512),
                bandh=bandh.reshape(16, 512), tri=tri)


def host_weights(inp):
    w = {}
    for k in ("norm_g", "w_in", "q_norm_g", "k_norm_g", "pool_w", "pool_scale", "sg_norm_g", "mu_rkv", "mu_lat", "w0", "w_up",
              "a0", "a_up", "k_k", "k_a", "r_k", "ln_g", "ln_b", "w_branch", "w_out"):
        w[k] = np.ascontiguousarray(np.asarray(inp[k], dtype=np.float32))
    w["sg_wT"] = np.ascontiguousarray(np.transpose(np.asarray(inp["sg_w"], np.float32), (0, 1, 3, 2)))
    w["sg_bT"] = np.ascontiguousarray(np.transpose(np.asarray(inp["sg_b"], np.float32), (0, 2, 1)))
    return w


_NC_CACHE = {}


def run_sequences(seqs, S, inp, branches=(0, 1, 2, 3), L=2, n_cores=8):
    key = (S, L, tuple(branches))
    if key not in _NC_CACHE:
        _NC_CACHE[key] = build_program(S, L=L, branches=branches)
    nc = _NC_CACHE[key]
    w = host_weights(inp)
    in_maps = []
    for ci in range(n_cores):
        if ci < len(seqs):
            x = np.zeros((S, D), np.float32)
            x[:seqs[ci].shape[0]] = seqs[ci]
            m = dict(x=x, **host_consts(S, seqs[ci].shape[0]))
        else:
            m = dict(x=np.zeros((S, D), np.float32), **host_consts(S, 0))
        m.update(w)
        in_maps.append(m)
    res = run_bass_kernel_spmd(nc, in_maps, core_ids=list(range(n_cores)))
    if DEBUG_SCRATCH:
        global LAST_RESULTS
        LAST_RESULTS = res.results
    return [res.results[ci]["y"][:seqs[ci].shape[0]] for ci in range(len(seqs))]


def kernel(**inputs):
    xp = np.asarray(inputs["x_prompt"], np.float32)
    xs = np.asarray(inputs["x_sample"], np.float32)
    S = xs.shape[1]
    seqs = [xp[b] for b in range(xp.shape[0])] + [xs[b] for b in range(xs.shape[0])]
    outs = run_sequences(seqs, S, inputs)
    nb = xp.shape[0]
    y_prompt = np.stack(outs[:nb], 0).astype(np.float32)
    y_sample = np.stack(outs[nb:], 0).astype(np.float32)
    return (y_prompt, y_sample)
```

```python
import numpy as np
import concourse.bass as bass
import concourse.mybir as mybir
from concourse.bass_utils import run_bass_kernel_spmd

F32 = mybir.dt.float32
BF16 = mybir.dt.bfloat16
AF = mybir.ActivationFunctionType
ALU = mybir.AluOpType
AX = mybir.AxisListType

EPOCH = 24000
STORES_ON_SP = False
RWKV_SKEW = 6
ATTACH_WAITS = True
NDMA_SEM = 12


class Res:
    __slots__ = ("name", "last_w", "reads")

    def __init__(self, name=""):
        self.name = name
        self.last_w = None
        self.reads = []


class Prog:
    ENG = ("pe", "act", "dve", "pool", "sp")

    def __init__(self, nc, same_engine_sync=True):
        self.nc = nc
        self.same_engine_sync = same_engine_sync
        self.ops = {e: [] for e in self.ENG}
        self.cnt = {e: 0 for e in self.ENG}
        self.sems = {e: [nc.alloc_semaphore(name=f"s_{e}_0")] for e in ("pe", "act", "dve", "pool")}
        self.seen = {e: {} for e in self.ENG}
        self.dma_sems = {q: [nc.alloc_semaphore(name=f"d_{q}_{k}") for k in range(NDMA_SEM)] for q in ("sp", "pool")}
        self.dma_n = {"sp": 0, "pool": 0}
        self.n_wait = 0

    def _deps(self, reads, writes):
        deps = []
        for r in reads:
            if r.last_w is not None:
                deps.extend(r.last_w)
        for w in writes:
            if w.last_w is not None:
                deps.extend(w.last_w)
            deps.extend(w.reads)
        return deps

    def _waits(self, e, deps, own_sem_ids):
        waits = {}
        seen = self.seen[e]
        for (sem, val) in deps:
            sid = id(sem)
            if sid in own_sem_ids and (e == "pe" or not self.same_engine_sync):
                continue
            if seen.get(sid, 0) >= val:
                continue
            if sid not in waits or waits[sid][1] < val:
                waits[sid] = (sem, val)
        for sid, (sem, val) in waits.items():
            seen[sid] = val
        self.n_wait += len(waits)
        return list(waits.values())

    def _commit(self, ev, reads, writes, is_dma=False):
        for r in reads:
            r.reads.append(ev)
        for w in writes:
            if is_dma and w.last_w is not None and not w.reads:
                w.last_w = w.last_w + [ev]
            else:
                w.last_w = [ev]
            w.reads = []

    def op(self, e, fn, reads=(), writes=()):
        deps = self._deps(reads, writes)
        own = {id(s) for s in self.sems[e]}
        waits = self._waits(e, deps, own)
        if self.cnt[e] >= EPOCH:
            self.sems[e].append(self.nc.alloc_semaphore(name=f"s_{e}_{len(self.sems[e])}"))
            self.cnt[e] = 0
        self.cnt[e] += 1
        sem = self.sems[e][-1]
        ev = (sem, self.cnt[e])
        self.ops[e].append((waits, fn, sem, 1))
        self._commit(ev, reads, writes)
        return ev

    def dma(self, q, out, in_, reads=(), writes=(), fn=None, **kw):
        if STORES_ON_SP and q == "pool" and fn is None and out.dtype == in_.dtype:
            q = "sp"
        deps = self._deps(reads, writes)
        n = self.dma_n[q]
        self.dma_n[q] += 1
        k = n % NDMA_SEM
        use = n // NDMA_SEM
        sem = self.dma_sems[q][k]
        if use > 0:
            deps.append((sem, 16 * use))
        own = {id(s) for s in self.sems[q]} if q in self.sems else set()
        waits = self._waits(q, deps, own)
        ev = (sem, 16 * (use + 1))
        if fn is None:
            fn = (lambda eng, o=out, i=in_, kw=kw: eng.dma_start(out=o, in_=i, **kw))
        self.ops[q].append((waits, fn, sem, 16))
        self._commit(ev, reads, writes, is_dma=True)
        return ev

    def final_wait(self, e, evs):
        waits = self._waits(e, list(evs), set())
        self.ops[e].append((waits, None, None, 0))

    def emit(self):
        nc = self.nc
        ops = self.ops

        def run(eng, lst, attach):
            for (waits, fn, sem, inc) in lst:
                if fn is not None and attach and waits and inc == 1:
                    for (s, v) in waits[:-1]:
                        eng.wait_ge(s, v)
                    ins = fn(eng)
                    ins._wait_ge(waits[-1][0], eng.lower_val(waits[-1][1]))
                    ins.then_inc(sem, inc)
                    continue
                for (s, v) in waits:
                    eng.wait_ge(s, v)
                if fn is not None:
                    ins = fn(eng)
                    ins.then_inc(sem, inc)

        with nc.Block() as block:
            @block.tensor
            def _(eng):
                run(eng, ops["pe"], ATTACH_WAITS)

            @block.scalar
            def _(eng):
                run(eng, ops["act"], ATTACH_WAITS)

            @block.vector
            def _(eng):
                run(eng, ops["dve"], ATTACH_WAITS)

            @block.gpsimd
            def _(eng):
                run(eng, ops["pool"], ATTACH_WAITS)

            @block.sync
            def _(eng):
                run(eng, ops["sp"], False)


D = 1024
PW = 9216
P1W = 5120
EPS = 1e-6
GN_EPS = 64e-5
ATT_W = (1, 2, 8)
ATT_D = (1, 4, 16)
C_Q, C_K, C_V, C_ZA, C_UB, C_ZB, C_UVC, C_ZC, C_RKV, C_LAT, C_ZD = 0, 768, 1536, 2304, 2560, 2816, 3072, 3584, 3840, 4608, 4864


class Ctx:
    pass


_UID = [0]


def uname(n):
    _UID[0] += 1
    return f"{n}_u{_UID[0]}"


DEBUG_SCRATCH = False
DEBUG_BARRIER = False
DBG_DIR = 0
DBG_TILE = 0


def build_program(S, L=2, branches=(0, 1, 2, 3), same_engine_sync=True):
    from contextlib import ExitStack
    NT = S // 128
    nc = bass.Bass("TRN2", target_bir_lowering=False)
    P = Prog(nc, same_engine_sync=same_engine_sync)
    c = Ctx()
    c.nc, c.P, c.S, c.NT, c.L, c.branches = nc, P, S, NT, L, branches

    def din(name, shape, dt=F32):
        return nc.dram_tensor(name, list(shape), dt, kind="ExternalInput").ap()

    def dscr(name, shape, dt=F32):
        return nc.dram_tensor(name, list(shape), dt, kind=("ExternalOutput" if DEBUG_SCRATCH else "Internal")).ap()

    c.x_in = din("x", [S, D])
    c.valid = din("valid", [128, NT])
    c.invc = din("invc", [128, NT * 4])
    c.ident = din("ident", [128, 128])
    c.amask = din("amask", [25, 128, 512])
    c.amask3 = din("amask3", [3, 128, 512])
    c.bandc = din("bandc", [128, 4 * 128])
    c.bandh = din("bandh", [16, 4 * 128])
    c.tri = din("tri", [128, 642])
    c.w = {}
    for name, shape in (("norm_g", [L, D]), ("w_in", [L, D, PW]), ("q_norm_g", [L, 64]), ("k_norm_g", [L, 64]),
                        ("pool_w", [L, 4, 64, 64]), ("pool_scale", [L, 256]), ("sg_norm_g", [L, 256]),
                        ("sg_wT", [L, 4, 128, 128]), ("sg_bT", [L, 128, 4]), ("mu_rkv", [L, 2, 768]), ("mu_lat", [L, 2, 128]),
                        ("w0", [L, 2, 256]), ("w_up", [L, 2, 64, 256]), ("a0", [L, 2, 256]), ("a_up", [L, 2, 64, 256]),
                        ("k_k", [L, 2, 256]), ("k_a", [L, 2, 256]), ("r_k", [L, 2, 256]), ("ln_g", [L, 256]), ("ln_b", [L, 256]),
                        ("w_branch", [L, 4, 256, D]), ("w_out", [L, D, D])):
        c.w[name] = din(name, shape)
    c.y_out = nc.dram_tensor("y", [S, D], F32, kind="ExternalOutput").ap()

    c.x1 = dscr("x1", [S, D])
    c.qT = dscr("qT_scr", [NT, 64, 1536], BF16)
    c.kT = dscr("kT_scr", [NT, 64, 1536], BF16)
    c.va = dscr("va_scr", [NT, 128, 780], BF16)
    c.zs = dscr("zs_scr", [S, 1024])
    c.ub = dscr("ub_scr", [S + 32, 256])
    c.uvc = dscr("uvc_scr", [S, 512])
    c.rkvl = dscr("rkvl_scr", [S + 2, 1024])
    c.ys = dscr("ys_scr", [S, 1024], BF16)
    c.nd3 = dscr("nd3_scr", [S, 260])
    c.osc = dscr("o_scr", [2, S, 256])
    c.r = {k: [Res(f"{k}{i}") for i in range(NT)] for k in ("x1", "qT", "kT", "va", "zs", "ub", "uvc", "rkvl", "ys", "osc0", "osc1", "y", "nd3")}
    c.r_pad = Res("pads")
    if DEBUG_SCRATCH:
        c.dbg_proj = dscr("dbg_proj", [S, P1W])
        c.dbg_hT = dscr("dbg_hT", [S, D], BF16)

    c.idb = nc.alloc_sbuf_tensor("idb", [128, 128], BF16)
    c.idf = nc.alloc_sbuf_tensor("idf", [128, 128], F32)
    c.zero = nc.alloc_sbuf_tensor("zero", [128, 1024], F32)
    c.validt = nc.alloc_sbuf_tensor("validt", [128, NT], F32)
    c.r_const = Res("const")
    P.dma("sp", c.idf[:], c.ident, writes=[c.r_const])
    P.dma("pool", c.idb[:], c.ident, writes=[c.r_const])
    P.dma("sp", c.validt[:], c.valid, writes=[c.r_const])
    P.op("pool", lambda e: e.memset(c.zero[:], 0.0), [], [c.r_const])
    P.dma("sp", c.ub[0:16, :], c.zero[0:16, 0:256], reads=[c.r_const], writes=[c.r_pad])
    P.dma("sp", c.ub[S + 16:S + 32, :], c.zero[0:16, 0:256], reads=[c.r_const], writes=[c.r_pad])
    P.dma("sp", c.rkvl[0:1, :], c.zero[0:1, :], reads=[c.r_const], writes=[c.r_pad])
    P.dma("sp", c.rkvl[S + 1:S + 2, :], c.zero[0:1, :], reads=[c.r_const], writes=[c.r_pad])

    c.pbf = [nc.alloc_psum_tensor(f"pbf{i}", [128, 1024], BF16) for i in range(2)]
    c.pf = [nc.alloc_psum_tensor(f"pf{i}", [128, 512], F32) for i in range(6)]
    c.r_pbf = [Res(f"pbf{i}") for i in range(2)]
    c.r_pf = [Res(f"pf{i}") for i in range(6)]

    for l in range(L):
        x_src = c.x_in if l == 0 else c.x1
        x_src_res = None if l == 0 else c.r["x1"]
        x_dst = c.x1 if l < L - 1 else c.y_out
        x_dst_res = c.r["x1"] if l < L - 1 else c.r["y"]
        phase1(c, l, x_src, x_src_res)
        barrier(c)
        if 2 in branches:
            phase2c(c, l)
            barrier(c)
        if 1 in branches:
            phase2b(c, l)
            barrier(c)
        if 0 in branches:
            phase2a(c, l)
            barrier(c)
        if 3 in branches:
            phase2d(c, l)
            barrier(c)
        phase3(c, l, x_src, x_src_res, x_dst, x_dst_res)
        barrier(c)
    P.emit()
    return nc


def barrier(c):
    P = c.P
    evs = []
    for e in ("pe", "act", "dve", "pool"):
        if P.cnt[e] > 0:
            evs.append((P.sems[e][-1], P.cnt[e]))
    for q in ("sp", "pool"):
        n = P.dma_n[q]
        for k, s in enumerate(P.dma_sems[q]):
            uses = (n - k + NDMA_SEM - 1) // NDMA_SEM if n > k else 0
            if uses > 0:
                evs.append((s, 16 * uses))
    for e in Prog.ENG:
        P.final_wait(e, evs)


class Pool2:
    def __init__(self, stack, nc, name, shape, dt, n=2):
        self.t = [stack.enter_context(nc.sbuf_tensor(uname(f"{name}{i}"), list(shape), dt)) for i in range(n)]
        self.r = [Res(f"{name}{i}") for i in range(n)]
        self.n = n
        self.i = -1

    def next(self):
        self.i = (self.i + 1) % self.n
        return self.t[self.i], self.r[self.i]


NL = 2


def run_lanes(lane_tiles, body, skew=2):
    K = len(lane_tiles)
    its = [iter(t) for t in lane_tiles]
    gens = [None] * K
    done = [False] * K
    rnd = 0
    while not all(done):
        for k in range(K):
            if done[k]:
                continue
            if gens[k] is None:
                if rnd < k * skew:
                    continue
                i = next(its[k], None)
                if i is None:
                    done[k] = True
                    continue
                gens[k] = body(i, k)
            try:
                next(gens[k])
            except StopIteration:
                gens[k] = None
        rnd += 1


def lane_banks(c, k):
    return [(c.pf[3 * k + j], c.r_pf[3 * k + j]) for j in range(3)], (c.pbf[k], c.r_pbf[k])


def rms_h(c, xt, r_xt, gbc, r_w, pl, width=1024):
    P = c.P
    junk, r_junk = pl["junk"].next()
    ss, r_ss = pl["ss"].next()
    h, r_h = pl["h"].next()
    P.op("act", lambda e: e.activation(out=junk[:], in_=xt[:], func=AF.Square, accum_out=ss[:, 0:1]), [r_xt], [r_junk, r_ss])
    P.op("dve", lambda e: e.tensor_scalar(out=ss[:, 1:2], in0=ss[:, 0:1], scalar1=1.0 / width, scalar2=EPS, op0=ALU.mult, op1=ALU.add), [r_ss], [r_ss])
    P.op("act", lambda e: e.activation(out=ss[:, 2:3], in_=ss[:, 1:2], func=AF.Sqrt), [r_ss], [r_ss])
    P.op("dve", lambda e: e.reciprocal(out=ss[:, 3:4], in_=ss[:, 2:3]), [r_ss], [r_ss])
    P.op("dve", lambda e: e.scalar_tensor_tensor(out=h[:], in0=xt[:], scalar=ss[:, 3:4], in1=gbc[:], op0=ALU.mult, op1=ALU.mult),
         [r_xt, r_ss, r_w], [r_h])
    return h, r_h


def transpose8(c, src, r_src, dst, r_dst, pbk, eng="act", n=8):
    P = c.P
    pb, r_pb = pbk
    for kc in range(n):
        P.op("pe", lambda e, kc=kc: e.transpose(out=pb[:, kc * 128:(kc + 1) * 128], in_=src[:, kc * 128:(kc + 1) * 128], identity=c.idb[:]),
             [r_src, c.r_const], [r_pb])
    if eng == "act":
        P.op("act", lambda e: e.copy(out=dst[:, 0:n * 128], in_=pb[:, 0:n * 128]), [r_pb], [r_dst])
    else:
        P.op("dve", lambda e: e.tensor_copy(out=dst[:, 0:n * 128], in_=pb[:, 0:n * 128]), [r_pb], [r_dst])


def phase1(c, l, x_src, x_src_res):
    from contextlib import ExitStack
    P, nc, NT = c.P, c.nc, c.NT
    with ExitStack() as st:
        sb = lambda n, s, d: st.enter_context(nc.sbuf_tensor(uname(n), list(s), d))
        W = sb("W1", [128, 8, P1W], BF16)
        r_W = Res("W1")
        for kc in range(8):
            P.dma("pool", W[:, kc, :], c.w["w_in"][l, kc * 128:(kc + 1) * 128, 0:P1W], writes=[r_W])
        gbc = sb("gbc", [128, D], F32)
        g64 = sb("g64", [128, 128], F32)
        gqk = sb("gqk", [128, 1536], F32)
        r_w = Res("p1w")
        P.dma("sp", gbc[:], c.w["norm_g"][l:l + 1, :].partition_broadcast(128), writes=[r_w])
        P.dma("sp", g64[:, 0:64], c.w["q_norm_g"][l:l + 1, :].partition_broadcast(128), writes=[r_w])
        P.dma("sp", g64[:, 64:128], c.w["k_norm_g"][l:l + 1, :].partition_broadcast(128), writes=[r_w])
        P.op("dve", lambda e: e.tensor_scalar(out=gqk[:, 0:768].rearrange("p (h d) -> p h d", d=64),
                                              in0=g64[:, 0:64].unsqueeze(1).to_broadcast([128, 12, 64]),
                                              scalar1=0.125, scalar2=None, op0=ALU.mult), [r_w], [r_w])
        P.op("dve", lambda e: e.tensor_copy(out=gqk[:, 768:1536].rearrange("p (h d) -> p h d", d=64),
                                            in_=g64[:, 64:128].unsqueeze(1).to_broadcast([128, 12, 64])), [r_w], [r_w])

        def mkpools(k):
            mk = lambda name, shape, dt, n=1: Pool2(st, nc, f"{name}_l{k}_", shape, dt, n)
            d = dict(junk=mk("junk", [128, D], BF16), ss=mk("ss", [128, 4], F32), h=mk("h", [128, D], BF16), hT=mk("hT", [128, D], BF16),
                     xt=mk("xt", [128, D], F32), proj=mk("proj", [128, P1W], F32), zsb=mk("zsb", [128, 1024], F32))
            if 0 in c.branches:
                d.update(sq=mk("sq", [128, 1536], F32), s24=mk("s24", [128, 72], F32), qkn=mk("qkn", [128, 1536], BF16),
                         qkT=mk("qkT", [64, 3072], BF16), vaug=mk("vaug", [128, 780], BF16))
            return d
        pools = [mkpools(k) for k in range(NL)]

        def body(i, k):
            pl = pools[k]
            banks, pbk = lane_banks(c, k)
            rows = slice(i * 128, (i + 1) * 128)
            xt, r_xt = pl["xt"].next()
            P.dma("sp", xt[:], x_src[rows, :], reads=([x_src_res[i]] if x_src_res else []), writes=[r_xt])
            h, r_h = rms_h(c, xt, r_xt, gbc, r_w, pl)
            yield
            hT, r_hT = pl["hT"].next()
            transpose8(c, h, r_h, hT, r_hT, pbk)
            yield
            proj, r_pj = pl["proj"].next()
            for blk in range(P1W // 512):
                ps, r_ps = banks[blk % 3]
                for kc in range(8):
                    P.op("pe", lambda e, kc=kc, blk=blk, ps=ps: e.matmul(out=ps[:, :], lhsT=hT[:, kc * 128:(kc + 1) * 128],
                                                                    rhs=W[:, kc, blk * 512:(blk + 1) * 512], start=(kc == 0), stop=(kc == 7)),
                         [r_hT, r_W], [r_ps])
                if blk % 2 == 0:
                    P.op("act", lambda e, blk=blk, ps=ps: e.copy(out=proj[:, blk * 512:(blk + 1) * 512], in_=ps[:, :]), [r_ps], [r_pj])
                else:
                    P.op("dve", lambda e, blk=blk, ps=ps: e.tensor_copy(out=proj[:, blk * 512:(blk + 1) * 512], in_=ps[:, :]), [r_ps], [r_pj])
                if blk % 2 == 1:
                    yield
            if 1 in c.branches:
                P.dma("pool", c.ub[16 + i * 128:16 + (i + 1) * 128, :], proj[:, C_UB:C_UB + 256], reads=[r_pj], writes=[c.r["ub"][i]])
            if 2 in c.branches:
                P.dma("pool", c.uvc[rows, :], proj[:, C_UVC:C_UVC + 512], reads=[r_pj], writes=[c.r["uvc"][i]])
            if 3 in c.branches:
                P.dma("pool", c.rkvl[1 + i * 128:1 + (i + 1) * 128, :], proj[:, C_RKV:C_RKV + 1024], reads=[r_pj], writes=[c.r["rkvl"][i]])
            zt, r_zt = pl["zsb"].next()
            for b, cz in enumerate((C_ZA, C_ZB, C_ZC, C_ZD)):
                P.op("act", lambda e, b=b, cz=cz: e.activation(out=zt[:, b * 256:(b + 1) * 256], in_=proj[:, cz:cz + 256], func=AF.Silu),
                     [r_pj], [r_zt])
            P.dma("pool", c.zs[rows, :], zt[:], reads=[r_zt], writes=[c.r["zs"][i]])
            yield
            if 0 in c.branches:
                sq, r_sq = pl["sq"].next()
                s, r_s = pl["s24"].next()
                qn, r_qn = pl["qkn"].next()
                v24 = lambda ap: ap.rearrange("p (h d) -> p h d", d=64)
                P.op("pool", lambda e: e.tensor_tensor(out=sq[:], in0=proj[:, 0:1536], in1=proj[:, 0:1536], op=ALU.mult), [r_pj], [r_sq])
                P.op("dve", lambda e: e.tensor_reduce(out=s[:, 0:24], in_=v24(sq[:]), axis=AX.X, op=ALU.add), [r_sq], [r_s])
                P.op("dve", lambda e: e.tensor_scalar(out=s[:, 24:48], in0=s[:, 0:24], scalar1=1.0 / 64, scalar2=EPS, op0=ALU.mult, op1=ALU.add), [r_s], [r_s])
                P.op("act", lambda e: e.activation(out=s[:, 0:24], in_=s[:, 24:48], func=AF.Sqrt), [r_s], [r_s])
                P.op("dve", lambda e: e.reciprocal(out=s[:, 48:72], in_=s[:, 0:24]), [r_s], [r_s])
                yield
                P.op("dve", lambda e: e.tensor_tensor(out=v24(sq[:]), in0=v24(proj[:, 0:1536]),
                                                      in1=s[:, 48:72].unsqueeze(2).to_broadcast([128, 24, 64]), op=ALU.mult), [r_pj, r_s], [r_sq])
                P.op("pool", lambda e: e.tensor_tensor(out=qn[:], in0=sq[:], in1=gqk[:], op=ALU.mult), [r_sq, r_w], [r_qn])
                va, r_va = pl["vaug"].next()
                P.op("act", lambda e: e.copy(out=va[:].rearrange("p (h d) -> p h d", d=65)[:, :, 0:64],
                                             in_=proj[:, C_V:C_V + 768].rearrange("p (h d) -> p h d", d=64)), [r_pj], [r_va])
                P.op("dve", lambda e: e.tensor_copy(out=va[:].rearrange("p (h d) -> p h d", d=65)[:, :, 64:65],
                                                    in_=c.validt[:, i:i + 1].unsqueeze(1).to_broadcast([128, 12, 1])), [c.r_const], [r_va])
                P.dma("pool", c.va[i], va[:], reads=[r_va], writes=[c.r["va"][i]])
                yield
                qT, r_qT = pl["qkT"].next()
                pb, r_pb = pbk
                for grp in range(3):
                    for j in range(8):
                        hh = grp * 8 + j
                        P.op("pe", lambda e, hh=hh, j=j: e.transpose(out=pb[0:64, j * 128:(j + 1) * 128], in_=qn[:, hh * 64:(hh + 1) * 64], identity=c.idb[:]),
                             [r_qn, c.r_const], [r_pb])
                    if grp % 2 == 0:
                        P.op("dve", lambda e, grp=grp: e.tensor_copy(out=qT[:, grp * 1024:(grp + 1) * 1024], in_=pb[0:64, :]), [r_pb], [r_qT])
                    else:
                        P.op("act", lambda e, grp=grp: e.copy(out=qT[:, grp * 1024:(grp + 1) * 1024], in_=pb[0:64, :]), [r_pb], [r_qT])
                    yield
                P.dma("pool", c.qT[i], qT[:, 0:1536], reads=[r_qT], writes=[c.r["qT"][i]])
                P.dma("pool", c.kT[i], qT[:, 1536:3072], reads=[r_qT], writes=[c.r["kT"][i]])

        run_lanes([list(range(k, NT, NL)) for k in range(NL)], body, skew=4)


def phase2c(c, l):
    from contextlib import ExitStack
    P, nc, NT = c.P, c.nc, c.NT
    with ExitStack() as st:
        sb = lambda n, s, d: st.enter_context(nc.sbuf_tensor(uname(n), list(s), d))
        sgw = sb("sgw", [128, 4, 128], BF16)
        sgn = sb("sgn", [128, 256], F32)
        sgb = sb("sgb", [128, 4], F32)
        r_w = Res("p2cw")
        P.dma("pool", sgw[:], c.w["sg_wT"][l].rearrange("g s t -> s g t"), writes=[r_w])
        P.dma("sp", sgn[:], c.w["sg_norm_g"][l:l + 1, :].partition_broadcast(128), writes=[r_w])
        P.dma("sp", sgb[:], c.w["sg_bT"][l], writes=[r_w])

        def mkpools(k):
            mk = lambda name, shape, dt, n=2: Pool2(st, nc, f"{name}_l{k}_", shape, dt, n)
            return dict(uv=mk("uv", [128, 512], F32), zc=mk("zc", [128, 256], F32), junk=mk("junkc", [128, 256], F32, 1), ss=mk("ssc", [128, 4], F32),
                        vn=mk("vn", [128, 256], BF16), sv=mk("sv", [128, 256], F32), ys=mk("ysc", [128, 256], BF16))
        pools = [mkpools(k) for k in range(6)]

        def body(i, k):
            pl = pools[k]
            banks = [(c.pf[k], c.r_pf[k])]
            rows = slice(i * 128, (i + 1) * 128)
            uv, r_uv = pl["uv"].next()
            zc, r_zc = pl["zc"].next()
            P.dma("sp", uv[:], c.uvc[rows, :], reads=[c.r["uvc"][i]], writes=[r_uv])
            P.dma("sp", zc[:], c.zs[rows, 512:768], reads=[c.r["zs"][i]], writes=[r_zc])
            junk, r_j = pl["junk"].next()
            ss, r_ss = pl["ss"].next()
            vn, r_vn = pl["vn"].next()
            sv, r_sv = pl["sv"].next()
            ysc, r_y = pl["ys"].next()
            P.op("act", lambda e: e.activation(out=junk[:], in_=uv[:, 256:512], func=AF.Square, accum_out=ss[:, 0:1]), [r_uv], [r_j, r_ss])
            P.op("dve", lambda e: e.tensor_scalar(out=ss[:, 1:2], in0=ss[:, 0:1], scalar1=1.0 / 256, scalar2=EPS, op0=ALU.mult, op1=ALU.add), [r_ss], [r_ss])
            yield
            P.op("act", lambda e: e.activation(out=ss[:, 2:3], in_=ss[:, 1:2], func=AF.Sqrt), [r_ss], [r_ss])
            P.op("dve", lambda e: e.reciprocal(out=ss[:, 3:4], in_=ss[:, 2:3]), [r_ss], [r_ss])
            P.op("dve", lambda e: e.scalar_tensor_tensor(out=vn[:], in0=uv[:, 256:512], scalar=ss[:, 3:4], in1=sgn[:], op0=ALU.mult, op1=ALU.mult),
                 [r_uv, r_ss, r_w], [r_vn])
            yield
            ps, r_ps = banks[0]
            for g in range(4):
                P.op("pe", lambda e, g=g: e.matmul(out=ps[:, g * 64:(g + 1) * 64], lhsT=sgw[:, g, :], rhs=vn[:, g * 64:(g + 1) * 64], start=True, stop=True),
                     [r_vn, r_w], [r_ps])
            P.op("dve", lambda e: e.tensor_tensor(out=sv[:].rearrange("p (g d) -> p g d", d=64), in0=ps[:, 0:256].rearrange("p (g d) -> p g d", d=64),
                                                  in1=sgb[:].unsqueeze(2).to_broadcast([128, 4, 64]), op=ALU.add), [r_ps, r_w], [r_sv])
            yield
            P.op("pool", lambda e: e.tensor_tensor(out=sv[:], in0=sv[:], in1=uv[:, 0:256], op=ALU.mult), [r_sv, r_uv], [r_sv])
            P.op("dve", lambda e: e.tensor_tensor(out=ysc[:], in0=sv[:], in1=zc[:], op=ALU.mult), [r_sv, r_zc], [r_y])
            P.dma("pool", c.ys[rows, 512:768], ysc[:], reads=[r_y], writes=[c.r["ys"][i]])

        run_lanes([list(range(k, NT, 6)) for k in range(6)], body, skew=1)


def phase3(c, l, x_src, x_src_res, x_dst, x_dst_res):
    from contextlib import ExitStack
    P, nc, NT = c.P, c.nc, c.NT
    with ExitStack() as st:
        sb = lambda n, s, d: st.enter_context(nc.sbuf_tensor(uname(n), list(s), d))
        Wg = sb("Wg", [128, 8, 4096], BF16)
        Wbr = sb("Wbr", [128, 8, 1024], BF16)
        Wo = sb("Wo", [128, 8, 1024], BF16)
        gbc = sb("gbc3", [128, D], F32)
        r_W = Res("W3")
        for kc in range(8):
            P.dma("pool", Wg[:, kc, :], c.w["w_in"][l, kc * 128:(kc + 1) * 128, P1W:PW], writes=[r_W])
        wbr_flat = c.w["w_branch"][l].rearrange("b c n -> (b c) n")
        for kc in range(8):
            P.dma("pool", Wbr[:, kc, :], wbr_flat[kc * 128:(kc + 1) * 128, :], writes=[r_W])
            P.dma("pool", Wo[:, kc, :], c.w["w_out"][l, kc * 128:(kc + 1) * 128, :], writes=[r_W])
        P.dma("sp", gbc[:], c.w["norm_g"][l:l + 1, :].partition_broadcast(128), writes=[r_W])

        def mkpools(k):
            mk = lambda name, shape, dt, n=1: Pool2(st, nc, f"{name}_l{k}_", shape, dt, n)
            return dict(junk=mk("junk3", [128, D], BF16), ss=mk("ss3", [128, 4], F32), h=mk("h3", [128, D], BF16), hT=mk("hT3", [128, D], BF16),
                        xt=mk("xt3", [128, D], F32), ys=mk("ys3", [128, D], BF16), ysT=mk("ysT3", [128, D], BF16), gs=mk("gs3", [128, 512], F32, 2),
                        tmp=mk("tmp3", [128, 512], F32, 2), acc=mk("acc3", [128, D], F32), m=mk("m3", [128, D], BF16), mT=mk("mT3", [128, D], BF16),
                        xo=mk("xo3", [128, D], F32))
        pools = [mkpools(k) for k in range(NL)]

        def body(i, k):
            pl = pools[k]
            banks, pbk = lane_banks(c, k)
            rows = slice(i * 128, (i + 1) * 128)
            xt, r_xt = pl["xt"].next()
            P.dma("sp", xt[:], x_src[rows, :], reads=([x_src_res[i]] if x_src_res else []), writes=[r_xt])
            ys, r_ys = pl["ys"].next()
            P.dma("sp", ys[:], c.ys[rows, :], reads=[c.r["ys"][i]], writes=[r_ys])
            h, r_h = rms_h(c, xt, r_xt, gbc, r_W, pl)
            yield
            hT, r_hT = pl["hT"].next()
            transpose8(c, h, r_h, hT, r_hT, pbk)
            yield
            ysT, r_ysT = pl["ysT"].next()
            transpose8(c, ys, r_ys, ysT, r_ysT, pbk, eng="dve")
            yield
            acc, r_acc = pl["acc"].next()
            for bi, b in enumerate(c.branches):
                for cb in range(2):
                    pg, r_pg = banks[0]
                    pbr, r_pbr = banks[1]
                    col = b * 1024 + cb * 512
                    for kc in range(8):
                        P.op("pe", lambda e, kc=kc, col=col: e.matmul(out=pg[:, :], lhsT=hT[:, kc * 128:(kc + 1) * 128], rhs=Wg[:, kc, col:col + 512],
                                                                      start=(kc == 0), stop=(kc == 7)), [r_hT, r_W], [r_pg])
                    gs, r_gs = pl["gs"].next()
                    P.op("act", lambda e, gs=gs: e.activation(out=gs[:], in_=pg[:, :], func=AF.Sigmoid), [r_pg], [r_gs])
                    for kk in range(2):
                        kc = 2 * b + kk
                        P.op("pe", lambda e, kc=kc, kk=kk, cb=cb: e.matmul(out=pbr[:, :], lhsT=ysT[:, kc * 128:(kc + 1) * 128],
                                                                             rhs=Wbr[:, kc, cb * 512:(cb + 1) * 512], start=(kk == 0), stop=(kk == 1)),
                             [r_ysT, r_W], [r_pbr])
                    if bi == 0:
                        P.op("dve", lambda e, cb=cb, gs=gs: e.tensor_tensor(out=acc[:, cb * 512:(cb + 1) * 512], in0=gs[:], in1=pbr[:, :], op=ALU.mult),
                             [r_gs, r_pbr], [r_acc])
                    else:
                        tmp, r_tmp = pl["tmp"].next()
                        P.op("dve", lambda e, gs=gs, tmp=tmp: e.tensor_tensor(out=tmp[:], in0=gs[:], in1=pbr[:, :], op=ALU.mult), [r_gs, r_pbr], [r_tmp])
                        P.op("pool", lambda e, cb=cb, tmp=tmp: e.tensor_tensor(out=acc[:, cb * 512:(cb + 1) * 512], in0=acc[:, cb * 512:(cb + 1) * 512], in1=tmp[:], op=ALU.add),
                             [r_tmp, r_acc], [r_acc])
                    yield
            m, r_m = pl["m"].next()
            mT, r_mT = pl["mT"].next()
            P.op("act", lambda e: e.copy(out=m[:], in_=acc[:]), [r_acc], [r_m])
            yield
            transpose8(c, m, r_m, mT, r_mT, pbk)
            yield
            xo, r_xo = pl["xo"].next()
            po, r_po = banks[2]
            for cb in range(2):
                for kc in range(8):
                    P.op("pe", lambda e, kc=kc, cb=cb: e.matmul(out=po[:, :], lhsT=mT[:, kc * 128:(kc + 1) * 128], rhs=Wo[:, kc, cb * 512:(cb + 1) * 512],
                                                                start=(kc == 0), stop=(kc == 7)), [r_mT, r_W], [r_po])
                P.op("dve", lambda e, cb=cb: e.tensor_tensor(out=xo[:, cb * 512:(cb + 1) * 512], in0=po[:, :], in1=xt[:, cb * 512:(cb + 1) * 512], op=ALU.add),
                     [r_po, r_xt], [r_xo])
                yield
            P.dma("pool", x_dst[rows, :], xo[:], reads=[r_xo], writes=[x_dst_res[i]])

        run_lanes([list(range(k, NT, NL)) for k in range(NL)], body, skew=5)


def phase2b(c, l):
    from contextlib import ExitStack
    P, nc, NT = c.P, c.nc, c.NT
    with ExitStack() as st:
        sb = lambda n, s, d: st.enter_context(nc.sbuf_tensor(uname(n), list(s), d))
        bandc = sb("bandc", [128, 512], F32)
        bandh = sb("bandh", [16, 512], F32)
        pw = sb("pw", [128, 2, 128], BF16)
        psc = sb("psc", [128, 256], F32)
        invc = sb("invc", [128, NT * 4], F32)
        r_w = Res("p2bw")
        P.dma("sp", bandc[:], c.bandc, writes=[r_w])
        P.dma("sp", bandh[:], c.bandh, writes=[r_w])
        P.dma("sp", psc[:], c.w["pool_scale"][l:l + 1, :].partition_broadcast(128), writes=[r_w])
        P.dma("sp", invc[:], c.invc, writes=[r_w])
        P.op("pool", lambda e: e.memset(pw[:], 0.0), [], [r_w])
        for g in range(4):
            j, gl = g // 2, g % 2
            P.dma("pool", pw[gl * 64:(gl + 1) * 64, j, gl * 64:(gl + 1) * 64], c.w["pool_w"][l, g], writes=[r_w])

        def mkpools(k):
            mk = lambda name, shape, dt, n=2: Pool2(st, nc, f"{name}_l{k}_", shape, dt, n)
            return dict(u=mk("ub", [128, 256], F32), uh=mk("uh", [16, 256], F32), zb=mk("zb", [128, 256], F32), d=mk("dpool", [128, 256], BF16),
                        dT=mk("dT", [128, 256], BF16), y=mk("ybf", [128, 256], F32), yo=mk("ybo", [128, 256], BF16))
        pools = [mkpools(k) for k in range(3)]

        def body(i, k):
            pl = pools[k]
            banks = [(c.pf[2 * k], c.r_pf[2 * k]), (c.pf[2 * k + 1], c.r_pf[2 * k + 1])]
            pbk = (c.pbf[k % 2], c.r_pbf[k % 2])
            rows = slice(i * 128, (i + 1) * 128)
            u, r_u = pl["u"].next()
            uh, r_uh = pl["uh"].next()
            zb, r_zb = pl["zb"].next()
            d, r_d = pl["d"].next()
            dT, r_dT = pl["dT"].next()
            y, r_y = pl["y"].next()
            yo, r_yo = pl["yo"].next()
            nb = [c.r["ub"][j] for j in (i - 1, i, i + 1) if 0 <= j < NT] + [c.r_pad]
            P.dma("sp", u[:], c.ub[16 + i * 128:16 + (i + 1) * 128, :], reads=[c.r["ub"][i]], writes=[r_u])
            P.dma("sp", uh[0:8, :], c.ub[16 + i * 128 - 8:16 + i * 128, :], reads=nb, writes=[r_uh])
            P.dma("sp", uh[8:16, :], c.ub[16 + (i + 1) * 128:16 + (i + 1) * 128 + 8, :], reads=nb, writes=[r_uh])
            P.dma("sp", zb[:], c.zs[rows, 256:512], reads=[c.r["zs"][i]], writes=[r_zb])
            yield
            ps, r_ps = banks[0]
            for g in range(4):
                P.op("pe", lambda e, g=g: e.matmul(out=ps[:, g * 64:(g + 1) * 64], lhsT=bandc[:, g * 128:(g + 1) * 128], rhs=u[:, g * 64:(g + 1) * 64],
                                                   start=True, stop=False), [r_u, r_w], [r_ps])
                P.op("pe", lambda e, g=g: e.matmul(out=ps[:, g * 64:(g + 1) * 64], lhsT=bandh[:, g * 128:(g + 1) * 128], rhs=uh[:, g * 64:(g + 1) * 64],
                                                   start=False, stop=True), [r_uh, r_w], [r_ps])
            for g in range(4):
                P.op("dve", lambda e, g=g: e.scalar_tensor_tensor(out=d[:, g * 64:(g + 1) * 64], in0=ps[:, g * 64:(g + 1) * 64],
                                                                  scalar=invc[:, i * 4 + g:i * 4 + g + 1], in1=u[:, g * 64:(g + 1) * 64],
                                                                  op0=ALU.mult, op1=ALU.subtract), [r_ps, r_u, r_w], [r_d])
            yield
            transpose8(c, d, r_d, dT, r_dT, pbk, n=2)
            yield
            ps2, r_ps2 = banks[1]
            for j in range(2):
                P.op("pe", lambda e, j=j: e.matmul(out=ps2[:, j * 128:(j + 1) * 128], lhsT=dT[:, j * 128:(j + 1) * 128], rhs=pw[:, j, :], start=True, stop=True),
                     [r_dT, r_w], [r_ps2])
            P.op("dve", lambda e: e.tensor_tensor(out=y[:], in0=ps2[:, 0:256], in1=psc[:], op=ALU.mult), [r_ps2, r_w], [r_y])
            yield
            P.op("pool", lambda e: e.tensor_tensor(out=yo[:], in0=y[:], in1=zb[:], op=ALU.mult), [r_y, r_zb], [r_yo])
            P.dma("pool", c.ys[rows, 256:512], yo[:], reads=[r_yo], writes=[c.r["ys"][i]])

        run_lanes([list(range(k, NT, 3)) for k in range(3)], body, skew=2)


def phase2a(c, l):
    from contextlib import ExitStack
    P, nc, NT = c.P, c.nc, c.NT
    use_sb = (NT % 16 == 0)
    NG = 2 if use_sb else 3
    if use_sb:
        phase2a_g3(c, l)
        barrier(c)
    with ExitStack() as st:
        sb = lambda n, s, d: st.enter_context(nc.sbuf_tensor(uname(n), list(s), d))
        masks = sb("amask", [128, 25, 512], BF16)
        r_w = Res("p2aw")
        for ci in range(25):
            P.dma("pool", masks[:, ci, :], c.amask[ci], writes=[r_w])
        depth = [2 * w + 2 for w in ATT_W]
        kring = [Pool2(st, nc, f"kr{g}_", [64, 512], BF16, depth[g]) for g in range(NG)]
        vring = [Pool2(st, nc, f"vr{g}_", [128, 260], BF16, depth[g]) for g in range(NG)]
        nd_ = Pool2(st, nc, "nd3t", [128, 260], F32, 2)
        loaded = [-1, -1, -1]
        q_ = Pool2(st, nc, "qTa", [64, 1536], BF16, 2)
        za_ = Pool2(st, nc, "za", [128, 256], F32, 2)
        e_ = Pool2(st, nc, "eexp", [128, 512], BF16, 3)
        p_ = Pool2(st, nc, "pexp", [128, 512], BF16, 6)
        dn_ = Pool2(st, nc, "den", [128, 8], F32, 2)
        ya_ = Pool2(st, nc, "yaf", [128, 256], F32, 2)
        yo_ = Pool2(st, nc, "yao", [128, 256], BF16, 2)
        mcol = []
        ci = 0
        for g in range(3):
            mcol.append({coff: ci + k for k, coff in enumerate(range(-ATT_W[g], ATT_W[g] + 1))})
            ci += 2 * ATT_W[g] + 1

        def ensure(g, upto):
            while loaded[g] < min(upto, NT - 1):
                j = loaded[g] + 1
                kt, r_kt = kring[g].t[j % depth[g]], kring[g].r[j % depth[g]]
                vt, r_vt = vring[g].t[j % depth[g]], vring[g].r[j % depth[g]]
                P.dma("sp", kt[:], c.kT[j][:, g * 512:(g + 1) * 512], reads=[c.r["kT"][j]], writes=[r_kt])
                P.dma("sp", vt[:], c.va[j][:, g * 260:(g + 1) * 260], reads=[c.r["va"][j]], writes=[r_vt])
                loaded[g] = j

        LOOK = 3
        tiles = {}

        def tile_begin(i):
            rows = slice(i * 128, (i + 1) * 128)
            for g in range(NG):
                ensure(g, i + ATT_W[g])
            qT, r_q = q_.next()
            za, r_za = za_.next()
            P.dma("sp", qT[:], c.qT[i], reads=[c.r["qT"][i]], writes=[r_q])
            P.dma("sp", za[:], c.zs[rows, 0:256], reads=[c.r["zs"][i]], writes=[r_za])
            ndt, r_ndt = (None, None)
            if use_sb:
                ndt, r_ndt = nd_.next()
                P.dma("sp", ndt[:], c.nd3[rows, :], reads=[c.r["nd3"][i // 16]], writes=[r_ndt])
            chunks = [(g, coff) for g in range(NG) for coff in range(-ATT_W[g], ATT_W[g] + 1) if 0 <= i + coff < NT]
            tiles[i] = dict(qT=qT, r_q=r_q, za=za, r_za=r_za, n=len(chunks), pso=c.pf[4 + i % 2], r_pso=c.r_pf[4 + i % 2], ndt=ndt, r_ndt=r_ndt)
            return chunks

        def stage_qk(i, idx, g, coff, seq):
            t = tiles[i]
            qT, r_q = t["qT"], t["r_q"]
            j = i + coff
            kt, r_kt = kring[g].t[j % depth[g]], kring[g].r[j % depth[g]]
            pss, r_pss = c.pf[seq % 4], c.r_pf[seq % 4]
            for h in range(4):
                P.op("pe", lambda e, h=h: e.matmul(out=pss[:, h * 128:(h + 1) * 128], lhsT=kt[:, h * 128:(h + 1) * 128],
                                                   rhs=qT[:, (g * 4 + h) * 128:(g * 4 + h + 1) * 128], start=True, stop=True),
                     [r_kt, r_q], [r_pss])
            ex, r_ex = e_.next()
            pp, r_pp = p_.next()
            P.op("act", lambda e: e.activation(out=ex[:], in_=pss[:, :], func=AF.Exp), [r_pss], [r_ex])
            mc = mcol[g][coff]
            P.op("dve", lambda e: e.tensor_tensor(out=pp[:], in0=ex[:], in1=masks[:, mc, :], op=ALU.mult), [r_ex, r_w], [r_pp])
            return pp, r_pp

        def stage_pv(i, idx, g, coff, pp, r_pp):
            t = tiles[i]
            pso, r_pso = t["pso"], t["r_pso"]
            j = i + coff
            vt, r_vt = vring[g].t[j % depth[g]], vring[g].r[j % depth[g]]
            n = t["n"]
            for h in range(4):
                P.op("pe", lambda e, h=h: e.matmul(out=pso[:, h * 65:(h + 1) * 65], lhsT=pp[:, h * 128:(h + 1) * 128],
                                                   rhs=vt[:, h * 65:(h + 1) * 65], start=(idx == 0 and h == 0), stop=(idx == n - 1), skip_group_check=True),
                     [r_pp, r_vt], [r_pso])
            if idx == n - 1:
                tile_end(i)

        def tile_end(i):
            rows = slice(i * 128, (i + 1) * 128)
            t = tiles.pop(i)
            pso, r_pso, za, r_za = t["pso"], t["r_pso"], t["za"], t["r_za"]
            dn, r_dn = dn_.next()
            ya, r_ya = ya_.next()
            yo, r_yo = yo_.next()
            if use_sb:
                ndt, r_ndt = t["ndt"], t["r_ndt"]
                P.op("dve", lambda e: e.tensor_tensor(out=ndt[:], in0=pso[:, 0:260], in1=ndt[:], op=ALU.add), [r_pso, r_ndt], [r_ndt])
                pv = ndt[:].rearrange("p (h d) -> p h d", d=65)
                r_pso = r_ndt
            else:
                pv = pso[:, 0:260].rearrange("p (h d) -> p h d", d=65)
            P.op("dve", lambda e: e.tensor_scalar(out=dn[:, 0:4].unsqueeze(2), in0=pv[:, :, 64:65], scalar1=1e-30, scalar2=None, op0=ALU.max), [r_pso], [r_dn])
            P.op("dve", lambda e: e.reciprocal(out=dn[:, 4:8], in_=dn[:, 0:4]), [r_dn], [r_dn])
            P.op("dve", lambda e: e.tensor_tensor(out=ya[:].rearrange("p (h d) -> p h d", d=64), in0=pv[:, :, 0:64],
                                                  in1=dn[:, 4:8].unsqueeze(2).to_broadcast([128, 4, 64]), op=ALU.mult), [r_pso, r_dn], [r_ya])
            P.op("pool", lambda e: e.tensor_tensor(out=yo[:], in0=ya[:], in1=za[:], op=ALU.mult), [r_ya, r_za], [r_yo])
            P.dma("pool", c.ys[rows, 0:256], yo[:], reads=[r_yo], writes=[c.r["ys"][i]])

        def stream():
            for i in range(NT):
                first = True
                chunks = None
                for idx in range(10 ** 9):
                    if first:
                        chunks = tile_begin(i)
                        first = False
                    if idx >= len(chunks):
                        break
                    g, coff = chunks[idx]
                    yield (i, idx, g, coff)

        pend = []
        for seq, (i, idx, g, coff) in enumerate(stream()):
            pp, r_pp = stage_qk(i, idx, g, coff, seq)
            pend.append((i, idx, g, coff, pp, r_pp))
            if len(pend) > LOOK:
                stage_pv(*pend.pop(0))
        while pend:
            stage_pv(*pend.pop(0))


def phase2a_g3(c, l):
    from contextlib import ExitStack
    P, nc, NT, S = c.P, c.nc, c.NT, c.S
    NSB = NT // 16
    vflat = c.va.rearrange("t p c -> (t p) c").rearrange("(m r) c -> r m c", r=16)
    ndv = c.nd3.rearrange("(m r) c -> r m c", r=16)
    with ExitStack() as st:
        sb_ = lambda n, s_, d: st.enter_context(nc.sbuf_tensor(uname(n), list(s_), d))
        masks = sb_("amask3", [128, 3, 512], BF16)
        r_w = Res("p2a3w")
        for ci in range(3):
            P.dma("pool", masks[:, ci, :], c.amask3[ci], writes=[r_w])
        stage_ = Pool2(st, nc, "g3stage", [64, 16 * 512], BF16, 2)
        kp_ = Pool2(st, nc, "g3kp", [64, 4 * 16 * 128], BF16, 3)
        qp_ = Pool2(st, nc, "g3qp", [64, 4 * 16 * 128], BF16, 2)
        vr_ = Pool2(st, nc, "g3v", [128, 260], BF16, 48)
        e_ = Pool2(st, nc, "g3e", [128, 512], BF16, 3)
        p_ = Pool2(st, nc, "g3p", [128, 512], BF16, 6)
        o_ = Pool2(st, nc, "g3o", [128, 260], F32, 3)
        kload = {}

        def permute(src_scr, sbi, dst, r_dst, col0):
            stg, r_stg = stage_.next()
            P.dma("sp", stg[:].rearrange("p (t c) -> p t c", c=512), src_scr[sbi * 16:(sbi + 1) * 16, :, col0:col0 + 512].rearrange("t p c -> p t c"),
                  reads=[c.r["qT" if src_scr is c.qT else "kT"][j] for j in range(sbi * 16, (sbi + 1) * 16)], writes=[r_stg])
            sv = stg[:].rearrange("p (t h pp r) -> p t h pp r", t=16, h=4, pp=8, r=16)
            dv = dst[:].rearrange("p (h r t pp) -> p h r t pp", h=4, r=16, t=16, pp=8)
            for h in range(4):
                eng = "act" if h % 2 == 0 else "pool"
                if eng == "act":
                    P.op("act", lambda e, h=h: e.copy(out=dv[:, h], in_=sv[:, :, h].rearrange("p t pp r -> p r t pp")), [r_stg], [r_dst])
                else:
                    P.op("pool", lambda e, h=h: e.tensor_copy(out=dv[:, h], in_=sv[:, :, h].rearrange("p t pp r -> p r t pp")), [r_stg], [r_dst])

        def ensure_k(sbi):
            if sbi in kload or not (0 <= sbi < NSB):
                return
            kp, r_kp = kp_.t[sbi % 3], kp_.r[sbi % 3]
            permute(c.kT, sbi, kp, r_kp, 1024)
            for r in range(16):
                vt, r_vt = vr_.t[(sbi % 3) * 16 + r], vr_.r[(sbi % 3) * 16 + r]
                P.dma("sp", vt[:], vflat[r, sbi * 128:(sbi + 1) * 128, 520:780], reads=[c.r["va"][j] for j in range(sbi * 16, (sbi + 1) * 16)], writes=[r_vt])
            kload[sbi] = True

        LOOK = 3
        units = []
        for sbi in range(NSB):
            for r in range(16):
                cl = [cf for cf in (-1, 0, 1) if 0 <= sbi + cf < NSB]
                for idx, cf in enumerate(cl):
                    units.append((sbi, r, cf, idx, len(cl)))
        qcur = {}
        pend = []

        def stage_pv(sbi, r, cf, idx, n, pp, r_pp, seq):
            pso, r_pso = c.pf[4 + (sbi * 16 + r) % 2], c.r_pf[4 + (sbi * 16 + r) % 2]
            sk = sbi + cf
            vt, r_vt = vr_.t[(sk % 3) * 16 + r], vr_.r[(sk % 3) * 16 + r]
            for h in range(4):
                P.op("pe", lambda e, h=h: e.matmul(out=pso[:, h * 65:(h + 1) * 65], lhsT=pp[:, h * 128:(h + 1) * 128], rhs=vt[:, h * 65:(h + 1) * 65],
                                                   start=(idx == 0 and h == 0), stop=(idx == n - 1), skip_group_check=True), [r_pp, r_vt], [r_pso])
            if idx == n - 1:
                ot, r_ot = o_.next()
                P.op("act", lambda e: e.copy(out=ot[:], in_=pso[:, 0:260]), [r_pso], [r_ot])
                P.dma("pool", ndv[r, sbi * 128:(sbi + 1) * 128, :], ot[:], reads=[r_ot], writes=[c.r["nd3"][sbi]])

        for seq, (sbi, r, cf, idx, n) in enumerate(units):
            if sbi not in qcur:
                while pend:
                    stage_pv(*pend.pop(0))
                for s2 in (sbi - 1, sbi, sbi + 1):
                    ensure_k(s2)
                qp, r_qp = qp_.next()
                permute(c.qT, sbi, qp, r_qp, 1024)
                qcur.clear()
                qcur[sbi] = (qp, r_qp)
            qp, r_qp = qcur[sbi]
            sk = sbi + cf
            kp, r_kp = kp_.t[sk % 3], kp_.r[sk % 3]
            pss, r_pss = c.pf[seq % 4], c.r_pf[seq % 4]
            for h in range(4):
                o0 = (h * 16 + r) * 128
                P.op("pe", lambda e, h=h, o0=o0, kp=kp, qp=qp, pss=pss: e.matmul(out=pss[:, h * 128:(h + 1) * 128], lhsT=kp[:, o0:o0 + 128], rhs=qp[:, o0:o0 + 128], start=True, stop=True),
                     [r_kp, r_qp], [r_pss])
            ex, r_ex = e_.next()
            pp, r_pp = p_.next()
            P.op("act", lambda e, ex=ex, pss=pss: e.activation(out=ex[:], in_=pss[:, :], func=AF.Exp), [r_pss], [r_ex])
            P.op("dve", lambda e, ex=ex, pp=pp, cf=cf: e.tensor_tensor(out=pp[:], in0=ex[:], in1=masks[:, cf + 1, :], op=ALU.mult), [r_ex, r_w], [r_pp])
            pend.append((sbi, r, cf, idx, n, pp, r_pp, seq))
            if len(pend) > LOOK:
                stage_pv(*pend.pop(0))
        while pend:
            stage_pv(*pend.pop(0))


def phase2d(c, l):
    from contextlib import ExitStack
    P, nc, NT = c.P, c.nc, c.NT
    NEG = -float(np.exp(-0.5))
    def setup_dir(d, st):
        if True:
            sb = lambda n, s, dt: st.enter_context(nc.sbuf_tensor(uname(n), list(s), dt))
            r_w = Res("p2dw")
            tri = sb("tri", [128, 642], F32)
            P.dma("sp", tri[:], c.tri, writes=[r_w])
            incl = tri[:, 128 * d:128 * d + 128]
            ones_m = tri[:, 512:640]
            ones_c = tri[:, 640:641]
            mask4 = sb("mask4", [128, 512], F32)
            maskT4 = sb("maskT4", [128, 512], F32)
            for q in range(4):
                src = tri[:, 256 + 128 * d:384 + 128 * d] if q % 2 == 0 else incl
                P.op("dve", lambda e, q=q, src=src: e.tensor_copy(out=mask4[:, q * 128:(q + 1) * 128], in_=src), [r_w], [r_w])
                P.op("dve", lambda e, q=q: e.tensor_copy(out=maskT4[:, q * 128:(q + 1) * 128], in_=tri[:, 256 + 128 * (1 - d):384 + 128 * (1 - d)]), [r_w], [r_w])
            identB = sb("identB", [64, 4, 64], F32)
            P.op("dve", lambda e: e.tensor_copy(out=identB[:], in_=c.idf[0:64, 0:64].unsqueeze(1).to_broadcast([64, 4, 64])), [c.r_const], [r_w])
            mu_r = sb("mu_r", [128, 768], F32)
            mu_l = sb("mu_l", [128, 128], F32)
            bias_wa = sb("bias_wa", [128, 512], F32)
            kkv = sb("kkv", [128, 256], F32)
            kav = sb("kav", [128, 256], F32)
            rkp = sb("rkp", [128, 256], F32)
            lng = sb("lng", [128, 256], F32)
            lnb = sb("lnb", [128, 256], F32)
            Wud = sb("Wud", [128, 512], BF16)
            bc = lambda ap: ap.partition_broadcast(128)
            P.dma("sp", mu_r[:], bc(c.w["mu_rkv"][l, d:d + 1, :]), writes=[r_w])
            P.dma("sp", mu_l[:], bc(c.w["mu_lat"][l, d:d + 1, :]), writes=[r_w])
            P.dma("sp", bias_wa[:, 0:256], bc(c.w["w0"][l, d:d + 1, :]), writes=[r_w])
            P.dma("sp", bias_wa[:, 256:512], bc(c.w["a0"][l, d:d + 1, :]), writes=[r_w])
            P.dma("sp", kkv[:], bc(c.w["k_k"][l, d:d + 1, :]), writes=[r_w])
            P.dma("sp", kav[:], bc(c.w["k_a"][l, d:d + 1, :]), writes=[r_w])
            P.dma("sp", rkp[:], bc(c.w["r_k"][l, d:d + 1, :]), writes=[r_w])
            P.dma("sp", lng[:], bc(c.w["ln_g"][l:l + 1, :]), writes=[r_w])
            P.dma("sp", lnb[:], bc(c.w["ln_b"][l:l + 1, :]), writes=[r_w])
            P.op("pool", lambda e: e.memset(Wud[:], 0.0), [], [r_w])
            P.dma("pool", Wud[0:64, 0:256], c.w["w_up"][l, d], writes=[r_w])
            P.dma("pool", Wud[64:128, 256:512], c.w["a_up"][l, d], writes=[r_w])
            ST = Pool2(st, nc, f"ST_d{d}_", [64, 256], F32, 2)
            st0, r_st0 = ST.next()
            P.op("pool", lambda e: e.memset(st0[:], 0.0), [], [r_st0])
            state = [st0, r_st0]

            def mk(name, shape, dt, n=1):
                return Pool2(st, nc, f"{name}_d{d}_", shape, dt, n)
            cur_, sh_ = mk("cur", [128, 1024], F32, 2), mk("sh", [128, 1024], F32, 2)
            xr_, xl_, tl_, tlT_ = mk("xr", [128, 768], F32), mk("xl", [128, 128], F32), mk("tl", [128, 128], BF16), mk("tlT", [128, 128], BF16)
            sg_, lw_ = mk("sg", [128, 512], F32), mk("logw", [128, 256], F32)
            kk_, sq_, s4_, k2_, bon_, beta_ = mk("kk", [128, 256], F32), mk("sqd", [128, 256], F32), mk("s4", [128, 32], F32), mk("k2", [128, 256], F32), mk("bon", [128, 256], F32), mk("beta", [128, 256], F32)
            cum_, ex_, tmp_ = mk("cum", [128, 256], F32), mk("exps", [128, 1024], F32), mk("tmpd", [128, 256], F32, 2)
            opb_ = mk("opb", [128, 7, 256], BF16)
            fTA_, fTB_ = mk("fTA", [64, 1024], BF16), mk("fTB", [64, 1024], BF16)
            AB_, AK_, NTa_ = mk("AB", [128, 1024], BF16), mk("AK", [128, 1024], BF16), mk("NTa", [128, 512], BF16, 3)
            Nn_ = mk("Nn", [128, 512], BF16, 3)
            Z_ = mk("Z", [128, 512], BF16, 3)
            gcol_, Q_, Pm_, Dm_ = mk("gcol", [64, 4], F32), mk("Q", [64, 512], F32), mk("Pm", [64, 256], F32), mk("Dm", [64, 256], F32)
            osb_, on_, yo_ = mk("osb", [128, 256], F32), mk("on", [128, 256], F32), mk("yod", [128, 256], BF16)
            banks, pbk = lane_banks(c, d)
            bank_i = [0]

            def nb():
                bank_i[0] = (bank_i[0] + 1) % 2
                return banks[bank_i[0]]

            def body(i):
                rows = slice(i * 128, (i + 1) * 128)
                v3 = lambda ap, dd=64: ap.rearrange("p (h d) -> p h d", d=dd)
                dbg_on = False

                def dbg(name, t, shape, dt, rs):
                    if dbg_on:
                        o = nc.dram_tensor("dbg_" + name, list(shape), dt, kind="ExternalOutput").ap()
                        P.dma("pool", o, t, reads=rs, writes=[Res()])
                cur, r_cur = cur_.next()
                sh, r_sh = sh_.next()
                nbr = [c.r["rkvl"][j] for j in (i - 1, i, i + 1) if 0 <= j < NT] + [c.r_pad]
                P.dma("sp", cur[:], c.rkvl[1 + i * 128:1 + (i + 1) * 128, :], reads=[c.r["rkvl"][i]], writes=[r_cur])
                so = 0 if d == 0 else 2
                P.dma("sp", sh[:], c.rkvl[so + i * 128:so + (i + 1) * 128, :], reads=nbr, writes=[r_sh])
                xr, r_xr = xr_.next()
                xl, r_xl = xl_.next()
                P.op("pool", lambda e: e.tensor_tensor(out=xr[:], in0=sh[:, 0:768], in1=cur[:, 0:768], op=ALU.subtract), [r_sh, r_cur], [r_xr])
                P.op("dve", lambda e: e.tensor_tensor(out=xr[:], in0=xr[:], in1=mu_r[:], op=ALU.mult), [r_xr, r_w], [r_xr])
                P.op("pool", lambda e: e.tensor_tensor(out=xr[:], in0=xr[:], in1=cur[:, 0:768], op=ALU.add), [r_xr, r_cur], [r_xr])
                lo = 768 + 128 * d
                P.op("dve", lambda e: e.tensor_tensor(out=xl[:], in0=sh[:, lo:lo + 128], in1=cur[:, lo:lo + 128], op=ALU.subtract), [r_sh, r_cur], [r_xl])
                P.op("dve", lambda e: e.tensor_tensor(out=xl[:], in0=xl[:], in1=mu_l[:], op=ALU.mult), [r_xl, r_w], [r_xl])
                P.op("dve", lambda e: e.tensor_tensor(out=xl[:], in0=xl[:], in1=cur[:, lo:lo + 128], op=ALU.add), [r_xl, r_cur], [r_xl])
                yield
                r_, k_, v_ = xr[:, 0:256], xr[:, 256:512], xr[:, 512:768]
                tl, r_tl = tl_.next()
                tlT, r_tlT = tlT_.next()
                P.op("act", lambda e: e.activation(out=tl[:, 0:64], in_=xl[:, 0:64], func=AF.Tanh), [r_xl], [r_tl])
                P.op("act", lambda e: e.copy(out=tl[:, 64:128], in_=xl[:, 64:128]), [r_xl], [r_tl])
                pb, r_pb = pbk
                P.op("pe", lambda e: e.transpose(out=pb[:, 0:128], in_=tl[:], identity=c.idb[:]), [r_tl, c.r_const], [r_pb])
                P.op("act", lambda e: e.copy(out=tlT[:], in_=pb[:, 0:128]), [r_pb], [r_tlT])
                yield
                ps, r_ps = nb()
                P.op("pe", lambda e: e.matmul(out=ps[:, :], lhsT=tlT[:], rhs=Wud[:], start=True, stop=True), [r_tlT, r_w], [r_ps])
                sg, r_sg = sg_.next()
                lw, r_lw = lw_.next()
                P.op("dve", lambda e: e.tensor_tensor(out=sg[:], in0=ps[:, :], in1=bias_wa[:], op=ALU.add), [r_ps, r_w], [r_sg])
                P.op("act", lambda e: e.activation(out=sg[:], in_=sg[:], func=AF.Sigmoid), [r_sg], [r_sg])
                P.op("act", lambda e: e.mul(out=lw[:], in_=sg[:, 0:256], mul=NEG), [r_sg], [r_lw])
                yield
                a_ = sg[:, 256:512]
                kk, r_kk = kk_.next()
                sq, r_sq = sq_.next()
                s4, r_s4 = s4_.next()
                P.op("dve", lambda e: e.tensor_tensor(out=kk[:], in0=k_, in1=kkv[:], op=ALU.mult), [r_xr, r_w], [r_kk])
                P.op("pool", lambda e: e.tensor_tensor(out=sq[:], in0=kk[:], in1=kk[:], op=ALU.mult), [r_kk], [r_sq])
                P.op("dve", lambda e: e.tensor_reduce(out=s4[:, 0:4], in_=v3(sq[:]), axis=AX.X, op=ALU.add), [r_sq], [r_s4])
                P.op("dve", lambda e: e.tensor_scalar(out=s4[:, 4:8], in0=s4[:, 0:4], scalar1=1e-12, scalar2=None, op0=ALU.add), [r_s4], [r_s4])
                P.op("act", lambda e: e.activation(out=s4[:, 0:4], in_=s4[:, 4:8], func=AF.Sqrt), [r_s4], [r_s4])
                P.op("dve", lambda e: e.reciprocal(out=s4[:, 8:12], in_=s4[:, 0:4]), [r_s4], [r_s4])
                P.op("dve", lambda e: e.tensor_tensor(out=v3(kk[:]), in0=v3(kk[:]), in1=s4[:, 8:12].unsqueeze(2).to_broadcast([128, 4, 64]), op=ALU.mult), [r_kk, r_s4], [r_kk])
                k2, r_k2 = k2_.next()
                P.op("dve", lambda e: e.scalar_tensor_tensor(out=k2[:], in0=a_, scalar=-1.0, in1=kav[:], op0=ALU.add, op1=ALU.mult), [r_sg, r_w], [r_k2])
                P.op("dve", lambda e: e.scalar_tensor_tensor(out=k2[:], in0=k2[:], scalar=1.0, in1=k_, op0=ALU.add, op1=ALU.mult), [r_k2, r_xr], [r_k2])
                bon, r_bon = bon_.next()
                P.op("pool", lambda e: e.tensor_tensor(out=bon[:], in0=r_, in1=k2[:], op=ALU.mult), [r_xr, r_k2], [r_bon])
                P.op("pool", lambda e: e.tensor_tensor(out=bon[:], in0=bon[:], in1=rkp[:], op=ALU.mult), [r_bon, r_w], [r_bon])
                P.op("dve", lambda e: e.tensor_reduce(out=s4[:, 12:16], in_=v3(bon[:]), axis=AX.X, op=ALU.add), [r_bon], [r_s4])
                P.op("dve", lambda e: e.tensor_tensor(out=v3(bon[:]), in0=v3(v_), in1=s4[:, 12:16].unsqueeze(2).to_broadcast([128, 4, 64]), op=ALU.mult), [r_xr, r_s4], [r_bon])
                beta, r_beta = beta_.next()
                P.op("pool", lambda e: e.tensor_tensor(out=beta[:], in0=kk[:], in1=a_, op=ALU.mult), [r_kk, r_sg], [r_beta])
                yield
                pc, r_pc = nb()
                P.op("pe", lambda e: e.matmul(out=pc[:, 0:256], lhsT=incl, rhs=lw[:], start=True, stop=True), [r_lw, r_w], [r_pc])
                P.op("pe", lambda e: e.matmul(out=pc[:, 256:512], lhsT=ones_m, rhs=lw[:], start=True, stop=True), [r_lw, r_w], [r_pc])
                pg, r_pg = nb()
                for h in range(4):
                    P.op("pe", lambda e, h=h: e.matmul(out=pg[0:64, h:h + 1], lhsT=lw[:, h * 64:(h + 1) * 64], rhs=ones_c, start=True, stop=True), [r_lw, r_w], [r_pg])
                gcol, r_gcol = gcol_.next()
                P.op("act", lambda e: e.activation(out=gcol[:], in_=pg[0:64, 0:4], func=AF.Exp), [r_pg], [r_gcol])
                yield
                cum, r_cum = cum_.next()
                ex, r_ex = ex_.next()
                t3, r_t3 = tmp_.next()
                t4, r_t4 = tmp_.next()
                ecum, encum, eexc, edec = ex[:, 0:256], ex[:, 256:512], ex[:, 512:768], ex[:, 768:1024]
                P.op("act", lambda e: e.copy(out=cum[:], in_=pc[:, 0:256]), [r_pc], [r_cum])
                P.op("act", lambda e: e.activation(out=ecum, in_=pc[:, 0:256], func=AF.Exp), [r_pc], [r_ex])
                P.op("act", lambda e: e.activation(out=encum, in_=pc[:, 0:256], func=AF.Exp, scale=-1.0), [r_pc], [r_ex])
                P.op("dve", lambda e: e.tensor_tensor(out=t3[:], in0=cum[:], in1=lw[:], op=ALU.subtract), [r_cum, r_lw], [r_t3])
                P.op("act", lambda e: e.activation(out=eexc, in_=t3[:], func=AF.Exp), [r_t3], [r_ex])
                P.op("dve", lambda e: e.tensor_tensor(out=t4[:], in0=pc[:, 256:512], in1=cum[:], op=ALU.subtract), [r_pc, r_cum], [r_t4])
                P.op("act", lambda e: e.activation(out=edec, in_=t4[:], func=AF.Exp), [r_t4], [r_ex])
                yield
                opb, r_opb = opb_.next()
                P.op("dve", lambda e: e.tensor_tensor(out=opb[:, 0, :], in0=r_, in1=ecum, op=ALU.mult), [r_xr, r_ex], [r_opb])
                P.op("pool", lambda e: e.tensor_tensor(out=opb[:, 1, :], in0=k2[:], in1=encum, op=ALU.mult), [r_k2, r_ex], [r_opb])
                P.op("dve", lambda e: e.tensor_tensor(out=opb[:, 2, :], in0=beta[:], in1=encum, op=ALU.mult), [r_beta, r_ex], [r_opb])
                P.op("dve", lambda e: e.scalar_tensor_tensor(out=opb[:, 3, :], in0=kk[:], scalar=-1.0, in1=eexc, op0=ALU.mult, op1=ALU.mult), [r_kk, r_ex], [r_opb])
                P.op("pool", lambda e: e.tensor_tensor(out=opb[:, 4, :], in0=k2[:], in1=edec, op=ALU.mult), [r_k2, r_ex], [r_opb])
                P.op("pool", lambda e: e.tensor_tensor(out=opb[:, 5, :], in0=beta[:], in1=edec, op=ALU.mult), [r_beta, r_ex], [r_opb])
                P.op("act", lambda e: e.copy(out=opb[:, 6, :], in_=v_), [r_xr], [r_opb])
                yield
                rb, kt, bt, ab, Kh, Bh, vb = (opb[:, q, :] for q in range(7))
                pa, r_pa = pbk
                fTA, r_fTA = fTA_.next()
                fTB, r_fTB = fTB_.next()
                for h in range(4):
                    hs = slice(h * 64, (h + 1) * 64)
                    P.op("pe", lambda e, h=h, hs=hs: e.transpose(out=pa[0:64, h * 128:(h + 1) * 128], in_=bt[:, hs], identity=c.idb[:]), [r_opb, c.r_const], [r_pa])
                    P.op("pe", lambda e, h=h, hs=hs: e.transpose(out=pa[0:64, (4 + h) * 128:(5 + h) * 128], in_=kt[:, hs], identity=c.idb[:]), [r_opb, c.r_const], [r_pa])
                P.op("act", lambda e: e.copy(out=fTA[:], in_=pa[0:64, :]), [r_pa], [r_fTA])
                yield
                for h in range(4):
                    hs = slice(h * 64, (h + 1) * 64)
                    P.op("pe", lambda e, h=h, hs=hs: e.transpose(out=pa[0:64, (2 * h) * 128:(2 * h + 1) * 128], in_=ab[:, hs], identity=c.idb[:]), [r_opb, c.r_const], [r_pa])
                    P.op("pe", lambda e, h=h, hs=hs: e.transpose(out=pa[0:64, (2 * h + 1) * 128:(2 * h + 2) * 128], in_=rb[:, hs], identity=c.idb[:]), [r_opb, c.r_const], [r_pa])
                P.op("dve", lambda e: e.tensor_copy(out=fTB[:], in_=pa[0:64, :]), [r_pa], [r_fTB])
                yield
                AB, r_AB = AB_.next()
                AK, r_AK = AK_.next()
                for (dst, r_dst, off) in ((AB, r_AB, 0), (AK, r_AK, 4)):
                    for pr in range(2):
                        px, r_px = nb()
                        for hh in range(2):
                            h = 2 * pr + hh
                            P.op("pe", lambda e, h=h, hh=hh, px=px, off=off: e.matmul(out=px[:, hh * 256:(hh + 1) * 256], lhsT=fTA[:, (off + h) * 128:(off + h + 1) * 128],
                                                                               rhs=fTB[:, 2 * h * 128:(2 * h + 2) * 128], start=True, stop=True), [r_fTA, r_fTB], [r_px])
                        P.op("dve", lambda e, pr=pr, px=px, dst=dst: e.tensor_tensor(out=dst[:, pr * 512:(pr + 1) * 512], in0=px[:, :], in1=mask4[:], op=ALU.mult), [r_px, r_w], [r_dst])
                        yield
                py, r_py = nb()
                for h in range(4):
                    P.op("pe", lambda e, h=h: e.matmul(out=py[:, h * 128:(h + 1) * 128], lhsT=fTB[:, 2 * h * 128:(2 * h + 1) * 128], rhs=fTA[:, h * 128:(h + 1) * 128],
                                                       start=True, stop=True), [r_fTA, r_fTB], [r_py])
                NTc, r_NTc = NTa_.next()
                P.op("dve", lambda e: e.tensor_tensor(out=NTc[:], in0=py[:, :], in1=maskT4[:], op=ALU.mult), [r_py, r_w], [r_NTc])
                yield
                pz, r_pz = nb()
                for h in range(4):
                    P.op("pe", lambda e, h=h: e.matmul(out=pz[:, h * 64:(h + 1) * 64], lhsT=AK[:, h * 256:h * 256 + 128], rhs=vb[:, h * 64:(h + 1) * 64], start=True, stop=True),
                         [r_AK, r_opb], [r_pz])
                Z, r_Z = Z_.next()
                P.op("dve", lambda e, Z=Z: e.tensor_copy(out=v3(Z[:], 128)[:, :, 0:64], in_=v3(ab)), [r_opb], [r_Z])
                P.op("act", lambda e, Z=Z: e.copy(out=v3(Z[:], 128)[:, :, 64:128], in_=v3(pz[:, 0:256])), [r_pz], [r_Z])
                yield
                dbg("Z0", Z[:], [128, 512], BF16, [r_Z]); dbg("NT0", NTc[:], [128, 512], BF16, [r_NTc]); dbg("AK", AK[:], [128, 1024], BF16, [r_AK])
                Ncur = [AB[:, h * 256:h * 256 + 128] for h in range(4)]
                r_N = r_AB
                for kq in range(7):
                    pq, r_pq = nb()
                    for h in range(4):
                        P.op("pe", lambda e, h=h, N=Ncur[h], Z=Z, pq=pq: e.matmul(out=pq[:, h * 128:(h + 1) * 128], lhsT=N, rhs=Z[:, h * 128:(h + 1) * 128], start=True, stop=True),
                             [r_N, r_Z], [r_pq])
                    Zn, r_Zn = Z_.next()
                    P.op("dve", lambda e, Z=Z, Zn=Zn, pq=pq: e.tensor_tensor(out=Zn[:], in0=pq[:, :], in1=Z[:], op=ALU.add), [r_pq, r_Z], [r_Zn])
                    yield
                    if kq < 6:
                        p1, r_p1 = nb()
                        for h in range(4):
                            P.op("pe", lambda e, h=h, N=Ncur[h], NTc=NTc, p1=p1: e.matmul(out=p1[:, h * 128:(h + 1) * 128], lhsT=NTc[:, h * 128:(h + 1) * 128], rhs=N, start=True, stop=True),
                                 [r_N, r_NTc], [r_p1])
                        p2, r_p2 = nb()
                        for h in range(4):
                            P.op("pe", lambda e, h=h, N=Ncur[h], NTc=NTc, p2=p2: e.matmul(out=p2[:, h * 128:(h + 1) * 128], lhsT=N, rhs=NTc[:, h * 128:(h + 1) * 128], start=True, stop=True),
                                 [r_N, r_NTc], [r_p2])
                        Nn, r_Nn = Nn_.next()
                        NTn, r_NTn = NTa_.next()
                        P.op("act", lambda e, Nn=Nn, p1=p1: e.copy(out=Nn[:], in_=p1[:, :]), [r_p1], [r_Nn])
                        P.op("dve", lambda e, NTn=NTn, p2=p2: e.tensor_copy(out=NTn[:], in_=p2[:, :]), [r_p2], [r_NTn])
                        yield
                        dbg(f"Nn{kq}", Nn[:], [128, 512], BF16, [r_Nn]); dbg(f"NTn{kq}", NTn[:], [128, 512], BF16, [r_NTn]); dbg(f"Zn{kq}", Zn[:], [128, 512], BF16, [r_Zn])
                        Ncur = [Nn[:, h * 128:(h + 1) * 128] for h in range(4)]
                        r_N, NTc, r_NTc = r_Nn, NTn, r_NTn
                    Z, r_Z = Zn, r_Zn
                WT = [Z[:, h * 128:h * 128 + 64] for h in range(4)]
                XT = [Z[:, h * 128 + 64:(h + 1) * 128] for h in range(4)]
                pQ, r_pQ = nb()
                for h in range(4):
                    P.op("pe", lambda e, h=h: e.matmul(out=pQ[0:64, h * 128:(h + 1) * 128], lhsT=WT[h], rhs=AB[:, h * 256 + 128:(h + 1) * 256], start=True, stop=True),
                         [r_Z, r_AB], [r_pQ])
                Q, r_Q = Q_.next()
                P.op("dve", lambda e: e.tensor_tensor(out=v3(Q[:], 128), in0=v3(pQ[0:64, :], 128), in1=fTB[:].rearrange("p (h two t) -> p h two t", two=2, t=128)[:, :, 1, :], op=ALU.add),
                     [r_pQ, r_fTB], [r_Q])
                yield
                pP, r_pP = nb()
                for h in range(4):
                    P.op("pe", lambda e, h=h: e.matmul(out=pP[0:64, h * 64:(h + 1) * 64], lhsT=WT[h], rhs=Bh[:, h * 64:(h + 1) * 64], start=True, stop=True), [r_Z, r_opb], [r_pP])
                Pm, r_Pm = Pm_.next()
                P.op("dve", lambda e: e.tensor_tensor(out=v3(Pm[:]), in0=identB[:], in1=gcol[:].unsqueeze(2).to_broadcast([64, 4, 64]), op=ALU.mult), [r_w, r_gcol], [r_Pm])
                P.op("dve", lambda e: e.tensor_tensor(out=Pm[:], in0=Pm[:], in1=pP[0:64, 0:256], op=ALU.add), [r_Pm, r_pP], [r_Pm])
                yield
                pD, r_pD = nb()
                for h in range(4):
                    P.op("pe", lambda e, h=h: e.matmul(out=pD[0:64, h * 64:(h + 1) * 64], lhsT=Bh[:, h * 64:(h + 1) * 64], rhs=XT[h], start=(h == 0), stop=False, skip_group_check=True),
                         [r_Z, r_opb], [r_pD])
                    P.op("pe", lambda e, h=h: e.matmul(out=pD[0:64, h * 64:(h + 1) * 64], lhsT=Kh[:, h * 64:(h + 1) * 64], rhs=vb[:, h * 64:(h + 1) * 64], start=False, stop=True, skip_group_check=True),
                         [r_opb], [r_pD])
                Dm, r_Dm = Dm_.next()
                P.op("act", lambda e: e.copy(out=Dm[:], in_=pD[0:64, 0:256]), [r_pD], [r_Dm])
                yield
                S0, r_S0 = state
                pO, r_pO = banks[2]
                for h in range(4):
                    P.op("pe", lambda e, h=h: e.matmul(out=pO[:, h * 64:(h + 1) * 64], lhsT=AB[:, h * 256 + 128:(h + 1) * 256], rhs=XT[h], start=(h == 0), stop=False, skip_group_check=True),
                         [r_AB, r_Z], [r_pO])
                    P.op("pe", lambda e, h=h: e.matmul(out=pO[:, h * 64:(h + 1) * 64], lhsT=AK[:, h * 256 + 128:(h + 1) * 256], rhs=vb[:, h * 64:(h + 1) * 64], start=False, stop=False, skip_group_check=True),
                         [r_AK, r_opb], [r_pO])
                    P.op("pe", lambda e, h=h, S0=S0: e.matmul(out=pO[:, h * 64:(h + 1) * 64], lhsT=Q[:, h * 128:(h + 1) * 128], rhs=S0[:, h * 64:(h + 1) * 64], start=False, stop=True, skip_group_check=True),
                         [r_Q, r_S0], [r_pO])
                osb, r_osb = osb_.next()
                P.op("act", lambda e: e.copy(out=osb[:], in_=pO[:, 0:256]), [r_pO], [r_osb])
                yield
                pS, r_pS = banks[2]
                for h in range(4):
                    P.op("pe", lambda e, h=h, S0=S0: e.matmul(out=pS[0:64, h * 64:(h + 1) * 64], lhsT=Pm[:, h * 64:(h + 1) * 64], rhs=S0[:, h * 64:(h + 1) * 64], start=True, stop=True),
                         [r_Pm, r_S0], [r_pS])
                S1, r_S1 = ST.next()
                P.op("dve", lambda e, S1=S1: e.tensor_tensor(out=S1[:], in0=pS[0:64, 0:256], in1=Dm[:], op=ALU.add), [r_pS, r_Dm], [r_S1])
                state[0], state[1] = S1, r_S1
                yield
                on, r_on = on_.next()
                P.op("dve", lambda e: e.tensor_reduce(out=s4[:, 16:20], in_=v3(osb[:]), axis=AX.X, op=ALU.add), [r_osb], [r_s4])
                P.op("pool", lambda e: e.tensor_tensor(out=on[:], in0=osb[:], in1=osb[:], op=ALU.mult), [r_osb], [r_on])
                P.op("dve", lambda e: e.tensor_reduce(out=s4[:, 20:24], in_=v3(on[:]), axis=AX.X, op=ALU.add), [r_on], [r_s4])
                P.op("dve", lambda e: e.tensor_scalar(out=s4[:, 16:20], in0=s4[:, 16:20], scalar1=1.0 / 64, scalar2=None, op0=ALU.mult), [r_s4], [r_s4])
                P.op("dve", lambda e: e.tensor_tensor(out=s4[:, 24:28], in0=s4[:, 16:20], in1=s4[:, 16:20], op=ALU.mult), [r_s4], [r_s4])
                P.op("dve", lambda e: e.scalar_tensor_tensor(out=s4[:, 20:24], in0=s4[:, 20:24], scalar=1.0 / 64, in1=s4[:, 24:28], op0=ALU.mult, op1=ALU.subtract), [r_s4], [r_s4])
                P.op("dve", lambda e: e.tensor_scalar(out=s4[:, 20:24], in0=s4[:, 20:24], scalar1=GN_EPS, scalar2=None, op0=ALU.add), [r_s4], [r_s4])
                P.op("act", lambda e: e.activation(out=s4[:, 24:28], in_=s4[:, 20:24], func=AF.Sqrt), [r_s4], [r_s4])
                P.op("dve", lambda e: e.reciprocal(out=s4[:, 28:32], in_=s4[:, 24:28]), [r_s4], [r_s4])
                P.op("dve", lambda e: e.tensor_tensor(out=v3(on[:]), in0=v3(osb[:]), in1=s4[:, 16:20].unsqueeze(2).to_broadcast([128, 4, 64]), op=ALU.subtract), [r_osb, r_s4], [r_on])
                P.op("dve", lambda e: e.tensor_tensor(out=v3(on[:]), in0=v3(on[:]), in1=s4[:, 28:32].unsqueeze(2).to_broadcast([128, 4, 64]), op=ALU.mult), [r_on, r_s4], [r_on])
                P.op("pool", lambda e: e.tensor_tensor(out=on[:], in0=on[:], in1=lng[:], op=ALU.mult), [r_on, r_w], [r_on])
                P.op("pool", lambda e: e.tensor_tensor(out=on[:], in0=on[:], in1=lnb[:], op=ALU.add), [r_on, r_w], [r_on])
                P.op("pool", lambda e: e.tensor_tensor(out=on[:], in0=on[:], in1=bon[:], op=ALU.add), [r_on, r_bon], [r_on])
                P.dma("pool", c.osc[d, rows, :], on[:], reads=[r_on], writes=[c.r["osc%d" % d][i]])

            return body

    with ExitStack() as st:
        bodies = [setup_dir(d, st) for d in range(2)]
        run_lanes([list(range(NT)), list(range(NT - 1, -1, -1))], lambda i, k: bodies[k](i), skew=RWKV_SKEW)
    barrier(c)
    with ExitStack() as st:
        def mkpools(k):
            mk = lambda name, shape, dt, n=2: Pool2(st, nc, f"{name}_l{k}_", shape, dt, n)
            return dict(o0=mk("e_o0", [128, 256], F32), o1=mk("e_o1", [128, 256], F32), zd=mk("e_zd", [128, 256], F32), yo=mk("e_yo", [128, 256], BF16))
        pools = [mkpools(k) for k in range(2)]

        def body_e(i, k):
            pl = pools[k]
            rows = slice(i * 128, (i + 1) * 128)
            o0, r_o0 = pl["o0"].next()
            o1, r_o1 = pl["o1"].next()
            zd, r_zd = pl["zd"].next()
            yo, r_yo = pl["yo"].next()
            P.dma("sp", o0[:], c.osc[0, rows, :], reads=[c.r["osc0"][i]], writes=[r_o0])
            P.dma("sp", o1[:], c.osc[1, rows, :], reads=[c.r["osc1"][i]], writes=[r_o1])
            P.dma("sp", zd[:], c.zs[rows, 768:1024], reads=[c.r["zs"][i]], writes=[r_zd])
            yield
            P.op("dve", lambda e: e.tensor_tensor(out=o0[:], in0=o0[:], in1=o1[:], op=ALU.add), [r_o0, r_o1], [r_o0])
            P.op("dve", lambda e: e.tensor_tensor(out=yo[:], in0=o0[:], in1=zd[:], op=ALU.mult), [r_o0, r_zd], [r_yo])
            P.dma("pool", c.ys[rows, 768:1024], yo[:], reads=[r_yo], writes=[c.r["ys"][i]])

        run_lanes([list(range(k, NT, 2)) for k in range(2)], body_e, skew=1)


def host_consts(S, seq_len):
    NT = S // 128
    t = np.arange(S)
    valid = (t < seq_len).astype(np.float32).reshape(NT, 128).T.copy()
    invc = np.ones((S, 4), np.float32)
    for g, w in enumerate((2, 4, 8, 16)):
        h = w // 2
        cnt = np.minimum(t + h, seq_len) - np.maximum(t - h, 0)
        invc[:, g] = np.where(t < seq_len, 1.0 / np.maximum(cnt, 1), 1.0)
    invc = invc.reshape(NT, 128, 4).transpose(1, 0, 2).reshape(128, NT * 4).copy()
    ident = np.eye(128, dtype=np.float32)
    s = np.arange(128)[:, None]
    tt = np.arange(128)[None, :]
    bandc = np.zeros((128, 4, 128), np.float32)
    bandh = np.zeros((16, 4, 128), np.float32)
    for g, w in enumerate((2, 4, 8, 16)):
        h = w // 2
        bandc[:, g, :] = ((s >= tt - h) & (s <= tt + h - 1))
        r = np.arange(16)[:, None]
        srel = np.where(r < 8, r - 8, 128 + (r - 8))
        bandh[:, g, :] = ((srel >= tt - h) & (srel <= tt + h - 1))
    slopes = 2.0 ** (-8.0 * np.arange(1, 13) / 12.0)
    amask = np.zeros((25, 128, 4, 128), np.float32)
    ci = 0
    key = np.arange(128)[:, None]
    q = np.arange(128)[None, :]
    for g in range(3):
        d = ATT_D[g]
        for coff in range(-ATT_W[g], ATT_W[g] + 1):
            delta = coff * 128 + key - q
            ok = (delta % d == 0) & (np.abs(delta) <= 64 * d)
            for h in range(4):
                amask[ci, :, h, :] = np.where(ok, np.exp(-slopes[g * 4 + h] * np.abs(delta)), 0.0)
            ci += 1
    amask3 = np.zeros((3, 128, 4, 128), np.float32)
    for ci3, coff in enumerate((-1, 0, 1)):
        dm = coff * 128 + key - q
        ok = np.abs(dm) <= 64
        for h in range(4):
            amask3[ci3, :, h, :] = np.where(ok, np.exp(-slopes[8 + h] * 16.0 * np.abs(dm)), 0.0)
    tri = np.zeros((128, 642), np.float32)
    tri[:, 0:128] = (s <= tt)
    tri[:, 128:256] = (s >= tt)
    tri[:, 256:384] = (s < tt)
    tri[:, 384:512] = (s > tt)
    tri[:, 512:642] = 1.0
    return dict(valid=valid, invc=invc, ident=ident, amask=amask.reshape(25, 128, 512), amask3=amask3.reshape(3, 128, 512), bandc=bandc.reshape(128, 512),
                bandh=bandh.reshape(16, 512), tri=tri)


def host_weights(inp):
    w = {}
    for k in ("norm_g", "w_in", "q_norm_g", "k_norm_g", "pool_w", "pool_scale", "sg_norm_g", "mu_rkv", "mu_lat", "w0", "w_up",
              "a0", "a_up", "k_k", "k_a", "r_k", "ln_g", "ln_b", "w_branch", "w_out"):
        w[k] = np.ascontiguousarray(np.asarray(inp[k], dtype=np.float32))
    w["sg_wT"] = np.ascontiguousarray(np.transpose(np.asarray(inp["sg_w"], np.float32), (0, 1, 3, 2)))
    w["sg_bT"] = np.ascontiguousarray(np.transpose(np.asarray(inp["sg_b"], np.float32), (0, 2, 1)))
    return w


_NC_CACHE = {}


def run_sequences(seqs, S, inp, branches=(0, 1, 2, 3), L=2, n_cores=8):
    key = (S, L, tuple(branches))
    if key not in _NC_CACHE:
        _NC_CACHE[key] = build_program(S, L=L, branches=branches)
    nc = _NC_CACHE[key]
    w = host_weights(inp)
    in_maps = []
    for ci in range(n_cores):
        if ci < len(seqs):
            x = np.zeros((S, D), np.float32)
            x[:seqs[ci].shape[0]] = seqs[ci]
            m = dict(x=x, **host_consts(S, seqs[ci].shape[0]))
        else:
            m = dict(x=np.zeros((S, D), np.float32), **host_consts(S, 0))
        m.update(w)
        in_maps.append(m)
    res = run_bass_kernel_spmd(nc, in_maps, core_ids=list(range(n_cores)))
    if DEBUG_SCRATCH:
        global LAST_RESULTS
        LAST_RESULTS = res.results
    return [res.results[ci]["y"][:seqs[ci].shape[0]] for ci in range(len(seqs))]


def kernel(**inputs):
    xp = np.asarray(inputs["x_prompt"], np.float32)
    xs = np.asarray(inputs["x_sample"], np.float32)
    S = xs.shape[1]
    seqs = [xp[b] for b in range(xp.shape[0])] + [xs[b] for b in range(xs.shape[0])]
    outs = run_sequences(seqs, S, inputs)
    nb = xp.shape[0]
    y_prompt = np.stack(outs[:nb], 0).astype(np.float32)
    y_sample = np.stack(outs[nb:], 0).astype(np.float32)
    return (y_prompt, y_sample)
```

```python
import numpy as np
import concourse.bass as bass
import concourse.mybir as mybir
from concourse.bass_utils import run_bass_kernel_spmd

F32 = mybir.dt.float32
BF16 = mybir.dt.bfloat16
AF = mybir.ActivationFunctionType
ALU = mybir.AluOpType
AX = mybir.AxisListType

EPOCH = 24000
STORES_ON_SP = False
RWKV_SKEW = 6
ATTACH_WAITS = True
SAME_ENGINE_SYNC = True
FUSE_BC = True
NDMA_SEM = 12


class Res:
    __slots__ = ("name", "last_w", "reads")

    def __init__(self, name=""):
        self.name = name
        self.last_w = None
        self.reads = []


class Prog:
    ENG = ("pe", "act", "dve", "pool", "sp")

    def __init__(self, nc, same_engine_sync=True):
        self.nc = nc
        self.same_engine_sync = same_engine_sync
        self.ops = {e: [] for e in self.ENG}
        self.cnt = {e: 0 for e in self.ENG}
        self.sems = {e: [nc.alloc_semaphore(name=f"s_{e}_0")] for e in ("pe", "act", "dve", "pool")}
        self.seen = {e: {} for e in self.ENG}
        self.dma_sems = {q: [nc.alloc_semaphore(name=f"d_{q}_{k}") for k in range(NDMA_SEM)] for q in ("sp", "pool")}
        self.dma_n = {"sp": 0, "pool": 0}
        self.n_wait = 0

    def _deps(self, reads, writes):
        deps = []
        for r in reads:
            if r.last_w is not None:
                deps.extend(r.last_w)
        for w in writes:
            if w.last_w is not None:
                deps.extend(w.last_w)
            deps.extend(w.reads)
        return deps

    def _waits(self, e, deps, own_sem_ids):
        waits = {}
        seen = self.seen[e]
        for (sem, val) in deps:
            sid = id(sem)
            if sid in own_sem_ids and (e == "pe" or not self.same_engine_sync):
                continue
            if seen.get(sid, 0) >= val:
                continue
            if sid not in waits or waits[sid][1] < val:
                waits[sid] = (sem, val)
        for sid, (sem, val) in waits.items():
            seen[sid] = val
        self.n_wait += len(waits)
        return list(waits.values())

    def _commit(self, ev, reads, writes, is_dma=False):
        for r in reads:
            r.reads.append(ev)
        for w in writes:
            if is_dma and w.last_w is not None and not w.reads:
                w.last_w = w.last_w + [ev]
            else:
                w.last_w = [ev]
            w.reads = []

    def op(self, e, fn, reads=(), writes=()):
        deps = self._deps(reads, writes)
        own = {id(s) for s in self.sems[e]}
        waits = self._waits(e, deps, own)
        if self.cnt[e] >= EPOCH:
            self.sems[e].append(self.nc.alloc_semaphore(name=f"s_{e}_{len(self.sems[e])}"))
            self.cnt[e] = 0
        self.cnt[e] += 1
        sem = self.sems[e][-1]
        ev = (sem, self.cnt[e])
        self.ops[e].append((waits, fn, sem, 1))
        self._commit(ev, reads, writes)
        return ev

    def dma(self, q, out, in_, reads=(), writes=(), fn=None, **kw):
        if STORES_ON_SP and q == "pool" and fn is None and out.dtype == in_.dtype:
            q = "sp"
        deps = self._deps(reads, writes)
        n = self.dma_n[q]
        self.dma_n[q] += 1
        k = n % NDMA_SEM
        use = n // NDMA_SEM
        sem = self.dma_sems[q][k]
        if use > 0:
            deps.append((sem, 16 * use))
        own = {id(s) for s in self.sems[q]} if q in self.sems else set()
        waits = self._waits(q, deps, own)
        ev = (sem, 16 * (use + 1))
        if fn is None:
            fn = (lambda eng, o=out, i=in_, kw=kw: eng.dma_start(out=o, in_=i, **kw))
        self.ops[q].append((waits, fn, sem, 16))
        self._commit(ev, reads, writes, is_dma=True)
        return ev

    def final_wait(self, e, evs):
        waits = self._waits(e, list(evs), set())
        self.ops[e].append((waits, None, None, 0))

    def emit(self):
        nc = self.nc
        ops = self.ops

        def run(eng, lst, attach):
            for (waits, fn, sem, inc) in lst:
                if fn is not None and attach and waits and inc == 1:
                    for (s, v) in waits[:-1]:
                        eng.wait_ge(s, v)
                    ins = fn(eng)
                    ins._wait_ge(waits[-1][0], eng.lower_val(waits[-1][1]))
                    ins.then_inc(sem, inc)
                    continue
                for (s, v) in waits:
                    eng.wait_ge(s, v)
                if fn is not None:
                    ins = fn(eng)
                    ins.then_inc(sem, inc)

        with nc.Block() as block:
            @block.tensor
            def _(eng):
                run(eng, ops["pe"], ATTACH_WAITS)

            @block.scalar
            def _(eng):
                run(eng, ops["act"], ATTACH_WAITS)

            @block.vector
            def _(eng):
                run(eng, ops["dve"], ATTACH_WAITS)

            @block.gpsimd
            def _(eng):
                run(eng, ops["pool"], ATTACH_WAITS)

            @block.sync
            def _(eng):
                run(eng, ops["sp"], False)


D = 1024
PW = 9216
P1W = 5120
EPS = 1e-6
GN_EPS = 64e-5
ATT_W = (1, 2, 8)
ATT_D = (1, 4, 16)
C_Q, C_K, C_V, C_ZA, C_UB, C_ZB, C_UVC, C_ZC, C_RKV, C_LAT, C_ZD = 0, 768, 1536, 2304, 2560, 2816, 3072, 3584, 3840, 4608, 4864


class Ctx:
    pass


_UID = [0]


def uname(n):
    _UID[0] += 1
    return f"{n}_u{_UID[0]}"


DEBUG_SCRATCH = False
DEBUG_BARRIER = False
DBG_DIR = 0
DBG_TILE = 0


def build_program(S, L=2, branches=(0, 1, 2, 3), same_engine_sync=None):
    if same_engine_sync is None:
        same_engine_sync = SAME_ENGINE_SYNC
    from contextlib import ExitStack
    NT = S // 128
    nc = bass.Bass("TRN2", target_bir_lowering=False)
    P = Prog(nc, same_engine_sync=same_engine_sync)
    c = Ctx()
    c.nc, c.P, c.S, c.NT, c.L, c.branches = nc, P, S, NT, L, branches

    def din(name, shape, dt=F32):
        return nc.dram_tensor(name, list(shape), dt, kind="ExternalInput").ap()

    def dscr(name, shape, dt=F32):
        return nc.dram_tensor(name, list(shape), dt, kind=("ExternalOutput" if DEBUG_SCRATCH else "Internal")).ap()

    c.x_in = din("x", [S, D])
    c.valid = din("valid", [128, NT])
    c.invc = din("invc", [128, NT * 4])
    c.ident = din("ident", [128, 128])
    c.amask = din("amask", [25, 128, 512])
    c.amask3 = din("amask3", [3, 128, 512])
    c.bandc = din("bandc", [128, 4 * 128])
    c.bandh = din("bandh", [16, 4 * 128])
    c.tri = din("tri", [128, 642])
    c.w = {}
    for name, shape in (("norm_g", [L, D]), ("w_in", [L, D, PW]), ("q_norm_g", [L, 64]), ("k_norm_g", [L, 64]),
                        ("pool_w", [L, 4, 64, 64]), ("pool_scale", [L, 256]), ("sg_norm_g", [L, 256]),
                        ("sg_wT", [L, 4, 128, 128]), ("sg_bT", [L, 128, 4]), ("mu_rkv", [L, 2, 768]), ("mu_lat", [L, 2, 128]),
                        ("w0", [L, 2, 256]), ("w_up", [L, 2, 64, 256]), ("a0", [L, 2, 256]), ("a_up", [L, 2, 64, 256]),
                        ("k_k", [L, 2, 256]), ("k_a", [L, 2, 256]), ("r_k", [L, 2, 256]), ("ln_g", [L, 256]), ("ln_b", [L, 256]),
                        ("w_branch", [L, 4, 256, D]), ("w_out", [L, D, D])):
        c.w[name] = din(name, shape)
    c.y_out = nc.dram_tensor("y", [S, D], F32, kind="ExternalOutput").ap()

    c.x1 = dscr("x1", [S, D])
    c.qT = dscr("qT_scr", [NT, 64, 1536], BF16)
    c.kT = dscr("kT_scr", [NT, 64, 1536], BF16)
    c.va = dscr("va_scr", [NT, 128, 780], BF16)
    c.zs = dscr("zs_scr", [S, 1024])
    c.ub = dscr("ub_scr", [S + 32, 256])
    c.uvc = dscr("uvc_scr", [S, 512])
    c.rkvl = dscr("rkvl_scr", [S + 2, 1024])
    c.ys = dscr("ys_scr", [S, 1024], BF16)
    c.nd3 = dscr("nd3_scr", [S, 260])
    c.osc = dscr("o_scr", [2, S, 256])
    c.r = {k: [Res(f"{k}{i}") for i in range(NT)] for k in ("x1", "qT", "kT", "va", "zs", "ub", "uvc", "rkvl", "ys", "osc0", "osc1", "y", "nd3")}
    c.r_pad = Res("pads")
    if DEBUG_SCRATCH:
        c.dbg_proj = dscr("dbg_proj", [S, P1W])
        c.dbg_hT = dscr("dbg_hT", [S, D], BF16)

    c.idb = nc.alloc_sbuf_tensor("idb", [128, 128], BF16)
    c.idf = nc.alloc_sbuf_tensor("idf", [128, 128], F32)
    c.zero = nc.alloc_sbuf_tensor("zero", [128, 1024], F32)
    c.validt = nc.alloc_sbuf_tensor("validt", [128, NT], F32)
    c.r_const = Res("const")
    P.dma("sp", c.idf[:], c.ident, writes=[c.r_const])
    P.dma("pool", c.idb[:], c.ident, writes=[c.r_const])
    P.dma("sp", c.validt[:], c.valid, writes=[c.r_const])
    P.op("pool", lambda e: e.memset(c.zero[:], 0.0), [], [c.r_const])
    P.dma("sp", c.ub[0:16, :], c.zero[0:16, 0:256], reads=[c.r_const], writes=[c.r_pad])
    P.dma("sp", c.ub[S + 16:S + 32, :], c.zero[0:16, 0:256], reads=[c.r_const], writes=[c.r_pad])
    P.dma("sp", c.rkvl[0:1, :], c.zero[0:1, :], reads=[c.r_const], writes=[c.r_pad])
    P.dma("sp", c.rkvl[S + 1:S + 2, :], c.zero[0:1, :], reads=[c.r_const], writes=[c.r_pad])

    c.pbf = [nc.alloc_psum_tensor(f"pbf{i}", [128, 1024], BF16) for i in range(2)]
    c.pf = [nc.alloc_psum_tensor(f"pf{i}", [128, 512], F32) for i in range(6)]
    c.r_pbf = [Res(f"pbf{i}") for i in range(2)]
    c.r_pf = [Res(f"pf{i}") for i in range(6)]

    for l in range(L):
        x_src = c.x_in if l == 0 else c.x1
        x_src_res = None if l == 0 else c.r["x1"]
        x_dst = c.x1 if l < L - 1 else c.y_out
        x_dst_res = c.r["x1"] if l < L - 1 else c.r["y"]
        phase1(c, l, x_src, x_src_res)
        barrier(c)
        if 2 in branches and not FUSE_BC:
            phase2c(c, l)
            barrier(c)
        if 1 in branches and not FUSE_BC:
            phase2b(c, l)
            barrier(c)
        if 0 in branches:
            phase2a(c, l)
            barrier(c)
        if 3 in branches:
            phase2d(c, l)
            barrier(c)
        phase3(c, l, x_src, x_src_res, x_dst, x_dst_res)
        barrier(c)
    P.emit()
    return nc


def barrier(c):
    P = c.P
    evs = []
    for e in ("pe", "act", "dve", "pool"):
        if P.cnt[e] > 0:
            evs.append((P.sems[e][-1], P.cnt[e]))
    for q in ("sp", "pool"):
        n = P.dma_n[q]
        for k, s in enumerate(P.dma_sems[q]):
            uses = (n - k + NDMA_SEM - 1) // NDMA_SEM if n > k else 0
            if uses > 0:
                evs.append((s, 16 * uses))
    for e in Prog.ENG:
        P.final_wait(e, evs)


class Pool2:
    def __init__(self, stack, nc, name, shape, dt, n=2):
        self.t = [stack.enter_context(nc.sbuf_tensor(uname(f"{name}{i}"), list(shape), dt)) for i in range(n)]
        self.r = [Res(f"{name}{i}") for i in range(n)]
        self.n = n
        self.i = -1

    def next(self):
        self.i = (self.i + 1) % self.n
        return self.t[self.i], self.r[self.i]


NL = 2


def run_lanes(lane_tiles, body, skew=2):
    K = len(lane_tiles)
    its = [iter(t) for t in lane_tiles]
    gens = [None] * K
    done = [False] * K
    rnd = 0
    while not all(done):
        for k in range(K):
            if done[k]:
                continue
            if gens[k] is None:
                if rnd < k * skew:
                    continue
                i = next(its[k], None)
                if i is None:
                    done[k] = True
                    continue
                gens[k] = body(i, k)
            try:
                next(gens[k])
            except StopIteration:
                gens[k] = None
        rnd += 1


def lane_banks(c, k):
    return [(c.pf[3 * k + j], c.r_pf[3 * k + j]) for j in range(3)], (c.pbf[k], c.r_pbf[k])


def rms_h(c, xt, r_xt, gbc, r_w, pl, width=1024):
    P = c.P
    junk, r_junk = pl["junk"].next()
    ss, r_ss = pl["ss"].next()
    h, r_h = pl["h"].next()
    P.op("act", lambda e: e.activation(out=junk[:], in_=xt[:], func=AF.Square, accum_out=ss[:, 0:1]), [r_xt], [r_junk, r_ss])
    P.op("dve", lambda e: e.tensor_scalar(out=ss[:, 1:2], in0=ss[:, 0:1], scalar1=1.0 / width, scalar2=EPS, op0=ALU.mult, op1=ALU.add), [r_ss], [r_ss])
    P.op("act", lambda e: e.activation(out=ss[:, 2:3], in_=ss[:, 1:2], func=AF.Sqrt), [r_ss], [r_ss])
    P.op("dve", lambda e: e.reciprocal(out=ss[:, 3:4], in_=ss[:, 2:3]), [r_ss], [r_ss])
    P.op("dve", lambda e: e.scalar_tensor_tensor(out=h[:], in0=xt[:], scalar=ss[:, 3:4], in1=gbc[:], op0=ALU.mult, op1=ALU.mult),
         [r_xt, r_ss, r_w], [r_h])
    return h, r_h


def transpose8(c, src, r_src, dst, r_dst, pbk, eng="act", n=8):
    P = c.P
    pb, r_pb = pbk
    for kc in range(n):
        P.op("pe", lambda e, kc=kc: e.transpose(out=pb[:, kc * 128:(kc + 1) * 128], in_=src[:, kc * 128:(kc + 1) * 128], identity=c.idb[:]),
             [r_src, c.r_const], [r_pb])
    if eng == "act":
        P.op("act", lambda e: e.copy(out=dst[:, 0:n * 128], in_=pb[:, 0:n * 128]), [r_pb], [r_dst])
    else:
        P.op("dve", lambda e: e.tensor_copy(out=dst[:, 0:n * 128], in_=pb[:, 0:n * 128]), [r_pb], [r_dst])


def phase1(c, l, x_src, x_src_res):
    from contextlib import ExitStack
    P, nc, NT = c.P, c.nc, c.NT
    with ExitStack() as st:
        sb = lambda n, s, d: st.enter_context(nc.sbuf_tensor(uname(n), list(s), d))
        W = sb("W1", [128, 8, P1W], BF16)
        r_W = Res("W1")
        for kc in range(8):
            P.dma("pool", W[:, kc, :], c.w["w_in"][l, kc * 128:(kc + 1) * 128, 0:P1W], writes=[r_W])
        gbc = sb("gbc", [128, D], F32)
        g64 = sb("g64", [128, 128], F32)
        gqk = sb("gqk", [128, 1536], F32)
        r_w = Res("p1w")
        P.dma("sp", gbc[:], c.w["norm_g"][l:l + 1, :].partition_broadcast(128), writes=[r_w])
        P.dma("sp", g64[:, 0:64], c.w["q_norm_g"][l:l + 1, :].partition_broadcast(128), writes=[r_w])
        P.dma("sp", g64[:, 64:128], c.w["k_norm_g"][l:l + 1, :].partition_broadcast(128), writes=[r_w])
        P.op("dve", lambda e: e.tensor_scalar(out=gqk[:, 0:768].rearrange("p (h d) -> p h d", d=64),
                                              in0=g64[:, 0:64].unsqueeze(1).to_broadcast([128, 12, 64]),
                                              scalar1=0.125, scalar2=None, op0=ALU.mult), [r_w], [r_w])
        P.op("dve", lambda e: e.tensor_copy(out=gqk[:, 768:1536].rearrange("p (h d) -> p h d", d=64),
                                            in_=g64[:, 64:128].unsqueeze(1).to_broadcast([128, 12, 64])), [r_w], [r_w])

        def mkpools(k):
            mk = lambda name, shape, dt, n=1: Pool2(st, nc, f"{name}_l{k}_", shape, dt, n)
            d = dict(junk=mk("junk", [128, D], BF16), ss=mk("ss", [128, 4], F32), h=mk("h", [128, D], BF16), hT=mk("hT", [128, D], BF16),
                     xt=mk("xt", [128, D], F32), proj=mk("proj", [128, P1W], F32), zsb=mk("zsb", [128, 1024], F32))
            if 0 in c.branches:
                d.update(sq=mk("sq", [128, 1536], F32), s24=mk("s24", [128, 72], F32), qkn=mk("qkn", [128, 1536], BF16),
                         qkT=mk("qkT", [64, 3072], BF16), vaug=mk("vaug", [128, 780], BF16))
            return d
        pools = [mkpools(k) for k in range(NL)]

        def body(i, k):
            pl = pools[k]
            banks, pbk = lane_banks(c, k)
            rows = slice(i * 128, (i + 1) * 128)
            xt, r_xt = pl["xt"].next()
            P.dma("sp", xt[:], x_src[rows, :], reads=([x_src_res[i]] if x_src_res else []), writes=[r_xt])
            h, r_h = rms_h(c, xt, r_xt, gbc, r_w, pl)
            yield
            hT, r_hT = pl["hT"].next()
            transpose8(c, h, r_h, hT, r_hT, pbk)
            yield
            proj, r_pj = pl["proj"].next()
            for blk in range(P1W // 512):
                ps, r_ps = banks[blk % 3]
                for kc in range(8):
                    P.op("pe", lambda e, kc=kc, blk=blk, ps=ps: e.matmul(out=ps[:, :], lhsT=hT[:, kc * 128:(kc + 1) * 128],
                                                                    rhs=W[:, kc, blk * 512:(blk + 1) * 512], start=(kc == 0), stop=(kc == 7)),
                         [r_hT, r_W], [r_ps])
                if blk % 2 == 0:
                    P.op("act", lambda e, blk=blk, ps=ps: e.copy(out=proj[:, blk * 512:(blk + 1) * 512], in_=ps[:, :]), [r_ps], [r_pj])
                else:
                    P.op("dve", lambda e, blk=blk, ps=ps: e.tensor_copy(out=proj[:, blk * 512:(blk + 1) * 512], in_=ps[:, :]), [r_ps], [r_pj])
                if blk % 2 == 1:
                    yield
            if 1 in c.branches:
                P.dma("pool", c.ub[16 + i * 128:16 + (i + 1) * 128, :], proj[:, C_UB:C_UB + 256], reads=[r_pj], writes=[c.r["ub"][i]])
            if 2 in c.branches:
                P.dma("pool", c.uvc[rows, :], proj[:, C_UVC:C_UVC + 512], reads=[r_pj], writes=[c.r["uvc"][i]])
            if 3 in c.branches:
                P.dma("pool", c.rkvl[1 + i * 128:1 + (i + 1) * 128, :], proj[:, C_RKV:C_RKV + 1024], reads=[r_pj], writes=[c.r["rkvl"][i]])
            zt, r_zt = pl["zsb"].next()
            for b, cz in enumerate((C_ZA, C_ZB, C_ZC, C_ZD)):
                P.op("act", lambda e, b=b, cz=cz: e.activation(out=zt[:, b * 256:(b + 1) * 256], in_=proj[:, cz:cz + 256], func=AF.Silu),
                     [r_pj], [r_zt])
            P.dma("pool", c.zs[rows, :], zt[:], reads=[r_zt], writes=[c.r["zs"][i]])
            yield
            if 0 in c.branches:
                sq, r_sq = pl["sq"].next()
                s, r_s = pl["s24"].next()
                qn, r_qn = pl["qkn"].next()
                v24 = lambda ap: ap.rearrange("p (h d) -> p h d", d=64)
                P.op("pool", lambda e: e.tensor_tensor(out=sq[:], in0=proj[:, 0:1536], in1=proj[:, 0:1536], op=ALU.mult), [r_pj], [r_sq])
                P.op("dve", lambda e: e.tensor_reduce(out=s[:, 0:24], in_=v24(sq[:]), axis=AX.X, op=ALU.add), [r_sq], [r_s])
                P.op("dve", lambda e: e.tensor_scalar(out=s[:, 24:48], in0=s[:, 0:24], scalar1=1.0 / 64, scalar2=EPS, op0=ALU.mult, op1=ALU.add), [r_s], [r_s])
                P.op("act", lambda e: e.activation(out=s[:, 0:24], in_=s[:, 24:48], func=AF.Sqrt), [r_s], [r_s])
                P.op("dve", lambda e: e.reciprocal(out=s[:, 48:72], in_=s[:, 0:24]), [r_s], [r_s])
                yield
                P.op("dve", lambda e: e.tensor_tensor(out=v24(sq[:]), in0=v24(proj[:, 0:1536]),
                                                      in1=s[:, 48:72].unsqueeze(2).to_broadcast([128, 24, 64]), op=ALU.mult), [r_pj, r_s], [r_sq])
                P.op("pool", lambda e: e.tensor_tensor(out=qn[:], in0=sq[:], in1=gqk[:], op=ALU.mult), [r_sq, r_w], [r_qn])
                va, r_va = pl["vaug"].next()
                P.op("act", lambda e: e.copy(out=va[:].rearrange("p (h d) -> p h d", d=65)[:, :, 0:64],
                                             in_=proj[:, C_V:C_V + 768].rearrange("p (h d) -> p h d", d=64)), [r_pj], [r_va])
                P.op("dve", lambda e: e.tensor_copy(out=va[:].rearrange("p (h d) -> p h d", d=65)[:, :, 64:65],
                                                    in_=c.validt[:, i:i + 1].unsqueeze(1).to_broadcast([128, 12, 1])), [c.r_const], [r_va])
                P.dma("pool", c.va[i], va[:], reads=[r_va], writes=[c.r["va"][i]])
                yield
                qT, r_qT = pl["qkT"].next()
                pb, r_pb = pbk
                for grp in range(3):
                    for j in range(8):
                        hh = grp * 8 + j
                        P.op("pe", lambda e, hh=hh, j=j: e.transpose(out=pb[0:64, j * 128:(j + 1) * 128], in_=qn[:, hh * 64:(hh + 1) * 64], identity=c.idb[:]),
                             [r_qn, c.r_const], [r_pb])
                    if grp % 2 == 0:
                        P.op("dve", lambda e, grp=grp: e.tensor_copy(out=qT[:, grp * 1024:(grp + 1) * 1024], in_=pb[0:64, :]), [r_pb], [r_qT])
                    else:
                        P.op("act", lambda e, grp=grp: e.copy(out=qT[:, grp * 1024:(grp + 1) * 1024], in_=pb[0:64, :]), [r_pb], [r_qT])
                    yield
                P.dma("pool", c.qT[i], qT[:, 0:1536], reads=[r_qT], writes=[c.r["qT"][i]])
                P.dma("pool", c.kT[i], qT[:, 1536:3072], reads=[r_qT], writes=[c.r["kT"][i]])

        run_lanes([list(range(k, NT, NL)) for k in range(NL)], body, skew=4)


def phase2c(c, l):
    from contextlib import ExitStack
    P, nc, NT = c.P, c.nc, c.NT
    with ExitStack() as st:
        sb = lambda n, s, d: st.enter_context(nc.sbuf_tensor(uname(n), list(s), d))
        sgw = sb("sgw", [128, 4, 128], BF16)
        sgn = sb("sgn", [128, 256], F32)
        sgb = sb("sgb", [128, 4], F32)
        r_w = Res("p2cw")
        P.dma("pool", sgw[:], c.w["sg_wT"][l].rearrange("g s t -> s g t"), writes=[r_w])
        P.dma("sp", sgn[:], c.w["sg_norm_g"][l:l + 1, :].partition_broadcast(128), writes=[r_w])
        P.dma("sp", sgb[:], c.w["sg_bT"][l], writes=[r_w])

        def mkpools(k):
            mk = lambda name, shape, dt, n=2: Pool2(st, nc, f"{name}_l{k}_", shape, dt, n)
            return dict(uv=mk("uv", [128, 512], F32), zc=mk("zc", [128, 256], F32), junk=mk("junkc", [128, 256], F32, 1), ss=mk("ssc", [128, 4], F32),
                        vn=mk("vn", [128, 256], BF16), sv=mk("sv", [128, 256], F32), ys=mk("ysc", [128, 256], BF16))
        pools = [mkpools(k) for k in range(6)]

        def body(i, k):
            pl = pools[k]
            banks = [(c.pf[k], c.r_pf[k])]
            rows = slice(i * 128, (i + 1) * 128)
            uv, r_uv = pl["uv"].next()
            zc, r_zc = pl["zc"].next()
            P.dma("sp", uv[:], c.uvc[rows, :], reads=[c.r["uvc"][i]], writes=[r_uv])
            P.dma("sp", zc[:], c.zs[rows, 512:768], reads=[c.r["zs"][i]], writes=[r_zc])
            junk, r_j = pl["junk"].next()
            ss, r_ss = pl["ss"].next()
            vn, r_vn = pl["vn"].next()
            sv, r_sv = pl["sv"].next()
            ysc, r_y = pl["ys"].next()
            P.op("act", lambda e: e.activation(out=junk[:], in_=uv[:, 256:512], func=AF.Square, accum_out=ss[:, 0:1]), [r_uv], [r_j, r_ss])
            P.op("dve", lambda e: e.tensor_scalar(out=ss[:, 1:2], in0=ss[:, 0:1], scalar1=1.0 / 256, scalar2=EPS, op0=ALU.mult, op1=ALU.add), [r_ss], [r_ss])
            yield
            P.op("act", lambda e: e.activation(out=ss[:, 2:3], in_=ss[:, 1:2], func=AF.Sqrt), [r_ss], [r_ss])
            P.op("dve", lambda e: e.reciprocal(out=ss[:, 3:4], in_=ss[:, 2:3]), [r_ss], [r_ss])
            P.op("dve", lambda e: e.scalar_tensor_tensor(out=vn[:], in0=uv[:, 256:512], scalar=ss[:, 3:4], in1=sgn[:], op0=ALU.mult, op1=ALU.mult),
                 [r_uv, r_ss, r_w], [r_vn])
            yield
            ps, r_ps = banks[0]
            for g in range(4):
                P.op("pe", lambda e, g=g: e.matmul(out=ps[:, g * 64:(g + 1) * 64], lhsT=sgw[:, g, :], rhs=vn[:, g * 64:(g + 1) * 64], start=True, stop=True),
                     [r_vn, r_w], [r_ps])
            P.op("dve", lambda e: e.tensor_tensor(out=sv[:].rearrange("p (g d) -> p g d", d=64), in0=ps[:, 0:256].rearrange("p (g d) -> p g d", d=64),
                                                  in1=sgb[:].unsqueeze(2).to_broadcast([128, 4, 64]), op=ALU.add), [r_ps, r_w], [r_sv])
            yield
            P.op("pool", lambda e: e.tensor_tensor(out=sv[:], in0=sv[:], in1=uv[:, 0:256], op=ALU.mult), [r_sv, r_uv], [r_sv])
            P.op("dve", lambda e: e.tensor_tensor(out=ysc[:], in0=sv[:], in1=zc[:], op=ALU.mult), [r_sv, r_zc], [r_y])
            P.dma("pool", c.ys[rows, 512:768], ysc[:], reads=[r_y], writes=[c.r["ys"][i]])

        run_lanes([list(range(k, NT, 6)) for k in range(6)], body, skew=1)


def phase3(c, l, x_src, x_src_res, x_dst, x_dst_res):
    from contextlib import ExitStack
    P, nc, NT = c.P, c.nc, c.NT
    with ExitStack() as st:
        sb = lambda n, s, d: st.enter_context(nc.sbuf_tensor(uname(n), list(s), d))
        Wg = sb("Wg", [128, 8, 4096], BF16)
        Wbr = sb("Wbr", [128, 8, 1024], BF16)
        Wo = sb("Wo", [128, 8, 1024], BF16)
        gbc = sb("gbc3", [128, D], F32)
        r_W = Res("W3")
        for kc in range(8):
            P.dma("pool", Wg[:, kc, :], c.w["w_in"][l, kc * 128:(kc + 1) * 128, P1W:PW], writes=[r_W])
        wbr_flat = c.w["w_branch"][l].rearrange("b c n -> (b c) n")
        for kc in range(8):
            P.dma("pool", Wbr[:, kc, :], wbr_flat[kc * 128:(kc + 1) * 128, :], writes=[r_W])
            P.dma("pool", Wo[:, kc, :], c.w["w_out"][l, kc * 128:(kc + 1) * 128, :], writes=[r_W])
        P.dma("sp", gbc[:], c.w["norm_g"][l:l + 1, :].partition_broadcast(128), writes=[r_W])
        fuse_b = FUSE_BC and 1 in c.branches
        fuse_c = FUSE_BC and 2 in c.branches
        if fuse_c:
            sgw = sb("sgw3", [128, 4, 128], BF16)
            sgn = sb("sgn3", [128, 256], F32)
            sgb = sb("sgb3", [128, 4], F32)
            P.dma("pool", sgw[:], c.w["sg_wT"][l].rearrange("g s t -> s g t"), writes=[r_W])
            P.dma("sp", sgn[:], c.w["sg_norm_g"][l:l + 1, :].partition_broadcast(128), writes=[r_W])
            P.dma("sp", sgb[:], c.w["sg_bT"][l], writes=[r_W])
        if fuse_b:
            bandc = sb("bandc3", [128, 512], F32)
            bandh = sb("bandh3", [16, 512], F32)
            pw = sb("pw3", [128, 2, 128], BF16)
            psc = sb("psc3", [128, 256], F32)
            invc = sb("invc3", [128, NT * 4], F32)
            P.dma("sp", bandc[:], c.bandc, writes=[r_W])
            P.dma("sp", bandh[:], c.bandh, writes=[r_W])
            P.dma("sp", psc[:], c.w["pool_scale"][l:l + 1, :].partition_broadcast(128), writes=[r_W])
            P.dma("sp", invc[:], c.invc, writes=[r_W])
            P.op("pool", lambda e: e.memset(pw[:], 0.0), [], [r_W])
            for g in range(4):
                j, gl = g // 2, g % 2
                P.dma("pool", pw[gl * 64:(gl + 1) * 64, j, gl * 64:(gl + 1) * 64], c.w["pool_w"][l, g], writes=[r_W])

        def mkpools(k):
            mk = lambda name, shape, dt, n=1: Pool2(st, nc, f"{name}_l{k}_", shape, dt, n)
            return dict(junk=mk("junk3", [128, D], BF16), ss=mk("ss3", [128, 4], F32), h=mk("h3", [128, D], BF16), hT=mk("hT3", [128, D], BF16),
                        xt=mk("xt3", [128, D], F32), ys=mk("ys3", [128, D], BF16), ysT=mk("ysT3", [128, D], BF16), gs=mk("gs3", [128, 512], F32, 2),
                        tmp=mk("tmp3", [128, 512], F32, 2), acc=mk("acc3", [128, D], F32), m=mk("m3", [128, D], BF16), mT=mk("mT3", [128, D], BF16),
                        xo=mk("xo3", [128, D], F32),
                        uv=mk("uv3", [128, 512], F32), zbc=mk("zbc3", [128, 512], F32), junkc=mk("junkc3", [128, 256], F32), ssc=mk("ssc3", [128, 4], F32),
                        vn=mk("vn3", [128, 256], BF16), sv=mk("sv3", [128, 256], F32), u=mk("ub3", [128, 256], F32), uh=mk("uh3", [16, 256], F32),
                        dp=mk("dp3", [128, 256], BF16), dT=mk("dT3", [128, 256], BF16), yb=mk("yb3", [128, 256], F32))
        pools = [mkpools(k) for k in range(NL)]

        def body(i, k):
            pl = pools[k]
            banks, pbk = lane_banks(c, k)
            rows = slice(i * 128, (i + 1) * 128)
            xt, r_xt = pl["xt"].next()
            P.dma("sp", xt[:], x_src[rows, :], reads=([x_src_res[i]] if x_src_res else []), writes=[r_xt])
            ys, r_ys = pl["ys"].next()
            P.dma("sp", ys[:], c.ys[rows, :], reads=[c.r["ys"][i]], writes=[r_ys])
            if fuse_b or fuse_c:
                zbc, r_zbc = pl["zbc"].next()
                P.dma("sp", zbc[:], c.zs[rows, 256:768], reads=[c.r["zs"][i]], writes=[r_zbc])
            if fuse_c:
                uv, r_uv = pl["uv"].next()
                P.dma("sp", uv[:], c.uvc[rows, :], reads=[c.r["uvc"][i]], writes=[r_uv])
            if fuse_b:
                u, r_u = pl["u"].next()
                uh, r_uh = pl["uh"].next()
                nbr = [c.r["ub"][j] for j in (i - 1, i, i + 1) if 0 <= j < NT] + [c.r_pad]
                P.dma("sp", u[:], c.ub[16 + i * 128:16 + (i + 1) * 128, :], reads=[c.r["ub"][i]], writes=[r_u])
                P.dma("sp", uh[0:8, :], c.ub[16 + i * 128 - 8:16 + i * 128, :], reads=nbr, writes=[r_uh])
                P.dma("sp", uh[8:16, :], c.ub[16 + (i + 1) * 128:16 + (i + 1) * 128 + 8, :], reads=nbr, writes=[r_uh])
            h, r_h = rms_h(c, xt, r_xt, gbc, r_W, pl)
            yield
            hT, r_hT = pl["hT"].next()
            transpose8(c, h, r_h, hT, r_hT, pbk)
            yield
            pm, r_pm = banks[2]
            if fuse_c:
                junk, r_j = pl["junkc"].next()
                ss, r_ss = pl["ssc"].next()
                vn, r_vn = pl["vn"].next()
                sv, r_sv = pl["sv"].next()
                P.op("act", lambda e: e.activation(out=junk[:], in_=uv[:, 256:512], func=AF.Square, accum_out=ss[:, 0:1]), [r_uv], [r_j, r_ss])
                P.op("dve", lambda e: e.tensor_scalar(out=ss[:, 1:2], in0=ss[:, 0:1], scalar1=1.0 / 256, scalar2=EPS, op0=ALU.mult, op1=ALU.add), [r_ss], [r_ss])
                P.op("act", lambda e: e.activation(out=ss[:, 2:3], in_=ss[:, 1:2], func=AF.Sqrt), [r_ss], [r_ss])
                P.op("dve", lambda e: e.reciprocal(out=ss[:, 3:4], in_=ss[:, 2:3]), [r_ss], [r_ss])
                P.op("dve", lambda e: e.scalar_tensor_tensor(out=vn[:], in0=uv[:, 256:512], scalar=ss[:, 3:4], in1=sgn[:], op0=ALU.mult, op1=ALU.mult),
                     [r_uv, r_ss, r_W], [r_vn])
                yield
                for g in range(4):
                    P.op("pe", lambda e, g=g: e.matmul(out=pm[:, g * 64:(g + 1) * 64], lhsT=sgw[:, g, :], rhs=vn[:, g * 64:(g + 1) * 64], start=True, stop=True),
                         [r_vn, r_W], [r_pm])
                P.op("dve", lambda e: e.tensor_tensor(out=sv[:].rearrange("p (g d) -> p g d", d=64), in0=pm[:, 0:256].rearrange("p (g d) -> p g d", d=64),
                                                      in1=sgb[:].unsqueeze(2).to_broadcast([128, 4, 64]), op=ALU.add), [r_pm, r_W], [r_sv])
                yield
                P.op("pool", lambda e: e.tensor_tensor(out=sv[:], in0=sv[:], in1=uv[:, 0:256], op=ALU.mult), [r_sv, r_uv], [r_sv])
                P.op("dve", lambda e: e.tensor_tensor(out=ys[:, 512:768], in0=sv[:], in1=zbc[:, 256:512], op=ALU.mult), [r_sv, r_zbc], [r_ys])
            if fuse_b:
                dp, r_dp = pl["dp"].next()
                dT, r_dT = pl["dT"].next()
                yb, r_yb = pl["yb"].next()
                for g in range(4):
                    P.op("pe", lambda e, g=g: e.matmul(out=pm[:, g * 64:(g + 1) * 64], lhsT=bandc[:, g * 128:(g + 1) * 128], rhs=u[:, g * 64:(g + 1) * 64],
                                                       start=True, stop=False), [r_u, r_W], [r_pm])
                    P.op("pe", lambda e, g=g: e.matmul(out=pm[:, g * 64:(g + 1) * 64], lhsT=bandh[:, g * 128:(g + 1) * 128], rhs=uh[:, g * 64:(g + 1) * 64],
                                                       start=False, stop=True), [r_uh, r_W], [r_pm])
                for g in range(4):
                    P.op("dve", lambda e, g=g: e.scalar_tensor_tensor(out=dp[:, g * 64:(g + 1) * 64], in0=pm[:, g * 64:(g + 1) * 64],
                                                                      scalar=invc[:, i * 4 + g:i * 4 + g + 1], in1=u[:, g * 64:(g + 1) * 64],
                                                                      op0=ALU.mult, op1=ALU.subtract), [r_pm, r_u, r_W], [r_dp])
                yield
                transpose8(c, dp, r_dp, dT, r_dT, pbk, n=2)
                yield
                for j in range(2):
                    P.op("pe", lambda e, j=j: e.matmul(out=pm[:, j * 128:(j + 1) * 128], lhsT=dT[:, j * 128:(j + 1) * 128], rhs=pw[:, j, :], start=True, stop=True),
                         [r_dT, r_W], [r_pm])
                P.op("dve", lambda e: e.tensor_tensor(out=yb[:], in0=pm[:, 0:256], in1=psc[:], op=ALU.mult), [r_pm, r_W], [r_yb])
                yield
                P.op("pool", lambda e: e.tensor_tensor(out=ys[:, 256:512], in0=yb[:], in1=zbc[:, 0:256], op=ALU.mult), [r_yb, r_zbc], [r_ys])
            yield
            ysT, r_ysT = pl["ysT"].next()
            transpose8(c, ys, r_ys, ysT, r_ysT, pbk, eng="dve")
            yield
            acc, r_acc = pl["acc"].next()
            for bi, b in enumerate(c.branches):
                for cb in range(2):
                    pg, r_pg = banks[0]
                    pbr, r_pbr = banks[1]
                    col = b * 1024 + cb * 512
                    for kc in range(8):
                        P.op("pe", lambda e, kc=kc, col=col: e.matmul(out=pg[:, :], lhsT=hT[:, kc * 128:(kc + 1) * 128], rhs=Wg[:, kc, col:col + 512],
                                                                      start=(kc == 0), stop=(kc == 7)), [r_hT, r_W], [r_pg])
                    gs, r_gs = pl["gs"].next()
                    P.op("act", lambda e, gs=gs: e.activation(out=gs[:], in_=pg[:, :], func=AF.Sigmoid), [r_pg], [r_gs])
                    for kk in range(2):
                        kc = 2 * b + kk
                        P.op("pe", lambda e, kc=kc, kk=kk, cb=cb: e.matmul(out=pbr[:, :], lhsT=ysT[:, kc * 128:(kc + 1) * 128],
                                                                             rhs=Wbr[:, kc, cb * 512:(cb + 1) * 512], start=(kk == 0), stop=(kk == 1)),
                             [r_ysT, r_W], [r_pbr])
                    if bi == 0:
                        P.op("dve", lambda e, cb=cb, gs=gs: e.tensor_tensor(out=acc[:, cb * 512:(cb + 1) * 512], in0=gs[:], in1=pbr[:, :], op=ALU.mult),
                             [r_gs, r_pbr], [r_acc])
                    else:
                        tmp, r_tmp = pl["tmp"].next()
                        P.op("dve", lambda e, gs=gs, tmp=tmp: e.tensor_tensor(out=tmp[:], in0=gs[:], in1=pbr[:, :], op=ALU.mult), [r_gs, r_pbr], [r_tmp])
                        P.op("pool", lambda e, cb=cb, tmp=tmp: e.tensor_tensor(out=acc[:, cb * 512:(cb + 1) * 512], in0=acc[:, cb * 512:(cb + 1) * 512], in1=tmp[:], op=ALU.add),
                             [r_tmp, r_acc], [r_acc])
                    yield
            m, r_m = pl["m"].next()
            mT, r_mT = pl["mT"].next()
            P.op("act", lambda e: e.copy(out=m[:], in_=acc[:]), [r_acc], [r_m])
            yield
            transpose8(c, m, r_m, mT, r_mT, pbk)
            yield
            xo, r_xo = pl["xo"].next()
            po, r_po = banks[2]
            for cb in range(2):
                for kc in range(8):
                    P.op("pe", lambda e, kc=kc, cb=cb: e.matmul(out=po[:, :], lhsT=mT[:, kc * 128:(kc + 1) * 128], rhs=Wo[:, kc, cb * 512:(cb + 1) * 512],
                                                                start=(kc == 0), stop=(kc == 7)), [r_mT, r_W], [r_po])
                P.op("dve", lambda e, cb=cb: e.tensor_tensor(out=xo[:, cb * 512:(cb + 1) * 512], in0=po[:, :], in1=xt[:, cb * 512:(cb + 1) * 512], op=ALU.add),
                     [r_po, r_xt], [r_xo])
                yield
            P.dma("pool", x_dst[rows, :], xo[:], reads=[r_xo], writes=[x_dst_res[i]])

        run_lanes([list(range(k, NT, NL)) for k in range(NL)], body, skew=5)


def phase2b(c, l):
    from contextlib import ExitStack
    P, nc, NT = c.P, c.nc, c.NT
    with ExitStack() as st:
        sb = lambda n, s, d: st.enter_context(nc.sbuf_tensor(uname(n), list(s), d))
        bandc = sb("bandc", [128, 512], F32)
        bandh = sb("bandh", [16, 512], F32)
        pw = sb("pw", [128, 2, 128], BF16)
        psc = sb("psc", [128, 256], F32)
        invc = sb("invc", [128, NT * 4], F32)
        r_w = Res("p2bw")
        P.dma("sp", bandc[:], c.bandc, writes=[r_w])
        P.dma("sp", bandh[:], c.bandh, writes=[r_w])
        P.dma("sp", psc[:], c.w["pool_scale"][l:l + 1, :].partition_broadcast(128), writes=[r_w])
        P.dma("sp", invc[:], c.invc, writes=[r_w])
        P.op("pool", lambda e: e.memset(pw[:], 0.0), [], [r_w])
        for g in range(4):
            j, gl = g // 2, g % 2
            P.dma("pool", pw[gl * 64:(gl + 1) * 64, j, gl * 64:(gl + 1) * 64], c.w["pool_w"][l, g], writes=[r_w])

        def mkpools(k):
            mk = lambda name, shape, dt, n=2: Pool2(st, nc, f"{name}_l{k}_", shape, dt, n)
            return dict(u=mk("ub", [128, 256], F32), uh=mk("uh", [16, 256], F32), zb=mk("zb", [128, 256], F32), d=mk("dpool", [128, 256], BF16),
                        dT=mk("dT", [128, 256], BF16), y=mk("ybf", [128, 256], F32), yo=mk("ybo", [128, 256], BF16))
        pools = [mkpools(k) for k in range(3)]

        def body(i, k):
            pl = pools[k]
            banks = [(c.pf[2 * k], c.r_pf[2 * k]), (c.pf[2 * k + 1], c.r_pf[2 * k + 1])]
            pbk = (c.pbf[k % 2], c.r_pbf[k % 2])
            rows = slice(i * 128, (i + 1) * 128)
            u, r_u = pl["u"].next()
            uh, r_uh = pl["uh"].next()
            zb, r_zb = pl["zb"].next()
            d, r_d = pl["d"].next()
            dT, r_dT = pl["dT"].next()
            y, r_y = pl["y"].next()
            yo, r_yo = pl["yo"].next()
            nb = [c.r["ub"][j] for j in (i - 1, i, i + 1) if 0 <= j < NT] + [c.r_pad]
            P.dma("sp", u[:], c.ub[16 + i * 128:16 + (i + 1) * 128, :], reads=[c.r["ub"][i]], writes=[r_u])
            P.dma("sp", uh[0:8, :], c.ub[16 + i * 128 - 8:16 + i * 128, :], reads=nb, writes=[r_uh])
            P.dma("sp", uh[8:16, :], c.ub[16 + (i + 1) * 128:16 + (i + 1) * 128 + 8, :], reads=nb, writes=[r_uh])
            P.dma("sp", zb[:], c.zs[rows, 256:512], reads=[c.r["zs"][i]], writes=[r_zb])
            yield
            ps, r_ps = banks[0]
            for g in range(4):
                P.op("pe", lambda e, g=g: e.matmul(out=ps[:, g * 64:(g + 1) * 64], lhsT=bandc[:, g * 128:(g + 1) * 128], rhs=u[:, g * 64:(g + 1) * 64],
                                                   start=True, stop=False), [r_u, r_w], [r_ps])
                P.op("pe", lambda e, g=g: e.matmul(out=ps[:, g * 64:(g + 1) * 64], lhsT=bandh[:, g * 128:(g + 1) * 128], rhs=uh[:, g * 64:(g + 1) * 64],
                                                   start=False, stop=True), [r_uh, r_w], [r_ps])
            for g in range(4):
                P.op("dve", lambda e, g=g: e.scalar_tensor_tensor(out=d[:, g * 64:(g + 1) * 64], in0=ps[:, g * 64:(g + 1) * 64],
                                                                  scalar=invc[:, i * 4 + g:i * 4 + g + 1], in1=u[:, g * 64:(g + 1) * 64],
                                                                  op0=ALU.mult, op1=ALU.subtract), [r_ps, r_u, r_w], [r_d])
            yield
            transpose8(c, d, r_d, dT, r_dT, pbk, n=2)
            yield
            ps2, r_ps2 = banks[1]
            for j in range(2):
                P.op("pe", lambda e, j=j: e.matmul(out=ps2[:, j * 128:(j + 1) * 128], lhsT=dT[:, j * 128:(j + 1) * 128], rhs=pw[:, j, :], start=True, stop=True),
                     [r_dT, r_w], [r_ps2])
            P.op("dve", lambda e: e.tensor_tensor(out=y[:], in0=ps2[:, 0:256], in1=psc[:], op=ALU.mult), [r_ps2, r_w], [r_y])
            yield
            P.op("pool", lambda e: e.tensor_tensor(out=yo[:], in0=y[:], in1=zb[:], op=ALU.mult), [r_y, r_zb], [r_yo])
            P.dma("pool", c.ys[rows, 256:512], yo[:], reads=[r_yo], writes=[c.r["ys"][i]])

        run_lanes([list(range(k, NT, 3)) for k in range(3)], body, skew=2)


def phase2a(c, l):
    from contextlib import ExitStack
    P, nc, NT = c.P, c.nc, c.NT
    use_sb = (NT % 16 == 0)
    NG = 2 if use_sb else 3
    if use_sb:
        phase2a_g3(c, l)
        barrier(c)
    with ExitStack() as st:
        sb = lambda n, s, d: st.enter_context(nc.sbuf_tensor(uname(n), list(s), d))
        masks = sb("amask", [128, 25, 512], BF16)
        r_w = Res("p2aw")
        for ci in range(25):
            P.dma("pool", masks[:, ci, :], c.amask[ci], writes=[r_w])
        depth = [2 * w + 2 for w in ATT_W]
        kring = [Pool2(st, nc, f"kr{g}_", [64, 512], BF16, depth[g]) for g in range(NG)]
        vring = [Pool2(st, nc, f"vr{g}_", [128, 260], BF16, depth[g]) for g in range(NG)]
        nd_ = Pool2(st, nc, "nd3t", [128, 260], F32, 2)
        loaded = [-1, -1, -1]
        q_ = Pool2(st, nc, "qTa", [64, 1536], BF16, 2)
        za_ = Pool2(st, nc, "za", [128, 256], F32, 2)
        e_ = Pool2(st, nc, "eexp", [128, 512], BF16, 3)
        p_ = Pool2(st, nc, "pexp", [128, 512], BF16, 6)
        dn_ = Pool2(st, nc, "den", [128, 8], F32, 2)
        ya_ = Pool2(st, nc, "yaf", [128, 256], F32, 2)
        yo_ = Pool2(st, nc, "yao", [128, 256], BF16, 2)
        mcol = []
        ci = 0
        for g in range(3):
            mcol.append({coff: ci + k for k, coff in enumerate(range(-ATT_W[g], ATT_W[g] + 1))})
            ci += 2 * ATT_W[g] + 1

        def ensure(g, upto):
            while loaded[g] < min(upto, NT - 1):
                j = loaded[g] + 1
                kt, r_kt = kring[g].t[j % depth[g]], kring[g].r[j % depth[g]]
                vt, r_vt = vring[g].t[j % depth[g]], vring[g].r[j % depth[g]]
                P.dma("sp", kt[:], c.kT[j][:, g * 512:(g + 1) * 512], reads=[c.r["kT"][j]], writes=[r_kt])
                P.dma("sp", vt[:], c.va[j][:, g * 260:(g + 1) * 260], reads=[c.r["va"][j]], writes=[r_vt])
                loaded[g] = j

        LOOK = 3
        tiles = {}

        def tile_begin(i):
            rows = slice(i * 128, (i + 1) * 128)
            for g in range(NG):
                ensure(g, i + ATT_W[g])
            qT, r_q = q_.next()
            za, r_za = za_.next()
            P.dma("sp", qT[:], c.qT[i], reads=[c.r["qT"][i]], writes=[r_q])
            P.dma("sp", za[:], c.zs[rows, 0:256], reads=[c.r["zs"][i]], writes=[r_za])
            ndt, r_ndt = (None, None)
            if use_sb:
                ndt, r_ndt = nd_.next()
                P.dma("sp", ndt[:], c.nd3[rows, :], reads=[c.r["nd3"][i // 16]], writes=[r_ndt])
            chunks = [(g, coff) for g in range(NG) for coff in range(-ATT_W[g], ATT_W[g] + 1) if 0 <= i + coff < NT]
            tiles[i] = dict(qT=qT, r_q=r_q, za=za, r_za=r_za, n=len(chunks), pso=c.pf[4 + i % 2], r_pso=c.r_pf[4 + i % 2], ndt=ndt, r_ndt=r_ndt)
            return chunks

        def stage_qk(i, idx, g, coff, seq):
            t = tiles[i]
            qT, r_q = t["qT"], t["r_q"]
            j = i + coff
            kt, r_kt = kring[g].t[j % depth[g]], kring[g].r[j % depth[g]]
            pss, r_pss = c.pf[seq % 4], c.r_pf[seq % 4]
            for h in range(4):
                P.op("pe", lambda e, h=h: e.matmul(out=pss[:, h * 128:(h + 1) * 128], lhsT=kt[:, h * 128:(h + 1) * 128],
                                                   rhs=qT[:, (g * 4 + h) * 128:(g * 4 + h + 1) * 128], start=True, stop=True),
                     [r_kt, r_q], [r_pss])
            ex, r_ex = e_.next()
            pp, r_pp = p_.next()
            P.op("act", lambda e: e.activation(out=ex[:], in_=pss[:, :], func=AF.Exp), [r_pss], [r_ex])
            mc = mcol[g][coff]
            P.op("dve", lambda e: e.tensor_tensor(out=pp[:], in0=ex[:], in1=masks[:, mc, :], op=ALU.mult), [r_ex, r_w], [r_pp])
            return pp, r_pp

        def stage_pv(i, idx, g, coff, pp, r_pp):
            t = tiles[i]
            pso, r_pso = t["pso"], t["r_pso"]
            j = i + coff
            vt, r_vt = vring[g].t[j % depth[g]], vring[g].r[j % depth[g]]
            n = t["n"]
            for h in range(4):
                P.op("pe", lambda e, h=h: e.matmul(out=pso[:, h * 65:(h + 1) * 65], lhsT=pp[:, h * 128:(h + 1) * 128],
                                                   rhs=vt[:, h * 65:(h + 1) * 65], start=(idx == 0 and h == 0), stop=(idx == n - 1), skip_group_check=True),
                     [r_pp, r_vt], [r_pso])
            if idx == n - 1:
                tile_end(i)

        def tile_end(i):
            rows = slice(i * 128, (i + 1) * 128)
            t = tiles.pop(i)
            pso, r_pso, za, r_za = t["pso"], t["r_pso"], t["za"], t["r_za"]
            dn, r_dn = dn_.next()
            ya, r_ya = ya_.next()
            yo, r_yo = yo_.next()
            if use_sb:
                ndt, r_ndt = t["ndt"], t["r_ndt"]
                P.op("dve", lambda e: e.tensor_tensor(out=ndt[:], in0=pso[:, 0:260], in1=ndt[:], op=ALU.add), [r_pso, r_ndt], [r_ndt])
                pv = ndt[:].rearrange("p (h d) -> p h d", d=65)
                r_pso = r_ndt
            else:
                pv = pso[:, 0:260].rearrange("p (h d) -> p h d", d=65)
            P.op("dve", lambda e: e.tensor_scalar(out=dn[:, 0:4].unsqueeze(2), in0=pv[:, :, 64:65], scalar1=1e-30, scalar2=None, op0=ALU.max), [r_pso], [r_dn])
            P.op("dve", lambda e: e.reciprocal(out=dn[:, 4:8], in_=dn[:, 0:4]), [r_dn], [r_dn])
            P.op("dve", lambda e: e.tensor_tensor(out=ya[:].rearrange("p (h d) -> p h d", d=64), in0=pv[:, :, 0:64],
                                                  in1=dn[:, 4:8].unsqueeze(2).to_broadcast([128, 4, 64]), op=ALU.mult), [r_pso, r_dn], [r_ya])
            P.op("pool", lambda e: e.tensor_tensor(out=yo[:], in0=ya[:], in1=za[:], op=ALU.mult), [r_ya, r_za], [r_yo])
            P.dma("pool", c.ys[rows, 0:256], yo[:], reads=[r_yo], writes=[c.r["ys"][i]])

        def stream():
            for i in range(NT):
                first = True
                chunks = None
                for idx in range(10 ** 9):
                    if first:
                        chunks = tile_begin(i)
                        first = False
                    if idx >= len(chunks):
                        break
                    g, coff = chunks[idx]
                    yield (i, idx, g, coff)

        pend = []
        for seq, (i, idx, g, coff) in enumerate(stream()):
            pp, r_pp = stage_qk(i, idx, g, coff, seq)
            pend.append((i, idx, g, coff, pp, r_pp))
            if len(pend) > LOOK:
                stage_pv(*pend.pop(0))
        while pend:
            stage_pv(*pend.pop(0))


def phase2a_g3(c, l):
    from contextlib import ExitStack
    P, nc, NT, S = c.P, c.nc, c.NT, c.S
    NSB = NT // 16
    vflat = c.va.rearrange("t p c -> (t p) c").rearrange("(m r) c -> r m c", r=16)
    ndv = c.nd3.rearrange("(m r) c -> r m c", r=16)
    with ExitStack() as st:
        sb_ = lambda n, s_, d: st.enter_context(nc.sbuf_tensor(uname(n), list(s_), d))
        masks = sb_("amask3", [128, 3, 512], BF16)
        r_w = Res("p2a3w")
        for ci in range(3):
            P.dma("pool", masks[:, ci, :], c.amask3[ci], writes=[r_w])
        stage_ = Pool2(st, nc, "g3stage", [64, 16 * 512], BF16, 2)
        kp_ = Pool2(st, nc, "g3kp", [64, 4 * 16 * 128], BF16, 3)
        qp_ = Pool2(st, nc, "g3qp", [64, 4 * 16 * 128], BF16, 2)
        vr_ = Pool2(st, nc, "g3v", [128, 260], BF16, 48)
        e_ = Pool2(st, nc, "g3e", [128, 512], BF16, 3)
        p_ = Pool2(st, nc, "g3p", [128, 512], BF16, 6)
        o_ = Pool2(st, nc, "g3o", [128, 260], F32, 3)
        kload = {}

        def permute(src_scr, sbi, dst, r_dst, col0):
            stg, r_stg = stage_.next()
            P.dma("sp", stg[:].rearrange("p (t c) -> p t c", c=512), src_scr[sbi * 16:(sbi + 1) * 16, :, col0:col0 + 512].rearrange("t p c -> p t c"),
                  reads=[c.r["qT" if src_scr is c.qT else "kT"][j] for j in range(sbi * 16, (sbi + 1) * 16)], writes=[r_stg])
            sv = stg[:].rearrange("p (t h pp r) -> p t h pp r", t=16, h=4, pp=8, r=16)
            dv = dst[:].rearrange("p (h r t pp) -> p h r t pp", h=4, r=16, t=16, pp=8)
            for h in range(4):
                eng = "act" if h % 2 == 0 else "pool"
                if eng == "act":
                    P.op("act", lambda e, h=h: e.copy(out=dv[:, h], in_=sv[:, :, h].rearrange("p t pp r -> p r t pp")), [r_stg], [r_dst])
                else:
                    P.op("pool", lambda e, h=h: e.tensor_copy(out=dv[:, h], in_=sv[:, :, h].rearrange("p t pp r -> p r t pp")), [r_stg], [r_dst])

        def ensure_k(sbi):
            if sbi in kload or not (0 <= sbi < NSB):
                return
            kp, r_kp = kp_.t[sbi % 3], kp_.r[sbi % 3]
            permute(c.kT, sbi, kp, r_kp, 1024)
            for r in range(16):
                vt, r_vt = vr_.t[(sbi % 3) * 16 + r], vr_.r[(sbi % 3) * 16 + r]
                P.dma("sp", vt[:], vflat[r, sbi * 128:(sbi + 1) * 128, 520:780], reads=[c.r["va"][j] for j in range(sbi * 16, (sbi + 1) * 16)], writes=[r_vt])
            kload[sbi] = True

        LOOK = 3
        units = []
        for sbi in range(NSB):
            for r in range(16):
                cl = [cf for cf in (-1, 0, 1) if 0 <= sbi + cf < NSB]
                for idx, cf in enumerate(cl):
                    units.append((sbi, r, cf, idx, len(cl)))
        qcur = {}
        pend = []

        def stage_pv(sbi, r, cf, idx, n, pp, r_pp, seq):
            pso, r_pso = c.pf[4 + (sbi * 16 + r) % 2], c.r_pf[4 + (sbi * 16 + r) % 2]
            sk = sbi + cf
            vt, r_vt = vr_.t[(sk % 3) * 16 + r], vr_.r[(sk % 3) * 16 + r]
            for h in range(4):
                P.op("pe", lambda e, h=h: e.matmul(out=pso[:, h * 65:(h + 1) * 65], lhsT=pp[:, h * 128:(h + 1) * 128], rhs=vt[:, h * 65:(h + 1) * 65],
                                                   start=(idx == 0 and h == 0), stop=(idx == n - 1), skip_group_check=True), [r_pp, r_vt], [r_pso])
            if idx == n - 1:
                ot, r_ot = o_.next()
                P.op("act", lambda e: e.copy(out=ot[:], in_=pso[:, 0:260]), [r_pso], [r_ot])
                P.dma("pool", ndv[r, sbi * 128:(sbi + 1) * 128, :], ot[:], reads=[r_ot], writes=[c.r["nd3"][sbi]])

        for seq, (sbi, r, cf, idx, n) in enumerate(units):
            if sbi not in qcur:
                while pend:
                    stage_pv(*pend.pop(0))
                for s2 in (sbi - 1, sbi, sbi + 1):
                    ensure_k(s2)
                qp, r_qp = qp_.next()
                permute(c.qT, sbi, qp, r_qp, 1024)
                qcur.clear()
                qcur[sbi] = (qp, r_qp)
            qp, r_qp = qcur[sbi]
            sk = sbi + cf
            kp, r_kp = kp_.t[sk % 3], kp_.r[sk % 3]
            pss, r_pss = c.pf[seq % 4], c.r_pf[seq % 4]
            for h in range(4):
                o0 = (h * 16 + r) * 128
                P.op("pe", lambda e, h=h, o0=o0, kp=kp, qp=qp, pss=pss: e.matmul(out=pss[:, h * 128:(h + 1) * 128], lhsT=kp[:, o0:o0 + 128], rhs=qp[:, o0:o0 + 128], start=True, stop=True),
                     [r_kp, r_qp], [r_pss])
            ex, r_ex = e_.next()
            pp, r_pp = p_.next()
            P.op("act", lambda e, ex=ex, pss=pss: e.activation(out=ex[:], in_=pss[:, :], func=AF.Exp), [r_pss], [r_ex])
            P.op("dve", lambda e, ex=ex, pp=pp, cf=cf: e.tensor_tensor(out=pp[:], in0=ex[:], in1=masks[:, cf + 1, :], op=ALU.mult), [r_ex, r_w], [r_pp])
            pend.append((sbi, r, cf, idx, n, pp, r_pp, seq))
            if len(pend) > LOOK:
                stage_pv(*pend.pop(0))
        while pend:
            stage_pv(*pend.pop(0))


def phase2d(c, l):
    from contextlib import ExitStack
    P, nc, NT = c.P, c.nc, c.NT
    NEG = -float(np.exp(-0.5))
    def setup_dir(d, st):
        if True:
            sb = lambda n, s, dt: st.enter_context(nc.sbuf_tensor(uname(n), list(s), dt))
            r_w = Res("p2dw")
            tri = sb("tri", [128, 642], F32)
            P.dma("sp", tri[:], c.tri, writes=[r_w])
            incl = tri[:, 128 * d:128 * d + 128]
            ones_m = tri[:, 512:640]
            ones_c = tri[:, 640:641]
            mask4 = sb("mask4", [128, 512], F32)
            maskT4 = sb("maskT4", [128, 512], F32)
            for q in range(4):
                src = tri[:, 256 + 128 * d:384 + 128 * d] if q % 2 == 0 else incl
                P.op("dve", lambda e, q=q, src=src: e.tensor_copy(out=mask4[:, q * 128:(q + 1) * 128], in_=src), [r_w], [r_w])
                P.op("dve", lambda e, q=q: e.tensor_copy(out=maskT4[:, q * 128:(q + 1) * 128], in_=tri[:, 256 + 128 * (1 - d):384 + 128 * (1 - d)]), [r_w], [r_w])
            identB = sb("identB", [64, 4, 64], F32)
            P.op("dve", lambda e: e.tensor_copy(out=identB[:], in_=c.idf[0:64, 0:64].unsqueeze(1).to_broadcast([64, 4, 64])), [c.r_const], [r_w])
            mu_r = sb("mu_r", [128, 768], F32)
            mu_l = sb("mu_l", [128, 128], F32)
            bias_wa = sb("bias_wa", [128, 512], F32)
            kkv = sb("kkv", [128, 256], F32)
            kav = sb("kav", [128, 256], F32)
            rkp = sb("rkp", [128, 256], F32)
            lng = sb("lng", [128, 256], F32)
            lnb = sb("lnb", [128, 256], F32)
            Wud = sb("Wud", [128, 512], BF16)
            bc = lambda ap: ap.partition_broadcast(128)
            P.dma("sp", mu_r[:], bc(c.w["mu_rkv"][l, d:d + 1, :]), writes=[r_w])
            P.dma("sp", mu_l[:], bc(c.w["mu_lat"][l, d:d + 1, :]), writes=[r_w])
            P.dma("sp", bias_wa[:, 0:256], bc(c.w["w0"][l, d:d + 1, :]), writes=[r_w])
            P.dma("sp", bias_wa[:, 256:512], bc(c.w["a0"][l, d:d + 1, :]), writes=[r_w])
            P.dma("sp", kkv[:], bc(c.w["k_k"][l, d:d + 1, :]), writes=[r_w])
            P.dma("sp", kav[:], bc(c.w["k_a"][l, d:d + 1, :]), writes=[r_w])
            P.dma("sp", rkp[:], bc(c.w["r_k"][l, d:d + 1, :]), writes=[r_w])
            P.dma("sp", lng[:], bc(c.w["ln_g"][l:l + 1, :]), writes=[r_w])
            P.dma("sp", lnb[:], bc(c.w["ln_b"][l:l + 1, :]), writes=[r_w])
            P.op("pool", lambda e: e.memset(Wud[:], 0.0), [], [r_w])
            P.dma("pool", Wud[0:64, 0:256], c.w["w_up"][l, d], writes=[r_w])
            P.dma("pool", Wud[64:128, 256:512], c.w["a_up"][l, d], writes=[r_w])
            ST = Pool2(st, nc, f"ST_d{d}_", [64, 256], F32, 2)
            st0, r_st0 = ST.next()
            P.op("pool", lambda e: e.memset(st0[:], 0.0), [], [r_st0])
            state = [st0, r_st0]

            def mk(name, shape, dt, n=1):
                return Pool2(st, nc, f"{name}_d{d}_", shape, dt, n)
            cur_, sh_ = mk("cur", [128, 1024], F32, 2), mk("sh", [128, 1024], F32, 2)
            xr_, xl_, tl_, tlT_ = mk("xr", [128, 768], F32), mk("xl", [128, 128], F32), mk("tl", [128, 128], BF16), mk("tlT", [128, 128], BF16)
            sg_, lw_ = mk("sg", [128, 512], F32), mk("logw", [128, 256], F32)
            kk_, sq_, s4_, k2_, bon_, beta_ = mk("kk", [128, 256], F32), mk("sqd", [128, 256], F32), mk("s4", [128, 32], F32), mk("k2", [128, 256], F32), mk("bon", [128, 256], F32), mk("beta", [128, 256], F32)
            cum_, ex_, tmp_ = mk("cum", [128, 256], F32), mk("exps", [128, 1024], F32), mk("tmpd", [128, 256], F32, 2)
            opb_ = mk("opb", [128, 7, 256], BF16)
            fTA_, fTB_ = mk("fTA", [64, 1024], BF16), mk("fTB", [64, 1024], BF16)
            AB_, AK_, NTa_ = mk("AB", [128, 1024], BF16), mk("AK", [128, 1024], BF16), mk("NTa", [128, 512], BF16, 3)
            Nn_ = mk("Nn", [128, 512], BF16, 3)
            Z_ = mk("Z", [128, 512], BF16, 3)
            gcol_, Q_, Pm_, Dm_ = mk("gcol", [64, 4], F32), mk("Q", [64, 512], F32), mk("Pm", [64, 256], F32), mk("Dm", [64, 256], F32)
            osb_, on_, yo_ = mk("osb", [128, 256], F32), mk("on", [128, 256], F32), mk("yod", [128, 256], BF16)
            banks, pbk = lane_banks(c, d)
            bank_i = [0]

            def nb():
                bank_i[0] = (bank_i[0] + 1) % 2
                return banks[bank_i[0]]

            def body(i):
                rows = slice(i * 128, (i + 1) * 128)
                v3 = lambda ap, dd=64: ap.rearrange("p (h d) -> p h d", d=dd)
                dbg_on = False

                def dbg(name, t, shape, dt, rs):
                    if dbg_on:
                        o = nc.dram_tensor("dbg_" + name, list(shape), dt, kind="ExternalOutput").ap()
                        P.dma("pool", o, t, reads=rs, writes=[Res()])
                cur, r_cur = cur_.next()
                sh, r_sh = sh_.next()
                nbr = [c.r["rkvl"][j] for j in (i - 1, i, i + 1) if 0 <= j < NT] + [c.r_pad]
                P.dma("sp", cur[:], c.rkvl[1 + i * 128:1 + (i + 1) * 128, :], reads=[c.r["rkvl"][i]], writes=[r_cur])
                so = 0 if d == 0 else 2
                P.dma("sp", sh[:], c.rkvl[so + i * 128:so + (i + 1) * 128, :], reads=nbr, writes=[r_sh])
                xr, r_xr = xr_.next()
                xl, r_xl = xl_.next()
                P.op("pool", lambda e: e.tensor_tensor(out=xr[:], in0=sh[:, 0:768], in1=cur[:, 0:768], op=ALU.subtract), [r_sh, r_cur], [r_xr])
                P.op("dve", lambda e: e.tensor_tensor(out=xr[:], in0=xr[:], in1=mu_r[:], op=ALU.mult), [r_xr, r_w], [r_xr])
                P.op("pool", lambda e: e.tensor_tensor(out=xr[:], in0=xr[:], in1=cur[:, 0:768], op=ALU.add), [r_xr, r_cur], [r_xr])
                lo = 768 + 128 * d
                P.op("dve", lambda e: e.tensor_tensor(out=xl[:], in0=sh[:, lo:lo + 128], in1=cur[:, lo:lo + 128], op=ALU.subtract), [r_sh, r_cur], [r_xl])
                P.op("dve", lambda e: e.tensor_tensor(out=xl[:], in0=xl[:], in1=mu_l[:], op=ALU.mult), [r_xl, r_w], [r_xl])
                P.op("dve", lambda e: e.tensor_tensor(out=xl[:], in0=xl[:], in1=cur[:, lo:lo + 128], op=ALU.add), [r_xl, r_cur], [r_xl])
                yield
                r_, k_, v_ = xr[:, 0:256], xr[:, 256:512], xr[:, 512:768]
                tl, r_tl = tl_.next()
                tlT, r_tlT = tlT_.next()
                P.op("act", lambda e: e.activation(out=tl[:, 0:64], in_=xl[:, 0:64], func=AF.Tanh), [r_xl], [r_tl])
                P.op("act", lambda e: e.copy(out=tl[:, 64:128], in_=xl[:, 64:128]), [r_xl], [r_tl])
                pb, r_pb = pbk
                P.op("pe", lambda e: e.transpose(out=pb[:, 0:128], in_=tl[:], identity=c.idb[:]), [r_tl, c.r_const], [r_pb])
                P.op("act", lambda e: e.copy(out=tlT[:], in_=pb[:, 0:128]), [r_pb], [r_tlT])
                yield
                ps, r_ps = nb()
                P.op("pe", lambda e: e.matmul(out=ps[:, :], lhsT=tlT[:], rhs=Wud[:], start=True, stop=True), [r_tlT, r_w], [r_ps])
                sg, r_sg = sg_.next()
                lw, r_lw = lw_.next()
                P.op("dve", lambda e: e.tensor_tensor(out=sg[:], in0=ps[:, :], in1=bias_wa[:], op=ALU.add), [r_ps, r_w], [r_sg])
                P.op("act", lambda e: e.activation(out=sg[:], in_=sg[:], func=AF.Sigmoid), [r_sg], [r_sg])
                P.op("act", lambda e: e.mul(out=lw[:], in_=sg[:, 0:256], mul=NEG), [r_sg], [r_lw])
                yield
                a_ = sg[:, 256:512]
                kk, r_kk = kk_.next()
                sq, r_sq = sq_.next()
                s4, r_s4 = s4_.next()
                P.op("dve", lambda e: e.tensor_tensor(out=kk[:], in0=k_, in1=kkv[:], op=ALU.mult), [r_xr, r_w], [r_kk])
                P.op("pool", lambda e: e.tensor_tensor(out=sq[:], in0=kk[:], in1=kk[:], op=ALU.mult), [r_kk], [r_sq])
                P.op("dve", lambda e: e.tensor_reduce(out=s4[:, 0:4], in_=v3(sq[:]), axis=AX.X, op=ALU.add), [r_sq], [r_s4])
                P.op("dve", lambda e: e.tensor_scalar(out=s4[:, 4:8], in0=s4[:, 0:4], scalar1=1e-12, scalar2=None, op0=ALU.add), [r_s4], [r_s4])
                P.op("act", lambda e: e.activation(out=s4[:, 0:4], in_=s4[:, 4:8], func=AF.Sqrt), [r_s4], [r_s4])
                P.op("dve", lambda e: e.reciprocal(out=s4[:, 8:12], in_=s4[:, 0:4]), [r_s4], [r_s4])
                P.op("dve", lambda e: e.tensor_tensor(out=v3(kk[:]), in0=v3(kk[:]), in1=s4[:, 8:12].unsqueeze(2).to_broadcast([128, 4, 64]), op=ALU.mult), [r_kk, r_s4], [r_kk])
                k2, r_k2 = k2_.next()
                P.op("dve", lambda e: e.scalar_tensor_tensor(out=k2[:], in0=a_, scalar=-1.0, in1=kav[:], op0=ALU.add, op1=ALU.mult), [r_sg, r_w], [r_k2])
                P.op("dve", lambda e: e.scalar_tensor_tensor(out=k2[:], in0=k2[:], scalar=1.0, in1=k_, op0=ALU.add, op1=ALU.mult), [r_k2, r_xr], [r_k2])
                bon, r_bon = bon_.next()
                P.op("pool", lambda e: e.tensor_tensor(out=bon[:], in0=r_, in1=k2[:], op=ALU.mult), [r_xr, r_k2], [r_bon])
                P.op("pool", lambda e: e.tensor_tensor(out=bon[:], in0=bon[:], in1=rkp[:], op=ALU.mult), [r_bon, r_w], [r_bon])
                P.op("dve", lambda e: e.tensor_reduce(out=s4[:, 12:16], in_=v3(bon[:]), axis=AX.X, op=ALU.add), [r_bon], [r_s4])
                P.op("dve", lambda e: e.tensor_tensor(out=v3(bon[:]), in0=v3(v_), in1=s4[:, 12:16].unsqueeze(2).to_broadcast([128, 4, 64]), op=ALU.mult), [r_xr, r_s4], [r_bon])
                beta, r_beta = beta_.next()
                P.op("pool", lambda e: e.tensor_tensor(out=beta[:], in0=kk[:], in1=a_, op=ALU.mult), [r_kk, r_sg], [r_beta])
                yield
                pc, r_pc = nb()
                P.op("pe", lambda e: e.matmul(out=pc[:, 0:256], lhsT=incl, rhs=lw[:], start=True, stop=True), [r_lw, r_w], [r_pc])
                P.op("pe", lambda e: e.matmul(out=pc[:, 256:512], lhsT=ones_m, rhs=lw[:], start=True, stop=True), [r_lw, r_w], [r_pc])
                pg, r_pg = nb()
                for h in range(4):
                    P.op("pe", lambda e, h=h: e.matmul(out=pg[0:64, h:h + 1], lhsT=lw[:, h * 64:(h + 1) * 64], rhs=ones_c, start=True, stop=True), [r_lw, r_w], [r_pg])
                gcol, r_gcol = gcol_.next()
                P.op("act", lambda e: e.activation(out=gcol[:], in_=pg[0:64, 0:4], func=AF.Exp), [r_pg], [r_gcol])
                yield
                cum, r_cum = cum_.next()
                ex, r_ex = ex_.next()
                t3, r_t3 = tmp_.next()
                t4, r_t4 = tmp_.next()
                ecum, encum, eexc, edec = ex[:, 0:256], ex[:, 256:512], ex[:, 512:768], ex[:, 768:1024]
                P.op("act", lambda e: e.copy(out=cum[:], in_=pc[:, 0:256]), [r_pc], [r_cum])
                P.op("act", lambda e: e.activation(out=ecum, in_=pc[:, 0:256], func=AF.Exp), [r_pc], [r_ex])
                P.op("act", lambda e: e.activation(out=encum, in_=pc[:, 0:256], func=AF.Exp, scale=-1.0), [r_pc], [r_ex])
                P.op("dve", lambda e: e.tensor_tensor(out=t3[:], in0=cum[:], in1=lw[:], op=ALU.subtract), [r_cum, r_lw], [r_t3])
                P.op("act", lambda e: e.activation(out=eexc, in_=t3[:], func=AF.Exp), [r_t3], [r_ex])
                P.op("dve", lambda e: e.tensor_tensor(out=t4[:], in0=pc[:, 256:512], in1=cum[:], op=ALU.subtract), [r_pc, r_cum], [r_t4])
                P.op("act", lambda e: e.activation(out=edec, in_=t4[:], func=AF.Exp), [r_t4], [r_ex])
                yield
                opb, r_opb = opb_.next()
                P.op("dve", lambda e: e.tensor_tensor(out=opb[:, 0, :], in0=r_, in1=ecum, op=ALU.mult), [r_xr, r_ex], [r_opb])
                P.op("pool", lambda e: e.tensor_tensor(out=opb[:, 1, :], in0=k2[:], in1=encum, op=ALU.mult), [r_k2, r_ex], [r_opb])
                P.op("dve", lambda e: e.tensor_tensor(out=opb[:, 2, :], in0=beta[:], in1=encum, op=ALU.mult), [r_beta, r_ex], [r_opb])
                P.op("dve", lambda e: e.scalar_tensor_tensor(out=opb[:, 3, :], in0=kk[:], scalar=-1.0, in1=eexc, op0=ALU.mult, op1=ALU.mult), [r_kk, r_ex], [r_opb])
                P.op("pool", lambda e: e.tensor_tensor(out=opb[:, 4, :], in0=k2[:], in1=edec, op=ALU.mult), [r_k2, r_ex], [r_opb])
                P.op("pool", lambda e: e.tensor_tensor(out=opb[:, 5, :], in0=beta[:], in1=edec, op=ALU.mult), [r_beta, r_ex], [r_opb])
                P.op("act", lambda e: e.copy(out=opb[:, 6, :], in_=v_), [r_xr], [r_opb])
                yield
                rb, kt, bt, ab, Kh, Bh, vb = (opb[:, q, :] for q in range(7))
                pa, r_pa = pbk
                fTA, r_fTA = fTA_.next()
                fTB, r_fTB = fTB_.next()
                for h in range(4):
                    hs = slice(h * 64, (h + 1) * 64)
                    P.op("pe", lambda e, h=h, hs=hs: e.transpose(out=pa[0:64, h * 128:(h + 1) * 128], in_=bt[:, hs], identity=c.idb[:]), [r_opb, c.r_const], [r_pa])
                    P.op("pe", lambda e, h=h, hs=hs: e.transpose(out=pa[0:64, (4 + h) * 128:(5 + h) * 128], in_=kt[:, hs], identity=c.idb[:]), [r_opb, c.r_const], [r_pa])
                P.op("act", lambda e: e.copy(out=fTA[:], in_=pa[0:64, :]), [r_pa], [r_fTA])
                yield
                for h in range(4):
                    hs = slice(h * 64, (h + 1) * 64)
                    P.op("pe", lambda e, h=h, hs=hs: e.transpose(out=pa[0:64, (2 * h) * 128:(2 * h + 1) * 128], in_=ab[:, hs], identity=c.idb[:]), [r_opb, c.r_const], [r_pa])
                    P.op("pe", lambda e, h=h, hs=hs: e.transpose(out=pa[0:64, (2 * h + 1) * 128:(2 * h + 2) * 128], in_=rb[:, hs], identity=c.idb[:]), [r_opb, c.r_const], [r_pa])
                P.op("dve", lambda e: e.tensor_copy(out=fTB[:], in_=pa[0:64, :]), [r_pa], [r_fTB])
                yield
                AB, r_AB = AB_.next()
                AK, r_AK = AK_.next()
                for (dst, r_dst, off) in ((AB, r_AB, 0), (AK, r_AK, 4)):
                    for pr in range(2):
                        px, r_px = nb()
                        for hh in range(2):
                            h = 2 * pr + hh
                            P.op("pe", lambda e, h=h, hh=hh, px=px, off=off: e.matmul(out=px[:, hh * 256:(hh + 1) * 256], lhsT=fTA[:, (off + h) * 128:(off + h + 1) * 128],
                                                                               rhs=fTB[:, 2 * h * 128:(2 * h + 2) * 128], start=True, stop=True), [r_fTA, r_fTB], [r_px])
                        P.op("dve", lambda e, pr=pr, px=px, dst=dst: e.tensor_tensor(out=dst[:, pr * 512:(pr + 1) * 512], in0=px[:, :], in1=mask4[:], op=ALU.mult), [r_px, r_w], [r_dst])
                        yield
                py, r_py = nb()
                for h in range(4):
                    P.op("pe", lambda e, h=h: e.matmul(out=py[:, h * 128:(h + 1) * 128], lhsT=fTB[:, 2 * h * 128:(2 * h + 1) * 128], rhs=fTA[:, h * 128:(h + 1) * 128],
                                                       start=True, stop=True), [r_fTA, r_fTB], [r_py])
                NTc, r_NTc = NTa_.next()
                P.op("dve", lambda e: e.tensor_tensor(out=NTc[:], in0=py[:, :], in1=maskT4[:], op=ALU.mult), [r_py, r_w], [r_NTc])
                yield
                pz, r_pz = nb()
                for h in range(4):
                    P.op("pe", lambda e, h=h: e.matmul(out=pz[:, h * 64:(h + 1) * 64], lhsT=AK[:, h * 256:h * 256 + 128], rhs=vb[:, h * 64:(h + 1) * 64], start=True, stop=True),
                         [r_AK, r_opb], [r_pz])
                Z, r_Z = Z_.next()
                P.op("dve", lambda e, Z=Z: e.tensor_copy(out=v3(Z[:], 128)[:, :, 0:64], in_=v3(ab)), [r_opb], [r_Z])
                P.op("act", lambda e, Z=Z: e.copy(out=v3(Z[:], 128)[:, :, 64:128], in_=v3(pz[:, 0:256])), [r_pz], [r_Z])
                yield
                dbg("Z0", Z[:], [128, 512], BF16, [r_Z]); dbg("NT0", NTc[:], [128, 512], BF16, [r_NTc]); dbg("AK", AK[:], [128, 1024], BF16, [r_AK])
                Ncur = [AB[:, h * 256:h * 256 + 128] for h in range(4)]
                r_N = r_AB
                for kq in range(7):
                    pq, r_pq = nb()
                    for h in range(4):
                        P.op("pe", lambda e, h=h, N=Ncur[h], Z=Z, pq=pq: e.matmul(out=pq[:, h * 128:(h + 1) * 128], lhsT=N, rhs=Z[:, h * 128:(h + 1) * 128], start=True, stop=True),
                             [r_N, r_Z], [r_pq])
                    Zn, r_Zn = Z_.next()
                    P.op("dve", lambda e, Z=Z, Zn=Zn, pq=pq: e.tensor_tensor(out=Zn[:], in0=pq[:, :], in1=Z[:], op=ALU.add), [r_pq, r_Z], [r_Zn])
                    yield
                    if kq < 6:
                        p1, r_p1 = nb()
                        for h in range(4):
                            P.op("pe", lambda e, h=h, N=Ncur[h], NTc=NTc, p1=p1: e.matmul(out=p1[:, h * 128:(h + 1) * 128], lhsT=NTc[:, h * 128:(h + 1) * 128], rhs=N, start=True, stop=True),
                                 [r_N, r_NTc], [r_p1])
                        p2, r_p2 = nb()
                        for h in range(4):
                            P.op("pe", lambda e, h=h, N=Ncur[h], NTc=NTc, p2=p2: e.matmul(out=p2[:, h * 128:(h + 1) * 128], lhsT=N, rhs=NTc[:, h * 128:(h + 1) * 128], start=True, stop=True),
                                 [r_N, r_NTc], [r_p2])
                        Nn, r_Nn = Nn_.next()
                        NTn, r_NTn = NTa_.next()
                        P.op("act", lambda e, Nn=Nn, p1=p1: e.copy(out=Nn[:], in_=p1[:, :]), [r_p1], [r_Nn])
                        P.op("dve", lambda e, NTn=NTn, p2=p2: e.tensor_copy(out=NTn[:], in_=p2[:, :]), [r_p2], [r_NTn])
                        yield
                        dbg(f"Nn{kq}", Nn[:], [128, 512], BF16, [r_Nn]); dbg(f"NTn{kq}", NTn[:], [128, 512], BF16, [r_NTn]); dbg(f"Zn{kq}", Zn[:], [128, 512], BF16, [r_Zn])
                        Ncur = [Nn[:, h * 128:(h + 1) * 128] for h in range(4)]
                        r_N, NTc, r_NTc = r_Nn, NTn, r_NTn
                    Z, r_Z = Zn, r_Zn
                WT = [Z[:, h * 128:h * 128 + 64] for h in range(4)]
                XT = [Z[:, h * 128 + 64:(h + 1) * 128] for h in range(4)]
                pQ, r_pQ = nb()
                for h in range(4):
                    P.op("pe", lambda e, h=h: e.matmul(out=pQ[0:64, h * 128:(h + 1) * 128], lhsT=WT[h], rhs=AB[:, h * 256 + 128:(h + 1) * 256], start=True, stop=True),
                         [r_Z, r_AB], [r_pQ])
                Q, r_Q = Q_.next()
                P.op("dve", lambda e: e.tensor_tensor(out=v3(Q[:], 128), in0=v3(pQ[0:64, :], 128), in1=fTB[:].rearrange("p (h two t) -> p h two t", two=2, t=128)[:, :, 1, :], op=ALU.add),
                     [r_pQ, r_fTB], [r_Q])
                yield
                pP, r_pP = nb()
                for h in range(4):
                    P.op("pe", lambda e, h=h: e.matmul(out=pP[0:64, h * 64:(h + 1) * 64], lhsT=WT[h], rhs=Bh[:, h * 64:(h + 1) * 64], start=True, stop=True), [r_Z, r_opb], [r_pP])
                Pm, r_Pm = Pm_.next()
                P.op("dve", lambda e: e.tensor_tensor(out=v3(Pm[:]), in0=identB[:], in1=gcol[:].unsqueeze(2).to_broadcast([64, 4, 64]), op=ALU.mult), [r_w, r_gcol], [r_Pm])
                P.op("dve", lambda e: e.tensor_tensor(out=Pm[:], in0=Pm[:], in1=pP[0:64, 0:256], op=ALU.add), [r_Pm, r_pP], [r_Pm])
                yield
                pD, r_pD = nb()
                for h in range(4):
                    P.op("pe", lambda e, h=h: e.matmul(out=pD[0:64, h * 64:(h + 1) * 64], lhsT=Bh[:, h * 64:(h + 1) * 64], rhs=XT[h], start=(h == 0), stop=False, skip_group_check=True),
                         [r_Z, r_opb], [r_pD])
                    P.op("pe", lambda e, h=h: e.matmul(out=pD[0:64, h * 64:(h + 1) * 64], lhsT=Kh[:, h * 64:(h + 1) * 64], rhs=vb[:, h * 64:(h + 1) * 64], start=False, stop=True, skip_group_check=True),
                         [r_opb], [r_pD])
                Dm, r_Dm = Dm_.next()
                P.op("act", lambda e: e.copy(out=Dm[:], in_=pD[0:64, 0:256]), [r_pD], [r_Dm])
                yield
                S0, r_S0 = state
                pO, r_pO = banks[2]
                for h in range(4):
                    P.op("pe", lambda e, h=h: e.matmul(out=pO[:, h * 64:(h + 1) * 64], lhsT=AB[:, h * 256 + 128:(h + 1) * 256], rhs=XT[h], start=(h == 0), stop=False, skip_group_check=True),
                         [r_AB, r_Z], [r_pO])
                    P.op("pe", lambda e, h=h: e.matmul(out=pO[:, h * 64:(h + 1) * 64], lhsT=AK[:, h * 256 + 128:(h + 1) * 256], rhs=vb[:, h * 64:(h + 1) * 64], start=False, stop=False, skip_group_check=True),
                         [r_AK, r_opb], [r_pO])
                    P.op("pe", lambda e, h=h, S0=S0: e.matmul(out=pO[:, h * 64:(h + 1) * 64], lhsT=Q[:, h * 128:(h + 1) * 128], rhs=S0[:, h * 64:(h + 1) * 64], start=False, stop=True, skip_group_check=True),
                         [r_Q, r_S0], [r_pO])
                osb, r_osb = osb_.next()
                P.op("act", lambda e: e.copy(out=osb[:], in_=pO[:, 0:256]), [r_pO], [r_osb])
                yield
                pS, r_pS = banks[2]
                for h in range(4):
                    P.op("pe", lambda e, h=h, S0=S0: e.matmul(out=pS[0:64, h * 64:(h + 1) * 64], lhsT=Pm[:, h * 64:(h + 1) * 64], rhs=S0[:, h * 64:(h + 1) * 64], start=True, stop=True),
                         [r_Pm, r_S0], [r_pS])
                S1, r_S1 = ST.next()
                P.op("dve", lambda e, S1=S1: e.tensor_tensor(out=S1[:], in0=pS[0:64, 0:256], in1=Dm[:], op=ALU.add), [r_pS, r_Dm], [r_S1])
                state[0], state[1] = S1, r_S1
                yield
                on, r_on = on_.next()
                P.op("dve", lambda e: e.tensor_reduce(out=s4[:, 16:20], in_=v3(osb[:]), axis=AX.X, op=ALU.add), [r_osb], [r_s4])
                P.op("pool", lambda e: e.tensor_tensor(out=on[:], in0=osb[:], in1=osb[:], op=ALU.mult), [r_osb], [r_on])
                P.op("dve", lambda e: e.tensor_reduce(out=s4[:, 20:24], in_=v3(on[:]), axis=AX.X, op=ALU.add), [r_on], [r_s4])
                P.op("dve", lambda e: e.tensor_scalar(out=s4[:, 16:20], in0=s4[:, 16:20], scalar1=1.0 / 64, scalar2=None, op0=ALU.mult), [r_s4], [r_s4])
                P.op("dve", lambda e: e.tensor_tensor(out=s4[:, 24:28], in0=s4[:, 16:20], in1=s4[:, 16:20], op=ALU.mult), [r_s4], [r_s4])
                P.op("dve", lambda e: e.scalar_tensor_tensor(out=s4[:, 20:24], in0=s4[:, 20:24], scalar=1.0 / 64, in1=s4[:, 24:28], op0=ALU.mult, op1=ALU.subtract), [r_s4], [r_s4])
                P.op("dve", lambda e: e.tensor_scalar(out=s4[:, 20:24], in0=s4[:, 20:24], scalar1=GN_EPS, scalar2=None, op0=ALU.add), [r_s4], [r_s4])
                P.op("act", lambda e: e.activation(out=s4[:, 24:28], in_=s4[:, 20:24], func=AF.Sqrt), [r_s4], [r_s4])
                P.op("dve", lambda e: e.reciprocal(out=s4[:, 28:32], in_=s4[:, 24:28]), [r_s4], [r_s4])
                P.op("dve", lambda e: e.tensor_tensor(out=v3(on[:]), in0=v3(osb[:]), in1=s4[:, 16:20].unsqueeze(2).to_broadcast([128, 4, 64]), op=ALU.subtract), [r_osb, r_s4], [r_on])
                P.op("dve", lambda e: e.tensor_tensor(out=v3(on[:]), in0=v3(on[:]), in1=s4[:, 28:32].unsqueeze(2).to_broadcast([128, 4, 64]), op=ALU.mult), [r_on, r_s4], [r_on])
                P.op("pool", lambda e: e.tensor_tensor(out=on[:], in0=on[:], in1=lng[:], op=ALU.mult), [r_on, r_w], [r_on])
                P.op("pool", lambda e: e.tensor_tensor(out=on[:], in0=on[:], in1=lnb[:], op=ALU.add), [r_on, r_w], [r_on])
                P.op("pool", lambda e: e.tensor_tensor(out=on[:], in0=on[:], in1=bon[:], op=ALU.add), [r_on, r_bon], [r_on])
                P.dma("pool", c.osc[d, rows, :], on[:], reads=[r_on], writes=[c.r["osc%d" % d][i]])

            return body

    with ExitStack() as st:
        bodies = [setup_dir(d, st) for d in range(2)]
        run_lanes([list(range(NT)), list(range(NT - 1, -1, -1))], lambda i, k: bodies[k](i), skew=RWKV_SKEW)
    barrier(c)
    with ExitStack() as st:
        def mkpools(k):
            mk = lambda name, shape, dt, n=2: Pool2(st, nc, f"{name}_l{k}_", shape, dt, n)
            return dict(o0=mk("e_o0", [128, 256], F32), o1=mk("e_o1", [128, 256], F32), zd=mk("e_zd", [128, 256], F32), yo=mk("e_yo", [128, 256], BF16))
        pools = [mkpools(k) for k in range(2)]

        def body_e(i, k):
            pl = pools[k]
            rows = slice(i * 128, (i + 1) * 128)
            o0, r_o0 = pl["o0"].next()
            o1, r_o1 = pl["o1"].next()
            zd, r_zd = pl["zd"].next()
            yo, r_yo = pl["yo"].next()
            P.dma("sp", o0[:], c.osc[0, rows, :], reads=[c.r["osc0"][i]], writes=[r_o0])
            P.dma("sp", o1[:], c.osc[1, rows, :], reads=[c.r["osc1"][i]], writes=[r_o1])
            P.dma("sp", zd[:], c.zs[rows, 768:1024], reads=[c.r["zs"][i]], writes=[r_zd])
            yield
            P.op("dve", lambda e: e.tensor_tensor(out=o0[:], in0=o0[:], in1=o1[:], op=ALU.add), [r_o0, r_o1], [r_o0])
            P.op("dve", lambda e: e.tensor_tensor(out=yo[:], in0=o0[:], in1=zd[:], op=ALU.mult), [r_o0, r_zd], [r_yo])
            P.dma("pool", c.ys[rows, 768:1024], yo[:], reads=[r_yo], writes=[c.r["ys"][i]])

        run_lanes([list(range(k, NT, 2)) for k in range(2)], body_e, skew=1)


def host_consts(S, seq_len):
    NT = S // 128
    t = np.arange(S)
    valid = (t < seq_len).astype(np.float32).reshape(NT, 128).T.copy()
    invc = np.ones((S, 4), np.float32)
    for g, w in enumerate((2, 4, 8, 16)):
        h = w // 2
        cnt = np.minimum(t + h, seq_len) - np.maximum(t - h, 0)
        invc[:, g] = np.where(t < seq_len, 1.0 / np.maximum(cnt, 1), 1.0)
    invc = invc.reshape(NT, 128, 4).transpose(1, 0, 2).reshape(128, NT * 4).copy()
    ident = np.eye(128, dtype=np.float32)
    s = np.arange(128)[:, None]
    tt = np.arange(128)[None, :]
    bandc = np.zeros((128, 4, 128), np.float32)
    bandh = np.zeros((16, 4, 128), np.float32)
    for g, w in enumerate((2, 4, 8, 16)):
        h = w // 2
        bandc[:, g, :] = ((s >= tt - h) & (s <= tt + h - 1))
        r = np.arange(16)[:, None]
        srel = np.where(r < 8, r - 8, 128 + (r - 8))
        bandh[:, g, :] = ((srel >= tt - h) & (srel <= tt + h - 1))
    slopes = 2.0 ** (-8.0 * np.arange(1, 13) / 12.0)
    amask = np.zeros((25, 128, 4, 128), np.float32)
    ci = 0
    key = np.arange(128)[:, None]
    q = np.arange(128)[None, :]
    for g in range(3):
        d = ATT_D[g]
        for coff in range(-ATT_W[g], ATT_W[g] + 1):
            delta = coff * 128 + key - q
            ok = (delta % d == 0) & (np.abs(delta) <= 64 * d)
            for h in range(4):
                amask[ci, :, h, :] = np.where(ok, np.exp(-slopes[g * 4 + h] * np.abs(delta)), 0.0)
            ci += 1
    amask3 = np.zeros((3, 128, 4, 128), np.float32)
    for ci3, coff in enumerate((-1, 0, 1)):
        dm = coff * 128 + key - q
        ok = np.abs(dm) <= 64
        for h in range(4):
            amask3[ci3, :, h, :] = np.where(ok, np.exp(-slopes[8 + h] * 16.0 * np.abs(dm)), 0.0)
    tri = np.zeros((128, 642), np.float32)
    tri[:, 0:128] = (s <= tt)
    tri[:, 128:256] = (s >= tt)
    tri[:, 256:384] = (s < tt)
    tri[:, 384:512] = (s > tt)
    tri[:, 512:642] = 1.0
    return dict(valid=valid, invc=invc, ident=ident, amask=amask.reshape(25, 128, 512), amask3=amask3.reshape(3, 128, 512), bandc=bandc.reshape(128, 512),
                bandh=bandh.reshape(16, 512), tri=tri)


def host_weights(inp):
    w = {}
    for k in ("norm_g", "w_in", "q_norm_g", "k_norm_g", "pool_w", "pool_scale", "sg_norm_g", "mu_rkv", "mu_lat", "w0", "w_up",
              "a0", "a_up", "k_k", "k_a", "r_k", "ln_g", "ln_b", "w_branch", "w_out"):
        w[k] = np.ascontiguousarray(np.asarray(inp[k], dtype=np.float32))
    w["sg_wT"] = np.ascontiguousarray(np.transpose(np.asarray(inp["sg_w"], np.float32), (0, 1, 3, 2)))
    w["sg_bT"] = np.ascontiguousarray(np.transpose(np.asarray(inp["sg_b"], np.float32), (0, 2, 1)))
    return w


_NC_CACHE = {}


def run_sequences(seqs, S, inp, branches=(0, 1, 2, 3), L=2, n_cores=8):
    key = (S, L, tuple(branches))
    if key not in _NC_CACHE:
        _NC_CACHE[key] = build_program(S, L=L, branches=branches)
    nc = _NC_CACHE[key]
    w = host_weights(inp)
    in_maps = []
    for ci in range(n_cores):
        if ci < len(seqs):
            x = np.zeros((S, D), np.float32)
            x[:seqs[ci].shape[0]] = seqs[ci]
            m = dict(x=x, **host_consts(S, seqs[ci].shape[0]))
        else:
            m = dict(x=np.zeros((S, D), np.float32), **host_consts(S, 0))
        m.update(w)
        in_maps.append(m)
    res = run_bass_kernel_spmd(nc, in_maps, core_ids=list(range(n_cores)))
    if DEBUG_SCRATCH:
        global LAST_RESULTS
        LAST_RESULTS = res.results
    return [res.results[ci]["y"][:seqs[ci].shape[0]] for ci in range(len(seqs))]


def kernel(**inputs):
    xp = np.asarray(inputs["x_prompt"], np.float32)
    xs = np.asarray(inputs["x_sample"], np.float32)
    S = xs.shape[1]
    seqs = [xp[b] for b in range(xp.shape[0])] + [xs[b] for b in range(xs.shape[0])]
    outs = run_sequences(seqs, S, inputs)
    nb = xp.shape[0]
    y_prompt = np.stack(outs[:nb], 0).astype(np.float32)
    y_sample = np.stack(outs[nb:], 0).astype(np.float32)
    return (y_prompt, y_sample)
```

```python
import numpy as np
import concourse.bass as bass
import concourse.mybir as mybir
from concourse.bass_utils import run_bass_kernel_spmd

F32 = mybir.dt.float32
BF16 = mybir.dt.bfloat16
AF = mybir.ActivationFunctionType
ALU = mybir.AluOpType
AX = mybir.AxisListType

EPOCH = 24000
STORES_ON_SP = False
RWKV_SKEW = 6
ATTACH_WAITS = True
SAME_ENGINE_SYNC = True
QK_T_DELAY = 3
FUSE_BC = True
NDMA_SEM = 12


class Res:
    __slots__ = ("name", "last_w", "reads")

    def __init__(self, name=""):
        self.name = name
        self.last_w = None
        self.reads = []


class Prog:
    ENG = ("pe", "act", "dve", "pool", "sp")

    def __init__(self, nc, same_engine_sync=True):
        self.nc = nc
        self.same_engine_sync = same_engine_sync
        self.ops = {e: [] for e in self.ENG}
        self.cnt = {e: 0 for e in self.ENG}
        self.sems = {e: [nc.alloc_semaphore(name=f"s_{e}_0")] for e in ("pe", "act", "dve", "pool")}
        self.seen = {e: {} for e in self.ENG}
        self.dma_sems = {q: [nc.alloc_semaphore(name=f"d_{q}_{k}") for k in range(NDMA_SEM)] for q in ("sp", "pool")}
        self.dma_n = {"sp": 0, "pool": 0}
        self.n_wait = 0

    def _deps(self, reads, writes):
        deps = []
        for r in reads:
            if r.last_w is not None:
                deps.extend(r.last_w)
        for w in writes:
            if w.last_w is not None:
                deps.extend(w.last_w)
            deps.extend(w.reads)
        return deps

    def _waits(self, e, deps, own_sem_ids):
        waits = {}
        seen = self.seen[e]
        for (sem, val) in deps:
            sid = id(sem)
            if sid in own_sem_ids and (e == "pe" or not self.same_engine_sync):
                continue
            if seen.get(sid, 0) >= val:
                continue
            if sid not in waits or waits[sid][1] < val:
                waits[sid] = (sem, val)
        for sid, (sem, val) in waits.items():
            seen[sid] = val
        self.n_wait += len(waits)
        return list(waits.values())

    def _commit(self, ev, reads, writes, is_dma=False):
        for r in reads:
            r.reads.append(ev)
        for w in writes:
            if is_dma and w.last_w is not None and not w.reads:
                w.last_w = w.last_w + [ev]
            else:
                w.last_w = [ev]
            w.reads = []

    def op(self, e, fn, reads=(), writes=()):
        deps = self._deps(reads, writes)
        own = {id(s) for s in self.sems[e]}
        waits = self._waits(e, deps, own)
        if self.cnt[e] >= EPOCH:
            self.sems[e].append(self.nc.alloc_semaphore(name=f"s_{e}_{len(self.sems[e])}"))
            self.cnt[e] = 0
        self.cnt[e] += 1
        sem = self.sems[e][-1]
        ev = (sem, self.cnt[e])
        self.ops[e].append((waits, fn, sem, 1))
        self._commit(ev, reads, writes)
        return ev

    def dma(self, q, out, in_, reads=(), writes=(), fn=None, **kw):
        if STORES_ON_SP and q == "pool" and fn is None and out.dtype == in_.dtype:
            q = "sp"
        deps = self._deps(reads, writes)
        n = self.dma_n[q]
        self.dma_n[q] += 1
        k = n % NDMA_SEM
        use = n // NDMA_SEM
        sem = self.dma_sems[q][k]
        if use > 0:
            deps.append((sem, 16 * use))
        own = {id(s) for s in self.sems[q]} if q in self.sems else set()
        waits = self._waits(q, deps, own)
        ev = (sem, 16 * (use + 1))
        if fn is None:
            fn = (lambda eng, o=out, i=in_, kw=kw: eng.dma_start(out=o, in_=i, **kw))
        self.ops[q].append((waits, fn, sem, 16))
        self._commit(ev, reads, writes, is_dma=True)
        return ev

    def final_wait(self, e, evs):
        waits = self._waits(e, list(evs), set())
        self.ops[e].append((waits, None, None, 0))

    def emit(self):
        nc = self.nc
        ops = self.ops

        def run(eng, lst, attach):
            for (waits, fn, sem, inc) in lst:
                if fn is not None and attach and waits and inc == 1:
                    for (s, v) in waits[:-1]:
                        eng.wait_ge(s, v)
                    ins = fn(eng)
                    ins._wait_ge(waits[-1][0], eng.lower_val(waits[-1][1]))
                    ins.then_inc(sem, inc)
                    continue
                for (s, v) in waits:
                    eng.wait_ge(s, v)
                if fn is not None:
                    ins = fn(eng)
                    ins.then_inc(sem, inc)

        with nc.Block() as block:
            @block.tensor
            def _(eng):
                run(eng, ops["pe"], ATTACH_WAITS)

            @block.scalar
            def _(eng):
                run(eng, ops["act"], ATTACH_WAITS)

            @block.vector
            def _(eng):
                run(eng, ops["dve"], ATTACH_WAITS)

            @block.gpsimd
            def _(eng):
                run(eng, ops["pool"], ATTACH_WAITS)

            @block.sync
            def _(eng):
                run(eng, ops["sp"], False)


D = 1024
PW = 9216
P1W = 5120
EPS = 1e-6
GN_EPS = 64e-5
ATT_W = (1, 2, 8)
ATT_D = (1, 4, 16)
C_Q, C_K, C_V, C_ZA, C_UB, C_ZB, C_UVC, C_ZC, C_RKV, C_LAT, C_ZD = 0, 768, 1536, 2304, 2560, 2816, 3072, 3584, 3840, 4608, 4864


class Ctx:
    pass


_UID = [0]


def uname(n):
    _UID[0] += 1
    return f"{n}_u{_UID[0]}"


DEBUG_SCRATCH = False
DEBUG_BARRIER = False
DBG_DIR = 0
DBG_TILE = 0


def build_program(S, L=2, branches=(0, 1, 2, 3), same_engine_sync=None):
    if same_engine_sync is None:
        same_engine_sync = SAME_ENGINE_SYNC
    from contextlib import ExitStack
    NT = S // 128
    nc = bass.Bass("TRN2", target_bir_lowering=False)
    P = Prog(nc, same_engine_sync=same_engine_sync)
    c = Ctx()
    c.nc, c.P, c.S, c.NT, c.L, c.branches = nc, P, S, NT, L, branches

    def din(name, shape, dt=F32):
        return nc.dram_tensor(name, list(shape), dt, kind="ExternalInput").ap()

    def dscr(name, shape, dt=F32):
        return nc.dram_tensor(name, list(shape), dt, kind=("ExternalOutput" if DEBUG_SCRATCH else "Internal")).ap()

    c.x_in = din("x", [S, D])
    c.valid = din("valid", [128, NT])
    c.invc = din("invc", [128, NT * 4])
    c.ident = din("ident", [128, 128])
    c.amask = din("amask", [25, 128, 512])
    c.amask3 = din("amask3", [3, 128, 512])
    c.bandc = din("bandc", [128, 4 * 128])
    c.bandh = din("bandh", [16, 4 * 128])
    c.tri = din("tri", [128, 642])
    c.w = {}
    for name, shape in (("norm_g", [L, D]), ("w_in", [L, D, PW]), ("q_norm_g", [L, 64]), ("k_norm_g", [L, 64]),
                        ("pool_w", [L, 4, 64, 64]), ("pool_scale", [L, 256]), ("sg_norm_g", [L, 256]),
                        ("sg_wT", [L, 4, 128, 128]), ("sg_bT", [L, 128, 4]), ("mu_rkv", [L, 2, 768]), ("mu_lat", [L, 2, 128]),
                        ("w0", [L, 2, 256]), ("w_up", [L, 2, 64, 256]), ("a0", [L, 2, 256]), ("a_up", [L, 2, 64, 256]),
                        ("k_k", [L, 2, 256]), ("k_a", [L, 2, 256]), ("r_k", [L, 2, 256]), ("ln_g", [L, 256]), ("ln_b", [L, 256]),
                        ("w_branch", [L, 4, 256, D]), ("w_out", [L, D, D])):
        c.w[name] = din(name, shape)
    c.y_out = nc.dram_tensor("y", [S, D], F32, kind="ExternalOutput").ap()

    c.x1 = dscr("x1", [S, D])
    c.qT = dscr("qT_scr", [NT, 64, 1536], BF16)
    c.kT = dscr("kT_scr", [NT, 64, 1536], BF16)
    c.va = dscr("va_scr", [NT, 128, 780], BF16)
    c.zs = dscr("zs_scr", [S, 1024])
    c.ub = dscr("ub_scr", [S + 32, 256])
    c.uvc = dscr("uvc_scr", [S, 512])
    c.rkvl = dscr("rkvl_scr", [S + 2, 1024])
    c.ys = dscr("ys_scr", [S, 1024], BF16)
    c.nd3 = dscr("nd3_scr", [S, 260])
    c.osc = dscr("o_scr", [2, S, 256])
    c.r = {k: [Res(f"{k}{i}") for i in range(NT)] for k in ("x1", "qT", "kT", "va", "zs", "ub", "uvc", "rkvl", "ys", "osc0", "osc1", "y", "nd3")}
    c.r_pad = Res("pads")
    if DEBUG_SCRATCH:
        c.dbg_proj = dscr("dbg_proj", [S, P1W])
        c.dbg_hT = dscr("dbg_hT", [S, D], BF16)

    c.idb = nc.alloc_sbuf_tensor("idb", [128, 128], BF16)
    c.idf = nc.alloc_sbuf_tensor("idf", [128, 128], F32)
    c.zero = nc.alloc_sbuf_tensor("zero", [128, 1024], F32)
    c.validt = nc.alloc_sbuf_tensor("validt", [128, NT], F32)
    c.r_const = Res("const")
    P.dma("sp", c.idf[:], c.ident, writes=[c.r_const])
    P.dma("pool", c.idb[:], c.ident, writes=[c.r_const])
    P.dma("sp", c.validt[:], c.valid, writes=[c.r_const])
    P.op("pool", lambda e: e.memset(c.zero[:], 0.0), [], [c.r_const])
    P.dma("sp", c.ub[0:16, :], c.zero[0:16, 0:256], reads=[c.r_const], writes=[c.r_pad])
    P.dma("sp", c.ub[S + 16:S + 32, :], c.zero[0:16, 0:256], reads=[c.r_const], writes=[c.r_pad])
    P.dma("sp", c.rkvl[0:1, :], c.zero[0:1, :], reads=[c.r_const], writes=[c.r_pad])
    P.dma("sp", c.rkvl[S + 1:S + 2, :], c.zero[0:1, :], reads=[c.r_const], writes=[c.r_pad])

    c.pbf = [nc.alloc_psum_tensor(f"pbf{i}", [128, 1024], BF16) for i in range(2)]
    c.pf = [nc.alloc_psum_tensor(f"pf{i}", [128, 512], F32) for i in range(6)]
    c.r_pbf = [Res(f"pbf{i}") for i in range(2)]
    c.r_pf = [Res(f"pf{i}") for i in range(6)]

    for l in range(L):
        x_src = c.x_in if l == 0 else c.x1
        x_src_res = None if l == 0 else c.r["x1"]
        x_dst = c.x1 if l < L - 1 else c.y_out
        x_dst_res = c.r["x1"] if l < L - 1 else c.r["y"]
        phase1(c, l, x_src, x_src_res)
        barrier(c)
        if 2 in branches and not FUSE_BC:
            phase2c(c, l)
            barrier(c)
        if 1 in branches and not FUSE_BC:
            phase2b(c, l)
            barrier(c)
        if 0 in branches:
            phase2a(c, l)
            barrier(c)
        if 3 in branches:
            phase2d(c, l)
            barrier(c)
        phase3(c, l, x_src, x_src_res, x_dst, x_dst_res)
        barrier(c)
    P.emit()
    return nc


def barrier(c):
    P = c.P
    evs = []
    for e in ("pe", "act", "dve", "pool"):
        if P.cnt[e] > 0:
            evs.append((P.sems[e][-1], P.cnt[e]))
    for q in ("sp", "pool"):
        n = P.dma_n[q]
        for k, s in enumerate(P.dma_sems[q]):
            uses = (n - k + NDMA_SEM - 1) // NDMA_SEM if n > k else 0
            if uses > 0:
                evs.append((s, 16 * uses))
    for e in Prog.ENG:
        P.final_wait(e, evs)


class Pool2:
    def __init__(self, stack, nc, name, shape, dt, n=2):
        self.t = [stack.enter_context(nc.sbuf_tensor(uname(f"{name}{i}"), list(shape), dt)) for i in range(n)]
        self.r = [Res(f"{name}{i}") for i in range(n)]
        self.n = n
        self.i = -1

    def next(self):
        self.i = (self.i + 1) % self.n
        return self.t[self.i], self.r[self.i]


NL = 2


def run_lanes(lane_tiles, body, skew=2):
    K = len(lane_tiles)
    its = [iter(t) for t in lane_tiles]
    gens = [None] * K
    done = [False] * K
    rnd = 0
    while not all(done):
        for k in range(K):
            if done[k]:
                continue
            if gens[k] is None:
                if rnd < k * skew:
                    continue
                i = next(its[k], None)
                if i is None:
                    done[k] = True
                    continue
                gens[k] = body(i, k)
            try:
                next(gens[k])
            except StopIteration:
                gens[k] = None
        rnd += 1


def lane_banks(c, k):
    return [(c.pf[3 * k + j], c.r_pf[3 * k + j]) for j in range(3)], (c.pbf[k], c.r_pbf[k])


def rms_h(c, xt, r_xt, gbc, r_w, pl, width=1024):
    P = c.P
    junk, r_junk = pl["junk"].next()
    ss, r_ss = pl["ss"].next()
    h, r_h = pl["h"].next()
    P.op("act", lambda e: e.activation(out=junk[:], in_=xt[:], func=AF.Square, accum_out=ss[:, 0:1]), [r_xt], [r_junk, r_ss])
    P.op("dve", lambda e: e.tensor_scalar(out=ss[:, 1:2], in0=ss[:, 0:1], scalar1=1.0 / width, scalar2=EPS, op0=ALU.mult, op1=ALU.add), [r_ss], [r_ss])
    P.op("act", lambda e: e.activation(out=ss[:, 2:3], in_=ss[:, 1:2], func=AF.Sqrt), [r_ss], [r_ss])
    P.op("dve", lambda e: e.reciprocal(out=ss[:, 3:4], in_=ss[:, 2:3]), [r_ss], [r_ss])
    P.op("dve", lambda e: e.scalar_tensor_tensor(out=h[:], in0=xt[:], scalar=ss[:, 3:4], in1=gbc[:], op0=ALU.mult, op1=ALU.mult),
         [r_xt, r_ss, r_w], [r_h])
    return h, r_h


def transpose8(c, src, r_src, dst, r_dst, pbk, eng="act", n=8):
    P = c.P
    pb, r_pb = pbk
    for kc in range(n):
        P.op("pe", lambda e, kc=kc: e.transpose(out=pb[:, kc * 128:(kc + 1) * 128], in_=src[:, kc * 128:(kc + 1) * 128], identity=c.idb[:]),
             [r_src, c.r_const], [r_pb])
    if eng == "act":
        P.op("act", lambda e: e.copy(out=dst[:, 0:n * 128], in_=pb[:, 0:n * 128]), [r_pb], [r_dst])
    else:
        P.op("dve", lambda e: e.tensor_copy(out=dst[:, 0:n * 128], in_=pb[:, 0:n * 128]), [r_pb], [r_dst])


def phase1(c, l, x_src, x_src_res):
    from contextlib import ExitStack
    P, nc, NT = c.P, c.nc, c.NT
    with ExitStack() as st:
        sb = lambda n, s, d: st.enter_context(nc.sbuf_tensor(uname(n), list(s), d))
        W = sb("W1", [128, 8, P1W], BF16)
        r_W = Res("W1")
        for kc in range(8):
            P.dma("pool", W[:, kc, :], c.w["w_in"][l, kc * 128:(kc + 1) * 128, 0:P1W], writes=[r_W])
        gbc = sb("gbc", [128, D], F32)
        g64 = sb("g64", [128, 128], F32)
        gqk = sb("gqk", [128, 1536], F32)
        r_w = Res("p1w")
        P.dma("sp", gbc[:], c.w["norm_g"][l:l + 1, :].partition_broadcast(128), writes=[r_w])
        P.dma("sp", g64[:, 0:64], c.w["q_norm_g"][l:l + 1, :].partition_broadcast(128), writes=[r_w])
        P.dma("sp", g64[:, 64:128], c.w["k_norm_g"][l:l + 1, :].partition_broadcast(128), writes=[r_w])
        P.op("dve", lambda e: e.tensor_scalar(out=gqk[:, 0:768].rearrange("p (h d) -> p h d", d=64),
                                              in0=g64[:, 0:64].unsqueeze(1).to_broadcast([128, 12, 64]),
                                              scalar1=0.125, scalar2=None, op0=ALU.mult), [r_w], [r_w])
        P.op("dve", lambda e: e.tensor_copy(out=gqk[:, 768:1536].rearrange("p (h d) -> p h d", d=64),
                                            in_=g64[:, 64:128].unsqueeze(1).to_broadcast([128, 12, 64])), [r_w], [r_w])

        def mkpools(k):
            mk = lambda name, shape, dt, n=1: Pool2(st, nc, f"{name}_l{k}_", shape, dt, n)
            d = dict(junk=mk("junk", [128, D], BF16), ss=mk("ss", [128, 4], F32), h=mk("h", [128, D], BF16), hT=mk("hT", [128, D], BF16),
                     xt=mk("xt", [128, D], F32), proj=mk("proj", [128, P1W], F32), zsb=mk("zsb", [128, 1024], F32))
            if 0 in c.branches:
                d.update(sq=mk("sq", [128, 1536], F32), s24=mk("s24", [128, 72], F32), qkn=mk("qkn", [128, 1536], BF16),
                         qkT=mk("qkT", [64, 3072], BF16), vaug=mk("vaug", [128, 780], BF16))
            return d
        pools = [mkpools(k) for k in range(NL)]

        def body(i, k):
            pl = pools[k]
            banks, pbk = lane_banks(c, k)
            rows = slice(i * 128, (i + 1) * 128)
            xt, r_xt = pl["xt"].next()
            P.dma("sp", xt[:], x_src[rows, :], reads=([x_src_res[i]] if x_src_res else []), writes=[r_xt])
            h, r_h = rms_h(c, xt, r_xt, gbc, r_w, pl)
            yield
            hT, r_hT = pl["hT"].next()
            transpose8(c, h, r_h, hT, r_hT, pbk)
            yield
            proj, r_pj = pl["proj"].next()
            for blk in range(P1W // 512):
                ps, r_ps = banks[blk % 3]
                for kc in range(8):
                    P.op("pe", lambda e, kc=kc, blk=blk, ps=ps: e.matmul(out=ps[:, :], lhsT=hT[:, kc * 128:(kc + 1) * 128],
                                                                    rhs=W[:, kc, blk * 512:(blk + 1) * 512], start=(kc == 0), stop=(kc == 7)),
                         [r_hT, r_W], [r_ps])
                if blk % 2 == 0:
                    P.op("act", lambda e, blk=blk, ps=ps: e.copy(out=proj[:, blk * 512:(blk + 1) * 512], in_=ps[:, :]), [r_ps], [r_pj])
                else:
                    P.op("dve", lambda e, blk=blk, ps=ps: e.tensor_copy(out=proj[:, blk * 512:(blk + 1) * 512], in_=ps[:, :]), [r_ps], [r_pj])
                if blk % 2 == 1:
                    yield
            if 1 in c.branches:
                P.dma("pool", c.ub[16 + i * 128:16 + (i + 1) * 128, :], proj[:, C_UB:C_UB + 256], reads=[r_pj], writes=[c.r["ub"][i]])
            if 2 in c.branches:
                P.dma("pool", c.uvc[rows, :], proj[:, C_UVC:C_UVC + 512], reads=[r_pj], writes=[c.r["uvc"][i]])
            if 3 in c.branches:
                P.dma("pool", c.rkvl[1 + i * 128:1 + (i + 1) * 128, :], proj[:, C_RKV:C_RKV + 1024], reads=[r_pj], writes=[c.r["rkvl"][i]])
            zt, r_zt = pl["zsb"].next()
            for b, cz in enumerate((C_ZA, C_ZB, C_ZC, C_ZD)):
                P.op("act", lambda e, b=b, cz=cz: e.activation(out=zt[:, b * 256:(b + 1) * 256], in_=proj[:, cz:cz + 256], func=AF.Silu),
                     [r_pj], [r_zt])
            P.dma("pool", c.zs[rows, :], zt[:], reads=[r_zt], writes=[c.r["zs"][i]])
            yield
            if 0 in c.branches:
                sq, r_sq = pl["sq"].next()
                s, r_s = pl["s24"].next()
                qn, r_qn = pl["qkn"].next()
                v24 = lambda ap: ap.rearrange("p (h d) -> p h d", d=64)
                P.op("dve", lambda e: e.tensor_tensor(out=sq[:], in0=proj[:, 0:1536], in1=proj[:, 0:1536], op=ALU.mult), [r_pj], [r_sq])
                va, r_va = pl["vaug"].next()
                P.op("act", lambda e: e.copy(out=va[:].rearrange("p (h d) -> p h d", d=65)[:, :, 0:64],
                                             in_=proj[:, C_V:C_V + 768].rearrange("p (h d) -> p h d", d=64)), [r_pj], [r_va])
                yield
                yield
                P.op("dve", lambda e: e.tensor_reduce(out=s[:, 0:24], in_=v24(sq[:]), axis=AX.X, op=ALU.add), [r_sq], [r_s])
                P.op("dve", lambda e: e.tensor_scalar(out=s[:, 24:48], in0=s[:, 0:24], scalar1=1.0 / 64, scalar2=EPS, op0=ALU.mult, op1=ALU.add), [r_s], [r_s])
                P.op("dve", lambda e: e.tensor_copy(out=va[:].rearrange("p (h d) -> p h d", d=65)[:, :, 64:65],
                                                    in_=c.validt[:, i:i + 1].unsqueeze(1).to_broadcast([128, 12, 1])), [c.r_const], [r_va])
                P.dma("pool", c.va[i], va[:], reads=[r_va], writes=[c.r["va"][i]])
                yield
                P.op("act", lambda e: e.activation(out=s[:, 0:24], in_=s[:, 24:48], func=AF.Sqrt), [r_s], [r_s])
                yield
                P.op("dve", lambda e: e.reciprocal(out=s[:, 48:72], in_=s[:, 0:24]), [r_s], [r_s])
                P.op("dve", lambda e: e.tensor_tensor(out=v24(sq[:]), in0=v24(proj[:, 0:1536]),
                                                      in1=s[:, 48:72].unsqueeze(2).to_broadcast([128, 24, 64]), op=ALU.mult), [r_pj, r_s], [r_sq])
                yield
                P.op("dve", lambda e: e.tensor_tensor(out=qn[:], in0=sq[:], in1=gqk[:], op=ALU.mult), [r_sq, r_w], [r_qn])
                yield
                for _ in range(QK_T_DELAY):
                    yield
                qT, r_qT = pl["qkT"].next()
                pb, r_pb = pbk
                for grp in range(3):
                    for j in range(8):
                        hh = grp * 8 + j
                        P.op("pe", lambda e, hh=hh, j=j: e.transpose(out=pb[0:64, j * 128:(j + 1) * 128], in_=qn[:, hh * 64:(hh + 1) * 64], identity=c.idb[:]),
                             [r_qn, c.r_const], [r_pb])
                    if grp % 2 == 0:
                        P.op("dve", lambda e, grp=grp: e.tensor_copy(out=qT[:, grp * 1024:(grp + 1) * 1024], in_=pb[0:64, :]), [r_pb], [r_qT])
                    else:
                        P.op("act", lambda e, grp=grp: e.copy(out=qT[:, grp * 1024:(grp + 1) * 1024], in_=pb[0:64, :]), [r_pb], [r_qT])
                    yield
                P.dma("pool", c.qT[i], qT[:, 0:1536], reads=[r_qT], writes=[c.r["qT"][i]])
                P.dma("pool", c.kT[i], qT[:, 1536:3072], reads=[r_qT], writes=[c.r["kT"][i]])

        run_lanes([list(range(k, NT, NL)) for k in range(NL)], body, skew=4)


def phase2c(c, l):
    from contextlib import ExitStack
    P, nc, NT = c.P, c.nc, c.NT
    with ExitStack() as st:
        sb = lambda n, s, d: st.enter_context(nc.sbuf_tensor(uname(n), list(s), d))
        sgw = sb("sgw", [128, 4, 128], BF16)
        sgn = sb("sgn", [128, 256], F32)
        sgb = sb("sgb", [128, 4], F32)
        r_w = Res("p2cw")
        P.dma("pool", sgw[:], c.w["sg_wT"][l].rearrange("g s t -> s g t"), writes=[r_w])
        P.dma("sp", sgn[:], c.w["sg_norm_g"][l:l + 1, :].partition_broadcast(128), writes=[r_w])
        P.dma("sp", sgb[:], c.w["sg_bT"][l], writes=[r_w])

        def mkpools(k):
            mk = lambda name, shape, dt, n=2: Pool2(st, nc, f"{name}_l{k}_", shape, dt, n)
            return dict(uv=mk("uv", [128, 512], F32), zc=mk("zc", [128, 256], F32), junk=mk("junkc", [128, 256], F32, 1), ss=mk("ssc", [128, 4], F32),
                        vn=mk("vn", [128, 256], BF16), sv=mk("sv", [128, 256], F32), ys=mk("ysc", [128, 256], BF16))
        pools = [mkpools(k) for k in range(6)]

        def body(i, k):
            pl = pools[k]
            banks = [(c.pf[k], c.r_pf[k])]
            rows = slice(i * 128, (i + 1) * 128)
            uv, r_uv = pl["uv"].next()
            zc, r_zc = pl["zc"].next()
            P.dma("sp", uv[:], c.uvc[rows, :], reads=[c.r["uvc"][i]], writes=[r_uv])
            P.dma("sp", zc[:], c.zs[rows, 512:768], reads=[c.r["zs"][i]], writes=[r_zc])
            junk, r_j = pl["junk"].next()
            ss, r_ss = pl["ss"].next()
            vn, r_vn = pl["vn"].next()
            sv, r_sv = pl["sv"].next()
            ysc, r_y = pl["ys"].next()
            P.op("act", lambda e: e.activation(out=junk[:], in_=uv[:, 256:512], func=AF.Square, accum_out=ss[:, 0:1]), [r_uv], [r_j, r_ss])
            P.op("dve", lambda e: e.tensor_scalar(out=ss[:, 1:2], in0=ss[:, 0:1], scalar1=1.0 / 256, scalar2=EPS, op0=ALU.mult, op1=ALU.add), [r_ss], [r_ss])
            yield
            P.op("act", lambda e: e.activation(out=ss[:, 2:3], in_=ss[:, 1:2], func=AF.Sqrt), [r_ss], [r_ss])
            P.op("dve", lambda e: e.reciprocal(out=ss[:, 3:4], in_=ss[:, 2:3]), [r_ss], [r_ss])
            P.op("dve", lambda e: e.scalar_tensor_tensor(out=vn[:], in0=uv[:, 256:512], scalar=ss[:, 3:4], in1=sgn[:], op0=ALU.mult, op1=ALU.mult),
                 [r_uv, r_ss, r_w], [r_vn])
            yield
            ps, r_ps = banks[0]
            for g in range(4):
                P.op("pe", lambda e, g=g: e.matmul(out=ps[:, g * 64:(g + 1) * 64], lhsT=sgw[:, g, :], rhs=vn[:, g * 64:(g + 1) * 64], start=True, stop=True),
                     [r_vn, r_w], [r_ps])
            P.op("dve", lambda e: e.tensor_tensor(out=sv[:].rearrange("p (g d) -> p g d", d=64), in0=ps[:, 0:256].rearrange("p (g d) -> p g d", d=64),
                                                  in1=sgb[:].unsqueeze(2).to_broadcast([128, 4, 64]), op=ALU.add), [r_ps, r_w], [r_sv])
            yield
            P.op("pool", lambda e: e.tensor_tensor(out=sv[:], in0=sv[:], in1=uv[:, 0:256], op=ALU.mult), [r_sv, r_uv], [r_sv])
            P.op("dve", lambda e: e.tensor_tensor(out=ysc[:], in0=sv[:], in1=zc[:], op=ALU.mult), [r_sv, r_zc], [r_y])
            P.dma("pool", c.ys[rows, 512:768], ysc[:], reads=[r_y], writes=[c.r["ys"][i]])

        run_lanes([list(range(k, NT, 6)) for k in range(6)], body, skew=1)


def phase3(c, l, x_src, x_src_res, x_dst, x_dst_res):
    from contextlib import ExitStack
    P, nc, NT = c.P, c.nc, c.NT
    with ExitStack() as st:
        sb = lambda n, s, d: st.enter_context(nc.sbuf_tensor(uname(n), list(s), d))
        Wg = sb("Wg", [128, 8, 4096], BF16)
        Wbr = sb("Wbr", [128, 8, 1024], BF16)
        Wo = sb("Wo", [128, 8, 1024], BF16)
        gbc = sb("gbc3", [128, D], F32)
        r_W = Res("W3")
        for kc in range(8):
            P.dma("pool", Wg[:, kc, :], c.w["w_in"][l, kc * 128:(kc + 1) * 128, P1W:PW], writes=[r_W])
        wbr_flat = c.w["w_branch"][l].rearrange("b c n -> (b c) n")
        for kc in range(8):
            P.dma("pool", Wbr[:, kc, :], wbr_flat[kc * 128:(kc + 1) * 128, :], writes=[r_W])
            P.dma("pool", Wo[:, kc, :], c.w["w_out"][l, kc * 128:(kc + 1) * 128, :], writes=[r_W])
        P.dma("sp", gbc[:], c.w["norm_g"][l:l + 1, :].partition_broadcast(128), writes=[r_W])
        fuse_b = FUSE_BC and 1 in c.branches
        fuse_c = FUSE_BC and 2 in c.branches
        if fuse_c:
            sgw = sb("sgw3", [128, 4, 128], BF16)
            sgn = sb("sgn3", [128, 256], F32)
            sgb = sb("sgb3", [128, 4], F32)
            P.dma("pool", sgw[:], c.w["sg_wT"][l].rearrange("g s t -> s g t"), writes=[r_W])
            P.dma("sp", sgn[:], c.w["sg_norm_g"][l:l + 1, :].partition_broadcast(128), writes=[r_W])
            P.dma("sp", sgb[:], c.w["sg_bT"][l], writes=[r_W])
        if fuse_b:
            bandc = sb("bandc3", [128, 512], F32)
            bandh = sb("bandh3", [16, 512], F32)
            pw = sb("pw3", [128, 2, 128], BF16)
            psc = sb("psc3", [128, 256], F32)
            invc = sb("invc3", [128, NT * 4], F32)
            P.dma("sp", bandc[:], c.bandc, writes=[r_W])
            P.dma("sp", bandh[:], c.bandh, writes=[r_W])
            P.dma("sp", psc[:], c.w["pool_scale"][l:l + 1, :].partition_broadcast(128), writes=[r_W])
            P.dma("sp", invc[:], c.invc, writes=[r_W])
            P.op("pool", lambda e: e.memset(pw[:], 0.0), [], [r_W])
            for g in range(4):
                j, gl = g // 2, g % 2
                P.dma("pool", pw[gl * 64:(gl + 1) * 64, j, gl * 64:(gl + 1) * 64], c.w["pool_w"][l, g], writes=[r_W])

        def mkpools(k):
            mk = lambda name, shape, dt, n=1: Pool2(st, nc, f"{name}_l{k}_", shape, dt, n)
            return dict(junk=mk("junk3", [128, D], BF16), ss=mk("ss3", [128, 4], F32), h=mk("h3", [128, D], BF16), hT=mk("hT3", [128, D], BF16),
                        xt=mk("xt3", [128, D], F32), ys=mk("ys3", [128, D], BF16), ysT=mk("ysT3", [128, D], BF16), gs=mk("gs3", [128, 512], F32, 2),
                        tmp=mk("tmp3", [128, 512], F32, 2), acc=mk("acc3", [128, D], F32), m=mk("m3", [128, D], BF16), mT=mk("mT3", [128, D], BF16),
                        xo=mk("xo3", [128, D], F32),
                        uv=mk("uv3", [128, 512], F32), zbc=mk("zbc3", [128, 512], F32), junkc=mk("junkc3", [128, 256], F32), ssc=mk("ssc3", [128, 4], F32),
                        vn=mk("vn3", [128, 256], BF16), sv=mk("sv3", [128, 256], F32), u=mk("ub3", [128, 256], F32), uh=mk("uh3", [16, 256], F32),
                        dp=mk("dp3", [128, 256], BF16), dT=mk("dT3", [128, 256], BF16), yb=mk("yb3", [128, 256], F32))
        pools = [mkpools(k) for k in range(NL)]

        def body(i, k):
            pl = pools[k]
            banks, pbk = lane_banks(c, k)
            rows = slice(i * 128, (i + 1) * 128)
            xt, r_xt = pl["xt"].next()
            P.dma("sp", xt[:], x_src[rows, :], reads=([x_src_res[i]] if x_src_res else []), writes=[r_xt])
            ys, r_ys = pl["ys"].next()
            P.dma("sp", ys[:], c.ys[rows, :], reads=[c.r["ys"][i]], writes=[r_ys])
            if fuse_b or fuse_c:
                zbc, r_zbc = pl["zbc"].next()
                P.dma("sp", zbc[:], c.zs[rows, 256:768], reads=[c.r["zs"][i]], writes=[r_zbc])
            if fuse_c:
                uv, r_uv = pl["uv"].next()
                P.dma("sp", uv[:], c.uvc[rows, :], reads=[c.r["uvc"][i]], writes=[r_uv])
            if fuse_b:
                u, r_u = pl["u"].next()
                uh, r_uh = pl["uh"].next()
                nbr = [c.r["ub"][j] for j in (i - 1, i, i + 1) if 0 <= j < NT] + [c.r_pad]
                P.dma("sp", u[:], c.ub[16 + i * 128:16 + (i + 1) * 128, :], reads=[c.r["ub"][i]], writes=[r_u])
                P.dma("sp", uh[0:8, :], c.ub[16 + i * 128 - 8:16 + i * 128, :], reads=nbr, writes=[r_uh])
                P.dma("sp", uh[8:16, :], c.ub[16 + (i + 1) * 128:16 + (i + 1) * 128 + 8, :], reads=nbr, writes=[r_uh])
            h, r_h = rms_h(c, xt, r_xt, gbc, r_W, pl)
            yield
            hT, r_hT = pl["hT"].next()
            transpose8(c, h, r_h, hT, r_hT, pbk)
            yield
            pm, r_pm = banks[2]
            if fuse_c:
                junk, r_j = pl["junkc"].next()
                ss, r_ss = pl["ssc"].next()
                vn, r_vn = pl["vn"].next()
                sv, r_sv = pl["sv"].next()
                P.op("act", lambda e: e.activation(out=junk[:], in_=uv[:, 256:512], func=AF.Square, accum_out=ss[:, 0:1]), [r_uv], [r_j, r_ss])
                P.op("dve", lambda e: e.tensor_scalar(out=ss[:, 1:2], in0=ss[:, 0:1], scalar1=1.0 / 256, scalar2=EPS, op0=ALU.mult, op1=ALU.add), [r_ss], [r_ss])
                P.op("act", lambda e: e.activation(out=ss[:, 2:3], in_=ss[:, 1:2], func=AF.Sqrt), [r_ss], [r_ss])
                P.op("dve", lambda e: e.reciprocal(out=ss[:, 3:4], in_=ss[:, 2:3]), [r_ss], [r_ss])
                P.op("dve", lambda e: e.scalar_tensor_tensor(out=vn[:], in0=uv[:, 256:512], scalar=ss[:, 3:4], in1=sgn[:], op0=ALU.mult, op1=ALU.mult),
                     [r_uv, r_ss, r_W], [r_vn])
                yield
                for g in range(4):
                    P.op("pe", lambda e, g=g: e.matmul(out=pm[:, g * 64:(g + 1) * 64], lhsT=sgw[:, g, :], rhs=vn[:, g * 64:(g + 1) * 64], start=True, stop=True),
                         [r_vn, r_W], [r_pm])
                P.op("dve", lambda e: e.tensor_tensor(out=sv[:].rearrange("p (g d) -> p g d", d=64), in0=pm[:, 0:256].rearrange("p (g d) -> p g d", d=64),
                                                      in1=sgb[:].unsqueeze(2).to_broadcast([128, 4, 64]), op=ALU.add), [r_pm, r_W], [r_sv])
                yield
                P.op("pool", lambda e: e.tensor_tensor(out=sv[:], in0=sv[:], in1=uv[:, 0:256], op=ALU.mult), [r_sv, r_uv], [r_sv])
                P.op("dve", lambda e: e.tensor_tensor(out=ys[:, 512:768], in0=sv[:], in1=zbc[:, 256:512], op=ALU.mult), [r_sv, r_zbc], [r_ys])
            if fuse_b:
                dp, r_dp = pl["dp"].next()
                dT, r_dT = pl["dT"].next()
                yb, r_yb = pl["yb"].next()
                for g in range(4):
                    P.op("pe", lambda e, g=g: e.matmul(out=pm[:, g * 64:(g + 1) * 64], lhsT=bandc[:, g * 128:(g + 1) * 128], rhs=u[:, g * 64:(g + 1) * 64],
                                                       start=True, stop=False), [r_u, r_W], [r_pm])
                    P.op("pe", lambda e, g=g: e.matmul(out=pm[:, g * 64:(g + 1) * 64], lhsT=bandh[:, g * 128:(g + 1) * 128], rhs=uh[:, g * 64:(g + 1) * 64],
                                                       start=False, stop=True), [r_uh, r_W], [r_pm])
                for g in range(4):
                    P.op("dve", lambda e, g=g: e.scalar_tensor_tensor(out=dp[:, g * 64:(g + 1) * 64], in0=pm[:, g * 64:(g + 1) * 64],
                                                                      scalar=invc[:, i * 4 + g:i * 4 + g + 1], in1=u[:, g * 64:(g + 1) * 64],
                                                                      op0=ALU.mult, op1=ALU.subtract), [r_pm, r_u, r_W], [r_dp])
                yield
                transpose8(c, dp, r_dp, dT, r_dT, pbk, n=2)
                yield
                for j in range(2):
                    P.op("pe", lambda e, j=j: e.matmul(out=pm[:, j * 128:(j + 1) * 128], lhsT=dT[:, j * 128:(j + 1) * 128], rhs=pw[:, j, :], start=True, stop=True),
                         [r_dT, r_W], [r_pm])
                P.op("dve", lambda e: e.tensor_tensor(out=yb[:], in0=pm[:, 0:256], in1=psc[:], op=ALU.mult), [r_pm, r_W], [r_yb])
                yield
                P.op("pool", lambda e: e.tensor_tensor(out=ys[:, 256:512], in0=yb[:], in1=zbc[:, 0:256], op=ALU.mult), [r_yb, r_zbc], [r_ys])
            yield
            ysT, r_ysT = pl["ysT"].next()
            transpose8(c, ys, r_ys, ysT, r_ysT, pbk, eng="dve")
            yield
            acc, r_acc = pl["acc"].next()
            for bi, b in enumerate(c.branches):
                for cb in range(2):
                    pg, r_pg = banks[0]
                    pbr, r_pbr = banks[1]
                    col = b * 1024 + cb * 512
                    for kc in range(8):
                        P.op("pe", lambda e, kc=kc, col=col: e.matmul(out=pg[:, :], lhsT=hT[:, kc * 128:(kc + 1) * 128], rhs=Wg[:, kc, col:col + 512],
                                                                      start=(kc == 0), stop=(kc == 7)), [r_hT, r_W], [r_pg])
                    gs, r_gs = pl["gs"].next()
                    P.op("act", lambda e, gs=gs: e.activation(out=gs[:], in_=pg[:, :], func=AF.Sigmoid), [r_pg], [r_gs])
                    for kk in range(2):
                        kc = 2 * b + kk
                        P.op("pe", lambda e, kc=kc, kk=kk, cb=cb: e.matmul(out=pbr[:, :], lhsT=ysT[:, kc * 128:(kc + 1) * 128],
                                                                             rhs=Wbr[:, kc, cb * 512:(cb + 1) * 512], start=(kk == 0), stop=(kk == 1)),
                             [r_ysT, r_W], [r_pbr])
                    if bi == 0:
                        P.op("dve", lambda e, cb=cb, gs=gs: e.tensor_tensor(out=acc[:, cb * 512:(cb + 1) * 512], in0=gs[:], in1=pbr[:, :], op=ALU.mult),
                             [r_gs, r_pbr], [r_acc])
                    else:
                        tmp, r_tmp = pl["tmp"].next()
                        P.op("dve", lambda e, gs=gs, tmp=tmp: e.tensor_tensor(out=tmp[:], in0=gs[:], in1=pbr[:, :], op=ALU.mult), [r_gs, r_pbr], [r_tmp])
                        P.op("pool", lambda e, cb=cb, tmp=tmp: e.tensor_tensor(out=acc[:, cb * 512:(cb + 1) * 512], in0=acc[:, cb * 512:(cb + 1) * 512], in1=tmp[:], op=ALU.add),
                             [r_tmp, r_acc], [r_acc])
                    yield
            m, r_m = pl["m"].next()
            mT, r_mT = pl["mT"].next()
            P.op("act", lambda e: e.copy(out=m[:], in_=acc[:]), [r_acc], [r_m])
            yield
            transpose8(c, m, r_m, mT, r_mT, pbk)
            yield
            xo, r_xo = pl["xo"].next()
            po, r_po = banks[2]
            for cb in range(2):
                for kc in range(8):
                    P.op("pe", lambda e, kc=kc, cb=cb: e.matmul(out=po[:, :], lhsT=mT[:, kc * 128:(kc + 1) * 128], rhs=Wo[:, kc, cb * 512:(cb + 1) * 512],
                                                                start=(kc == 0), stop=(kc == 7)), [r_mT, r_W], [r_po])
                P.op("dve", lambda e, cb=cb: e.tensor_tensor(out=xo[:, cb * 512:(cb + 1) * 512], in0=po[:, :], in1=xt[:, cb * 512:(cb + 1) * 512], op=ALU.add),
                     [r_po, r_xt], [r_xo])
                yield
            P.dma("pool", x_dst[rows, :], xo[:], reads=[r_xo], writes=[x_dst_res[i]])

        run_lanes([list(range(k, NT, NL)) for k in range(NL)], body, skew=5)


def phase2b(c, l):
    from contextlib import ExitStack
    P, nc, NT = c.P, c.nc, c.NT
    with ExitStack() as st:
        sb = lambda n, s, d: st.enter_context(nc.sbuf_tensor(uname(n), list(s), d))
        bandc = sb("bandc", [128, 512], F32)
        bandh = sb("bandh", [16, 512], F32)
        pw = sb("pw", [128, 2, 128], BF16)
        psc = sb("psc", [128, 256], F32)
        invc = sb("invc", [128, NT * 4], F32)
        r_w = Res("p2bw")
        P.dma("sp", bandc[:], c.bandc, writes=[r_w])
        P.dma("sp", bandh[:], c.bandh, writes=[r_w])
        P.dma("sp", psc[:], c.w["pool_scale"][l:l + 1, :].partition_broadcast(128), writes=[r_w])
        P.dma("sp", invc[:], c.invc, writes=[r_w])
        P.op("pool", lambda e: e.memset(pw[:], 0.0), [], [r_w])
        for g in range(4):
            j, gl = g // 2, g % 2
            P.dma("pool", pw[gl * 64:(gl + 1) * 64, j, gl * 64:(gl + 1) * 64], c.w["pool_w"][l, g], writes=[r_w])

        def mkpools(k):
            mk = lambda name, shape, dt, n=2: Pool2(st, nc, f"{name}_l{k}_", shape, dt, n)
            return dict(u=mk("ub", [128, 256], F32), uh=mk("uh", [16, 256], F32), zb=mk("zb", [128, 256], F32), d=mk("dpool", [128, 256], BF16),
                        dT=mk("dT", [128, 256], BF16), y=mk("ybf", [128, 256], F32), yo=mk("ybo", [128, 256], BF16))
        pools = [mkpools(k) for k in range(3)]

        def body(i, k):
            pl = pools[k]
            banks = [(c.pf[2 * k], c.r_pf[2 * k]), (c.pf[2 * k + 1], c.r_pf[2 * k + 1])]
            pbk = (c.pbf[k % 2], c.r_pbf[k % 2])
            rows = slice(i * 128, (i + 1) * 128)
            u, r_u = pl["u"].next()
            uh, r_uh = pl["uh"].next()
            zb, r_zb = pl["zb"].next()
            d, r_d = pl["d"].next()
            dT, r_dT = pl["dT"].next()
            y, r_y = pl["y"].next()
            yo, r_yo = pl["yo"].next()
            nb = [c.r["ub"][j] for j in (i - 1, i, i + 1) if 0 <= j < NT] + [c.r_pad]
            P.dma("sp", u[:], c.ub[16 + i * 128:16 + (i + 1) * 128, :], reads=[c.r["ub"][i]], writes=[r_u])
            P.dma("sp", uh[0:8, :], c.ub[16 + i * 128 - 8:16 + i * 128, :], reads=nb, writes=[r_uh])
            P.dma("sp", uh[8:16, :], c.ub[16 + (i + 1) * 128:16 + (i + 1) * 128 + 8, :], reads=nb, writes=[r_uh])
            P.dma("sp", zb[:], c.zs[rows, 256:512], reads=[c.r["zs"][i]], writes=[r_zb])
            yield
            ps, r_ps = banks[0]
            for g in range(4):
                P.op("pe", lambda e, g=g: e.matmul(out=ps[:, g * 64:(g + 1) * 64], lhsT=bandc[:, g * 128:(g + 1) * 128], rhs=u[:, g * 64:(g + 1) * 64],
                                                   start=True, stop=False), [r_u, r_w], [r_ps])
                P.op("pe", lambda e, g=g: e.matmul(out=ps[:, g * 64:(g + 1) * 64], lhsT=bandh[:, g * 128:(g + 1) * 128], rhs=uh[:, g * 64:(g + 1) * 64],
                                                   start=False, stop=True), [r_uh, r_w], [r_ps])
            for g in range(4):
                P.op("dve", lambda e, g=g: e.scalar_tensor_tensor(out=d[:, g * 64:(g + 1) * 64], in0=ps[:, g * 64:(g + 1) * 64],
                                                                  scalar=invc[:, i * 4 + g:i * 4 + g + 1], in1=u[:, g * 64:(g + 1) * 64],
                                                                  op0=ALU.mult, op1=ALU.subtract), [r_ps, r_u, r_w], [r_d])
            yield
            transpose8(c, d, r_d, dT, r_dT, pbk, n=2)
            yield
            ps2, r_ps2 = banks[1]
            for j in range(2):
                P.op("pe", lambda e, j=j: e.matmul(out=ps2[:, j * 128:(j + 1) * 128], lhsT=dT[:, j * 128:(j + 1) * 128], rhs=pw[:, j, :], start=True, stop=True),
                     [r_dT, r_w], [r_ps2])
            P.op("dve", lambda e: e.tensor_tensor(out=y[:], in0=ps2[:, 0:256], in1=psc[:], op=ALU.mult), [r_ps2, r_w], [r_y])
            yield
            P.op("pool", lambda e: e.tensor_tensor(out=yo[:], in0=y[:], in1=zb[:], op=ALU.mult), [r_y, r_zb], [r_yo])
            P.dma("pool", c.ys[rows, 256:512], yo[:], reads=[r_yo], writes=[c.r["ys"][i]])

        run_lanes([list(range(k, NT, 3)) for k in range(3)], body, skew=2)


def phase2a(c, l):
    from contextlib import ExitStack
    P, nc, NT = c.P, c.nc, c.NT
    use_sb = (NT % 16 == 0)
    NG = 2 if use_sb else 3
    if use_sb:
        phase2a_g3(c, l)
        barrier(c)
    with ExitStack() as st:
        sb = lambda n, s, d: st.enter_context(nc.sbuf_tensor(uname(n), list(s), d))
        masks = sb("amask", [128, 25, 512], BF16)
        r_w = Res("p2aw")
        for ci in range(25):
            P.dma("pool", masks[:, ci, :], c.amask[ci], writes=[r_w])
        depth = [2 * w + 2 for w in ATT_W]
        kring = [Pool2(st, nc, f"kr{g}_", [64, 512], BF16, depth[g]) for g in range(NG)]
        vring = [Pool2(st, nc, f"vr{g}_", [128, 260], BF16, depth[g]) for g in range(NG)]
        nd_ = Pool2(st, nc, "nd3t", [128, 260], F32, 2)
        loaded = [-1, -1, -1]
        q_ = Pool2(st, nc, "qTa", [64, 1536], BF16, 2)
        za_ = Pool2(st, nc, "za", [128, 256], F32, 2)
        e_ = Pool2(st, nc, "eexp", [128, 512], BF16, 3)
        p_ = Pool2(st, nc, "pexp", [128, 512], BF16, 6)
        dn_ = Pool2(st, nc, "den", [128, 8], F32, 2)
        ya_ = Pool2(st, nc, "yaf", [128, 256], F32, 2)
        yo_ = Pool2(st, nc, "yao", [128, 256], BF16, 2)
        mcol = []
        ci = 0
        for g in range(3):
            mcol.append({coff: ci + k for k, coff in enumerate(range(-ATT_W[g], ATT_W[g] + 1))})
            ci += 2 * ATT_W[g] + 1

        def ensure(g, upto):
            while loaded[g] < min(upto, NT - 1):
                j = loaded[g] + 1
                kt, r_kt = kring[g].t[j % depth[g]], kring[g].r[j % depth[g]]
                vt, r_vt = vring[g].t[j % depth[g]], vring[g].r[j % depth[g]]
                P.dma("sp", kt[:], c.kT[j][:, g * 512:(g + 1) * 512], reads=[c.r["kT"][j]], writes=[r_kt])
                P.dma("sp", vt[:], c.va[j][:, g * 260:(g + 1) * 260], reads=[c.r["va"][j]], writes=[r_vt])
                loaded[g] = j

        LOOK = 3
        tiles = {}

        def tile_begin(i):
            rows = slice(i * 128, (i + 1) * 128)
            for g in range(NG):
                ensure(g, i + ATT_W[g])
            qT, r_q = q_.next()
            za, r_za = za_.next()
            P.dma("sp", qT[:], c.qT[i], reads=[c.r["qT"][i]], writes=[r_q])
            P.dma("sp", za[:], c.zs[rows, 0:256], reads=[c.r["zs"][i]], writes=[r_za])
            ndt, r_ndt = (None, None)
            if use_sb:
                ndt, r_ndt = nd_.next()
                P.dma("sp", ndt[:], c.nd3[rows, :], reads=[c.r["nd3"][i // 16]], writes=[r_ndt])
            chunks = [(g, coff) for g in range(NG) for coff in range(-ATT_W[g], ATT_W[g] + 1) if 0 <= i + coff < NT]
            tiles[i] = dict(qT=qT, r_q=r_q, za=za, r_za=r_za, n=len(chunks), pso=c.pf[4 + i % 2], r_pso=c.r_pf[4 + i % 2], ndt=ndt, r_ndt=r_ndt)
            return chunks

        def stage_qk(i, idx, g, coff, seq):
            t = tiles[i]
            qT, r_q = t["qT"], t["r_q"]
            j = i + coff
            kt, r_kt = kring[g].t[j % depth[g]], kring[g].r[j % depth[g]]
            pss, r_pss = c.pf[seq % 4], c.r_pf[seq % 4]
            for h in range(4):
                P.op("pe", lambda e, h=h: e.matmul(out=pss[:, h * 128:(h + 1) * 128], lhsT=kt[:, h * 128:(h + 1) * 128],
                                                   rhs=qT[:, (g * 4 + h) * 128:(g * 4 + h + 1) * 128], start=True, stop=True),
                     [r_kt, r_q], [r_pss])
            ex, r_ex = e_.next()
            pp, r_pp = p_.next()
            P.op("act", lambda e: e.activation(out=ex[:], in_=pss[:, :], func=AF.Exp), [r_pss], [r_ex])
            mc = mcol[g][coff]
            P.op("dve", lambda e: e.tensor_tensor(out=pp[:], in0=ex[:], in1=masks[:, mc, :], op=ALU.mult), [r_ex, r_w], [r_pp])
            return pp, r_pp

        def stage_pv(i, idx, g, coff, pp, r_pp):
            t = tiles[i]
            pso, r_pso = t["pso"], t["r_pso"]
            j = i + coff
            vt, r_vt = vring[g].t[j % depth[g]], vring[g].r[j % depth[g]]
            n = t["n"]
            for h in range(4):
                P.op("pe", lambda e, h=h: e.matmul(out=pso[:, h * 65:(h + 1) * 65], lhsT=pp[:, h * 128:(h + 1) * 128],
                                                   rhs=vt[:, h * 65:(h + 1) * 65], start=(idx == 0 and h == 0), stop=(idx == n - 1), skip_group_check=True),
                     [r_pp, r_vt], [r_pso])
            if idx == n - 1:
                tile_end(i)

        def tile_end(i):
            rows = slice(i * 128, (i + 1) * 128)
            t = tiles.pop(i)
            pso, r_pso, za, r_za = t["pso"], t["r_pso"], t["za"], t["r_za"]
            dn, r_dn = dn_.next()
            ya, r_ya = ya_.next()
            yo, r_yo = yo_.next()
            if use_sb:
                ndt, r_ndt = t["ndt"], t["r_ndt"]
                P.op("dve", lambda e: e.tensor_tensor(out=ndt[:], in0=pso[:, 0:260], in1=ndt[:], op=ALU.add), [r_pso, r_ndt], [r_ndt])
                pv = ndt[:].rearrange("p (h d) -> p h d", d=65)
                r_pso = r_ndt
            else:
                pv = pso[:, 0:260].rearrange("p (h d) -> p h d", d=65)
            P.op("dve", lambda e: e.tensor_scalar(out=dn[:, 0:4].unsqueeze(2), in0=pv[:, :, 64:65], scalar1=1e-30, scalar2=None, op0=ALU.max), [r_pso], [r_dn])
            P.op("dve", lambda e: e.reciprocal(out=dn[:, 4:8], in_=dn[:, 0:4]), [r_dn], [r_dn])
            P.op("dve", lambda e: e.tensor_tensor(out=ya[:].rearrange("p (h d) -> p h d", d=64), in0=pv[:, :, 0:64],
                                                  in1=dn[:, 4:8].unsqueeze(2).to_broadcast([128, 4, 64]), op=ALU.mult), [r_pso, r_dn], [r_ya])
            P.op("pool", lambda e: e.tensor_tensor(out=yo[:], in0=ya[:], in1=za[:], op=ALU.mult), [r_ya, r_za], [r_yo])
            P.dma("pool", c.ys[rows, 0:256], yo[:], reads=[r_yo], writes=[c.r["ys"][i]])

        def stream():
            for i in range(NT):
                first = True
                chunks = None
                for idx in range(10 ** 9):
                    if first:
                        chunks = tile_begin(i)
                        first = False
                    if idx >= len(chunks):
                        break
                    g, coff = chunks[idx]
                    yield (i, idx, g, coff)

        pend = []
        for seq, (i, idx, g, coff) in enumerate(stream()):
            pp, r_pp = stage_qk(i, idx, g, coff, seq)
            pend.append((i, idx, g, coff, pp, r_pp))
            if len(pend) > LOOK:
                stage_pv(*pend.pop(0))
        while pend:
            stage_pv(*pend.pop(0))


def phase2a_g3(c, l):
    from contextlib import ExitStack
    P, nc, NT, S = c.P, c.nc, c.NT, c.S
    NSB = NT // 16
    vflat = c.va.rearrange("t p c -> (t p) c").rearrange("(m r) c -> r m c", r=16)
    ndv = c.nd3.rearrange("(m r) c -> r m c", r=16)
    with ExitStack() as st:
        sb_ = lambda n, s_, d: st.enter_context(nc.sbuf_tensor(uname(n), list(s_), d))
        masks = sb_("amask3", [128, 3, 512], BF16)
        r_w = Res("p2a3w")
        for ci in range(3):
            P.dma("pool", masks[:, ci, :], c.amask3[ci], writes=[r_w])
        stage_ = Pool2(st, nc, "g3stage", [64, 16 * 512], BF16, 2)
        kp_ = Pool2(st, nc, "g3kp", [64, 4 * 16 * 128], BF16, 3)
        qp_ = Pool2(st, nc, "g3qp", [64, 4 * 16 * 128], BF16, 2)
        vr_ = Pool2(st, nc, "g3v", [128, 260], BF16, 48)
        e_ = Pool2(st, nc, "g3e", [128, 512], BF16, 3)
        p_ = Pool2(st, nc, "g3p", [128, 512], BF16, 6)
        o_ = Pool2(st, nc, "g3o", [128, 260], F32, 3)
        kload = {}

        def permute(src_scr, sbi, dst, r_dst, col0):
            stg, r_stg = stage_.next()
            P.dma("sp", stg[:].rearrange("p (t c) -> p t c", c=512), src_scr[sbi * 16:(sbi + 1) * 16, :, col0:col0 + 512].rearrange("t p c -> p t c"),
                  reads=[c.r["qT" if src_scr is c.qT else "kT"][j] for j in range(sbi * 16, (sbi + 1) * 16)], writes=[r_stg])
            sv = stg[:].rearrange("p (t h pp r) -> p t h pp r", t=16, h=4, pp=8, r=16)
            dv = dst[:].rearrange("p (h r t pp) -> p h r t pp", h=4, r=16, t=16, pp=8)
            for h in range(4):
                eng = "act" if h % 2 == 0 else "pool"
                if eng == "act":
                    P.op("act", lambda e, h=h: e.copy(out=dv[:, h], in_=sv[:, :, h].rearrange("p t pp r -> p r t pp")), [r_stg], [r_dst])
                else:
                    P.op("pool", lambda e, h=h: e.tensor_copy(out=dv[:, h], in_=sv[:, :, h].rearrange("p t pp r -> p r t pp")), [r_stg], [r_dst])

        def ensure_k(sbi):
            if sbi in kload or not (0 <= sbi < NSB):
                return
            kp, r_kp = kp_.t[sbi % 3], kp_.r[sbi % 3]
            permute(c.kT, sbi, kp, r_kp, 1024)
            for r in range(16):
                vt, r_vt = vr_.t[(sbi % 3) * 16 + r], vr_.r[(sbi % 3) * 16 + r]
                P.dma("sp", vt[:], vflat[r, sbi * 128:(sbi + 1) * 128, 520:780], reads=[c.r["va"][j] for j in range(sbi * 16, (sbi + 1) * 16)], writes=[r_vt])
            kload[sbi] = True

        LOOK = 3
        units = []
        for sbi in range(NSB):
            for r in range(16):
                cl = [cf for cf in (-1, 0, 1) if 0 <= sbi + cf < NSB]
                for idx, cf in enumerate(cl):
                    units.append((sbi, r, cf, idx, len(cl)))
        qcur = {}
        pend = []

        def stage_pv(sbi, r, cf, idx, n, pp, r_pp, seq):
            pso, r_pso = c.pf[4 + (sbi * 16 + r) % 2], c.r_pf[4 + (sbi * 16 + r) % 2]
            sk = sbi + cf
            vt, r_vt = vr_.t[(sk % 3) * 16 + r], vr_.r[(sk % 3) * 16 + r]
            for h in range(4):
                P.op("pe", lambda e, h=h: e.matmul(out=pso[:, h * 65:(h + 1) * 65], lhsT=pp[:, h * 128:(h + 1) * 128], rhs=vt[:, h * 65:(h + 1) * 65],
                                                   start=(idx == 0 and h == 0), stop=(idx == n - 1), skip_group_check=True), [r_pp, r_vt], [r_pso])
            if idx == n - 1:
                ot, r_ot = o_.next()
                P.op("act", lambda e: e.copy(out=ot[:], in_=pso[:, 0:260]), [r_pso], [r_ot])
                P.dma("pool", ndv[r, sbi * 128:(sbi + 1) * 128, :], ot[:], reads=[r_ot], writes=[c.r["nd3"][sbi]])

        for seq, (sbi, r, cf, idx, n) in enumerate(units):
            if sbi not in qcur:
                while pend:
                    stage_pv(*pend.pop(0))
                for s2 in (sbi - 1, sbi, sbi + 1):
                    ensure_k(s2)
                qp, r_qp = qp_.next()
                permute(c.qT, sbi, qp, r_qp, 1024)
                qcur.clear()
                qcur[sbi] = (qp, r_qp)
            qp, r_qp = qcur[sbi]
            sk = sbi + cf
            kp, r_kp = kp_.t[sk % 3], kp_.r[sk % 3]
            pss, r_pss = c.pf[seq % 4], c.r_pf[seq % 4]
            for h in range(4):
                o0 = (h * 16 + r) * 128
                P.op("pe", lambda e, h=h, o0=o0, kp=kp, qp=qp, pss=pss: e.matmul(out=pss[:, h * 128:(h + 1) * 128], lhsT=kp[:, o0:o0 + 128], rhs=qp[:, o0:o0 + 128], start=True, stop=True),
                     [r_kp, r_qp], [r_pss])
            ex, r_ex = e_.next()
            pp, r_pp = p_.next()
            P.op("act", lambda e, ex=ex, pss=pss: e.activation(out=ex[:], in_=pss[:, :], func=AF.Exp), [r_pss], [r_ex])
            P.op("dve", lambda e, ex=ex, pp=pp, cf=cf: e.tensor_tensor(out=pp[:], in0=ex[:], in1=masks[:, cf + 1, :], op=ALU.mult), [r_ex, r_w], [r_pp])
            pend.append((sbi, r, cf, idx, n, pp, r_pp, seq))
            if len(pend) > LOOK:
                stage_pv(*pend.pop(0))
        while pend:
            stage_pv(*pend.pop(0))


def phase2d(c, l):
    from contextlib import ExitStack
    P, nc, NT = c.P, c.nc, c.NT
    NEG = -float(np.exp(-0.5))
    def setup_dir(d, st):
        if True:
            sb = lambda n, s, dt: st.enter_context(nc.sbuf_tensor(uname(n), list(s), dt))
            r_w = Res("p2dw")
            tri = sb("tri", [128, 642], F32)
            P.dma("sp", tri[:], c.tri, writes=[r_w])
            incl = tri[:, 128 * d:128 * d + 128]
            ones_m = tri[:, 512:640]
            ones_c = tri[:, 640:641]
            mask4 = sb("mask4", [128, 512], F32)
            maskT4 = sb("maskT4", [128, 512], F32)
            for q in range(4):
                src = tri[:, 256 + 128 * d:384 + 128 * d] if q % 2 == 0 else incl
                P.op("dve", lambda e, q=q, src=src: e.tensor_copy(out=mask4[:, q * 128:(q + 1) * 128], in_=src), [r_w], [r_w])
                P.op("dve", lambda e, q=q: e.tensor_copy(out=maskT4[:, q * 128:(q + 1) * 128], in_=tri[:, 256 + 128 * (1 - d):384 + 128 * (1 - d)]), [r_w], [r_w])
            identB = sb("identB", [64, 4, 64], F32)
            P.op("dve", lambda e: e.tensor_copy(out=identB[:], in_=c.idf[0:64, 0:64].unsqueeze(1).to_broadcast([64, 4, 64])), [c.r_const], [r_w])
            mu_r = sb("mu_r", [128, 768], F32)
            mu_l = sb("mu_l", [128, 128], F32)
            bias_wa = sb("bias_wa", [128, 512], F32)
            kkv = sb("kkv", [128, 256], F32)
            kav = sb("kav", [128, 256], F32)
            rkp = sb("rkp", [128, 256], F32)
            lng = sb("lng", [128, 256], F32)
            lnb = sb("lnb", [128, 256], F32)
            Wud = sb("Wud", [128, 512], BF16)
            bc = lambda ap: ap.partition_broadcast(128)
            P.dma("sp", mu_r[:], bc(c.w["mu_rkv"][l, d:d + 1, :]), writes=[r_w])
            P.dma("sp", mu_l[:], bc(c.w["mu_lat"][l, d:d + 1, :]), writes=[r_w])
            P.dma("sp", bias_wa[:, 0:256], bc(c.w["w0"][l, d:d + 1, :]), writes=[r_w])
            P.dma("sp", bias_wa[:, 256:512], bc(c.w["a0"][l, d:d + 1, :]), writes=[r_w])
            P.dma("sp", kkv[:], bc(c.w["k_k"][l, d:d + 1, :]), writes=[r_w])
            P.dma("sp", kav[:], bc(c.w["k_a"][l, d:d + 1, :]), writes=[r_w])
            P.dma("sp", rkp[:], bc(c.w["r_k"][l, d:d + 1, :]), writes=[r_w])
            P.dma("sp", lng[:], bc(c.w["ln_g"][l:l + 1, :]), writes=[r_w])
            P.dma("sp", lnb[:], bc(c.w["ln_b"][l:l + 1, :]), writes=[r_w])
            P.op("pool", lambda e: e.memset(Wud[:], 0.0), [], [r_w])
            P.dma("pool", Wud[0:64, 0:256], c.w["w_up"][l, d], writes=[r_w])
            P.dma("pool", Wud[64:128, 256:512], c.w["a_up"][l, d], writes=[r_w])
            ST = Pool2(st, nc, f"ST_d{d}_", [64, 256], F32, 2)
            st0, r_st0 = ST.next()
            P.op("pool", lambda e: e.memset(st0[:], 0.0), [], [r_st0])
            state = [st0, r_st0]

            def mk(name, shape, dt, n=1):
                return Pool2(st, nc, f"{name}_d{d}_", shape, dt, n)
            cur_, sh_ = mk("cur", [128, 1024], F32, 2), mk("sh", [128, 1024], F32, 2)
            xr_, xl_, tl_, tlT_ = mk("xr", [128, 768], F32), mk("xl", [128, 128], F32), mk("tl", [128, 128], BF16), mk("tlT", [128, 128], BF16)
            sg_, lw_ = mk("sg", [128, 512], F32), mk("logw", [128, 256], F32)
            kk_, sq_, s4_, k2_, bon_, beta_ = mk("kk", [128, 256], F32), mk("sqd", [128, 256], F32), mk("s4", [128, 32], F32), mk("k2", [128, 256], F32), mk("bon", [128, 256], F32), mk("beta", [128, 256], F32)
            cum_, ex_, tmp_ = mk("cum", [128, 256], F32), mk("exps", [128, 1024], F32), mk("tmpd", [128, 256], F32, 2)
            opb_ = mk("opb", [128, 7, 256], BF16)
            fTA_, fTB_ = mk("fTA", [64, 1024], BF16), mk("fTB", [64, 1024], BF16)
            AB_, AK_, NTa_ = mk("AB", [128, 1024], BF16), mk("AK", [128, 1024], BF16), mk("NTa", [128, 512], BF16, 3)
            Nn_ = mk("Nn", [128, 512], BF16, 3)
            Z_ = mk("Z", [128, 512], BF16, 3)
            gcol_, Q_, Pm_, Dm_ = mk("gcol", [64, 4], F32), mk("Q", [64, 512], F32), mk("Pm", [64, 256], F32), mk("Dm", [64, 256], F32)
            osb_, on_, yo_ = mk("osb", [128, 256], F32), mk("on", [128, 256], F32), mk("yod", [128, 256], BF16)
            banks, pbk = lane_banks(c, d)
            bank_i = [0]

            def nb():
                bank_i[0] = (bank_i[0] + 1) % 2
                return banks[bank_i[0]]

            def body(i):
                rows = slice(i * 128, (i + 1) * 128)
                v3 = lambda ap, dd=64: ap.rearrange("p (h d) -> p h d", d=dd)
                dbg_on = False

                def dbg(name, t, shape, dt, rs):
                    if dbg_on:
                        o = nc.dram_tensor("dbg_" + name, list(shape), dt, kind="ExternalOutput").ap()
                        P.dma("pool", o, t, reads=rs, writes=[Res()])
                cur, r_cur = cur_.next()
                sh, r_sh = sh_.next()
                nbr = [c.r["rkvl"][j] for j in (i - 1, i, i + 1) if 0 <= j < NT] + [c.r_pad]
                P.dma("sp", cur[:], c.rkvl[1 + i * 128:1 + (i + 1) * 128, :], reads=[c.r["rkvl"][i]], writes=[r_cur])
                so = 0 if d == 0 else 2
                P.dma("sp", sh[:], c.rkvl[so + i * 128:so + (i + 1) * 128, :], reads=nbr, writes=[r_sh])
                xr, r_xr = xr_.next()
                xl, r_xl = xl_.next()
                P.op("pool", lambda e: e.tensor_tensor(out=xr[:], in0=sh[:, 0:768], in1=cur[:, 0:768], op=ALU.subtract), [r_sh, r_cur], [r_xr])
                P.op("dve", lambda e: e.tensor_tensor(out=xr[:], in0=xr[:], in1=mu_r[:], op=ALU.mult), [r_xr, r_w], [r_xr])
                P.op("pool", lambda e: e.tensor_tensor(out=xr[:], in0=xr[:], in1=cur[:, 0:768], op=ALU.add), [r_xr, r_cur], [r_xr])
                lo = 768 + 128 * d
                P.op("dve", lambda e: e.tensor_tensor(out=xl[:], in0=sh[:, lo:lo + 128], in1=cur[:, lo:lo + 128], op=ALU.subtract), [r_sh, r_cur], [r_xl])
                P.op("dve", lambda e: e.tensor_tensor(out=xl[:], in0=xl[:], in1=mu_l[:], op=ALU.mult), [r_xl, r_w], [r_xl])
                P.op("dve", lambda e: e.tensor_tensor(out=xl[:], in0=xl[:], in1=cur[:, lo:lo + 128], op=ALU.add), [r_xl, r_cur], [r_xl])
                yield
                r_, k_, v_ = xr[:, 0:256], xr[:, 256:512], xr[:, 512:768]
                tl, r_tl = tl_.next()
                tlT, r_tlT = tlT_.next()
                P.op("act", lambda e: e.activation(out=tl[:, 0:64], in_=xl[:, 0:64], func=AF.Tanh), [r_xl], [r_tl])
                P.op("act", lambda e: e.copy(out=tl[:, 64:128], in_=xl[:, 64:128]), [r_xl], [r_tl])
                pb, r_pb = pbk
                P.op("pe", lambda e: e.transpose(out=pb[:, 0:128], in_=tl[:], identity=c.idb[:]), [r_tl, c.r_const], [r_pb])
                P.op("act", lambda e: e.copy(out=tlT[:], in_=pb[:, 0:128]), [r_pb], [r_tlT])
                yield
                ps, r_ps = nb()
                P.op("pe", lambda e: e.matmul(out=ps[:, :], lhsT=tlT[:], rhs=Wud[:], start=True, stop=True), [r_tlT, r_w], [r_ps])
                sg, r_sg = sg_.next()
                lw, r_lw = lw_.next()
                P.op("dve", lambda e: e.tensor_tensor(out=sg[:], in0=ps[:, :], in1=bias_wa[:], op=ALU.add), [r_ps, r_w], [r_sg])
                P.op("act", lambda e: e.activation(out=sg[:], in_=sg[:], func=AF.Sigmoid), [r_sg], [r_sg])
                P.op("act", lambda e: e.mul(out=lw[:], in_=sg[:, 0:256], mul=NEG), [r_sg], [r_lw])
                yield
                a_ = sg[:, 256:512]
                kk, r_kk = kk_.next()
                sq, r_sq = sq_.next()
                s4, r_s4 = s4_.next()
                P.op("dve", lambda e: e.tensor_tensor(out=kk[:], in0=k_, in1=kkv[:], op=ALU.mult), [r_xr, r_w], [r_kk])
                P.op("pool", lambda e: e.tensor_tensor(out=sq[:], in0=kk[:], in1=kk[:], op=ALU.mult), [r_kk], [r_sq])
                P.op("dve", lambda e: e.tensor_reduce(out=s4[:, 0:4], in_=v3(sq[:]), axis=AX.X, op=ALU.add), [r_sq], [r_s4])
                P.op("dve", lambda e: e.tensor_scalar(out=s4[:, 4:8], in0=s4[:, 0:4], scalar1=1e-12, scalar2=None, op0=ALU.add), [r_s4], [r_s4])
                P.op("act", lambda e: e.activation(out=s4[:, 0:4], in_=s4[:, 4:8], func=AF.Sqrt), [r_s4], [r_s4])
                P.op("dve", lambda e: e.reciprocal(out=s4[:, 8:12], in_=s4[:, 0:4]), [r_s4], [r_s4])
                P.op("dve", lambda e: e.tensor_tensor(out=v3(kk[:]), in0=v3(kk[:]), in1=s4[:, 8:12].unsqueeze(2).to_broadcast([128, 4, 64]), op=ALU.mult), [r_kk, r_s4], [r_kk])
                k2, r_k2 = k2_.next()
                P.op("dve", lambda e: e.scalar_tensor_tensor(out=k2[:], in0=a_, scalar=-1.0, in1=kav[:], op0=ALU.add, op1=ALU.mult), [r_sg, r_w], [r_k2])
                P.op("dve", lambda e: e.scalar_tensor_tensor(out=k2[:], in0=k2[:], scalar=1.0, in1=k_, op0=ALU.add, op1=ALU.mult), [r_k2, r_xr], [r_k2])
                bon, r_bon = bon_.next()
                P.op("pool", lambda e: e.tensor_tensor(out=bon[:], in0=r_, in1=k2[:], op=ALU.mult), [r_xr, r_k2], [r_bon])
                P.op("pool", lambda e: e.tensor_tensor(out=bon[:], in0=bon[:], in1=rkp[:], op=ALU.mult), [r_bon, r_w], [r_bon])
                P.op("dve", lambda e: e.tensor_reduce(out=s4[:, 12:16], in_=v3(bon[:]), axis=AX.X, op=ALU.add), [r_bon], [r_s4])
                P.op("dve", lambda e: e.tensor_tensor(out=v3(bon[:]), in0=v3(v_), in1=s4[:, 12:16].unsqueeze(2).to_broadcast([128, 4, 64]), op=ALU.mult), [r_xr, r_s4], [r_bon])
                beta, r_beta = beta_.next()
                P.op("pool", lambda e: e.tensor_tensor(out=beta[:], in0=kk[:], in1=a_, op=ALU.mult), [r_kk, r_sg], [r_beta])
                yield
                pc, r_pc = nb()
                P.op("pe", lambda e: e.matmul(out=pc[:, 0:256], lhsT=incl, rhs=lw[:], start=True, stop=True), [r_lw, r_w], [r_pc])
                P.op("pe", lambda e: e.matmul(out=pc[:, 256:512], lhsT=ones_m, rhs=lw[:], start=True, stop=True), [r_lw, r_w], [r_pc])
                pg, r_pg = nb()
                for h in range(4):
                    P.op("pe", lambda e, h=h: e.matmul(out=pg[0:64, h:h + 1], lhsT=lw[:, h * 64:(h + 1) * 64], rhs=ones_c, start=True, stop=True), [r_lw, r_w], [r_pg])
                gcol, r_gcol = gcol_.next()
                P.op("act", lambda e: e.activation(out=gcol[:], in_=pg[0:64, 0:4], func=AF.Exp), [r_pg], [r_gcol])
                yield
                cum, r_cum = cum_.next()
                ex, r_ex = ex_.next()
                t3, r_t3 = tmp_.next()
                t4, r_t4 = tmp_.next()
                ecum, encum, eexc, edec = ex[:, 0:256], ex[:, 256:512], ex[:, 512:768], ex[:, 768:1024]
                P.op("act", lambda e: e.copy(out=cum[:], in_=pc[:, 0:256]), [r_pc], [r_cum])
                P.op("act", lambda e: e.activation(out=ecum, in_=pc[:, 0:256], func=AF.Exp), [r_pc], [r_ex])
                P.op("act", lambda e: e.activation(out=encum, in_=pc[:, 0:256], func=AF.Exp, scale=-1.0), [r_pc], [r_ex])
                P.op("dve", lambda e: e.tensor_tensor(out=t3[:], in0=cum[:], in1=lw[:], op=ALU.subtract), [r_cum, r_lw], [r_t3])
                P.op("act", lambda e: e.activation(out=eexc, in_=t3[:], func=AF.Exp), [r_t3], [r_ex])
                P.op("dve", lambda e: e.tensor_tensor(out=t4[:], in0=pc[:, 256:512], in1=cum[:], op=ALU.subtract), [r_pc, r_cum], [r_t4])
                P.op("act", lambda e: e.activation(out=edec, in_=t4[:], func=AF.Exp), [r_t4], [r_ex])
                yield
                opb, r_opb = opb_.next()
                P.op("dve", lambda e: e.tensor_tensor(out=opb[:, 0, :], in0=r_, in1=ecum, op=ALU.mult), [r_xr, r_ex], [r_opb])
                P.op("pool", lambda e: e.tensor_tensor(out=opb[:, 1, :], in0=k2[:], in1=encum, op=ALU.mult), [r_k2, r_ex], [r_opb])
                P.op("dve", lambda e: e.tensor_tensor(out=opb[:, 2, :], in0=beta[:], in1=encum, op=ALU.mult), [r_beta, r_ex], [r_opb])
                P.op("dve", lambda e: e.scalar_tensor_tensor(out=opb[:, 3, :], in0=kk[:], scalar=-1.0, in1=eexc, op0=ALU.mult, op1=ALU.mult), [r_kk, r_ex], [r_opb])
                P.op("pool", lambda e: e.tensor_tensor(out=opb[:, 4, :], in0=k2[:], in1=edec, op=ALU.mult), [r_k2, r_ex], [r_opb])
                P.op("pool", lambda e: e.tensor_tensor(out=opb[:, 5, :], in0=beta[:], in1=edec, op=ALU.mult), [r_beta, r_ex], [r_opb])
                P.op("act", lambda e: e.copy(out=opb[:, 6, :], in_=v_), [r_xr], [r_opb])
                yield
                rb, kt, bt, ab, Kh, Bh, vb = (opb[:, q, :] for q in range(7))
                pa, r_pa = pbk
                fTA, r_fTA = fTA_.next()
                fTB, r_fTB = fTB_.next()
                for h in range(4):
                    hs = slice(h * 64, (h + 1) * 64)
                    P.op("pe", lambda e, h=h, hs=hs: e.transpose(out=pa[0:64, h * 128:(h + 1) * 128], in_=bt[:, hs], identity=c.idb[:]), [r_opb, c.r_const], [r_pa])
                    P.op("pe", lambda e, h=h, hs=hs: e.transpose(out=pa[0:64, (4 + h) * 128:(5 + h) * 128], in_=kt[:, hs], identity=c.idb[:]), [r_opb, c.r_const], [r_pa])
                P.op("act", lambda e: e.copy(out=fTA[:], in_=pa[0:64, :]), [r_pa], [r_fTA])
                yield
                for h in range(4):
                    hs = slice(h * 64, (h + 1) * 64)
                    P.op("pe", lambda e, h=h, hs=hs: e.transpose(out=pa[0:64, (2 * h) * 128:(2 * h + 1) * 128], in_=ab[:, hs], identity=c.idb[:]), [r_opb, c.r_const], [r_pa])
                    P.op("pe", lambda e, h=h, hs=hs: e.transpose(out=pa[0:64, (2 * h + 1) * 128:(2 * h + 2) * 128], in_=rb[:, hs], identity=c.idb[:]), [r_opb, c.r_const], [r_pa])
                P.op("dve", lambda e: e.tensor_copy(out=fTB[:], in_=pa[0:64, :]), [r_pa], [r_fTB])
                yield
                AB, r_AB = AB_.next()
                AK, r_AK = AK_.next()
                for (dst, r_dst, off) in ((AB, r_AB, 0), (AK, r_AK, 4)):
                    for pr in range(2):
                        px, r_px = nb()
                        for hh in range(2):
                            h = 2 * pr + hh
                            P.op("pe", lambda e, h=h, hh=hh, px=px, off=off: e.matmul(out=px[:, hh * 256:(hh + 1) * 256], lhsT=fTA[:, (off + h) * 128:(off + h + 1) * 128],
                                                                               rhs=fTB[:, 2 * h * 128:(2 * h + 2) * 128], start=True, stop=True), [r_fTA, r_fTB], [r_px])
                        P.op("dve", lambda e, pr=pr, px=px, dst=dst: e.tensor_tensor(out=dst[:, pr * 512:(pr + 1) * 512], in0=px[:, :], in1=mask4[:], op=ALU.mult), [r_px, r_w], [r_dst])
                        yield
                py, r_py = nb()
                for h in range(4):
                    P.op("pe", lambda e, h=h: e.matmul(out=py[:, h * 128:(h + 1) * 128], lhsT=fTB[:, 2 * h * 128:(2 * h + 1) * 128], rhs=fTA[:, h * 128:(h + 1) * 128],
                                                       start=True, stop=True), [r_fTA, r_fTB], [r_py])
                NTc, r_NTc = NTa_.next()
                P.op("dve", lambda e: e.tensor_tensor(out=NTc[:], in0=py[:, :], in1=maskT4[:], op=ALU.mult), [r_py, r_w], [r_NTc])
                yield
                pz, r_pz = nb()
                for h in range(4):
                    P.op("pe", lambda e, h=h: e.matmul(out=pz[:, h * 64:(h + 1) * 64], lhsT=AK[:, h * 256:h * 256 + 128], rhs=vb[:, h * 64:(h + 1) * 64], start=True, stop=True),
                         [r_AK, r_opb], [r_pz])
                Z, r_Z = Z_.next()
                P.op("dve", lambda e, Z=Z: e.tensor_copy(out=v3(Z[:], 128)[:, :, 0:64], in_=v3(ab)), [r_opb], [r_Z])
                P.op("act", lambda e, Z=Z: e.copy(out=v3(Z[:], 128)[:, :, 64:128], in_=v3(pz[:, 0:256])), [r_pz], [r_Z])
                yield
                dbg("Z0", Z[:], [128, 512], BF16, [r_Z]); dbg("NT0", NTc[:], [128, 512], BF16, [r_NTc]); dbg("AK", AK[:], [128, 1024], BF16, [r_AK])
                Ncur = [AB[:, h * 256:h * 256 + 128] for h in range(4)]
                r_N = r_AB
                for kq in range(7):
                    pq, r_pq = nb()
                    for h in range(4):
                        P.op("pe", lambda e, h=h, N=Ncur[h], Z=Z, pq=pq: e.matmul(out=pq[:, h * 128:(h + 1) * 128], lhsT=N, rhs=Z[:, h * 128:(h + 1) * 128], start=True, stop=True),
                             [r_N, r_Z], [r_pq])
                    Zn, r_Zn = Z_.next()
                    P.op("dve", lambda e, Z=Z, Zn=Zn, pq=pq: e.tensor_tensor(out=Zn[:], in0=pq[:, :], in1=Z[:], op=ALU.add), [r_pq, r_Z], [r_Zn])
                    yield
                    if kq < 6:
                        p1, r_p1 = nb()
                        for h in range(4):
                            P.op("pe", lambda e, h=h, N=Ncur[h], NTc=NTc, p1=p1: e.matmul(out=p1[:, h * 128:(h + 1) * 128], lhsT=NTc[:, h * 128:(h + 1) * 128], rhs=N, start=True, stop=True),
                                 [r_N, r_NTc], [r_p1])
                        p2, r_p2 = nb()
                        for h in range(4):
                            P.op("pe", lambda e, h=h, N=Ncur[h], NTc=NTc, p2=p2: e.matmul(out=p2[:, h * 128:(h + 1) * 128], lhsT=N, rhs=NTc[:, h * 128:(h + 1) * 128], start=True, stop=True),
                                 [r_N, r_NTc], [r_p2])
                        Nn, r_Nn = Nn_.next()
                        NTn, r_NTn = NTa_.next()
                        P.op("act", lambda e, Nn=Nn, p1=p1: e.copy(out=Nn[:], in_=p1[:, :]), [r_p1], [r_Nn])
                        P.op("dve", lambda e, NTn=NTn, p2=p2: e.tensor_copy(out=NTn[:], in_=p2[:, :]), [r_p2], [r_NTn])
                        yield
                        dbg(f"Nn{kq}", Nn[:], [128, 512], BF16, [r_Nn]); dbg(f"NTn{kq}", NTn[:], [128, 512], BF16, [r_NTn]); dbg(f"Zn{kq}", Zn[:], [128, 512], BF16, [r_Zn])
                        Ncur = [Nn[:, h * 128:(h + 1) * 128] for h in range(4)]
                        r_N, NTc, r_NTc = r_Nn, NTn, r_NTn
                    Z, r_Z = Zn, r_Zn
                WT = [Z[:, h * 128:h * 128 + 64] for h in range(4)]
                XT = [Z[:, h * 128 + 64:(h + 1) * 128] for h in range(4)]
                pQ, r_pQ = nb()
                for h in range(4):
                    P.op("pe", lambda e, h=h: e.matmul(out=pQ[0:64, h * 128:(h + 1) * 128], lhsT=WT[h], rhs=AB[:, h * 256 + 128:(h + 1) * 256], start=True, stop=True),
                         [r_Z, r_AB], [r_pQ])
                Q, r_Q = Q_.next()
                P.op("dve", lambda e: e.tensor_tensor(out=v3(Q[:], 128), in0=v3(pQ[0:64, :], 128), in1=fTB[:].rearrange("p (h two t) -> p h two t", two=2, t=128)[:, :, 1, :], op=ALU.add),
                     [r_pQ, r_fTB], [r_Q])
                yield
                pP, r_pP = nb()
                for h in range(4):
                    P.op("pe", lambda e, h=h: e.matmul(out=pP[0:64, h * 64:(h + 1) * 64], lhsT=WT[h], rhs=Bh[:, h * 64:(h + 1) * 64], start=True, stop=True), [r_Z, r_opb], [r_pP])
                Pm, r_Pm = Pm_.next()
                P.op("dve", lambda e: e.tensor_tensor(out=v3(Pm[:]), in0=identB[:], in1=gcol[:].unsqueeze(2).to_broadcast([64, 4, 64]), op=ALU.mult), [r_w, r_gcol], [r_Pm])
                P.op("dve", lambda e: e.tensor_tensor(out=Pm[:], in0=Pm[:], in1=pP[0:64, 0:256], op=ALU.add), [r_Pm, r_pP], [r_Pm])
                yield
                pD, r_pD = nb()
                for h in range(4):
                    P.op("pe", lambda e, h=h: e.matmul(out=pD[0:64, h * 64:(h + 1) * 64], lhsT=Bh[:, h * 64:(h + 1) * 64], rhs=XT[h], start=(h == 0), stop=False, skip_group_check=True),
                         [r_Z, r_opb], [r_pD])
                    P.op("pe", lambda e, h=h: e.matmul(out=pD[0:64, h * 64:(h + 1) * 64], lhsT=Kh[:, h * 64:(h + 1) * 64], rhs=vb[:, h * 64:(h + 1) * 64], start=False, stop=True, skip_group_check=True),
                         [r_opb], [r_pD])
                Dm, r_Dm = Dm_.next()
                P.op("act", lambda e: e.copy(out=Dm[:], in_=pD[0:64, 0:256]), [r_pD], [r_Dm])
                yield
                S0, r_S0 = state
                pO, r_pO = banks[2]
                for h in range(4):
                    P.op("pe", lambda e, h=h: e.matmul(out=pO[:, h * 64:(h + 1) * 64], lhsT=AB[:, h * 256 + 128:(h + 1) * 256], rhs=XT[h], start=(h == 0), stop=False, skip_group_check=True),
                         [r_AB, r_Z], [r_pO])
                    P.op("pe", lambda e, h=h: e.matmul(out=pO[:, h * 64:(h + 1) * 64], lhsT=AK[:, h * 256 + 128:(h + 1) * 256], rhs=vb[:, h * 64:(h + 1) * 64], start=False, stop=False, skip_group_check=True),
                         [r_AK, r_opb], [r_pO])
                    P.op("pe", lambda e, h=h, S0=S0: e.matmul(out=pO[:, h * 64:(h + 1) * 64], lhsT=Q[:, h * 128:(h + 1) * 128], rhs=S0[:, h * 64:(h + 1) * 64], start=False, stop=True, skip_group_check=True),
                         [r_Q, r_S0], [r_pO])
                osb, r_osb = osb_.next()
                P.op("act", lambda e: e.copy(out=osb[:], in_=pO[:, 0:256]), [r_pO], [r_osb])
                yield
                pS, r_pS = banks[2]
                for h in range(4):
                    P.op("pe", lambda e, h=h, S0=S0: e.matmul(out=pS[0:64, h * 64:(h + 1) * 64], lhsT=Pm[:, h * 64:(h + 1) * 64], rhs=S0[:, h * 64:(h + 1) * 64], start=True, stop=True),
                         [r_Pm, r_S0], [r_pS])
                S1, r_S1 = ST.next()
                P.op("dve", lambda e, S1=S1: e.tensor_tensor(out=S1[:], in0=pS[0:64, 0:256], in1=Dm[:], op=ALU.add), [r_pS, r_Dm], [r_S1])
                state[0], state[1] = S1, r_S1
                yield
                on, r_on = on_.next()
                P.op("dve", lambda e: e.tensor_reduce(out=s4[:, 16:20], in_=v3(osb[:]), axis=AX.X, op=ALU.add), [r_osb], [r_s4])
                P.op("pool", lambda e: e.tensor_tensor(out=on[:], in0=osb[:], in1=osb[:], op=ALU.mult), [r_osb], [r_on])
                P.op("dve", lambda e: e.tensor_reduce(out=s4[:, 20:24], in_=v3(on[:]), axis=AX.X, op=ALU.add), [r_on], [r_s4])
                P.op("dve", lambda e: e.tensor_scalar(out=s4[:, 16:20], in0=s4[:, 16:20], scalar1=1.0 / 64, scalar2=None, op0=ALU.mult), [r_s4], [r_s4])
                P.op("dve", lambda e: e.tensor_tensor(out=s4[:, 24:28], in0=s4[:, 16:20], in1=s4[:, 16:20], op=ALU.mult), [r_s4], [r_s4])
                P.op("dve", lambda e: e.scalar_tensor_tensor(out=s4[:, 20:24], in0=s4[:, 20:24], scalar=1.0 / 64, in1=s4[:, 24:28], op0=ALU.mult, op1=ALU.subtract), [r_s4], [r_s4])
                P.op("dve", lambda e: e.tensor_scalar(out=s4[:, 20:24], in0=s4[:, 20:24], scalar1=GN_EPS, scalar2=None, op0=ALU.add), [r_s4], [r_s4])
                P.op("act", lambda e: e.activation(out=s4[:, 24:28], in_=s4[:, 20:24], func=AF.Sqrt), [r_s4], [r_s4])
                P.op("dve", lambda e: e.reciprocal(out=s4[:, 28:32], in_=s4[:, 24:28]), [r_s4], [r_s4])
                P.op("dve", lambda e: e.tensor_tensor(out=v3(on[:]), in0=v3(osb[:]), in1=s4[:, 16:20].unsqueeze(2).to_broadcast([128, 4, 64]), op=ALU.subtract), [r_osb, r_s4], [r_on])
                P.op("dve", lambda e: e.tensor_tensor(out=v3(on[:]), in0=v3(on[:]), in1=s4[:, 28:32].unsqueeze(2).to_broadcast([128, 4, 64]), op=ALU.mult), [r_on, r_s4], [r_on])
                P.op("pool", lambda e: e.tensor_tensor(out=on[:], in0=on[:], in1=lng[:], op=ALU.mult), [r_on, r_w], [r_on])
                P.op("pool", lambda e: e.tensor_tensor(out=on[:], in0=on[:], in1=lnb[:], op=ALU.add), [r_on, r_w], [r_on])
                P.op("pool", lambda e: e.tensor_tensor(out=on[:], in0=on[:], in1=bon[:], op=ALU.add), [r_on, r_bon], [r_on])
                P.dma("pool", c.osc[d, rows, :], on[:], reads=[r_on], writes=[c.r["osc%d" % d][i]])

            return body

    with ExitStack() as st:
        bodies = [setup_dir(d, st) for d in range(2)]
        run_lanes([list(range(NT)), list(range(NT - 1, -1, -1))], lambda i, k: bodies[k](i), skew=RWKV_SKEW)
    barrier(c)
    with ExitStack() as st:
        def mkpools(k):
            mk = lambda name, shape, dt, n=2: Pool2(st, nc, f"{name}_l{k}_", shape, dt, n)
            return dict(o0=mk("e_o0", [128, 256], F32), o1=mk("e_o1", [128, 256], F32), zd=mk("e_zd", [128, 256], F32), yo=mk("e_yo", [128, 256], BF16))
        pools = [mkpools(k) for k in range(2)]

        def body_e(i, k):
            pl = pools[k]
            rows = slice(i * 128, (i + 1) * 128)
            o0, r_o0 = pl["o0"].next()
            o1, r_o1 = pl["o1"].next()
            zd, r_zd = pl["zd"].next()
            yo, r_yo = pl["yo"].next()
            P.dma("sp", o0[:], c.osc[0, rows, :], reads=[c.r["osc0"][i]], writes=[r_o0])
            P.dma("sp", o1[:], c.osc[1, rows, :], reads=[c.r["osc1"][i]], writes=[r_o1])
            P.dma("sp", zd[:], c.zs[rows, 768:1024], reads=[c.r["zs"][i]], writes=[r_zd])
            yield
            P.op("dve", lambda e: e.tensor_tensor(out=o0[:], in0=o0[:], in1=o1[:], op=ALU.add), [r_o0, r_o1], [r_o0])
            P.op("dve", lambda e: e.tensor_tensor(out=yo[:], in0=o0[:], in1=zd[:], op=ALU.mult), [r_o0, r_zd], [r_yo])
            P.dma("pool", c.ys[rows, 768:1024], yo[:], reads=[r_yo], writes=[c.r["ys"][i]])

        run_lanes([list(range(k, NT, 2)) for k in range(2)], body_e, skew=1)


def host_consts(S, seq_len):
    NT = S // 128
    t = np.arange(S)
    valid = (t < seq_len).astype(np.float32).reshape(NT, 128).T.copy()
    invc = np.ones((S, 4), np.float32)
    for g, w in enumerate((2, 4, 8, 16)):
        h = w // 2
        cnt = np.minimum(t + h, seq_len) - np.maximum(t - h, 0)
        invc[:, g] = np.where(t < seq_len, 1.0 / np.maximum(cnt, 1), 1.0)
    invc = invc.reshape(NT, 128, 4).transpose(1, 0, 2).reshape(128, NT * 4).copy()
    ident = np.eye(128, dtype=np.float32)
    s = np.arange(128)[:, None]
    tt = np.arange(128)[None, :]
    bandc = np.zeros((128, 4, 128), np.float32)
    bandh = np.zeros((16, 4, 128), np.float32)
    for g, w in enumerate((2, 4, 8, 16)):
        h = w // 2
        bandc[:, g, :] = ((s >= tt - h) & (s <= tt + h - 1))
        r = np.arange(16)[:, None]
        srel = np.where(r < 8, r - 8, 128 + (r - 8))
        bandh[:, g, :] = ((srel >= tt - h) & (srel <= tt + h - 1))
    slopes = 2.0 ** (-8.0 * np.arange(1, 13) / 12.0)
    amask = np.zeros((25, 128, 4, 128), np.float32)
    ci = 0
    key = np.arange(128)[:, None]
    q = np.arange(128)[None, :]
    for g in range(3):
        d = ATT_D[g]
        for coff in range(-ATT_W[g], ATT_W[g] + 1):
            delta = coff * 128 + key - q
            ok = (delta % d == 0) & (np.abs(delta) <= 64 * d)
            for h in range(4):
                amask[ci, :, h, :] = np.where(ok, np.exp(-slopes[g * 4 + h] * np.abs(delta)), 0.0)
            ci += 1
    amask3 = np.zeros((3, 128, 4, 128), np.float32)
    for ci3, coff in enumerate((-1, 0, 1)):
        dm = coff * 128 + key - q
        ok = np.abs(dm) <= 64
        for h in range(4):
            amask3[ci3, :, h, :] = np.where(ok, np.exp(-slopes[8 + h] * 16.0 * np.abs(dm)), 0.0)
    tri = np.zeros((128, 642), np.float32)
    tri[:, 0:128] = (s <= tt)
    tri[:, 128:256] = (s >= tt)
    tri[:, 256:384] = (s < tt)
    tri[:, 384:512] = (s > tt)
    tri[:, 512:642] = 1.0
    return dict(valid=valid, invc=invc, ident=ident, amask=amask.reshape(25, 128, 512), amask3=amask3.reshape(3, 128, 512), bandc=bandc.reshape(128, 512),
                bandh=bandh.reshape(16, 512), tri=tri)


def host_weights(inp):
    w = {}
    for k in ("norm_g", "w_in", "q_norm_g", "k_norm_g", "pool_w", "pool_scale", "sg_norm_g", "mu_rkv", "mu_lat", "w0", "w_up",
              "a0", "a_up", "k_k", "k_a", "r_k", "ln_g", "ln_b", "w_branch", "w_out"):
        w[k] = np.ascontiguousarray(np.asarray(inp[k], dtype=np.float32))
    w["sg_wT"] = np.ascontiguousarray(np.transpose(np.asarray(inp["sg_w"], np.float32), (0, 1, 3, 2)))
    w["sg_bT"] = np.ascontiguousarray(np.transpose(np.asarray(inp["sg_b"], np.float32), (0, 2, 1)))
    return w


_NC_CACHE = {}


def run_sequences(seqs, S, inp, branches=(0, 1, 2, 3), L=2, n_cores=8):
    key = (S, L, tuple(branches))
    if key not in _NC_CACHE:
        _NC_CACHE[key] = build_program(S, L=L, branches=branches)
    nc = _NC_CACHE[key]
    w = host_weights(inp)
    in_maps = []
    for ci in range(n_cores):
        if ci < len(seqs):
            x = np.zeros((S, D), np.float32)
            x[:seqs[ci].shape[0]] = seqs[ci]
            m = dict(x=x, **host_consts(S, seqs[ci].shape[0]))
        else:
            m = dict(x=np.zeros((S, D), np.float32), **host_consts(S, 0))
        m.update(w)
        in_maps.append(m)
    res = run_bass_kernel_spmd(nc, in_maps, core_ids=list(range(n_cores)))
    if DEBUG_SCRATCH:
        global LAST_RESULTS
        LAST_RESULTS = res.results
    return [res.results[ci]["y"][:seqs[ci].shape[0]] for ci in range(len(seqs))]


def kernel(**inputs):
    xp = np.asarray(inputs["x_prompt"], np.float32)
    xs = np.asarray(inputs["x_sample"], np.float32)
    S = xs.shape[1]
    seqs = [xp[b] for b in range(xp.shape[0])] + [xs[b] for b in range(xs.shape[0])]
    outs = run_sequences(seqs, S, inputs)
    nb = xp.shape[0]
    y_prompt = np.stack(outs[:nb], 0).astype(np.float32)
    y_sample = np.stack(outs[nb:], 0).astype(np.float32)
    return (y_prompt, y_sample)
```
